# Optimizing a Trainium2 kernel written in Bass

```python
import math
import jax
import jax.numpy as jnp
from jax import lax
import numpy as np

D_MODEL = 2048
BATCH = 4
SEQ = 4096
DEPTH = 2

GRID_W = 64
N_HEADS = 16
HEAD_DIM = D_MODEL // N_HEADS
WIN_H = 8
WIN_W = 16
S5_GROUP = 16
S5_GROUPS = D_MODEL // S5_GROUP
S5_STATE = 64
D_FF = ((8 * D_MODEL // 3 + 127) // 128) * 128
N_MIXERS = 2
N_A = (DEPTH + 1) // 2
N_B = DEPTH // 2
ALPHA = (2 * DEPTH) ** 0.25
BETA = (8 * DEPTH) ** -0.25
LN_EPS = 1e-5
MIN_NEG_RE = -1e-4

kernel_name = "hybrid_s5_natten_macaron_deepnorm"


def layer_norm(x, g, b):
    xf = x.astype(jnp.float32)
    mu = jnp.mean(xf, axis=-1, keepdims=True)
    xc = xf - mu
    var = jnp.mean(xc * xc, axis=-1, keepdims=True)
    y = xc * lax.rsqrt(var + LN_EPS) * g.astype(jnp.float32) + b.astype(jnp.float32)
    return y.astype(x.dtype)


def swiglu(x, w_gate, w_up, w_down):
    return (jax.nn.silu(x @ w_gate) * (x @ w_up)) @ w_down


def _complex_combine(left, right):
    a1r, a1i, b1r, b1i = left
    a2r, a2i, b2r, b2i = right
    ar = a2r * a1r - a2i * a1i
    ai = a2r * a1i + a2i * a1r
    br = a2r * b1r - a2i * b1i + b2r
    bi = a2r * b1i + a2i * b1r + b2i
    return (ar, ai, br, bi)


def s5_direction(u, lam_re, lam_im, log_dt, b_re, b_im, c_re, c_im, reverse):
    L = u.shape[1]
    lr = jnp.minimum(lam_re.astype(jnp.float32), MIN_NEG_RE)
    li = lam_im.astype(jnp.float32)
    dt = jnp.exp(log_dt.astype(jnp.float32))[:, None]
    mag = jnp.exp(lr * dt)
    lb_re = mag * jnp.cos(li * dt)
    lb_im = mag * jnp.sin(li * dt)
    den = lr * lr + li * li
    nr = lb_re - 1.0
    ni = lb_im
    f_re = (nr * lr + ni * li) / den
    f_im = (ni * lr - nr * li) / den
    br = b_re.astype(jnp.float32)
    bi = b_im.astype(jnp.float32)
    bb_re = f_re[:, :, None] * br - f_im[:, :, None] * bi
    bb_im = f_re[:, :, None] * bi + f_im[:, :, None] * br
    bu_re = jnp.einsum('blgc,gpc->blgp', u, bb_re)
    bu_im = jnp.einsum('blgc,gpc->blgp', u, bb_im)
    a_re = jnp.broadcast_to(lb_re[None, None], (1, L) + lb_re.shape)
    a_im = jnp.broadcast_to(lb_im[None, None], (1, L) + lb_im.shape)
    _, _, s_re, s_im = lax.associative_scan(
        _complex_combine, (a_re, a_im, bu_re, bu_im), reverse=reverse, axis=1)
    return (jnp.einsum('blgp,gcp->blgc', s_re, c_re.astype(jnp.float32))
            - jnp.einsum('blgp,gcp->blgc', s_im, c_im.astype(jnp.float32)))


def s5_mixer(x, lam_re, lam_im, log_dt, b_re, b_im, c_re, c_im, d_skip, w_val, w_gate):
    Bsz, L, D = x.shape
    u = x.astype(jnp.float32).reshape(Bsz, L, S5_GROUPS, S5_GROUP)
    y_f = s5_direction(u, lam_re[0], lam_im[0], log_dt[0], b_re[0], b_im[0],
                       c_re[0], c_im[0], reverse=False)
    y_b = s5_direction(u, lam_re[1], lam_im[1], log_dt[1], b_re[1], b_im[1],
                       c_re[1], c_im[1], reverse=True)
    y = (y_f + y_b).reshape(Bsz, L, D) + d_skip.astype(jnp.float32) * x.astype(jnp.float32)
    g = jax.nn.gelu(y).astype(x.dtype)
    return (g @ w_val) * jax.nn.sigmoid(g @ w_gate)


def natten_mixer(x, w_qkv, rpb, w_out):
    Bsz, L, D = x.shape
    rows = L // GRID_W
    kh = min(WIN_H, rows)
    qkv = (x @ w_qkv).reshape(Bsz, rows, GRID_W, 3, N_HEADS, HEAD_DIM)
    q = qkv[:, :, :, 0] * (HEAD_DIM ** -0.5)
    k = qkv[:, :, :, 1]
    v = qkv[:, :, :, 2]
    cols = np.arange(GRID_W)
    col_start = np.clip(cols - WIN_W // 2, 0, GRID_W - WIN_W)
    col_idx = col_start[:, None] + np.arange(WIN_W)[None, :]
    col_off = col_idx - cols[:, None] + (WIN_W - 1)
    win_rows = jnp.arange(kh)

    def one_row(r):
        rs = jnp.clip(r - kh // 2, 0, rows - kh)
        k_rows = lax.dynamic_slice_in_dim(k, rs, kh, axis=1)
        v_rows = lax.dynamic_slice_in_dim(v, rs, kh, axis=1)
        k_win = k_rows[:, :, col_idx]
        v_win = v_rows[:, :, col_idx]
        q_r = lax.dynamic_index_in_dim(q, r, axis=1, keepdims=False)
        s = jnp.einsum('bchd,bicjhd->bhcij', q_r, k_win).astype(jnp.float32)
        row_off = rs + win_rows - r + (WIN_H - 1)
        bias = rpb[:, row_off][:, :, col_off]
        s = s + jnp.transpose(bias, (0, 2, 1, 3)).astype(jnp.float32)[None]
        p = jax.nn.softmax(s.reshape(Bsz, N_HEADS, GRID_W, kh * WIN_W), axis=-1)
        p = p.reshape(Bsz, N_HEADS, GRID_W, kh, WIN_W).astype(v.dtype)
        return jnp.einsum('bhcij,bicjhd->bchd', p, v_win)

    out = lax.map(one_row, jnp.arange(rows))
    out = jnp.moveaxis(out, 0, 1).reshape(Bsz, L, D)
    return out @ w_out


def setup_inputs(seed: int = 0) -> dict:
    key = jax.random.key(seed)
    ks = jax.random.split(key, 20)
    D, F, G, P, GC = D_MODEL, D_FF, S5_GROUPS, S5_STATE, S5_GROUP
    x = jax.random.normal(ks[0], (BATCH, SEQ, D), jnp.float32)
    ffn_w_gate = jax.random.normal(ks[1], (DEPTH, 2, D, F), jnp.float32) * D ** -0.5
    ffn_w_up = jax.random.normal(ks[2], (DEPTH, 2, D, F), jnp.float32) * D ** -0.5
    ffn_w_down = jax.random.normal(ks[3], (DEPTH, 2, F, D), jnp.float32) * (F ** -0.5 * BETA)
    ln_g = 1.0 + 0.02 * jax.random.normal(ks[4], (DEPTH, 3, D), jnp.float32)
    ln_b = 0.02 * jax.random.normal(ks[5], (DEPTH, 3, D), jnp.float32)
    s5_lam_re = -0.5 + 0.01 * jax.random.normal(ks[6], (N_A, 2, G, P), jnp.float32)
    n_idx = jnp.arange(P, dtype=jnp.float32)
    s5_lam_im = math.pi * n_idx + 0.01 * jax.random.normal(ks[7], (N_A, 2, G, P), jnp.float32)
    s5_log_dt = jax.random.uniform(ks[8], (N_A, 2, G), jnp.float32,
                                   minval=math.log(1e-3), maxval=math.log(1e-1))
    s5_b_re = jax.random.normal(ks[9], (N_A, 2, G, P, GC), jnp.float32) * (2 * GC) ** -0.5
    s5_b_im = jax.random.normal(ks[10], (N_A, 2, G, P, GC), jnp.float32) * (2 * GC) ** -0.5
    s5_c_re = jax.random.normal(ks[11], (N_A, 2, G, GC, P), jnp.float32) * (2 * P) ** -0.5
    s5_c_im = jax.random.normal(ks[12], (N_A, 2, G, GC, P), jnp.float32) * (2 * P) ** -0.5
    s5_d = jax.random.normal(ks[13], (N_A, D), jnp.float32)
    s5_w_glu_val = jax.random.normal(ks[14], (N_A, D, D), jnp.float32) * (D ** -0.5 * BETA)
    s5_w_glu_gate = jax.random.normal(ks[15], (N_A, D, D), jnp.float32) * D ** -0.5
    na_w_qkv = jax.random.normal(ks[16], (N_B, D, 3 * D), jnp.float32) * D ** -0.5
    na_rpb = 0.02 * jax.random.normal(ks[17], (N_B, N_HEADS, 2 * WIN_H - 1, 2 * WIN_W - 1), jnp.float32)
    na_w_out = jax.random.normal(ks[18], (N_B, D, D), jnp.float32) * (D ** -0.5 * BETA)
    return {"x": x, "ffn_w_gate": ffn_w_gate, "ffn_w_up": ffn_w_up, "ffn_w_down": ffn_w_down,
            "ln_g": ln_g, "ln_b": ln_b,
            "s5_lam_re": s5_lam_re, "s5_lam_im": s5_lam_im, "s5_log_dt": s5_log_dt,
            "s5_b_re": s5_b_re, "s5_b_im": s5_b_im, "s5_c_re": s5_c_re, "s5_c_im": s5_c_im,
            "s5_d": s5_d, "s5_w_glu_val": s5_w_glu_val, "s5_w_glu_gate": s5_w_glu_gate,
            "na_w_qkv": na_w_qkv, "na_rpb": na_rpb, "na_w_out": na_w_out}


def reference(x, ffn_w_gate, ffn_w_up, ffn_w_down, ln_g, ln_b,
              s5_lam_re, s5_lam_im, s5_log_dt, s5_b_re, s5_b_im, s5_c_re, s5_c_im,
              s5_d, s5_w_glu_val, s5_w_glu_gate, na_w_qkv, na_rpb, na_w_out):
    for i in range(DEPTH):
        f1 = swiglu(x, ffn_w_gate[i, 0], ffn_w_up[i, 0], ffn_w_down[i, 0])
        x = layer_norm(ALPHA * x + 0.5 * f1, ln_g[i, 0], ln_b[i, 0])
        j = i // N_MIXERS
        if i % N_MIXERS == 0:
            m = s5_mixer(x, s5_lam_re[j], s5_lam_im[j], s5_log_dt[j], s5_b_re[j], s5_b_im[j],
                         s5_c_re[j], s5_c_im[j], s5_d[j], s5_w_glu_val[j], s5_w_glu_gate[j])
        else:
            m = natten_mixer(x, na_w_qkv[j], na_rpb[j], na_w_out[j])
        x = layer_norm(ALPHA * x + m, ln_g[i, 1], ln_b[i, 1])
        f2 = swiglu(x, ffn_w_gate[i, 1], ffn_w_up[i, 1], ffn_w_down[i, 1])
        x = layer_norm(ALPHA * x + 0.5 * f2, ln_g[i, 2], ln_b[i, 2])
    return x
```

```python
import numpy as np
from contextlib import ExitStack

import concourse.bass as bass
import concourse.mybir as mybir
from concourse.bass_utils import run_bass_kernel_spmd

F32 = mybir.dt.float32
BF16 = mybir.dt.bfloat16
ALU = mybir.AluOpType
AF = mybir.ActivationFunctionType

D = 2048
FF = 5504
NFC = FF // 128
NDC = D // 128
DEPTH = 2
ALPHA = (2 * DEPTH) ** 0.25
LN_EPS = 1e-5
NCORES = 8


class Sched:
    def __init__(self, nc, es, n_dma_sems=12):
        self.nc = nc
        self.eng = {"pe": nc.tensor, "act": nc.scalar, "dve": nc.vector, "pool": nc.gpsimd, "sp": nc.sync}
        self.sem = {k: es.enter_context(nc.semaphore("s_" + k)) for k in ["pe", "act", "dve", "pool"]}
        self.cnt = {k: 0 for k in self.sem}
        self.nd = n_dma_sems
        self.dsem = {q: [es.enter_context(nc.semaphore(f"d_{q}{i}")) for i in range(n_dma_sems)]
                     for q in ["sp", "pool"]}
        self.dcnt = {q: [0] * n_dma_sems for q in self.dsem}
        self.dnext = {q: 0 for q in self.dsem}
        self.waited = {e: {} for e in self.eng}
        self.lastw = {}
        self.readers = {}

    def _wait(self, e, tok):
        key, sem, val = tok
        if self.waited[e].get(key, 0) >= val:
            return
        self.eng[e].wait_ge(sem, val)
        self.waited[e][key] = val

    def _deps(self, e, reads, writes):
        toks = []
        for r in reads:
            t = self.lastw.get(r)
            if t is not None:
                toks.append(t)
        for w in writes:
            t = self.lastw.get(w)
            if t is not None:
                toks.append(t)
            toks.extend(self.readers.get(w, {}).values())
        for t in toks:
            if t[0] == "pe" and e == "pe":
                continue
            self._wait(e, t)

    def _record(self, tok, reads, writes):
        for r in reads:
            self.readers.setdefault(r, {})[tok[0]] = tok
        for w in writes:
            self.lastw[w] = tok
            self.readers[w] = {}

    def op(self, e, fn, reads=(), writes=(), signal=True):
        self._deps(e, reads, writes)
        inst = fn()
        if signal:
            self.cnt[e] += 1
            inst.then_inc(self.sem[e], 1)
            tok = (e, self.sem[e], self.cnt[e])
        else:
            tok = (e, self.sem[e], self.cnt[e] + 1)
        self._record(tok, reads, writes)
        return tok

    def dma(self, q, out, in_, reads=(), writes=()):
        i = self.dnext[q]
        self.dnext[q] = (i + 1) % self.nd
        sem = self.dsem[q][i]
        key = f"d_{q}{i}"
        if self.dcnt[q][i]:
            self._wait(q, (key, sem, self.dcnt[q][i] * 16))
        self._deps(q, reads, writes)
        self.eng[q].dma_start(out=out, in_=in_).then_inc(sem, 16)
        self.dcnt[q][i] += 1
        tok = (key, sem, self.dcnt[q][i] * 16)
        self._record(tok, reads, writes)
        return tok

    def barrier(self):
        toks = [(e, self.sem[e], self.cnt[e]) for e in self.sem if self.cnt[e]]
        for q in self.dsem:
            for i in range(self.nd):
                if self.dcnt[q][i]:
                    toks.append((f"d_{q}{i}", self.dsem[q][i], self.dcnt[q][i] * 16))
        for e in self.eng:
            for t in toks:
                if t[0] != e:
                    self._wait(e, t)

    def finish(self):
        for q in self.dsem:
            for i in range(self.nd):
                if self.dcnt[q][i]:
                    self._wait(q, (f"d_{q}{i}", self.dsem[q][i], self.dcnt[q][i] * 16))


_UNIQ = [0]


def _sb(nc, es, name, shape, dt):
    _UNIQ[0] += 1
    return es.enter_context(nc.sbuf_tensor(f"{name}_u{_UNIQ[0]}", shape, dt))


def emit_ffn(nc, S, es, P, x_ap, y_ap, wg, wu, wd, g_rep, b_rep, ident, NT, tag):
    TT = 512
    NS = TT // 128
    FW = 256
    DW = 256
    xs = [_sb(nc, es, f"{tag}_xs{i}", [128, D], F32) for i in range(2)]
    xT = _sb(nc, es, f"{tag}_xT", [128, NDC, TT], BF16)
    wgb = [_sb(nc, es, f"{tag}_wg{i}", [128, NDC, FW], BF16) for i in range(2)]
    wub = [_sb(nc, es, f"{tag}_wu{i}", [128, NDC, FW], BF16) for i in range(2)]
    hT = _sb(nc, es, f"{tag}_hT", [128, NFC, TT], BF16)
    wdb = [_sb(nc, es, f"{tag}_wd{i}", [128, NFC, DW], BF16) for i in range(2)]
    sg = [_sb(nc, es, f"{tag}_sg{i}", [128, TT], F32) for i in range(2)]
    z = [_sb(nc, es, f"{tag}_z{i}", [128, D], F32) for i in range(2)]
    st = [_sb(nc, es, f"{tag}_st{i}", [128, 4, nc.vector.BN_STATS_DIM], F32) for i in range(2)]
    mv = [_sb(nc, es, f"{tag}_mv{i}", [128, nc.vector.BN_AGGR_DIM], F32) for i in range(2)]
    rs = [_sb(nc, es, f"{tag}_rs{i}", [128, 1], F32) for i in range(2)]

    wg_v = wg.rearrange("(c p) f -> p c f", p=128)
    wu_v = wu.rearrange("(c p) f -> p c f", p=128)
    wd_v = wd.rearrange("(c p) d -> p c d", p=128)
    nfb = (FF + FW - 1) // FW
    ndb = D // DW
    xi = 0
    wi = 0
    di = 0
    pi = 0
    tiles = [(i * TT, TT) for i in range(NT // TT)]
    if NT % TT:
        tiles.append((NT - NT % TT, NT % TT))
    for (t0, tt) in tiles:
        NS = tt // 128
        for s in range(NS):
            xb = xs[xi % 2]
            rx = f"{tag}_xs{xi % 2}"
            xi += 1
            S.dma("sp", xb[:], x_ap[t0 + s * 128:t0 + (s + 1) * 128, :], writes=[rx])
            for cq in range(NDC // 4):
                bank = 4 + (cq % 4)
                for k in range(4):
                    c = cq * 4 + k
                    S.op("pe", lambda c=c, k=k, bank=bank: nc.tensor.transpose(
                        out=P[bank][:, k * 128:(k + 1) * 128], in_=xb[:, c * 128:(c + 1) * 128],
                        identity=ident[:]), reads=[rx, "ident"], writes=[f"ps{bank}"], signal=(k == 3))
                S.op("act" if cq % 2 else "dve",
                     (lambda cq=cq, bank=bank, s=s: nc.scalar.copy(
                         out=xT[:, cq * 4:(cq + 1) * 4, s * 128:(s + 1) * 128],
                         in_=P[bank][:].rearrange("p (k n) -> p k n", k=4))) if cq % 2 else
                     (lambda cq=cq, bank=bank, s=s: nc.vector.tensor_copy(
                         out=xT[:, cq * 4:(cq + 1) * 4, s * 128:(s + 1) * 128],
                         in_=P[bank][:].rearrange("p (k n) -> p k n", k=4))),
                     reads=[f"ps{bank}"], writes=[f"{tag}_xT"])
        for fb in range(nfb):
            f0 = fb * FW
            fw = min(FW, FF - f0)
            wgt, wut = wgb[wi % 2], wub[wi % 2]
            rwg, rwu = f"{tag}_wg{wi % 2}", f"{tag}_wu{wi % 2}"
            wi += 1
            S.dma("pool", wgt[:, :, 0:fw], wg_v[:, :, f0:f0 + fw], writes=[rwg])
            S.dma("pool", wut[:, :, 0:fw], wu_v[:, :, f0:f0 + fw], writes=[rwu])
            for jj in range(fw // 128):
                j = (f0 // 128) + jj
                pg, pu = (pi % 2) * 2, (pi % 2) * 2 + 1
                sgt = sg[pi % 2]
                rsg = f"{tag}_sg{pi % 2}"
                pi += 1
                for c in range(NDC):
                    S.op("pe", lambda c=c, jj=jj, pg=pg, wgt=wgt, tt=tt: nc.tensor.matmul(
                        out=P[pg][:, 0:tt], lhsT=wgt[:, c, jj * 128:(jj + 1) * 128], rhs=xT[:, c, 0:tt],
                        start=(c == 0), stop=(c == NDC - 1)),
                        reads=[rwg, f"{tag}_xT"], writes=[f"ps{pg}"], signal=(c == NDC - 1))
                for c in range(NDC):
                    S.op("pe", lambda c=c, jj=jj, pu=pu, wut=wut, tt=tt: nc.tensor.matmul(
                        out=P[pu][:, 0:tt], lhsT=wut[:, c, jj * 128:(jj + 1) * 128], rhs=xT[:, c, 0:tt],
                        start=(c == 0), stop=(c == NDC - 1)),
                        reads=[rwu, f"{tag}_xT"], writes=[f"ps{pu}"], signal=(c == NDC - 1))
                S.op("act", lambda pg=pg, sgt=sgt, tt=tt: nc.scalar.activation(
                    out=sgt[:, 0:tt], in_=P[pg][:, 0:tt], func=AF.Silu), reads=[f"ps{pg}"], writes=[rsg])
                S.op("dve", lambda j=j, pu=pu, sgt=sgt, tt=tt: nc.vector.tensor_tensor(
                    out=hT[:, j, 0:tt], in0=sgt[:, 0:tt], in1=P[pu][:, 0:tt], op=ALU.mult),
                    reads=[rsg, f"ps{pu}"], writes=[f"{tag}_hT"])
        for sp in range(NS // 2):
            zs = []
            for q in range(2):
                s = sp * 2 + q
                xb = xs[xi % 2]
                rx = f"{tag}_xs{xi % 2}"
                zt, rz = z[q], f"{tag}_z{q}"
                xi += 1
                S.dma("sp", xb[:], x_ap[t0 + s * 128:t0 + (s + 1) * 128, :], writes=[rx])
                zs.append((s, xb, rx, zt, rz))
            for db in range(ndb):
                wdt = wdb[di % 2]
                rwd = f"{tag}_wd{di % 2}"
                di += 1
                S.dma("pool", wdt[:], wd_v[:, :, db * DW:(db + 1) * DW], writes=[rwd])
                for q, (s, xb, rx, zt, rz) in enumerate(zs):
                    bank = 4 + ((db * 2 + q) % 4)
                    for j in range(NFC):
                        S.op("pe", lambda j=j, s=s, bank=bank, wdt=wdt: nc.tensor.matmul(
                            out=P[bank][:, 0:DW], lhsT=hT[:, j, s * 128:(s + 1) * 128], rhs=wdt[:, j, :],
                            start=(j == 0), stop=(j == NFC - 1)),
                            reads=[f"{tag}_hT", rwd], writes=[f"ps{bank}"], signal=(j == NFC - 1))
                    S.op("act", lambda xb=xb, db=db: nc.scalar.mul(
                        out=xb[:, db * DW:(db + 1) * DW], in_=xb[:, db * DW:(db + 1) * DW], mul=ALPHA),
                        reads=[rx], writes=[rx])
                    S.op("dve", lambda zt=zt, xb=xb, db=db, bank=bank: nc.vector.scalar_tensor_tensor(
                        out=zt[:, db * DW:(db + 1) * DW], in0=P[bank][:, 0:DW], scalar=0.5,
                        in1=xb[:, db * DW:(db + 1) * DW], op0=ALU.mult, op1=ALU.add),
                        reads=[f"ps{bank}", rx], writes=[rz])
            for q, (s, xb, rx, zt, rz) in enumerate(zs):
                emit_ln(nc, S, zt, rz, st[q], mv[q], rs[q], f"{tag}_ln{q}", g_rep, b_rep)
                S.dma("sp", y_ap[t0 + s * 128:t0 + (s + 1) * 128, :], zt[:], reads=[rz])


def emit_ln(nc, S, zt, rz, stt, mvt, rst, rtag, g_rep, b_rep):
    for k in range(4):
        S.op("dve", lambda k=k: nc.vector.bn_stats(out=stt[:, k, :], in_=zt[:, k * 512:(k + 1) * 512]),
             reads=[rz], writes=[rtag + "st"])
    S.op("dve", lambda: nc.vector.bn_aggr(out=mvt[:], in_=stt[:]), reads=[rtag + "st"], writes=[rtag + "mv"])
    S.op("dve", lambda: nc.vector.tensor_scalar(out=rst[:], in0=mvt[:, 1:2], scalar1=LN_EPS, scalar2=None,
                                                op0=ALU.add),
         reads=[rtag + "mv"], writes=[rtag + "rs"])
    S.op("act", lambda: nc.scalar.sqrt(out=rst[:], in_=rst[:]), reads=[rtag + "rs"], writes=[rtag + "rs"])
    S.op("dve", lambda: nc.vector.reciprocal(out=rst[:], in_=rst[:]), reads=[rtag + "rs"], writes=[rtag + "rs"])
    S.op("dve", lambda: nc.vector.tensor_scalar(out=zt[:], in0=zt[:], scalar1=mvt[:, 0:1], scalar2=rst[:, 0:1],
                                                op0=ALU.subtract, op1=ALU.mult),
         reads=[rz, rtag + "mv", rtag + "rs"], writes=[rz])
    S.op("dve", lambda: nc.vector.tensor_tensor(out=zt[:], in0=zt[:], in1=g_rep[:], op=ALU.mult),
         reads=[rz, "lng"], writes=[rz])
    S.op("dve", lambda: nc.vector.tensor_tensor(out=zt[:], in0=zt[:], in1=b_rep[:], op=ALU.add),
         reads=[rz, "lnb"], writes=[rz])


def build_ffn(NT):
    nc = bass.Bass("TRN2", target_bir_lowering=False)
    x = nc.dram_tensor("x", [NT, D], F32, kind="ExternalInput").ap()
    wg = nc.dram_tensor("wg", [D, FF], F32, kind="ExternalInput").ap()
    wu = nc.dram_tensor("wu", [D, FF], F32, kind="ExternalInput").ap()
    wd = nc.dram_tensor("wd", [FF, D], F32, kind="ExternalInput").ap()
    lng = nc.dram_tensor("lng", [128, D], F32, kind="ExternalInput").ap()
    lnb = nc.dram_tensor("lnb", [128, D], F32, kind="ExternalInput").ap()
    idn = nc.dram_tensor("idn", [128, 128], F32, kind="ExternalInput").ap()
    y = nc.dram_tensor("y", [NT, D], F32, kind="ExternalOutput").ap()
    with ExitStack() as es:
        es.enter_context(nc.allow_low_precision("bf16 matmul operands, fp32 accumulation"))
        es.enter_context(nc.allow_non_contiguous_dma("weight block loads"))
        S = Sched(nc, es)
        P = [es.enter_context(nc.psum_tensor(f"ps{i}", [128, 512], F32)) for i in range(8)]
        ident = _sb(nc, es, "ident", [128, 128], F32)
        g_rep = _sb(nc, es, "g_rep", [128, D], F32)
        b_rep = _sb(nc, es, "b_rep", [128, D], F32)
        S.dma("sp", ident[:], idn, writes=["ident"])
        S.dma("sp", g_rep[:], lng, writes=["lng"])
        S.dma("sp", b_rep[:], lnb, writes=["lnb"])
        emit_ffn(nc, S, es, P, x, y, wg, wu, wd, g_rep, b_rep, ident, NT, "f")
        S.finish()
    return nc


def _rep(v):
    return np.ascontiguousarray(np.broadcast_to(np.asarray(v, np.float32)[None, :], (128, v.shape[-1])))


def run_ffn(x_tok, wg, wu, wd, g, b):
    NT = x_tok.shape[0] // NCORES
    nc = build_ffn(NT)
    idn = np.eye(128, dtype=np.float32)
    common = {"wg": np.ascontiguousarray(wg), "wu": np.ascontiguousarray(wu), "wd": np.ascontiguousarray(wd),
              "lng": _rep(g), "lnb": _rep(b), "idn": idn}
    in_maps = [dict(common, x=np.ascontiguousarray(x_tok[c * NT:(c + 1) * NT])) for c in range(NCORES)]
    res = run_bass_kernel_spmd(nc, in_maps, core_ids=list(range(NCORES)))
    return np.concatenate([r["y"] for r in res.results], axis=0)


TC = 8
GC = 16
NP = 64
MIN_NEG_RE = -1e-4
TWO_PI = 6.283185307179586


class _V:
    def __init__(self, nc, S, e="dve"):
        self.nc, self.S, self.e = nc, S, e
        self.E = nc.vector if e == "dve" else nc.gpsimd

    def tt(self, out, a, b, op, R, W):
        self.S.op(self.e, lambda: self.E.tensor_tensor(out=out, in0=a, in1=b, op=op), reads=R, writes=W)

    def ts(self, out, a, s1, op0, R, W, s2=None, op1=None):
        if op1 is None:
            self.S.op(self.e, lambda: self.E.tensor_scalar(out=out, in0=a, scalar1=s1, scalar2=None, op0=op0),
                      reads=R, writes=W)
        else:
            self.S.op(self.e, lambda: self.E.tensor_scalar(out=out, in0=a, scalar1=s1, scalar2=s2, op0=op0,
                                                           op1=op1), reads=R, writes=W)

    def stt(self, out, a, s, b, op0, op1, R, W):
        self.S.op(self.e, lambda: self.nc.vector.scalar_tensor_tensor(out=out, in0=a, scalar=s, in1=b, op0=op0,
                                                                      op1=op1), reads=R, writes=W)

    def cp(self, out, a, R, W):
        self.S.op(self.e, lambda: self.E.tensor_copy(out=out, in_=a), reads=R, writes=W)


def _poly(V, out, x2, coefs, tmpn, R):
    first = True
    for c in reversed(coefs):
        if first:
            V.ts(out, x2, float(c), ALU.mult, R, [tmpn])
            first = False
        else:
            V.stt(out, out, float(c), x2, ALU.add, ALU.mult, R + [tmpn], [tmpn])


def emit_s5_params(nc, S, es, lam_re, lam_im, logdt, sg, GL, d, npw=8):
    V = _V(nc, S, "dve")
    n = [0]

    def T(nm):
        n[0] += 1
        return _sb(nc, es, f"s5p{d}_{nm}{n[0]}", [128, GL], F32), f"s5p{d}_{nm}{n[0]}"

    lr, lrn = T("lr")
    li, lin = T("li")
    dt, dtn = T("dt")
    S.dma("sp", lr[:], lam_re, writes=[lrn])
    S.dma("sp", li[:], lam_im, writes=[lin])
    S.dma("sp", dt[:], logdt, writes=[dtn])
    V.ts(lr[:], lr[:], MIN_NEG_RE, ALU.min, [lrn], [lrn])
    S.op("act", lambda: nc.scalar.activation(out=dt[:], in_=dt[:], func=AF.Exp), reads=[dtn], writes=[dtn])
    x, xn = T("x")
    V.tt(x[:], lr[:], dt[:], ALU.mult, [lrn, dtn], [xn])
    mag, magn = T("mag")
    cf = [1.0, 1 / 2.0, 1 / 6.0, 1 / 24.0, 1 / 120.0, 1 / 720.0, 1 / 5040.0]
    _poly(V, mag[:], x[:], cf, magn, [xn])
    V.ts(mag[:], mag[:], 1.0, ALU.add, [magn], [magn])
    th, thn = T("th")
    V.tt(th[:], li[:], dt[:], ALU.mult, [lin, dtn], [thn])
    ki = _sb(nc, es, f"s5p{d}_ki", [128, GL], mybir.dt.int32)
    kin = f"s5p{d}_ki"
    kf, kfn = T("kf")
    V.ts(kf[:], th[:], 1.0 / TWO_PI, ALU.mult, [thn], [kfn])
    V.cp(ki[:], kf[:], [kfn], [kin])
    V.cp(kf[:], ki[:], [kin], [kfn])
    C1 = 6.28125
    C2 = TWO_PI - C1
    r, rn = T("r")
    V.stt(r[:], kf[:], -C1, th[:], ALU.mult, ALU.add, [kfn, thn], [rn])
    V.stt(r[:], kf[:], -C2, r[:], ALU.mult, ALU.add, [kfn, rn], [rn])
    m, mn = T("m")
    V.ts(m[:], r[:], float(np.pi), ALU.is_gt, [rn], [mn], s2=-TWO_PI, op1=ALU.mult)
    V.tt(r[:], r[:], m[:], ALU.add, [rn, mn], [rn])
    V.ts(m[:], r[:], -float(np.pi), ALU.is_lt, [rn], [mn], s2=TWO_PI, op1=ALU.mult)
    V.tt(r[:], r[:], m[:], ALU.add, [rn, mn], [rn])
    r2, r2n = T("r2")
    V.tt(r2[:], r[:], r[:], ALU.mult, [rn], [r2n])
    import math
    sc = [(-1.0) ** k / math.factorial(2 * k + 1) for k in range(1, 12)]
    cc = [(-1.0) ** k / math.factorial(2 * k) for k in range(1, 12)]
    sn, snn = T("sn")
    cs, csn = T("cs")
    _poly(V, sn[:], r2[:], sc, snn, [r2n])
    V.stt(sn[:], sn[:], 1.0, r[:], ALU.add, ALU.mult, [snn, rn], [snn])
    _poly(V, cs[:], r2[:], cc, csn, [r2n])
    V.ts(cs[:], cs[:], 1.0, ALU.add, [csn], [csn])
    out = {}
    PR, PI, NR, NI = [], [], [], []
    p0r, p0rn = T("p0r")
    p0i, p0in = T("p0i")
    S.op("dve", lambda: nc.vector.memset(p0r[:], 1.0), writes=[p0rn])
    S.op("dve", lambda: nc.vector.memset(p0i[:], 0.0), writes=[p0in])
    l1r, l1rn = T("l1r")
    l1i, l1in = T("l1i")
    V.tt(l1r[:], mag[:], cs[:], ALU.mult, [magn, csn], [l1rn])
    V.tt(l1i[:], mag[:], sn[:], ALU.mult, [magn, snn], [l1in])
    PR.append((p0r, p0rn)); PI.append((p0i, p0in))
    PR.append((l1r, l1rn)); PI.append((l1i, l1in))
    t1, t1n = T("t1")
    t2, t2n = T("t2")

    def cmul(ar, ai, br, bi):
        (a_r, a_rn), (a_i, a_in), (b_r, b_rn), (b_i, b_in) = ar, ai, br, bi
        o_r, o_rn = T("pw")
        o_i, o_in = T("pw")
        V.tt(t1[:], a_r[:], b_r[:], ALU.mult, [a_rn, b_rn], [t1n])
        V.tt(t2[:], a_i[:], b_i[:], ALU.mult, [a_in, b_in], [t2n])
        V.tt(o_r[:], t1[:], t2[:], ALU.subtract, [t1n, t2n], [o_rn])
        V.tt(t1[:], a_r[:], b_i[:], ALU.mult, [a_rn, b_in], [t1n])
        V.tt(t2[:], a_i[:], b_r[:], ALU.mult, [a_in, b_rn], [t2n])
        V.tt(o_i[:], t1[:], t2[:], ALU.add, [t1n, t2n], [o_in])
        return (o_r, o_rn), (o_i, o_in)

    for j in range(2, npw + 1):
        pr, pi_ = cmul(PR[-1], PI[-1], PR[1], PI[1])
        PR.append(pr); PI.append(pi_)
    m2, m2n = T("m2")
    V.tt(m2[:], mag[:], mag[:], ALU.mult, [magn], [m2n])
    S.op("dve", lambda: nc.vector.reciprocal(out=m2[:], in_=m2[:]), reads=[m2n], writes=[m2n])
    n1r, n1rn = T("n1r")
    n1i, n1in = T("n1i")
    V.tt(n1r[:], l1r[:], m2[:], ALU.mult, [l1rn, m2n], [n1rn])
    V.stt(n1i[:], l1i[:], -1.0, m2[:], ALU.mult, ALU.mult, [l1in, m2n], [n1in])
    NR.append((p0r, p0rn)); NI.append((p0i, p0in))
    NR.append((n1r, n1rn)); NI.append((n1i, n1in))
    for j in range(2, npw):
        pr, pi_ = cmul(NR[-1], NI[-1], NR[1], NI[1])
        NR.append(pr); NI.append(pi_)
    den, denn = T("den")
    V.tt(den[:], lr[:], lr[:], ALU.mult, [lrn], [denn])
    V.tt(t1[:], li[:], li[:], ALU.mult, [lin], [t1n])
    V.tt(den[:], den[:], t1[:], ALU.add, [denn, t1n], [denn])
    S.op("dve", lambda: nc.vector.reciprocal(out=den[:], in_=den[:]), reads=[denn], writes=[denn])
    nr, nrn = T("nr")
    V.ts(nr[:], l1r[:], -1.0, ALU.add, [l1rn], [nrn])
    fr, frn = T("fr")
    fi, fin = T("fi")
    V.tt(t1[:], nr[:], lr[:], ALU.mult, [nrn, lrn], [t1n])
    V.tt(t2[:], l1i[:], li[:], ALU.mult, [l1in, lin], [t2n])
    V.tt(fr[:], t1[:], t2[:], ALU.add, [t1n, t2n], [frn])
    V.tt(fr[:], fr[:], den[:], ALU.mult, [frn, denn], [frn])
    V.tt(t1[:], l1i[:], lr[:], ALU.mult, [l1in, lrn], [t1n])
    V.tt(t2[:], nr[:], li[:], ALU.mult, [nrn, lin], [t2n])
    V.tt(fi[:], t1[:], t2[:], ALU.subtract, [t1n, t2n], [fin])
    V.tt(fi[:], fi[:], den[:], ALU.mult, [fin, denn], [fin])
    return {"PR": PR, "PI": PI, "NR": NR, "NI": NI, "fr": (fr, frn), "fi": (fi, fin)}


def emit_s5(nc, S, es, P, ident, din, y_out, GL, NK, GB=16):
    V = _V(nc, S, "dve")
    NKB = NK // 128
    sg = _sb(nc, es, "s5_sg", [128, 1], F32)
    sr = _sb(nc, es, "s5_sr", [128, 1], F32)
    S.dma("sp", sg[:], din["sg"], writes=["s5_sg"])
    V.ts(sr[:], sg[:], -1.0, ALU.mult, ["s5_sg"], ["s5_sr"])
    maskf = _sb(nc, es, "s5_mf", [128, 128], F32)
    maskb = _sb(nc, es, "s5_mb", [128, 128], F32)
    dcol = _sb(nc, es, "s5_dcol", [128, GL], F32)
    S.dma("sp", maskf[:], din["maskf"], writes=["s5_mf"])
    S.dma("sp", maskb[:], din["maskb"], writes=["s5_mb"])
    S.dma("sp", dcol[:], din["dcol"], writes=["s5_dcol"])
    prm = [emit_s5_params(nc, S, es, din["lam_re"][d], din["lam_im"][d], din["logdt"][d], sg, GL, d)
           for d in range(2)]
    def signed(tbl, sgn, sgn_n, d, nm):
        res = []
        for j, (t, tn) in enumerate(tbl):
            o = _sb(nc, es, f"s5s{d}_{nm}{j}", [128, GL], F32)
            on = f"s5s{d}_{nm}{j}"
            V.ts(o[:], t[:], sgn[:, 0:1], ALU.mult, [tn, sgn_n], [on])
            res.append((o, on))
        return res
    for d in range(2):
        p = prm[d]
        p["sgPI"] = signed(p["PI"], sg, "s5_sg", d, "gpi")
        p["srPR"] = signed(p["PR"], sr, "s5_sr", d, "rpr")
        p["sgNI"] = signed(p["NI"], sg, "s5_sg", d, "gni")
        p["srNR"] = signed(p["NR"], sr, "s5_sr", d, "rnr")
        p["sgfi"] = signed([p["fi"]], sg, "s5_sg", d, "gfi")[0]
        p["srfi"] = signed([p["fi"]], sr, "s5_sr", d, "rfi")[0]
        p["srPI8"] = signed([p["PI"][8]], sr, "s5_sr", d, "rpi8")[0]
    bb1, bb2, c1, c2 = [], [], [], []
    tmpA = _sb(nc, es, "s5_tmpA", [128, GL, GC], F32)
    tmpB = _sb(nc, es, "s5_tmpB", [128, GL, GC], F32)

    def bc(t):
        return t[:].unsqueeze(2).to_broadcast([128, GL, GC])

    for d in range(2):
        b1 = _sb(nc, es, f"s5_b1{d}", [128, GL, GC], F32)
        b2 = _sb(nc, es, f"s5_b2{d}", [128, GL, GC], F32)
        S.dma("sp", b1[:], din["b1"][d], writes=[f"s5_b1{d}"])
        S.dma("sp", b2[:], din["b2"][d], writes=[f"s5_b2{d}"])
        o1 = _sb(nc, es, f"s5_bb1{d}", [128, GL, GC], F32)
        o2 = _sb(nc, es, f"s5_bb2{d}", [128, GL, GC], F32)
        p = prm[d]
        (fr, frn), (sgfi, sgfin), (srfi, srfin) = p["fr"], p["sgfi"], p["srfi"]
        V.tt(tmpA[:], b1[:], bc(fr), ALU.mult, [f"s5_b1{d}", frn], ["s5_tmpA"])
        V.tt(tmpB[:], b2[:], bc(sgfi), ALU.mult, [f"s5_b2{d}", sgfin], ["s5_tmpB"])
        V.tt(o1[:], tmpA[:], tmpB[:], ALU.add, ["s5_tmpA", "s5_tmpB"], [f"s5_bb1{d}"])
        V.tt(tmpA[:], b2[:], bc(fr), ALU.mult, [f"s5_b2{d}", frn], ["s5_tmpA"])
        V.tt(tmpB[:], b1[:], bc(srfi), ALU.mult, [f"s5_b1{d}", srfin], ["s5_tmpB"])
        V.tt(o2[:], tmpA[:], tmpB[:], ALU.add, ["s5_tmpA", "s5_tmpB"], [f"s5_bb2{d}"])
        bb1.append((o1, f"s5_bb1{d}")); bb2.append((o2, f"s5_bb2{d}"))
        cc1 = _sb(nc, es, f"s5_c1{d}", [128, GL, GC], F32)
        cc2 = _sb(nc, es, f"s5_c2{d}", [128, GL, GC], F32)
        S.dma("sp", cc1[:], din["c1"][d], writes=[f"s5_c1{d}"])
        S.dma("sp", cc2[:], din["c2"][d], writes=[f"s5_c2{d}"])
        c1.append((cc1, f"s5_c1{d}")); c2.append((cc2, f"s5_c2{d}"))

    names = ["WTf", "Af", "Bf", "Vf", "WTb", "Bb", "Vb"]
    WfA = _sb(nc, es, "s5_WfA", [128, GB, 128], BF16)
    WfB = _sb(nc, es, "s5_WfB", [128, GB, 128], BF16)
    WbA = _sb(nc, es, "s5_WbA", [128, GB, 128], BF16)
    WbB = _sb(nc, es, "s5_WbB", [128, GB, 128], BF16)
    Vfb = _sb(nc, es, "s5_Vfb", [128, GB, 128], BF16)
    Vbb = _sb(nc, es, "s5_Vbb", [128, GB, 128], BF16)
    Mb = _sb(nc, es, "s5_Mb", [128, GB, 128], BF16)
    X = [_sb(nc, es, f"s5_X{d}", [128, 2, GB], F32) for d in range(2)]
    q1 = [_sb(nc, es, f"s5_q1{d}", [128, 2, GB], F32) for d in range(2)]
    q2 = [_sb(nc, es, f"s5_q2{d}", [128, 2, GB], F32) for d in range(2)]
    AR2 = [_sb(nc, es, f"s5_AR2{d}", [128, 2, GB], F32) for d in range(2)]
    AIx = [_sb(nc, es, f"s5_AIx{d}", [128, 2, GB], F32) for d in range(2)]
    tmp = {}
    bufs = {}

    def gen(dst, dstn, tsl, src1, src2, pa, pb, g0):
        tA, tB = bufs["tA"], bufs["tB"]
        (s1, s1n), (s2, s2n), (a, an), (b, bn) = src1, src2, pa, pb
        ab = a[:, g0:g0 + GB].unsqueeze(2).to_broadcast([128, GB, GC])
        bbb = b[:, g0:g0 + GB].unsqueeze(2).to_broadcast([128, GB, GC])
        V.tt(tA[:], s1[:, g0:g0 + GB, :], ab, ALU.mult, [s1n, an], ["s5_tA"])
        V.tt(tB[:], s2[:, g0:g0 + GB, :], bbb, ALU.mult, [s2n, bn], ["s5_tB"])
        V.tt(dst[:, :, tsl, :], tA[:], tB[:], ALU.add, ["s5_tA", "s5_tB"], [dstn])

    for g0 in range(0, GL, GB):
      pf, pb_ = prm[0], prm[1]
      S.barrier()
      with ExitStack() as es2:
        for nm in names:
            tmp[nm] = _sb(nc, es2, f"s5t_{nm}", [128, GB, TC, GC], F32)
        tA = bufs["tA"] = _sb(nc, es2, "s5_tA", [128, GB, GC], F32)
        tB = bufs["tB"] = _sb(nc, es2, "s5_tB", [128, GB, GC], F32)
        m1 = _sb(nc, es2, "s5_m1", [128, 128], F32)
        m2 = _sb(nc, es2, "s5_m2", [128, 128], F32)
        for j in range(TC):
            gen(tmp["WTf"], "s5t_WTf", j, bb1[0], bb2[0], pf["PR"][7 - j], pf["sgPI"][7 - j], g0)
            gen(tmp["Af"], "s5t_Af", j, bb1[0], bb2[0], pf["NR"][j], pf["sgNI"][j], g0)
            gen(tmp["WTb"], "s5t_WTb", j, bb1[1], bb2[1], pb_["PR"][j], pb_["sgPI"][j], g0)
        def genB(dst, dstn, tsl, d, srP, Pi):
            (s1, s1n), (s2, s2n), (a, an), (b, bn) = c1[d], c2[d], srP, Pi
            ab = a[:, g0:g0 + GB].unsqueeze(2).to_broadcast([128, GB, GC])
            bbb = b[:, g0:g0 + GB].unsqueeze(2).to_broadcast([128, GB, GC])
            V.tt(tA[:], s1[:, g0:g0 + GB, :], ab, ALU.mult, [s1n, an], ["s5_tA"])
            V.tt(tB[:], s2[:, g0:g0 + GB, :], bbb, ALU.mult, [s2n, bn], ["s5_tB"])
            V.tt(dst[:, :, tsl, :], tA[:], tB[:], ALU.subtract, ["s5_tA", "s5_tB"], [dstn])
        for t in range(TC):
            genB(tmp["Bf"], "s5t_Bf", t, 0, pf["srPR"][t], pf["PI"][t])
            genB(tmp["Vf"], "s5t_Vf", t, 0, pf["srPR"][t + 1], pf["PI"][t + 1])
            genB(tmp["Bb"], "s5t_Bb", t, 1, pb_["srNR"][t], pb_["NI"][t])
            genB(tmp["Vb"], "s5t_Vb", t, 1, pb_["srPR"][8 - t], pb_["PI"][8 - t])
        V.cp(Vfb[:], tmp["Vf"][:].rearrange("p g t c -> p g (t c)"), ["s5t_Vf"], ["s5_Vfb"])
        V.cp(Vbb[:], tmp["Vb"][:].rearrange("p g t c -> p g (t c)"), ["s5t_Vb"], ["s5_Vbb"])
        for d in range(2):
            p = prm[d]
            (ar, arn), (sgai, sgain), (srai, srain) = p["PR"][8], p["sgPI"][8], p["srPI8"]
            V.cp(AR2[d][:, 0, :], ar[:, g0:g0 + GB], [arn], [f"s5_AR2{d}"])
            V.cp(AR2[d][:, 1, :], ar[:, g0:g0 + GB], [arn], [f"s5_AR2{d}"])
            V.cp(AIx[d][:, 0, :], srai[:, g0:g0 + GB], [srain], [f"s5_AIx{d}"])
            V.cp(AIx[d][:, 1, :], sgai[:, g0:g0 + GB], [sgain], [f"s5_AIx{d}"])
        for gi in range(GB):
            g = g0 + gi
            for (src, srcn, WA, WAn, WB, WBn, bank) in [("WTf", "s5t_WTf", WfA, "s5_WfA", WfB, "s5_WfB", 0),
                                                       ("WTb", "s5t_WTb", WbA, "s5_WbA", WbB, "s5_WbB", 1)]:
                S.op("pe", lambda src=src, gi=gi, bank=bank: nc.tensor.transpose(
                    out=P[bank][:, 0:128], in_=tmp[src][:, gi].rearrange("p t c -> p (t c)"), identity=ident[:]),
                    reads=[srcn, "ident"], writes=[f"ps{bank}"])
                S.op("act", lambda WA=WA, gi=gi, bank=bank: nc.scalar.copy(out=WA[:, gi, :], in_=P[bank][:, 0:128]),
                     reads=[f"ps{bank}"], writes=[WAn])
                V.cp(WB[:, gi, 0:64], P[bank][:, 64:128], [f"ps{bank}"], [WBn])
                V.cp(WB[:, gi, 64:128], P[bank][:, 0:64], [f"ps{bank}"], [WBn])
            S.op("pe", lambda gi=gi: nc.tensor.matmul(
                out=P[2][:, 0:128], lhsT=tmp["Af"][:, gi].rearrange("p t c -> p (t c)"),
                rhs=tmp["Bf"][:, gi].rearrange("p t c -> p (t c)"), start=True, stop=True),
                reads=["s5t_Af", "s5t_Bf"], writes=["ps2"])
            S.op("pe", lambda gi=gi: nc.tensor.matmul(
                out=P[3][:, 0:128], lhsT=tmp["WTb"][:, gi].rearrange("p t c -> p (t c)"),
                rhs=tmp["Bb"][:, gi].rearrange("p t c -> p (t c)"), start=True, stop=True),
                reads=["s5t_WTb", "s5t_Bb"], writes=["ps3"])
            V.tt(m1[:], P[2][:, 0:128], maskf[:], ALU.mult, ["ps2", "s5_mf"], ["s5_m1"])
            V.tt(m2[:], P[3][:, 0:128], maskb[:], ALU.mult, ["ps3", "s5_mb"], ["s5_m2"])
            V.tt(m1[:], m1[:], m2[:], ALU.add, ["s5_m1", "s5_m2"], ["s5_m1"])
            V.stt(Mb[:, gi, :], ident[:], dcol[:, g:g + 1], m1[:], ALU.mult, ALU.add,
                  ["ident", "s5_dcol", "s5_m1"], ["s5_Mb"])
        S.barrier()
      with ExitStack() as es3:
        Ub = _sb(nc, es3, "s5_Ub", [128, GB, NK], BF16)
        St = [_sb(nc, es3, f"s5_St{d}", [128, NK + 1, 2, GB], BF16) for d in range(2)]
        Ysb = [_sb(nc, es3, f"s5_Y{i}", [128, TC, GB * GC], F32) for i in range(2)]
        S.dma("pool", Ub[:], din["u"][g0:g0 + GB].rearrange("g p k -> p g k"), writes=["s5_Ub"])
        for d in range(2):
            S.op("pool", lambda d=d: nc.gpsimd.memset(St[d][:, (0 if d == 0 else NK), :, :], 0.0),
                 writes=[f"s5_St{d}"])
            S.op("pool", lambda d=d: nc.gpsimd.memset(X[d][:], 0.0), writes=[f"s5_X{d}"])
        for gi in range(GB):
            for vi, (W, Wn, d, ab) in enumerate([(WfA, "s5_WfA", 0, 0), (WfB, "s5_WfB", 0, 1),
                                                 (WbA, "s5_WbA", 1, 0), (WbB, "s5_WbB", 1, 1)]):
                bank = 4 + vi
                S.op("pe", lambda W=W, gi=gi, bank=bank: nc.tensor.matmul(
                    out=P[bank][:, 0:NK], lhsT=W[:, gi, :], rhs=Ub[:, gi, :], start=True, stop=True),
                    reads=[Wn, "s5_Ub"], writes=[f"ps{bank}"])
                off = 1 if d == 0 else 0
                eng = "act" if vi % 2 else "dve"
                if eng == "act":
                    S.op("act", lambda d=d, ab=ab, gi=gi, bank=bank, off=off: nc.scalar.copy(
                        out=St[d][:, off:off + NK, ab, gi], in_=P[bank][:, 0:NK]),
                        reads=[f"ps{bank}"], writes=[f"s5_St{d}"])
                else:
                    V.cp(St[d][:, off:off + NK, ab, gi], P[bank][:, 0:NK], [f"ps{bank}"], [f"s5_St{d}"])
        for k in range(NK):
            for d, e in ((0, "dve"), (1, "pool")):
                E = nc.vector if e == "dve" else nc.gpsimd
                slot = k + 1 if d == 0 else NK - 1 - k
                Xn, q1n, q2n, Sn = f"s5_X{d}", f"s5_q1{d}", f"s5_q2{d}", f"s5_St{d}"
                S.op(e, lambda E=E, d=d: E.tensor_tensor(out=q1[d][:], in0=X[d][:], in1=AR2[d][:], op=ALU.mult),
                     reads=[Xn, f"s5_AR2{d}"], writes=[q1n])
                S.op(e, lambda E=E, d=d: E.tensor_tensor(out=q2[d][:], in0=X[d][:], in1=AIx[d][:], op=ALU.mult),
                     reads=[Xn, f"s5_AIx{d}"], writes=[q2n])
                S.op(e, lambda E=E, d=d, slot=slot: E.tensor_tensor(out=q1[d][:], in0=q1[d][:],
                                                                    in1=St[d][:, slot, :, :], op=ALU.add),
                     reads=[q1n, Sn], writes=[q1n])
                S.op(e, lambda E=E, d=d: E.tensor_tensor(out=X[d][:, 0, :], in0=q1[d][:, 0, :], in1=q2[d][:, 1, :],
                                                         op=ALU.add), reads=[q1n, q2n], writes=[Xn])
                S.op(e, lambda E=E, d=d: E.tensor_tensor(out=X[d][:, 1, :], in0=q1[d][:, 1, :], in1=q2[d][:, 0, :],
                                                         op=ALU.add), reads=[q1n, q2n], writes=[Xn])
                S.op(e, lambda E=E, d=d, slot=slot: E.tensor_copy(out=St[d][:, slot, :, :], in_=X[d][:]),
                     reads=[Xn], writes=[Sn])
        for kb in range(NKB):
            Yt, Yn = Ysb[kb % 2], f"s5_Y{kb % 2}"
            for gi in range(GB):
                bank = gi % 4
                ysl = P[bank][:, 0:128]
                S.op("pe", lambda gi=gi, kb=kb, ysl=ysl: nc.tensor.matmul(
                    out=ysl, lhsT=Ub[:, gi, kb * 128:(kb + 1) * 128], rhs=Mb[:, gi, :], start=True, stop=False),
                    reads=["s5_Ub", "s5_Mb"], writes=[f"ps{bank}"], signal=False)
                S.op("pe", lambda gi=gi, kb=kb, ysl=ysl: nc.tensor.matmul(
                    out=ysl, lhsT=St[0][:, kb * 128:(kb + 1) * 128, 0, gi], rhs=Vfb[:, gi, :], start=False,
                    stop=False), reads=["s5_St0", "s5_Vfb"], writes=[f"ps{bank}"], signal=False)
                S.op("pe", lambda gi=gi, kb=kb, ysl=ysl: nc.tensor.matmul(
                    out=ysl, lhsT=St[1][:, kb * 128 + 1:(kb + 1) * 128 + 1, 0, gi], rhs=Vbb[:, gi, :], start=False,
                    stop=True), reads=["s5_St1", "s5_Vbb"], writes=[f"ps{bank}"], signal=True)
                src = ysl.rearrange("p (t c) -> p t c", t=TC)
                if gi % 2:
                    S.op("act", lambda Yt=Yt, gi=gi, src=src: nc.scalar.copy(
                        out=Yt[:, :, gi * GC:(gi + 1) * GC], in_=src), reads=[f"ps{bank}"], writes=[Yn])
                else:
                    V.cp(Yt[:, :, gi * GC:(gi + 1) * GC], src, [f"ps{bank}"], [Yn])
            S.dma("sp", y_out[kb * 128:(kb + 1) * 128, :, g0 * GC:(g0 + GB) * GC], Yt[:], reads=[Yn])
      S.barrier()


def build_s5(GL, NK):
    nc = bass.Bass("TRN2", target_bir_lowering=False)
    def din_t(name, shape):
        return nc.dram_tensor(name, shape, F32, kind="ExternalInput").ap()
    din = {
        "u": din_t("u", [GL, 128, NK]),
        "lam_re": din_t("lam_re", [2, 128, GL]), "lam_im": din_t("lam_im", [2, 128, GL]),
        "logdt": din_t("logdt", [2, 128, GL]),
        "b1": din_t("b1", [2, 128, GL, GC]), "b2": din_t("b2", [2, 128, GL, GC]),
        "c1": din_t("c1", [2, 128, GL, GC]), "c2": din_t("c2", [2, 128, GL, GC]),
        "sg": din_t("sg", [128, 1]), "dcol": din_t("dcol", [128, GL]),
        "maskf": din_t("maskf", [128, 128]), "maskb": din_t("maskb", [128, 128]),
    }
    idn = din_t("idn", [128, 128])
    y = nc.dram_tensor("y", [NK, TC, GL * GC], F32, kind="ExternalOutput").ap()
    with ExitStack() as es:
        es.enter_context(nc.allow_low_precision("bf16 matmul operands, fp32 accumulation"))
        es.enter_context(nc.allow_non_contiguous_dma("small strided loads"))
        S = Sched(nc, es)
        P = [es.enter_context(nc.psum_tensor(f"ps{i}", [128, 512], F32)) for i in range(8)]
        ident = _sb(nc, es, "ident", [128, 128], F32)
        S.dma("sp", ident[:], idn, writes=["ident"])
        GH = GL // 2
        for h in range(2):
            gs = slice(h * GH, (h + 1) * GH)
            dsub = {"u": din["u"][gs], "sg": din["sg"], "dcol": din["dcol"][:, gs],
                    "maskf": din["maskf"], "maskb": din["maskb"]}
            for k in ("lam_re", "lam_im", "logdt"):
                dsub[k] = [din[k][d][:, gs] for d in range(2)]
            for k in ("b1", "b2", "c1", "c2"):
                dsub[k] = [din[k][d][:, gs, :] for d in range(2)]
            with ExitStack() as esh:
                emit_s5(nc, S, esh, P, ident, dsub, y[:, :, h * GH * GC:(h + 1) * GH * GC], GH, NK)
            S.barrier()
        S.finish()
    return nc


def s5_host_inputs(xb, G0, GL, lam_re, lam_im, log_dt, b_re, b_im, c_re, c_im, d_skip):
    L = xb.shape[0]
    NK = L // TC
    u = xb.reshape(NK, TC, D // GC, GC)[:, :, G0:G0 + GL, :]
    u = np.ascontiguousarray(u.transpose(2, 1, 3, 0).reshape(GL, TC * GC, NK))
    def pg(a):
        a = a[:, G0:G0 + GL, :].transpose(0, 2, 1)
        return np.ascontiguousarray(np.concatenate([a, a], axis=1))
    lr, li = pg(lam_re), pg(lam_im)
    ldt = np.ascontiguousarray(np.broadcast_to(log_dt[:, None, G0:G0 + GL], (2, 128, GL)))
    br = b_re[:, G0:G0 + GL].transpose(0, 2, 1, 3)
    bi = b_im[:, G0:G0 + GL].transpose(0, 2, 1, 3)
    cr = c_re[:, G0:G0 + GL].transpose(0, 3, 1, 2)
    ci = c_im[:, G0:G0 + GL].transpose(0, 3, 1, 2)
    st = lambda a, b: np.ascontiguousarray(np.concatenate([a, b], axis=1))
    sg = np.concatenate([-np.ones((64, 1), np.float32), np.ones((64, 1), np.float32)], 0)
    dcol = np.ascontiguousarray(np.tile(d_skip.reshape(D // GC, GC)[G0:G0 + GL].T, (TC, 1)))
    tt = np.arange(128) // GC
    maskf = (tt[None, :] >= tt[:, None]).astype(np.float32)
    maskb = (tt[None, :] <= tt[:, None]).astype(np.float32)
    return {"u": u, "lam_re": lr, "lam_im": li, "logdt": ldt, "b1": st(br, bi), "b2": st(bi, br),
            "c1": st(cr, ci), "c2": st(ci, cr), "sg": sg, "dcol": dcol, "maskf": maskf, "maskb": maskb,
            "idn": np.eye(128, dtype=np.float32)}


def run_s5(x_tok, lam_re, lam_im, log_dt, b_re, b_im, c_re, c_im, d_skip, B, L):
    GL = (D // GC) // 2
    NK = L // TC
    nc = build_s5(GL, NK)
    in_maps = []
    for c in range(NCORES):
        b, h = c // 2, c % 2
        in_maps.append(s5_host_inputs(x_tok[b * L:(b + 1) * L], h * GL, GL, lam_re, lam_im, log_dt,
                                      b_re, b_im, c_re, c_im, d_skip))
    res = run_bass_kernel_spmd(nc, in_maps, core_ids=list(range(NCORES)))
    y = np.empty((B * L, D), np.float32)
    for c in range(NCORES):
        b, h = c // 2, c % 2
        y[b * L:(b + 1) * L, h * GL * GC:(h + 1) * GL * GC] = res.results[c]["y"].reshape(L, GL * GC)
    return y


GELU_C = 0.044715
GELU_S = 2.0 * 0.7978845608028654


def emit_glu(nc, S, es, P, x_ap, ys_ap, y_ap, wv, wgt, g_rep, b_rep, ident, NT):
    Wv = _sb(nc, es, "gl_Wv", [128, NDC, D], BF16)
    Wg = _sb(nc, es, "gl_Wg", [128, NDC, D], BF16)
    wv_v = wv.rearrange("(c p) f -> p c f", p=128)
    wg_v = wgt.rearrange("(c p) f -> p c f", p=128)
    for q in range(4):
        S.dma("pool", Wv[:, :, q * 512:(q + 1) * 512], wv_v[:, :, q * 512:(q + 1) * 512], writes=["gl_Wv"])
        S.dma("pool", Wg[:, :, q * 512:(q + 1) * 512], wg_v[:, :, q * 512:(q + 1) * 512], writes=["gl_Wg"])
    yt = [_sb(nc, es, f"gl_y{i}", [128, D], F32) for i in range(2)]
    t1 = _sb(nc, es, "gl_t1", [128, D], F32)
    xt = [_sb(nc, es, f"gl_x{i}", [128, D], F32) for i in range(2)]
    gT = _sb(nc, es, "gl_gT", [128, NDC, 128], BF16)
    sgm = [_sb(nc, es, f"gl_sg{i}", [128, 512], F32) for i in range(2)]
    st = _sb(nc, es, "gl_st", [128, 4, nc.vector.BN_STATS_DIM], F32)
    mv = _sb(nc, es, "gl_mv", [128, nc.vector.BN_AGGR_DIM], F32)
    rs = _sb(nc, es, "gl_rs", [128, 1], F32)
    V = _V(nc, S, "dve")
    for t in range(NT // 128):
        y, yn = yt[t % 2], f"gl_y{t % 2}"
        x, xn = xt[t % 2], f"gl_x{t % 2}"
        S.dma("sp", y[:], ys_ap[t * 128:(t + 1) * 128, :], writes=[yn])
        S.dma("sp", x[:], x_ap[t * 128:(t + 1) * 128, :], writes=[xn])
        V.tt(t1[:], y[:], y[:], ALU.mult, [yn], ["gl_t1"])
        V.ts(t1[:], t1[:], GELU_C, ALU.mult, ["gl_t1"], ["gl_t1"], s2=1.0, op1=ALU.add)
        V.tt(t1[:], t1[:], y[:], ALU.mult, ["gl_t1", yn], ["gl_t1"])
        S.op("act", lambda: nc.scalar.activation(out=t1[:], in_=t1[:], func=AF.Sigmoid, scale=GELU_S),
             reads=["gl_t1"], writes=["gl_t1"])
        V.tt(y[:], y[:], t1[:], ALU.mult, [yn, "gl_t1"], [yn])
        for cq in range(NDC // 4):
            bank = 4 + (cq % 4)
            for k in range(4):
                c = cq * 4 + k
                S.op("pe", lambda c=c, k=k, bank=bank, y=y: nc.tensor.transpose(
                    out=P[bank][:, k * 128:(k + 1) * 128], in_=y[:, c * 128:(c + 1) * 128], identity=ident[:]),
                    reads=[yn, "ident"], writes=[f"ps{bank}"], signal=(k == 3))
            S.op("act", lambda cq=cq, bank=bank: nc.scalar.copy(
                out=gT[:, cq * 4:(cq + 1) * 4, :], in_=P[bank][:].rearrange("p (k n) -> p k n", k=4)),
                reads=[f"ps{bank}"], writes=["gl_gT"])
        for n in range(4):
            pv, pg = (n % 2) * 2, (n % 2) * 2 + 1
            for (W, Wn, bank) in ((Wv, "gl_Wv", pv), (Wg, "gl_Wg", pg)):
                for c in range(NDC):
                    S.op("pe", lambda W=W, c=c, n=n, bank=bank: nc.tensor.matmul(
                        out=P[bank][:], lhsT=gT[:, c, :], rhs=W[:, c, n * 512:(n + 1) * 512],
                        start=(c == 0), stop=(c == NDC - 1)),
                        reads=["gl_gT", Wn], writes=[f"ps{bank}"], signal=(c == NDC - 1))
            sg_, sgn = sgm[n % 2], f"gl_sg{n % 2}"
            S.op("act", lambda sg_=sg_, pg=pg: nc.scalar.activation(out=sg_[:], in_=P[pg][:], func=AF.Sigmoid),
                 reads=[f"ps{pg}"], writes=[sgn])
            V.tt(t1[:, n * 512:(n + 1) * 512], P[pv][:], sg_[:], ALU.mult, [f"ps{pv}", sgn], ["gl_t1"])
        V.stt(t1[:], x[:], ALPHA, t1[:], ALU.mult, ALU.add, [xn, "gl_t1"], ["gl_t1"])
        emit_ln(nc, S, t1, "gl_t1", st, mv, rs, "gl_ln", g_rep, b_rep)
        S.dma("sp", y_ap[t * 128:(t + 1) * 128, :], t1[:], reads=["gl_t1"])


def _std_prog(NT_in, NT_out, extra, body):
    nc = bass.Bass("TRN2", target_bir_lowering=False)
    x = nc.dram_tensor("x", [NT_in, D], F32, kind="ExternalInput").ap()
    lng = nc.dram_tensor("lng", [128, D], F32, kind="ExternalInput").ap()
    lnb = nc.dram_tensor("lnb", [128, D], F32, kind="ExternalInput").ap()
    idn = nc.dram_tensor("idn", [128, 128], F32, kind="ExternalInput").ap()
    ex = {k: nc.dram_tensor(k, shp, F32, kind="ExternalInput").ap() for k, shp in extra.items()}
    y = nc.dram_tensor("y", [NT_out, D], F32, kind="ExternalOutput").ap()
    with ExitStack() as es:
        es.enter_context(nc.allow_low_precision("bf16 matmul operands, fp32 accumulation"))
        es.enter_context(nc.allow_non_contiguous_dma("weight block loads"))
        S = Sched(nc, es)
        P = [es.enter_context(nc.psum_tensor(f"ps{i}", [128, 512], F32)) for i in range(8)]
        ident = _sb(nc, es, "ident", [128, 128], F32)
        g_rep = _sb(nc, es, "g_rep", [128, D], F32)
        b_rep = _sb(nc, es, "b_rep", [128, D], F32)
        S.dma("sp", ident[:], idn, writes=["ident"])
        S.dma("sp", g_rep[:], lng, writes=["lng"])
        S.dma("sp", b_rep[:], lnb, writes=["lnb"])
        body(nc, S, es, P, x, y, ex, g_rep, b_rep, ident)
        S.finish()
    return nc


def run_glu(x_tok, ys_tok, wv, wg, g, b):
    NT = x_tok.shape[0] // NCORES
    nc = _std_prog(NT, NT, {"ys": [NT, D], "wv": [D, D], "wg": [D, D]},
                   lambda nc, S, es, P, x, y, ex, gr, br, idt: emit_glu(nc, S, es, P, x, ex["ys"], y, ex["wv"],
                                                                       ex["wg"], gr, br, idt, NT))
    common = {"wv": np.ascontiguousarray(wv), "wg": np.ascontiguousarray(wg), "lng": _rep(g), "lnb": _rep(b),
              "idn": np.eye(128, dtype=np.float32)}
    in_maps = [dict(common, x=np.ascontiguousarray(x_tok[c * NT:(c + 1) * NT]),
                    ys=np.ascontiguousarray(ys_tok[c * NT:(c + 1) * NT])) for c in range(NCORES)]
    res = run_bass_kernel_spmd(nc, in_maps, core_ids=list(range(NCORES)))
    return np.concatenate([r["y"] for r in res.results], axis=0)


GW = 64
NH = 16
HD = 128
NR = 36
NEG = -30000.0


def emit_natten(nc, S, es, P, x_ap, y_ap, wqkv, wo, tab, g_rep, b_rep, ident):
    NTK = NR * GW
    V = _V(nc, S, "dve")
    at_dram = nc.dram_tensor("na_attn_scratch", [NH, 128, NTK], BF16, kind="Internal").ap()
    ones = _sb(nc, es, "na_ones", [128, 128], BF16)
    S.op("dve", lambda: nc.vector.memset(ones[:], 1.0), writes=["na_ones"])
    wq_v = wqkv.rearrange("(c p) f -> p c f", p=128)
    with ExitStack() as es2:
        xT = _sb(nc, es2, "na_xT", [128, NDC, NTK], BF16)
        xs = [_sb(nc, es2, f"na_xs{i}", [128, D], F32) for i in range(2)]
        for t in range(NTK // 128):
            xb, rx = xs[t % 2], f"na_xs{t % 2}"
            S.dma("sp", xb[:], x_ap[t * 128:(t + 1) * 128, :], writes=[rx])
            for cq in range(NDC // 4):
                bank = 4 + (cq % 4)
                for k in range(4):
                    c = cq * 4 + k
                    S.op("pe", lambda c=c, k=k, bank=bank, xb=xb: nc.tensor.transpose(
                        out=P[bank][:, k * 128:(k + 1) * 128], in_=xb[:, c * 128:(c + 1) * 128],
                        identity=ident[:]), reads=[rx, "ident"], writes=[f"ps{bank}"], signal=(k == 3))
                if cq % 2:
                    S.op("act", lambda cq=cq, bank=bank, t=t: nc.scalar.copy(
                        out=xT[:, cq * 4:(cq + 1) * 4, t * 128:(t + 1) * 128],
                        in_=P[bank][:].rearrange("p (k n) -> p k n", k=4)), reads=[f"ps{bank}"], writes=["na_xT"])
                else:
                    V.cp(xT[:, cq * 4:(cq + 1) * 4, t * 128:(t + 1) * 128],
                         P[bank][:].rearrange("p (k n) -> p k n", k=4), [f"ps{bank}"], ["na_xT"])
        wb = [[_sb(nc, es2, f"na_w{j}{i}", [128, NDC, HD], BF16) for j in range(3)] for i in range(2)]
        tb = [_sb(nc, es2, f"na_tab{i}", [GW, 8, 8, GW], F32) for i in range(2)]
        qT = _sb(nc, es2, "na_qT", [128, NTK], BF16)
        kT = _sb(nc, es2, "na_kT", [128, NTK], BF16)
        v2 = _sb(nc, es2, "na_v2", [GW, NR, HD], BF16)
        E = [_sb(nc, es2, f"na_E{i}", [GW, 8, GW], F32) for i in range(2)]
        Eb = [_sb(nc, es2, f"na_Eb{i}", [GW, 8, GW], BF16) for i in range(2)]
        rc = [_sb(nc, es2, f"na_rc{i}", [128, GW], F32) for i in range(2)]
        ath = [_sb(nc, es2, f"na_ath{i}", [128, NTK], BF16) for i in range(2)]
        tblocks = [(i * 512, min(512, NTK - i * 512)) for i in range((NTK + 511) // 512)]
        for h in range(NH):
            w3, wn = wb[h % 2], [f"na_w{j}{h % 2}" for j in range(3)]
            for j in range(3):
                S.dma("pool", w3[j][:], wq_v[:, :, j * D + h * HD:j * D + (h + 1) * HD], writes=[wn[j]])
            tbt, tbn = tb[h % 2], f"na_tab{h % 2}"
            S.dma("sp", tbt[:], tab[h], writes=[tbn])
            for j, (dst, dstn) in enumerate(((qT, "na_qT"), (kT, "na_kT"))):
                for bi, (t0, tw) in enumerate(tblocks):
                    bank = (j * len(tblocks) + bi) % 2
                    for c in range(NDC):
                        S.op("pe", lambda j=j, c=c, t0=t0, tw=tw, bank=bank, w3=w3: nc.tensor.matmul(
                            out=P[bank][:, 0:tw], lhsT=w3[j][:, c, :], rhs=xT[:, c, t0:t0 + tw],
                            start=(c == 0), stop=(c == NDC - 1)),
                            reads=[wn[j], "na_xT"], writes=[f"ps{bank}"], signal=(c == NDC - 1))
                    S.op("act", lambda dst=dst, t0=t0, tw=tw, bank=bank: nc.scalar.copy(
                        out=dst[:, t0:t0 + tw], in_=P[bank][:, 0:tw]), reads=[f"ps{bank}"], writes=[dstn])
            for r in range(NR):
                bank = 2 + (r % 2)
                for c in range(NDC):
                    S.op("pe", lambda r=r, c=c, bank=bank, w3=w3: nc.tensor.matmul(
                        out=P[bank][0:GW, 0:HD], lhsT=xT[:, c, r * GW:(r + 1) * GW],
                        rhs=w3[2][:, c, :], start=(c == 0), stop=(c == NDC - 1)),
                        reads=[wn[2], "na_xT"], writes=[f"ps{bank}"], signal=(c == NDC - 1))
                V.cp(v2[:, r, :], P[bank][0:GW, 0:HD], [f"ps{bank}"], ["na_v2"])
            for i in range(NR):
                rs_ = min(max(i - 4, 0), NR - 8)
                var = i - rs_
                e, en = E[i % 2], f"na_E{i % 2}"
                eb, ebn = Eb[i % 2], f"na_Eb{i % 2}"
                bs = 4 + (i % 2) * 2
                for kr in range(8):
                    S.op("pe", lambda kr=kr, i=i, rs_=rs_, bs=bs: nc.tensor.matmul(
                        out=P[bs][0:GW, kr * GW:(kr + 1) * GW], lhsT=kT[:, (rs_ + kr) * GW:(rs_ + kr + 1) * GW],
                        rhs=qT[:, i * GW:(i + 1) * GW], start=True, stop=True),
                        reads=["na_kT", "na_qT"], writes=[f"ps{bs}"], signal=(kr == 7))
                V.stt(e[:], P[bs][0:GW, :].rearrange("p (k q) -> p k q", k=8), float(HD) ** -0.5,
                      tbt[:, var, :, :], ALU.mult, ALU.add, [f"ps{bs}", tbn], [en])
                S.op("act", lambda e=e, eb=eb: nc.scalar.activation(out=eb[:], in_=e[:], func=AF.Exp),
                     reads=[en], writes=[ebn])
                for kr in range(8):
                    S.op("pe", lambda kr=kr, rs_=rs_, bs=bs, eb=eb: nc.tensor.matmul(
                        out=P[bs + 1][:, 0:GW], lhsT=v2[:, rs_ + kr, :], rhs=eb[:, kr, :],
                        start=(kr == 0), stop=(kr == 7)),
                        reads=["na_v2", ebn], writes=[f"ps{bs + 1}"], signal=False)
                for kr in range(8):
                    S.op("pe", lambda kr=kr, bs=bs, eb=eb: nc.tensor.matmul(
                        out=P[bs + 1][:, GW:2 * GW], lhsT=ones[0:GW, :], rhs=eb[:, kr, :], start=(kr == 0),
                        stop=(kr == 7)), reads=["na_ones", ebn], writes=[f"ps{bs + 1}"], signal=(kr == 7))
                rct, rcn = rc[i % 2], f"na_rc{i % 2}"
                S.op("dve", lambda rct=rct, bs=bs: nc.vector.reciprocal(out=rct[:], in_=P[bs + 1][:, GW:2 * GW]),
                     reads=[f"ps{bs + 1}"], writes=[rcn])
                V.tt(ath[h % 2][:, i * GW:(i + 1) * GW], P[bs + 1][:, 0:GW], rct[:], ALU.mult,
                     [f"ps{bs + 1}", rcn], [f"na_ath{h % 2}"])
            S.dma("sp", at_dram[h], ath[h % 2][:], reads=[f"na_ath{h % 2}"], writes=["na_at_dram"])
        S.barrier()
    attnT = _sb(nc, es, "na_attnT", [128, NH, NTK], BF16)
    S.dma("sp", attnT[:], at_dram.rearrange("h p n -> p h n"), reads=["na_at_dram"], writes=["na_attnT"])
    Wo = _sb(nc, es, "na_Wo", [128, NH, D], BF16)
    wo_v = wo.rearrange("(c p) f -> p c f", p=128)
    for q in range(4):
        S.dma("pool", Wo[:, :, q * 512:(q + 1) * 512], wo_v[:, :, q * 512:(q + 1) * 512], writes=["na_Wo"])
    xt = [_sb(nc, es, f"na_x{i}", [128, D], F32) for i in range(2)]
    zt = [_sb(nc, es, f"na_z{i}", [128, D], F32) for i in range(2)]
    st = _sb(nc, es, "na_st", [128, 4, nc.vector.BN_STATS_DIM], F32)
    mv = _sb(nc, es, "na_mv", [128, nc.vector.BN_AGGR_DIM], F32)
    rs = _sb(nc, es, "na_rs", [128, 1], F32)
    for t in range(NTK // 128):
        x, xn = xt[t % 2], f"na_x{t % 2}"
        z, zn = zt[t % 2], f"na_z{t % 2}"
        S.dma("sp", x[:], x_ap[t * 128:(t + 1) * 128, :], writes=[xn])
        for n in range(4):
            bank = n
            for hh in range(NH):
                S.op("pe", lambda hh=hh, n=n, t=t, bank=bank: nc.tensor.matmul(
                    out=P[bank][:], lhsT=attnT[:, hh, t * 128:(t + 1) * 128], rhs=Wo[:, hh, n * 512:(n + 1) * 512],
                    start=(hh == 0), stop=(hh == NH - 1)),
                    reads=["na_attnT", "na_Wo"], writes=[f"ps{bank}"], signal=(hh == NH - 1))
            V.stt(z[:, n * 512:(n + 1) * 512], x[:, n * 512:(n + 1) * 512], ALPHA, P[bank][:], ALU.mult, ALU.add,
                  [xn, f"ps{bank}"], [zn])
        emit_ln(nc, S, z, zn, st, mv, rs, "na_ln", g_rep, b_rep)
        S.dma("sp", y_ap[t * 128:(t + 1) * 128, :], z[:], reads=[zn])


def emit_natten_f(nc, S, es, P, x_ap, y_ap, wqkv, wo, tab, g_rep, b_rep, ident):
    NTK = NR * GW
    NQ = 32 * GW
    V = _V(nc, S, "dve")
    at_dram = nc.dram_tensor("naf_attn_scratch", [NH, 128, NQ], BF16, kind="Internal").ap()
    ones = _sb(nc, es, "na_ones", [128, 128], BF16)
    S.op("dve", lambda: nc.vector.memset(ones[:], 1.0), writes=["na_ones"])
    wq_v = wqkv.rearrange("(c p) f -> p c f", p=128)
    with ExitStack() as es2:
        xT = _sb(nc, es2, "na_xT", [128, NDC, NTK], BF16)
        xs = [_sb(nc, es2, f"na_xs{i}", [128, D], F32) for i in range(2)]
        for t in range(NTK // 128):
            xb, rx = xs[t % 2], f"na_xs{t % 2}"
            S.dma("sp", xb[:], x_ap[t * 128:(t + 1) * 128, :], writes=[rx])
            for cq in range(NDC // 4):
                bank = 4 + (cq % 4)
                for k in range(4):
                    c = cq * 4 + k
                    S.op("pe", lambda c=c, k=k, bank=bank, xb=xb: nc.tensor.transpose(
                        out=P[bank][:, k * 128:(k + 1) * 128], in_=xb[:, c * 128:(c + 1) * 128],
                        identity=ident[:]), reads=[rx, "ident"], writes=[f"ps{bank}"], signal=(k == 3))
                if cq % 2:
                    S.op("act", lambda cq=cq, bank=bank, t=t: nc.scalar.copy(
                        out=xT[:, cq * 4:(cq + 1) * 4, t * 128:(t + 1) * 128],
                        in_=P[bank][:].rearrange("p (k n) -> p k n", k=4)), reads=[f"ps{bank}"], writes=["na_xT"])
                else:
                    V.cp(xT[:, cq * 4:(cq + 1) * 4, t * 128:(t + 1) * 128],
                         P[bank][:].rearrange("p (k n) -> p k n", k=4), [f"ps{bank}"], ["na_xT"])
        wb = [[_sb(nc, es2, f"na_w{j}{i}", [128, NDC, HD], BF16) for j in range(3)] for i in range(2)]
        tb = [_sb(nc, es2, f"na_tab{i}", [GW, 5, 9, GW], F32) for i in range(2)]
        qT = _sb(nc, es2, "na_qT", [128, NTK], BF16)
        kT = _sb(nc, es2, "na_kT", [128, NTK], BF16)
        v2 = _sb(nc, es2, "na_v2", [GW, NR, HD], BF16)
        E = [_sb(nc, es2, f"na_E{i}", [GW, 9, GW], F32) for i in range(2)]
        Eb = [_sb(nc, es2, f"na_Eb{i}", [GW, 9, GW], BF16) for i in range(2)]
        rc = [_sb(nc, es2, f"na_rc{i}", [128, GW], F32) for i in range(2)]
        ath = [_sb(nc, es2, f"na_ath{i}", [128, NQ], BF16) for i in range(2)]
        tblocks = [(i * 512, min(512, NTK - i * 512)) for i in range((NTK + 511) // 512)]
        for h in range(NH):
            w3, wn = wb[h % 2], [f"na_w{j}{h % 2}" for j in range(3)]
            for j in range(3):
                S.dma("pool", w3[j][:], wq_v[:, :, j * D + h * HD:j * D + (h + 1) * HD], writes=[wn[j]])
            tbt, tbn = tb[h % 2], f"na_tab{h % 2}"
            S.dma("sp", tbt[:], tab[h], writes=[tbn])
            for j, (dst, dstn) in enumerate(((qT, "na_qT"), (kT, "na_kT"))):
                for bi, (t0, tw) in enumerate(tblocks):
                    bank = (j * len(tblocks) + bi) % 2
                    for c in range(NDC):
                        S.op("pe", lambda j=j, c=c, t0=t0, tw=tw, bank=bank, w3=w3: nc.tensor.matmul(
                            out=P[bank][:, 0:tw], lhsT=w3[j][:, c, :], rhs=xT[:, c, t0:t0 + tw],
                            start=(c == 0), stop=(c == NDC - 1)),
                            reads=[wn[j], "na_xT"], writes=[f"ps{bank}"], signal=(c == NDC - 1))
                    S.op("act", lambda dst=dst, t0=t0, tw=tw, bank=bank: nc.scalar.copy(
                        out=dst[:, t0:t0 + tw], in_=P[bank][:, 0:tw]), reads=[f"ps{bank}"], writes=[dstn])
            for r in range(NR):
                bank = 2 + (r % 2)
                for c in range(NDC):
                    S.op("pe", lambda r=r, c=c, bank=bank, w3=w3: nc.tensor.matmul(
                        out=P[bank][0:GW, 0:HD], lhsT=xT[:, c, r * GW:(r + 1) * GW],
                        rhs=w3[2][:, c, :], start=(c == 0), stop=(c == NDC - 1)),
                        reads=[wn[2], "na_xT"], writes=[f"ps{bank}"], signal=(c == NDC - 1))
                V.cp(v2[:, r, :], P[bank][0:GW, 0:HD], [f"ps{bank}"], ["na_v2"])
            for i in range(32):
                rows = list(range(9)) if i < 4 else list(range(i - 4, i + 5))
                var = i if i < 4 else 4
                e, en = E[i % 2], f"na_E{i % 2}"
                eb, ebn = Eb[i % 2], f"na_Eb{i % 2}"
                bs = 4 + (i % 2) * 2
                for kr in range(9):
                    dst = P[bs][0:GW, kr * GW:(kr + 1) * GW] if kr < 8 else P[bs + 1][0:GW, 2 * GW:3 * GW]
                    wr = f"ps{bs}" if kr < 8 else f"ps{bs + 1}"
                    S.op("pe", lambda kr=kr, i=i, dst=dst, rows=rows: nc.tensor.matmul(
                        out=dst, lhsT=kT[:, rows[kr] * GW:(rows[kr] + 1) * GW],
                        rhs=qT[:, i * GW:(i + 1) * GW], start=True, stop=True),
                        reads=["na_kT", "na_qT"], writes=[wr], signal=(kr >= 7))
                V.stt(e[:, 0:8, :], P[bs][0:GW, :].rearrange("p (k q) -> p k q", k=8), float(HD) ** -0.5,
                      tbt[:, var, 0:8, :], ALU.mult, ALU.add, [f"ps{bs}", tbn], [en])
                V.stt(e[:, 8, :], P[bs + 1][0:GW, 2 * GW:3 * GW], float(HD) ** -0.5,
                      tbt[:, var, 8, :], ALU.mult, ALU.add, [f"ps{bs + 1}", tbn], [en])
                S.op("act", lambda e=e, eb=eb: nc.scalar.activation(out=eb[:], in_=e[:], func=AF.Exp),
                     reads=[en], writes=[ebn])
                for kr in range(9):
                    S.op("pe", lambda kr=kr, rows=rows, bs=bs, eb=eb: nc.tensor.matmul(
                        out=P[bs + 1][:, 0:GW], lhsT=v2[:, rows[kr], :], rhs=eb[:, kr, :],
                        start=(kr == 0), stop=(kr == 8)),
                        reads=["na_v2", ebn], writes=[f"ps{bs + 1}"], signal=False)
                for kr in range(9):
                    S.op("pe", lambda kr=kr, bs=bs, eb=eb: nc.tensor.matmul(
                        out=P[bs + 1][:, GW:2 * GW], lhsT=ones[0:GW, :], rhs=eb[:, kr, :], start=(kr == 0),
                        stop=(kr == 8)), reads=["na_ones", ebn], writes=[f"ps{bs + 1}"], signal=(kr == 8))
                rct, rcn = rc[i % 2], f"na_rc{i % 2}"
                S.op("dve", lambda rct=rct, bs=bs: nc.vector.reciprocal(out=rct[:], in_=P[bs + 1][:, GW:2 * GW]),
                     reads=[f"ps{bs + 1}"], writes=[rcn])
                V.tt(ath[h % 2][:, i * GW:(i + 1) * GW], P[bs + 1][:, 0:GW], rct[:], ALU.mult,
                     [f"ps{bs + 1}", rcn], [f"na_ath{h % 2}"])
            S.dma("sp", at_dram[h], ath[h % 2][:], reads=[f"na_ath{h % 2}"], writes=["na_at_dram"])
        S.barrier()
    attnT = _sb(nc, es, "na_attnT", [128, NH, NQ], BF16)
    S.dma("sp", attnT[:], at_dram.rearrange("h p n -> p h n"), reads=["na_at_dram"], writes=["na_attnT"])
    Wo = _sb(nc, es, "na_Wo", [128, NH, D], BF16)
    wo_v = wo.rearrange("(c p) f -> p c f", p=128)
    for q in range(4):
        S.dma("pool", Wo[:, :, q * 512:(q + 1) * 512], wo_v[:, :, q * 512:(q + 1) * 512], writes=["na_Wo"])
    xt = [_sb(nc, es, f"na_x{i}", [128, D], F32) for i in range(2)]
    zt = [_sb(nc, es, f"na_z{i}", [128, D], F32) for i in range(2)]
    st = _sb(nc, es, "na_st", [128, 4, nc.vector.BN_STATS_DIM], F32)
    mv = _sb(nc, es, "na_mv", [128, nc.vector.BN_AGGR_DIM], F32)
    rs = _sb(nc, es, "na_rs", [128, 1], F32)
    for t in range(NQ // 128):
        x, xn = xt[t % 2], f"na_x{t % 2}"
        z, zn = zt[t % 2], f"na_z{t % 2}"
        S.dma("sp", x[:], x_ap[t * 128:(t + 1) * 128, :], writes=[xn])
        for n in range(4):
            bank = n
            for hh in range(NH):
                S.op("pe", lambda hh=hh, n=n, t=t, bank=bank: nc.tensor.matmul(
                    out=P[bank][:], lhsT=attnT[:, hh, t * 128:(t + 1) * 128], rhs=Wo[:, hh, n * 512:(n + 1) * 512],
                    start=(hh == 0), stop=(hh == NH - 1)),
                    reads=["na_attnT", "na_Wo"], writes=[f"ps{bank}"], signal=(hh == NH - 1))
            V.stt(z[:, n * 512:(n + 1) * 512], x[:, n * 512:(n + 1) * 512], ALPHA, P[bank][:], ALU.mult, ALU.add,
                  [xn, f"ps{bank}"], [zn])
        emit_ln(nc, S, z, zn, st, mv, rs, "na_ln", g_rep, b_rep)
        S.dma("sp", y_ap[t * 128:(t + 1) * 128, :], z[:], reads=[zn])


def natten_tables(rpb):
    H = rpb.shape[0]
    qc = np.arange(GW)
    cs = np.clip(qc - 8, 0, GW - 16)
    kc = np.arange(GW)
    inwin = (kc[:, None] >= cs[None, :]) & (kc[:, None] < cs[None, :] + 16)
    coff = np.clip(kc[:, None] - qc[None, :] + 15, 0, 30)
    tab = np.full((H, 8, 8, GW, GW), NEG, np.float32)
    for var in range(8):
        for kr in range(8):
            ro = kr - var + 7
            vals = rpb[:, ro][:, coff]
            tab[:, var, kr] = np.where(inwin[None], vals, np.float32(NEG))
    return np.ascontiguousarray(tab.transpose(0, 3, 1, 2, 4))


def run_natten(x_tok, wqkv, rpb, wo, g, b, B, L):
    NTK = NR * GW
    nc = _std_prog(NTK, NTK, {"wqkv": [D, 3 * D], "wo": [D, D], "tab": [NH, GW, 8, 8, GW]},
                   lambda nc, S, es, P, x, y, ex, gr, br, idt: emit_natten(nc, S, es, P, x, y, ex["wqkv"],
                                                                          ex["wo"], ex["tab"], gr, br, idt))
    common = {"wqkv": np.ascontiguousarray(wqkv), "wo": np.ascontiguousarray(wo), "tab": natten_tables(rpb),
              "lng": _rep(g), "lnb": _rep(b), "idn": np.eye(128, dtype=np.float32)}
    half = L // 2
    in_maps = []
    for c in range(NCORES):
        bq, hf = c // 2, c % 2
        t0 = bq * L + (0 if hf == 0 else half - 4 * GW)
        in_maps.append(dict(common, x=np.ascontiguousarray(x_tok[t0:t0 + NTK])))
    res = run_bass_kernel_spmd(nc, in_maps, core_ids=list(range(NCORES)))
    out = np.empty_like(x_tok)
    for c in range(NCORES):
        bq, hf = c // 2, c % 2
        yc = res.results[c]["y"]
        if hf == 0:
            out[bq * L:bq * L + half] = yc[:half]
        else:
            out[bq * L + half:(bq + 1) * L] = yc[4 * GW:]
    return out


def kernel_unfused(x, ffn_w_gate, ffn_w_up, ffn_w_down, ln_g, ln_b, s5_lam_re, s5_lam_im, s5_log_dt, s5_b_re, s5_b_im,
           s5_c_re, s5_c_im, s5_d, s5_w_glu_val, s5_w_glu_gate, na_w_qkv, na_rpb, na_w_out):
    f = lambda a: np.asarray(a, np.float32)
    B, L, _ = x.shape
    h = f(x).reshape(B * L, D)
    for i in range(DEPTH):
        h = run_ffn(h, f(ffn_w_gate[i, 0]), f(ffn_w_up[i, 0]), f(ffn_w_down[i, 0]), f(ln_g[i, 0]), f(ln_b[i, 0]))
        j = i // 2
        if i % 2 == 0:
            ys = run_s5(h, f(s5_lam_re[j]), f(s5_lam_im[j]), f(s5_log_dt[j]), f(s5_b_re[j]), f(s5_b_im[j]),
                        f(s5_c_re[j]), f(s5_c_im[j]), f(s5_d[j]), B, L)
            h = run_glu(h, ys, f(s5_w_glu_val[j]), f(s5_w_glu_gate[j]), f(ln_g[i, 1]), f(ln_b[i, 1]))
        else:
            h = run_natten(h, f(na_w_qkv[j]), f(na_rpb[j]), f(na_w_out[j]), f(ln_g[i, 1]), f(ln_b[i, 1]), B, L)
        h = run_ffn(h, f(ffn_w_gate[i, 1]), f(ffn_w_up[i, 1]), f(ffn_w_down[i, 1]), f(ln_g[i, 2]), f(ln_b[i, 2]))
    return h.reshape(B, L, D)


T2 = 16
NG = D // GC


def emit_s5f_ops(nc, S, P, ident, din, opW, opV, opM, h, GH=64, GB=8):
    V = _V(nc, S, "dve")
    VP = _V(nc, S, "pool")
    gsl = slice(h * GH, (h + 1) * GH)
    with ExitStack() as es:
        sg = _sb(nc, es, "o_sg", [128, 1], F32)
        sr = _sb(nc, es, "o_sr", [128, 1], F32)
        S.dma("sp", sg[:], din["sg"], writes=["o_sg"])
        V.ts(sr[:], sg[:], -1.0, ALU.mult, ["o_sg"], ["o_sr"])
        mf = _sb(nc, es, "o_mf", [128, 128], F32)
        mb = _sb(nc, es, "o_mb", [128, 128], F32)
        dcol = _sb(nc, es, "o_dcol", [128, GH], F32)
        S.dma("sp", mf[:], din["maskf"], writes=["o_mf"])
        S.dma("sp", mb[:], din["maskb"], writes=["o_mb"])
        S.dma("sp", dcol[:], din["dcol"][:, gsl], writes=["o_dcol"])
        stk = {}
        for d in range(2):
            for nm, n in (("PA", T2 + 1), ("PD", T2), ("NA", T2)):
                for ri in "ri":
                    stk[(d, nm, ri)] = (_sb(nc, es, f"o_{nm}{ri}{d}", [128, GH, n], F32), f"o_{nm}{ri}{d}")
        bb1, bb2s, c1s, c2 = [], [], [], []
        keep = []
        for d in range(2):
            b1 = _sb(nc, es, f"o_b1{d}", [128, GH, GC], F32)
            b2 = _sb(nc, es, f"o_b2{d}", [128, GH, GC], F32)
            o1 = _sb(nc, es, f"o_bb1{d}", [128, GH, GC], F32)
            o2 = _sb(nc, es, f"o_bb2{d}", [128, GH, GC], F32)
            keep.append((b1, b2, o1, o2))
        with ExitStack() as esp:
            tmpA = _sb(nc, esp, "o_tmpA", [128, GH, GC], F32)
            tmpB = _sb(nc, esp, "o_tmpB", [128, GH, GC], F32)

            def bc(t, n=GH):
                return t[:].unsqueeze(2).to_broadcast([128, n, GC])

            for d in range(2):
                p = emit_s5_params(nc, S, esp, din["lam_re"][d][:, gsl], din["lam_im"][d][:, gsl],
                                   din["logdt"][d][:, gsl], sg, GH, d, npw=T2)
                for j in range(T2 + 1):
                    for ri, key in (("r", "PR"), ("i", "PI")):
                        t, tn = p[key][j]
                        a, an = stk[(d, "PA", ri)]
                        V.cp(a[:, :, j], t[:], [tn], [an])
                        jd = (T2 - 1 - j) if d == 0 else (T2 - j)
                        if 0 <= jd < T2:
                            a, an = stk[(d, "PD", ri)]
                            V.cp(a[:, :, jd], t[:], [tn], [an])
                for j in range(T2):
                    for ri, key in (("r", "NR"), ("i", "NI")):
                        t, tn = p[key][j]
                        a, an = stk[(d, "NA", ri)]
                        V.cp(a[:, :, j], t[:], [tn], [an])
                b1, b2, o1, o2 = keep[d]
                S.dma("sp", b1[:], din["b1"][d][:, gsl, :], writes=[f"o_b1{d}"])
                S.dma("sp", b2[:], din["b2"][d][:, gsl, :], writes=[f"o_b2{d}"])
                (fr, frn), (fi, fin) = p["fr"], p["fi"]
                V.tt(tmpA[:], b1[:], bc(fr), ALU.mult, [f"o_b1{d}", frn], ["o_tmpA"])
                V.tt(tmpB[:], b2[:], bc(fi), ALU.mult, [f"o_b2{d}", fin], ["o_tmpB"])
                V.ts(tmpB[:], tmpB[:], sg[:, 0:1], ALU.mult, ["o_tmpB", "o_sg"], ["o_tmpB"])
                V.tt(o1[:], tmpA[:], tmpB[:], ALU.add, ["o_tmpA", "o_tmpB"], [f"o_bb1{d}"])
                V.tt(tmpA[:], b2[:], bc(fr), ALU.mult, [f"o_b2{d}", frn], ["o_tmpA"])
                V.tt(tmpB[:], b1[:], bc(fi), ALU.mult, [f"o_b1{d}", fin], ["o_tmpB"])
                V.ts(tmpB[:], tmpB[:], sr[:, 0:1], ALU.mult, ["o_tmpB", "o_sr"], ["o_tmpB"])
                V.tt(o2[:], tmpA[:], tmpB[:], ALU.add, ["o_tmpA", "o_tmpB"], [f"o_bb2{d}"])
                V.ts(o2[:], o2[:], sg[:, 0:1], ALU.mult, [f"o_bb2{d}", "o_sg"], [f"o_bb2{d}"])
                bb1.append((o1, f"o_bb1{d}")); bb2s.append((o2, f"o_bb2{d}"))
                S.dma("sp", b1[:], din["c1"][d][:, gsl, :], reads=[f"o_b1{d}"], writes=[f"o_b1{d}"])
                S.dma("sp", b2[:], din["c2"][d][:, gsl, :], reads=[f"o_b2{d}"], writes=[f"o_b2{d}"])
                V.ts(b1[:], b1[:], sr[:, 0:1], ALU.mult, [f"o_b1{d}", "o_sr"], [f"o_b1{d}"])
                c1s.append((b1, f"o_b1{d}")); c2.append((b2, f"o_b2{d}"))
            S.barrier()

        names = ["WTf", "Af", "WTb", "Bf", "Bb"]
        tmp = {nm: _sb(nc, es, f"ot_{nm}", [128, GB, T2, GC], F32) for nm in names}
        tAB = {e: (_sb(nc, es, f"o_tA{e}", [128, GB, T2, GC], F32), _sb(nc, es, f"o_tB{e}", [128, GB, T2, GC], F32))
               for e in ("dve", "pool")}
        m1 = _sb(nc, es, "o_m1", [128, 128], F32)
        m2 = _sb(nc, es, "o_m2", [128, 128], F32)
        Wt = _sb(nc, es, "o_Wst", [128, GB, 4, 2, 128], BF16)
        Vt = _sb(nc, es, "o_Vst", [128, GB, 2, T2 * GC], BF16)
        Mt = _sb(nc, es, "o_Mst", [128, GB, 2, T2 * GC], BF16)
        Wn, Vn, Mn = "o_Wst", "o_Vst", "o_Mst"

        def gen(VV, dst, dstn, src1, src2, d, tbl, j0, g0, op):
            (s1, s1n), (s2, s2n) = src1, src2
            (tr, trn), (ti, tin) = stk[(d, tbl, "r")], stk[(d, tbl, "i")]
            tA, tB = tAB[VV.e]
            shp = [128, GB, T2, GC]
            VV.tt(tA[:], s1[:, g0:g0 + GB, :].unsqueeze(2).to_broadcast(shp),
                  tr[:, g0:g0 + GB, j0:j0 + T2].unsqueeze(3).to_broadcast(shp), ALU.mult, [s1n, trn], [f"o_tA{VV.e}"])
            VV.tt(tB[:], s2[:, g0:g0 + GB, :].unsqueeze(2).to_broadcast(shp),
                  ti[:, g0:g0 + GB, j0:j0 + T2].unsqueeze(3).to_broadcast(shp), ALU.mult, [s2n, tin], [f"o_tB{VV.e}"])
            VV.tt(dst, tA[:], tB[:], op, [f"o_tA{VV.e}", f"o_tB{VV.e}"], [dstn])

        for bi, g0 in enumerate(range(0, GH, GB)):
            gen(V, tmp["WTf"][:], "ot_WTf", bb1[0], bb2s[0], 0, "PD", 0, g0, ALU.add)
            gen(V, tmp["Af"][:], "ot_Af", bb1[0], bb2s[0], 0, "NA", 0, g0, ALU.add)
            gen(V, tmp["WTb"][:], "ot_WTb", bb1[1], bb2s[1], 1, "PA", 0, g0, ALU.add)
            gen(V, tmp["Bf"][:], "ot_Bf", c1s[0], c2[0], 0, "PA", 0, g0, ALU.subtract)
            gen(V, tmp["Bb"][:], "ot_Bb", c1s[1], c2[1], 1, "NA", 0, g0, ALU.subtract)
            gen(VP, Vt[:, :, 0, :].rearrange("p g (t c) -> p g t c", t=T2), Vn, c1s[0], c2[0], 0, "PA", 1, g0,
                ALU.subtract)
            gen(VP, Vt[:, :, 1, :].rearrange("p g (t c) -> p g t c", t=T2), Vn, c1s[1], c2[1], 1, "PD", 0, g0,
                ALU.subtract)
            for gi in range(GB):
                g = g0 + gi
                for vi, (src, srcn) in enumerate((("WTf", "ot_WTf"), ("WTb", "ot_WTb"))):
                    for a in range(2):
                        bank = (vi * 2 + a) % 2
                        S.op("pe", lambda src=src, gi=gi, a=a, bank=bank: nc.tensor.transpose(
                            out=P[bank][:, 0:128],
                            in_=tmp[src][:, gi, a * 8:(a + 1) * 8, :].rearrange("p t c -> p (t c)"),
                            identity=ident[:]), reads=[srcn, "ident"], writes=[f"ps{bank}"])
                        S.op("act", lambda gi=gi, vi=vi, a=a, bank=bank: nc.scalar.copy(
                            out=Wt[:, gi, 2 * vi, a, :], in_=P[bank][:, 0:128]), reads=[f"ps{bank}"], writes=[Wn])
                        S.op("act", lambda gi=gi, vi=vi, a=a, bank=bank: nc.scalar.copy(
                            out=Wt[:, gi, 2 * vi + 1, a, 0:64], in_=P[bank][:, 64:128]),
                            reads=[f"ps{bank}"], writes=[Wn])
                        S.op("act", lambda gi=gi, vi=vi, a=a, bank=bank: nc.scalar.copy(
                            out=Wt[:, gi, 2 * vi + 1, a, 64:128], in_=P[bank][:, 0:64]),
                            reads=[f"ps{bank}"], writes=[Wn])
                for a in range(2):
                    for b in range(2):
                        asl = slice(a * 8, (a + 1) * 8)
                        bsl = slice(b * 8, (b + 1) * 8)
                        dst = Mt[:, gi, a, b * 128:(b + 1) * 128]
                        b1k, b2k = 2 + 2 * ((a * 2 + b) % 2), 3 + 2 * ((a * 2 + b) % 2)
                        if (a, b) != (1, 0):
                            S.op("pe", lambda gi=gi, asl=asl, bsl=bsl, b1k=b1k: nc.tensor.matmul(
                                out=P[b1k][:, 0:128], lhsT=tmp["Af"][:, gi, asl, :].rearrange("p t c -> p (t c)"),
                                rhs=tmp["Bf"][:, gi, bsl, :].rearrange("p t c -> p (t c)"), start=True, stop=True),
                                reads=["ot_Af", "ot_Bf"], writes=[f"ps{b1k}"])
                        if (a, b) != (0, 1):
                            S.op("pe", lambda gi=gi, asl=asl, bsl=bsl, b2k=b2k: nc.tensor.matmul(
                                out=P[b2k][:, 0:128], lhsT=tmp["WTb"][:, gi, asl, :].rearrange("p t c -> p (t c)"),
                                rhs=tmp["Bb"][:, gi, bsl, :].rearrange("p t c -> p (t c)"), start=True, stop=True),
                                reads=["ot_WTb", "ot_Bb"], writes=[f"ps{b2k}"])
                        if a == b:
                            V.tt(m1[:], P[b1k][:, 0:128], mf[:], ALU.mult, [f"ps{b1k}", "o_mf"], ["o_m1"])
                            V.tt(m2[:], P[b2k][:, 0:128], mb[:], ALU.mult, [f"ps{b2k}", "o_mb"], ["o_m2"])
                            V.tt(m1[:], m1[:], m2[:], ALU.add, ["o_m1", "o_m2"], ["o_m1"])
                            V.stt(dst, ident[:], dcol[:, g:g + 1], m1[:], ALU.mult, ALU.add,
                                  ["ident", "o_dcol", "o_m1"], [Mn])
                        elif (a, b) == (0, 1):
                            V.cp(dst, P[b1k][:, 0:128], [f"ps{b1k}"], [Mn])
                        else:
                            V.cp(dst, P[b2k][:, 0:128], [f"ps{b2k}"], [Mn])
            G0 = h * GH + g0
            S.dma("sp", opW[G0:G0 + GB].rearrange("g p n -> p g n"), Wt[:].rearrange("p g v a n -> p g (v a n)"),
                  reads=[Wn])
            S.dma("sp", opV[G0:G0 + GB].rearrange("g p n -> p g n"), Vt[:].rearrange("p g d n -> p g (d n)"),
                  reads=[Vn])
            S.dma("sp", opM[G0:G0 + GB].rearrange("g p n -> p g n"), Mt[:].rearrange("p g a n -> p g (a n)"),
                  reads=[Mn])
        S.barrier()


def emit_s5f_main(nc, S, P, ident, din, x_ap, ys_ap, opW, opV, opM, h, NK, NKF, GH=64, GB=8):
    V = _V(nc, S, "dve")
    gsl = slice(h * GH, (h + 1) * GH)
    x_k = x_ap.rearrange("(k t) d -> k t d", t=T2)
    y_k = ys_ap.rearrange("(k t) d -> k t d", t=T2)
    with ExitStack() as es:
        sg = _sb(nc, es, "m_sg", [128, 1], F32)
        sr = _sb(nc, es, "m_sr", [128, 1], F32)
        S.dma("sp", sg[:], din["sg"], writes=["m_sg"])
        V.ts(sr[:], sg[:], -1.0, ALU.mult, ["m_sg"], ["m_sr"])
        COEF = [_sb(nc, es, f"m_COEF{d}", [128, 2, 2, GH], F32) for d in range(2)]
        QR = [_sb(nc, es, f"m_QR{d}", [128, 2, 2, GH], F32) for d in range(2)]
        X2 = [[_sb(nc, es, f"m_X{d}{i}", [128, 2, GH], F32) for i in range(2)] for d in range(2)]
        with ExitStack() as esp:
            for d in range(2):
                p = emit_s5_params(nc, S, esp, din["lam_re"][d][:, gsl], din["lam_im"][d][:, gsl],
                                   din["logdt"][d][:, gsl], sg, GH, 10 + d, npw=T2)
                (ar, arn), (ai, ain) = p["PR"][T2], p["PI"][T2]
                V.cp(COEF[d][:, 0, 0, :], ar[:], [arn], [f"m_COEF{d}"])
                V.cp(COEF[d][:, 0, 1, :], ar[:], [arn], [f"m_COEF{d}"])
                V.ts(COEF[d][:, 1, 0, :], ai[:], sr[:, 0:1], ALU.mult, [ain, "m_sr"], [f"m_COEF{d}"])
                V.ts(COEF[d][:, 1, 1, :], ai[:], sg[:, 0:1], ALU.mult, [ain, "m_sg"], [f"m_COEF{d}"])
            S.barrier()
        NKS = [NKF, NK]
        St = [_sb(nc, es, f"m_St{d}", [128, NKS[d] + 1, 2, GH], BF16) for d in range(2)]
        Ubk = _sb(nc, es, "m_Ubk", [128, GH, 2, NKF], BF16)
        S.op("pool", lambda: nc.gpsimd.memset(St[0][:, 0, :, :], 0.0), writes=["m_St0"])
        S.op("pool", lambda: nc.gpsimd.memset(St[1][:, NK, :, :], 0.0), writes=["m_St1"])
        for d in range(2):
            for i in range(2):
                S.op("pool", lambda d=d, i=i: nc.gpsimd.memset(X2[d][i][:], 0.0), writes=[f"m_X{d}{i}"])
        with ExitStack() as es1:
            xt = [_sb(nc, es1, f"m_xt{i}", [128, T2, GB * GC], F32) for i in range(1)] * 2
            xr = [_sb(nc, es1, f"m_xr{i}", [128, GB, T2, GC], F32) for i in range(2)]
            Ut = [_sb(nc, es1, f"m_Ut{i}", [128, GB, 2, NK], BF16) for i in range(1)] * 2
            Wt = [_sb(nc, es1, f"m_Wt{i}", [128, GB, 4, 2, 128], BF16) for i in range(1)] * 2
            ci = 0
            for sb in range(GH // GB):
                G0 = h * GH + sb * GB
                c0 = G0 * GC
                W, Wn = Wt[0], "m_Wt0"
                U, Un = Ut[0], "m_Ut0"
                S.dma("sp", W[:].rearrange("p g v a n -> p g (v a n)"), opW[G0:G0 + GB].rearrange("g p n -> p g n"),
                      writes=[Wn])
                for kb in range(NK // 128):
                    xtt, xtn = xt[0], "m_xt0"
                    xrt, xrn = xr[ci % 2], f"m_xr{ci % 2}"
                    ci += 1
                    S.dma("sp", xtt[:], x_k[kb * 128:(kb + 1) * 128, :, c0:c0 + GB * GC], writes=[xtn])
                    S.op("act", lambda xtt=xtt, xrt=xrt: nc.scalar.copy(
                        out=xrt[:], in_=xtt[:].rearrange("p t (g c) -> p g t c", g=GB)), reads=[xtn], writes=[xrn])
                    for gi in range(GB):
                        bank = gi % 2
                        for a in range(2):
                            S.op("pe", lambda xrt=xrt, gi=gi, a=a, bank=bank: nc.tensor.transpose(
                                out=P[bank][:, a * 128:(a + 1) * 128],
                                in_=xrt[:, gi, a * 8:(a + 1) * 8, :].rearrange("p t c -> p (t c)"),
                                identity=ident[:]), reads=[xrn, "ident"], writes=[f"ps{bank}"], signal=(a == 1))
                        V.cp(U[:, gi, :, kb * 128:(kb + 1) * 128],
                             P[bank][:, 0:256].rearrange("p (a k) -> p a k", a=2), [f"ps{bank}"], [Un])
                S.op("act", lambda U=U, sb=sb: nc.scalar.copy(out=Ubk[:, sb * GB:(sb + 1) * GB, :, :],
                                                              in_=U[:, :, :, 0:NKF]), reads=[Un], writes=["m_Ubk"])
                for gi in range(GB):
                    gl = sb * GB + gi
                    for var in range(4):
                        d, ab = var // 2, var % 2
                        nk = NKS[d]
                        bank = 2 + (gi * 4 + var) % 6
                        for a in range(2):
                            S.op("pe", lambda W=W, U=U, gi=gi, var=var, a=a, nk=nk, bank=bank: nc.tensor.matmul(
                                out=P[bank][:, 0:nk], lhsT=W[:, gi, var, a, :], rhs=U[:, gi, a, 0:nk],
                                start=(a == 0), stop=(a == 1)), reads=[Wn, Un], writes=[f"ps{bank}"],
                                signal=(a == 1))
                        off = 1 if d == 0 else 0
                        if var % 2:
                            S.op("act", lambda d=d, ab=ab, gl=gl, bank=bank, off=off, nk=nk: nc.scalar.copy(
                                out=St[d][:, off:off + nk, ab, gl], in_=P[bank][:, 0:nk]),
                                reads=[f"ps{bank}"], writes=[f"m_St{d}"])
                        else:
                            V.cp(St[d][:, off:off + nk, ab, gl], P[bank][:, 0:nk], [f"ps{bank}"], [f"m_St{d}"])
            S.barrier()
        for k in range(NK):
            for d, e in ((1, "dve"), (0, "pool")):
                if d == 0 and k >= NKF:
                    continue
                E = nc.vector if e == "dve" else nc.gpsimd
                slot = k + 1 if d == 0 else NK - 1 - k
                Xp, Xpn = X2[d][(k + 1) % 2], f"m_X{d}{(k + 1) % 2}"
                Xc, Xcn = X2[d][k % 2], f"m_X{d}{k % 2}"
                Sn = f"m_St{d}_{slot}"
                S.op(e, lambda E=E, d=d, Xp=Xp: E.tensor_tensor(
                    out=QR[d][:], in0=COEF[d][:], in1=Xp[:].unsqueeze(1).to_broadcast([128, 2, 2, GH]),
                    op=ALU.mult), reads=[Xpn, f"m_COEF{d}"], writes=[f"m_QR{d}"])
                S.op(e, lambda E=E, d=d, slot=slot: E.tensor_tensor(
                    out=QR[d][:, 0, :, :], in0=QR[d][:, 0, :, :], in1=St[d][:, slot, :, :], op=ALU.add),
                    reads=[f"m_QR{d}", Sn], writes=[f"m_QR{d}"])
                S.op(e, lambda E=E, d=d, Xc=Xc: E.tensor_tensor(
                    out=Xc[:], in0=QR[d][:, 0, :, :], in1=QR[d][:, 1, ::-1, :], op=ALU.add),
                    reads=[f"m_QR{d}"], writes=[Xcn])
                S.op(e, lambda E=E, d=d, slot=slot, Xc=Xc: E.tensor_copy(out=St[d][:, slot, :, :], in_=Xc[:]),
                     reads=[Xcn], writes=[Sn])
        S.barrier()
        with ExitStack() as es3:
            Vt = [_sb(nc, es3, f"m_Vt{i}", [128, GB, 2, T2 * GC], BF16) for i in range(2)]
            Mt = [_sb(nc, es3, f"m_Mt{i}", [128, GB, 2, T2 * GC], BF16) for i in range(2)]
            Ysb = [_sb(nc, es3, f"m_Y{i}", [128, T2, GB * GC], F32) for i in range(2)]
            yi = 0
            for sb in range(GH // GB):
                G0 = h * GH + sb * GB
                c0 = G0 * GC
                Vv, Vn = Vt[sb % 2], f"m_Vt{sb % 2}"
                Mm, Mn = Mt[sb % 2], f"m_Mt{sb % 2}"
                S.dma("sp", Vv[:].rearrange("p g d n -> p g (d n)"), opV[G0:G0 + GB].rearrange("g p n -> p g n"),
                      writes=[Vn])
                S.dma("sp", Mm[:].rearrange("p g a n -> p g (a n)"), opM[G0:G0 + GB].rearrange("g p n -> p g n"),
                      writes=[Mn])
                for (k0, kn) in ((0, 128), (128, NKF - 128)):
                    Yt, Yn = Ysb[yi % 2], f"m_Y{yi % 2}"
                    yi += 1
                    for gi in range(GB):
                        gl = sb * GB + gi
                        bank = gi % 4
                        ysl = P[bank][0:kn, 0:T2 * GC]
                        ops = [(Ubk[:, gl, 0, k0:k0 + kn], Mm[:, gi, 0, :], ["m_Ubk", Mn]),
                               (Ubk[:, gl, 1, k0:k0 + kn], Mm[:, gi, 1, :], ["m_Ubk", Mn]),
                               (St[0][:, k0:k0 + kn, 0, gl], Vv[:, gi, 0, :], ["m_St0", Vn]),
                               (St[1][:, k0 + 1:k0 + kn + 1, 0, gl], Vv[:, gi, 1, :], ["m_St1", Vn])]
                        for oi, (l, r, rd) in enumerate(ops):
                            S.op("pe", lambda l=l, r=r, ysl=ysl, oi=oi: nc.tensor.matmul(
                                out=ysl, lhsT=l, rhs=r, start=(oi == 0), stop=(oi == 3)),
                                reads=rd, writes=[f"ps{bank}"], signal=(oi == 3))
                        src = ysl.rearrange("p (t c) -> p t c", t=T2)
                        if gi % 2:
                            S.op("act", lambda Yt=Yt, gi=gi, src=src, kn=kn: nc.scalar.copy(
                                out=Yt[0:kn, :, gi * GC:(gi + 1) * GC], in_=src), reads=[f"ps{bank}"], writes=[Yn])
                        else:
                            V.cp(Yt[0:kn, :, gi * GC:(gi + 1) * GC], src, [f"ps{bank}"], [Yn])
                    S.dma("sp", y_k[k0:k0 + kn, :, c0:c0 + GB * GC], Yt[0:kn, :, :], reads=[Yn])
            S.barrier()
        S.barrier()


def emit_s5f_main_old(nc, S, P, ident, din, x_ap, ys_ap, opW, opV, opM, h, NK, NKF, GH=64, GB=8):
    V = _V(nc, S, "dve")
    gsl = slice(h * GH, (h + 1) * GH)
    x_k = x_ap.rearrange("(k t) d -> k t d", t=T2)
    y_k = ys_ap.rearrange("(k t) d -> k t d", t=T2)
    with ExitStack() as es:
        sg = _sb(nc, es, "m_sg", [128, 1], F32)
        sr = _sb(nc, es, "m_sr", [128, 1], F32)
        S.dma("sp", sg[:], din["sg"], writes=["m_sg"])
        V.ts(sr[:], sg[:], -1.0, ALU.mult, ["m_sg"], ["m_sr"])
        AR2 = [_sb(nc, es, f"m_AR2{d}", [128, 2, GH], F32) for d in range(2)]
        AIx = [_sb(nc, es, f"m_AIx{d}", [128, 2, GH], F32) for d in range(2)]
        with ExitStack() as esp:
            for d in range(2):
                p = emit_s5_params(nc, S, esp, din["lam_re"][d][:, gsl], din["lam_im"][d][:, gsl],
                                   din["logdt"][d][:, gsl], sg, GH, 10 + d, npw=T2)
                (ar, arn), (ai, ain) = p["PR"][T2], p["PI"][T2]
                V.cp(AR2[d][:, 0, :], ar[:], [arn], [f"m_AR2{d}"])
                V.cp(AR2[d][:, 1, :], ar[:], [arn], [f"m_AR2{d}"])
                V.ts(AIx[d][:, 0, :], ai[:], sr[:, 0:1], ALU.mult, [ain, "m_sr"], [f"m_AIx{d}"])
                V.ts(AIx[d][:, 1, :], ai[:], sg[:, 0:1], ALU.mult, [ain, "m_sg"], [f"m_AIx{d}"])
            S.barrier()
        NKS = [NKF, NK]
        St = [_sb(nc, es, f"m_St{d}", [128, NKS[d] + 1, 2, GH], BF16) for d in range(2)]
        Ubk = _sb(nc, es, "m_Ubk", [128, GH, 2, NKF], BF16)
        X = [_sb(nc, es, f"m_X{d}", [128, 2, GH], F32) for d in range(2)]
        q1 = [_sb(nc, es, f"m_q1{d}", [128, 2, GH], F32) for d in range(2)]
        q2 = [_sb(nc, es, f"m_q2{d}", [128, 2, GH], F32) for d in range(2)]
        S.op("pool", lambda: nc.gpsimd.memset(St[0][:, 0, :, :], 0.0), writes=["m_St0"])
        S.op("pool", lambda: nc.gpsimd.memset(St[1][:, NK, :, :], 0.0), writes=["m_St1"])
        for d in range(2):
            S.op("pool", lambda d=d: nc.gpsimd.memset(X[d][:], 0.0), writes=[f"m_X{d}"])
        with ExitStack() as es1:
            xt = [_sb(nc, es1, f"m_xt{i}", [128, T2, GB * GC], F32) for i in range(1)] * 2
            xr = [_sb(nc, es1, f"m_xr{i}", [128, GB, T2, GC], F32) for i in range(2)]
            Ut = [_sb(nc, es1, f"m_Ut{i}", [128, GB, 2, NK], BF16) for i in range(1)] * 2
            Wt = [_sb(nc, es1, f"m_Wt{i}", [128, GB, 4, 2, 128], BF16) for i in range(1)] * 2
            ci = 0
            for sb in range(GH // GB):
                G0 = h * GH + sb * GB
                c0 = G0 * GC
                W, Wn = Wt[0], "m_Wt0"
                U, Un = Ut[0], "m_Ut0"
                S.dma("sp", W[:].rearrange("p g v a n -> p g (v a n)"), opW[G0:G0 + GB].rearrange("g p n -> p g n"),
                      writes=[Wn])
                for kb in range(NK // 128):
                    xtt, xtn = xt[0], "m_xt0"
                    xrt, xrn = xr[ci % 2], f"m_xr{ci % 2}"
                    ci += 1
                    S.dma("sp", xtt[:], x_k[kb * 128:(kb + 1) * 128, :, c0:c0 + GB * GC], writes=[xtn])
                    S.op("act", lambda xtt=xtt, xrt=xrt: nc.scalar.copy(
                        out=xrt[:], in_=xtt[:].rearrange("p t (g c) -> p g t c", g=GB)), reads=[xtn], writes=[xrn])
                    for gi in range(GB):
                        bank = gi % 2
                        for a in range(2):
                            S.op("pe", lambda xrt=xrt, gi=gi, a=a, bank=bank: nc.tensor.transpose(
                                out=P[bank][:, a * 128:(a + 1) * 128],
                                in_=xrt[:, gi, a * 8:(a + 1) * 8, :].rearrange("p t c -> p (t c)"),
                                identity=ident[:]), reads=[xrn, "ident"], writes=[f"ps{bank}"], signal=(a == 1))
                        V.cp(U[:, gi, :, kb * 128:(kb + 1) * 128],
                             P[bank][:, 0:256].rearrange("p (a k) -> p a k", a=2), [f"ps{bank}"], [Un])
                S.op("act", lambda U=U, sb=sb: nc.scalar.copy(out=Ubk[:, sb * GB:(sb + 1) * GB, :, :],
                                                              in_=U[:, :, :, 0:NKF]), reads=[Un], writes=["m_Ubk"])
                for gi in range(GB):
                    gl = sb * GB + gi
                    for var in range(4):
                        d, ab = var // 2, var % 2
                        nk = NKS[d]
                        bank = 2 + (gi * 4 + var) % 6
                        for a in range(2):
                            S.op("pe", lambda W=W, U=U, gi=gi, var=var, a=a, nk=nk, bank=bank: nc.tensor.matmul(
                                out=P[bank][:, 0:nk], lhsT=W[:, gi, var, a, :], rhs=U[:, gi, a, 0:nk],
                                start=(a == 0), stop=(a == 1)), reads=[Wn, Un], writes=[f"ps{bank}"],
                                signal=(a == 1))
                        off = 1 if d == 0 else 0
                        if var % 2:
                            S.op("act", lambda d=d, ab=ab, gl=gl, bank=bank, off=off, nk=nk: nc.scalar.copy(
                                out=St[d][:, off:off + nk, ab, gl], in_=P[bank][:, 0:nk]),
                                reads=[f"ps{bank}"], writes=[f"m_St{d}"])
                        else:
                            V.cp(St[d][:, off:off + nk, ab, gl], P[bank][:, 0:nk], [f"ps{bank}"], [f"m_St{d}"])
            S.barrier()
        for k in range(NK):
            for d, e in ((0, "dve"), (1, "pool")):
                if d == 0 and k >= NKF:
                    continue
                E = nc.vector if e == "dve" else nc.gpsimd
                slot = k + 1 if d == 0 else NK - 1 - k
                Xn, q1n, q2n, Sn = f"m_X{d}", f"m_q1{d}", f"m_q2{d}", f"m_St{d}"
                S.op(e, lambda E=E, d=d: E.tensor_tensor(out=q1[d][:], in0=X[d][:], in1=AR2[d][:], op=ALU.mult),
                     reads=[Xn, f"m_AR2{d}"], writes=[q1n])
                S.op(e, lambda E=E, d=d: E.tensor_tensor(out=q2[d][:], in0=X[d][:], in1=AIx[d][:], op=ALU.mult),
                     reads=[Xn, f"m_AIx{d}"], writes=[q2n])
                S.op(e, lambda E=E, d=d, slot=slot: E.tensor_tensor(out=q1[d][:], in0=q1[d][:],
                                                                    in1=St[d][:, slot, :, :], op=ALU.add),
                     reads=[q1n, Sn], writes=[q1n])
                S.op(e, lambda E=E, d=d: E.tensor_tensor(out=X[d][:, 0, :], in0=q1[d][:, 0, :], in1=q2[d][:, 1, :],
                                                         op=ALU.add), reads=[q1n, q2n], writes=[Xn])
                S.op(e, lambda E=E, d=d: E.tensor_tensor(out=X[d][:, 1, :], in0=q1[d][:, 1, :], in1=q2[d][:, 0, :],
                                                         op=ALU.add), reads=[q1n, q2n], writes=[Xn])
                S.op(e, lambda E=E, d=d, slot=slot: E.tensor_copy(out=St[d][:, slot, :, :], in_=X[d][:]),
                     reads=[Xn], writes=[Sn])
        with ExitStack() as es3:
            Vt = [_sb(nc, es3, f"m_Vt{i}", [128, GB, 2, T2 * GC], BF16) for i in range(2)]
            Mt = [_sb(nc, es3, f"m_Mt{i}", [128, GB, 2, T2 * GC], BF16) for i in range(2)]
            Ysb = [_sb(nc, es3, f"m_Y{i}", [128, T2, GB * GC], F32) for i in range(2)]
            yi = 0
            for sb in range(GH // GB):
                G0 = h * GH + sb * GB
                c0 = G0 * GC
                Vv, Vn = Vt[sb % 2], f"m_Vt{sb % 2}"
                Mm, Mn = Mt[sb % 2], f"m_Mt{sb % 2}"
                S.dma("sp", Vv[:].rearrange("p g d n -> p g (d n)"), opV[G0:G0 + GB].rearrange("g p n -> p g n"),
                      writes=[Vn])
                S.dma("sp", Mm[:].rearrange("p g a n -> p g (a n)"), opM[G0:G0 + GB].rearrange("g p n -> p g n"),
                      writes=[Mn])
                for (k0, kn) in ((0, 128), (128, NKF - 128)):
                    Yt, Yn = Ysb[yi % 2], f"m_Y{yi % 2}"
                    yi += 1
                    for gi in range(GB):
                        gl = sb * GB + gi
                        bank = gi % 4
                        ysl = P[bank][0:kn, 0:T2 * GC]
                        ops = [(Ubk[:, gl, 0, k0:k0 + kn], Mm[:, gi, 0, :], ["m_Ubk", Mn]),
                               (Ubk[:, gl, 1, k0:k0 + kn], Mm[:, gi, 1, :], ["m_Ubk", Mn]),
                               (St[0][:, k0:k0 + kn, 0, gl], Vv[:, gi, 0, :], ["m_St0", Vn]),
                               (St[1][:, k0 + 1:k0 + kn + 1, 0, gl], Vv[:, gi, 1, :], ["m_St1", Vn])]
                        for oi, (l, r, rd) in enumerate(ops):
                            S.op("pe", lambda l=l, r=r, ysl=ysl, oi=oi: nc.tensor.matmul(
                                out=ysl, lhsT=l, rhs=r, start=(oi == 0), stop=(oi == 3)),
                                reads=rd, writes=[f"ps{bank}"], signal=(oi == 3))
                        src = ysl.rearrange("p (t c) -> p t c", t=T2)
                        if gi % 2:
                            S.op("act", lambda Yt=Yt, gi=gi, src=src, kn=kn: nc.scalar.copy(
                                out=Yt[0:kn, :, gi * GC:(gi + 1) * GC], in_=src), reads=[f"ps{bank}"], writes=[Yn])
                        else:
                            V.cp(Yt[0:kn, :, gi * GC:(gi + 1) * GC], src, [f"ps{bank}"], [Yn])
                    S.dma("sp", y_k[k0:k0 + kn, :, c0:c0 + GB * GC], Yt[0:kn, :, :], reads=[Yn])
            S.barrier()
        S.barrier()


def emit_s5f_ops_old(nc, S, P, ident, din, opW, opV, opM, h, GH=64, GB=8):
    V = _V(nc, S, "dve")
    gsl = slice(h * GH, (h + 1) * GH)
    with ExitStack() as es:
        sg = _sb(nc, es, "o_sg", [128, 1], F32)
        sr = _sb(nc, es, "o_sr", [128, 1], F32)
        S.dma("sp", sg[:], din["sg"], writes=["o_sg"])
        V.ts(sr[:], sg[:], -1.0, ALU.mult, ["o_sg"], ["o_sr"])
        mf = _sb(nc, es, "o_mf", [128, 128], F32)
        mb = _sb(nc, es, "o_mb", [128, 128], F32)
        dcol = _sb(nc, es, "o_dcol", [128, GH], F32)
        S.dma("sp", mf[:], din["maskf"], writes=["o_mf"])
        S.dma("sp", mb[:], din["maskb"], writes=["o_mb"])
        S.dma("sp", dcol[:], din["dcol"][:, gsl], writes=["o_dcol"])
        prm = [emit_s5_params(nc, S, es, din["lam_re"][d][:, gsl], din["lam_im"][d][:, gsl],
                              din["logdt"][d][:, gsl], sg, GH, d, npw=T2) for d in range(2)]
        tmpA = _sb(nc, es, "o_tmpA", [128, GH, GC], F32)
        tmpB = _sb(nc, es, "o_tmpB", [128, GH, GC], F32)

        def bc(t, n=GH):
            return t[:].unsqueeze(2).to_broadcast([128, n, GC])

        bb1, bb2s, c1s, c2 = [], [], [], []
        for d in range(2):
            b1 = _sb(nc, es, f"o_b1{d}", [128, GH, GC], F32)
            b2 = _sb(nc, es, f"o_b2{d}", [128, GH, GC], F32)
            S.dma("sp", b1[:], din["b1"][d][:, gsl, :], writes=[f"o_b1{d}"])
            S.dma("sp", b2[:], din["b2"][d][:, gsl, :], writes=[f"o_b2{d}"])
            o1 = _sb(nc, es, f"o_bb1{d}", [128, GH, GC], F32)
            o2 = _sb(nc, es, f"o_bb2{d}", [128, GH, GC], F32)
            (fr, frn), (fi, fin) = prm[d]["fr"], prm[d]["fi"]
            V.tt(tmpA[:], b1[:], bc(fr), ALU.mult, [f"o_b1{d}", frn], ["o_tmpA"])
            V.tt(tmpB[:], b2[:], bc(fi), ALU.mult, [f"o_b2{d}", fin], ["o_tmpB"])
            V.ts(tmpB[:], tmpB[:], sg[:, 0:1], ALU.mult, ["o_tmpB", "o_sg"], ["o_tmpB"])
            V.tt(o1[:], tmpA[:], tmpB[:], ALU.add, ["o_tmpA", "o_tmpB"], [f"o_bb1{d}"])
            V.tt(tmpA[:], b2[:], bc(fr), ALU.mult, [f"o_b2{d}", frn], ["o_tmpA"])
            V.tt(tmpB[:], b1[:], bc(fi), ALU.mult, [f"o_b1{d}", fin], ["o_tmpB"])
            V.ts(tmpB[:], tmpB[:], sr[:, 0:1], ALU.mult, ["o_tmpB", "o_sr"], ["o_tmpB"])
            V.tt(o2[:], tmpA[:], tmpB[:], ALU.add, ["o_tmpA", "o_tmpB"], [f"o_bb2{d}"])
            V.ts(o2[:], o2[:], sg[:, 0:1], ALU.mult, [f"o_bb2{d}", "o_sg"], [f"o_bb2{d}"])
            bb1.append((o1, f"o_bb1{d}")); bb2s.append((o2, f"o_bb2{d}"))
            S.dma("sp", b1[:], din["c1"][d][:, gsl, :], reads=[f"o_b1{d}"], writes=[f"o_b1{d}"])
            S.dma("sp", b2[:], din["c2"][d][:, gsl, :], reads=[f"o_b2{d}"], writes=[f"o_b2{d}"])
            V.ts(b1[:], b1[:], sr[:, 0:1], ALU.mult, [f"o_b1{d}", "o_sr"], [f"o_b1{d}"])
            c1s.append((b1, f"o_b1{d}")); c2.append((b2, f"o_b2{d}"))

        names = ["WTf", "Af", "Bf", "Vf", "WTb", "Bb", "Vb"]
        tmp = {nm: _sb(nc, es, f"ot_{nm}", [128, GB, T2, GC], F32) for nm in names}
        tA = _sb(nc, es, "o_tA", [128, GB, GC], F32)
        tB = _sb(nc, es, "o_tB", [128, GB, GC], F32)
        m1 = _sb(nc, es, "o_m1", [128, 128], F32)
        m2 = _sb(nc, es, "o_m2", [128, 128], F32)
        Wst = [_sb(nc, es, f"o_Wst{i}", [128, GB, 4, 2, 128], BF16) for i in range(2)]
        Vst = [_sb(nc, es, f"o_Vst{i}", [128, GB, 2, T2 * GC], BF16) for i in range(2)]
        Mst = [_sb(nc, es, f"o_Mst{i}", [128, GB, 2, T2 * GC], BF16) for i in range(2)]

        def gen(dst, dstn, tsl, src1, src2, pa, pb, g0, op):
            (s1, s1n), (s2, s2n), (a, an), (b, bn) = src1, src2, pa, pb
            ab = a[:, g0:g0 + GB].unsqueeze(2).to_broadcast([128, GB, GC])
            bbb = b[:, g0:g0 + GB].unsqueeze(2).to_broadcast([128, GB, GC])
            V.tt(tA[:], s1[:, g0:g0 + GB, :], ab, ALU.mult, [s1n, an], ["o_tA"])
            V.tt(tB[:], s2[:, g0:g0 + GB, :], bbb, ALU.mult, [s2n, bn], ["o_tB"])
            V.tt(dst[:, :, tsl, :], tA[:], tB[:], op, ["o_tA", "o_tB"], [dstn])

        pf, pb_ = prm
        for bi, g0 in enumerate(range(0, GH, GB)):
            Wt, Wn = Wst[bi % 2], f"o_Wst{bi % 2}"
            Vt, Vn = Vst[bi % 2], f"o_Vst{bi % 2}"
            Mt, Mn = Mst[bi % 2], f"o_Mst{bi % 2}"
            for j in range(T2):
                gen(tmp["WTf"], "ot_WTf", j, bb1[0], bb2s[0], pf["PR"][T2 - 1 - j], pf["PI"][T2 - 1 - j], g0, ALU.add)
                gen(tmp["Af"], "ot_Af", j, bb1[0], bb2s[0], pf["NR"][j], pf["NI"][j], g0, ALU.add)
                gen(tmp["WTb"], "ot_WTb", j, bb1[1], bb2s[1], pb_["PR"][j], pb_["PI"][j], g0, ALU.add)
                gen(tmp["Bf"], "ot_Bf", j, c1s[0], c2[0], pf["PR"][j], pf["PI"][j], g0, ALU.subtract)
                gen(tmp["Vf"], "ot_Vf", j, c1s[0], c2[0], pf["PR"][j + 1], pf["PI"][j + 1], g0, ALU.subtract)
                gen(tmp["Bb"], "ot_Bb", j, c1s[1], c2[1], pb_["NR"][j], pb_["NI"][j], g0, ALU.subtract)
                gen(tmp["Vb"], "ot_Vb", j, c1s[1], c2[1], pb_["PR"][T2 - j], pb_["PI"][T2 - j], g0, ALU.subtract)
            V.cp(Vt[:, :, 0, :], tmp["Vf"][:].rearrange("p g t c -> p g (t c)"), ["ot_Vf"], [Vn])
            V.cp(Vt[:, :, 1, :], tmp["Vb"][:].rearrange("p g t c -> p g (t c)"), ["ot_Vb"], [Vn])
            for gi in range(GB):
                g = g0 + gi
                for vi, (src, srcn) in enumerate((("WTf", "ot_WTf"), ("WTb", "ot_WTb"))):
                    for a in range(2):
                        bank = (vi * 2 + a) % 2
                        S.op("pe", lambda src=src, gi=gi, a=a, bank=bank: nc.tensor.transpose(
                            out=P[bank][:, 0:128],
                            in_=tmp[src][:, gi, a * 8:(a + 1) * 8, :].rearrange("p t c -> p (t c)"),
                            identity=ident[:]), reads=[srcn, "ident"], writes=[f"ps{bank}"])
                        S.op("act", lambda Wt=Wt, gi=gi, vi=vi, a=a, bank=bank: nc.scalar.copy(
                            out=Wt[:, gi, 2 * vi, a, :], in_=P[bank][:, 0:128]), reads=[f"ps{bank}"], writes=[Wn])
                        V.cp(Wt[:, gi, 2 * vi + 1, a, 0:64], P[bank][:, 64:128], [f"ps{bank}"], [Wn])
                        V.cp(Wt[:, gi, 2 * vi + 1, a, 64:128], P[bank][:, 0:64], [f"ps{bank}"], [Wn])
                for a in range(2):
                    for b in range(2):
                        asl = slice(a * 8, (a + 1) * 8)
                        bsl = slice(b * 8, (b + 1) * 8)
                        dst = Mt[:, gi, a, b * 128:(b + 1) * 128]
                        if (a, b) != (1, 0):
                            S.op("pe", lambda gi=gi, asl=asl, bsl=bsl: nc.tensor.matmul(
                                out=P[2][:, 0:128], lhsT=tmp["Af"][:, gi, asl, :].rearrange("p t c -> p (t c)"),
                                rhs=tmp["Bf"][:, gi, bsl, :].rearrange("p t c -> p (t c)"), start=True, stop=True),
                                reads=["ot_Af", "ot_Bf"], writes=["ps2"])
                        if (a, b) != (0, 1):
                            S.op("pe", lambda gi=gi, asl=asl, bsl=bsl: nc.tensor.matmul(
                                out=P[3][:, 0:128], lhsT=tmp["WTb"][:, gi, asl, :].rearrange("p t c -> p (t c)"),
                                rhs=tmp["Bb"][:, gi, bsl, :].rearrange("p t c -> p (t c)"), start=True, stop=True),
                                reads=["ot_WTb", "ot_Bb"], writes=["ps3"])
                        if a == b:
                            V.tt(m1[:], P[2][:, 0:128], mf[:], ALU.mult, ["ps2", "o_mf"], ["o_m1"])
                            V.tt(m2[:], P[3][:, 0:128], mb[:], ALU.mult, ["ps3", "o_mb"], ["o_m2"])
                            V.tt(m1[:], m1[:], m2[:], ALU.add, ["o_m1", "o_m2"], ["o_m1"])
                            V.stt(dst, ident[:], dcol[:, g:g + 1], m1[:], ALU.mult, ALU.add,
                                  ["ident", "o_dcol", "o_m1"], [Mn])
                        elif (a, b) == (0, 1):
                            V.cp(dst, P[2][:, 0:128], ["ps2"], [Mn])
                        else:
                            V.cp(dst, P[3][:, 0:128], ["ps3"], [Mn])
            G0 = h * GH + g0
            S.dma("sp", opW[G0:G0 + GB].rearrange("g p n -> p g n"), Wt[:].rearrange("p g v a n -> p g (v a n)"),
                  reads=[Wn])
            S.dma("sp", opV[G0:G0 + GB].rearrange("g p n -> p g n"), Vt[:].rearrange("p g d n -> p g (d n)"),
                  reads=[Vn])
            S.dma("sp", opM[G0:G0 + GB].rearrange("g p n -> p g n"), Mt[:].rearrange("p g a n -> p g (a n)"),
                  reads=[Mn])
        S.barrier()
    return prm


def s5f_host_inputs(lam_re, lam_im, log_dt, b_re, b_im, c_re, c_im, d_skip, swap):
    o = [1, 0] if swap else [0, 1]
    def pg(a):
        a = a[o].transpose(0, 2, 1)
        return np.ascontiguousarray(np.concatenate([a, a], axis=1))
    lr, li = pg(lam_re), pg(lam_im)
    ldt = np.ascontiguousarray(np.broadcast_to(log_dt[o][:, None, :], (2, 128, NG)))
    br = b_re[o].transpose(0, 2, 1, 3)
    bi = b_im[o].transpose(0, 2, 1, 3)
    cr = c_re[o].transpose(0, 3, 1, 2)
    ci = c_im[o].transpose(0, 3, 1, 2)
    st = lambda a, b: np.ascontiguousarray(np.concatenate([a, b], axis=1))
    sg = np.concatenate([-np.ones((64, 1), np.float32), np.ones((64, 1), np.float32)], 0)
    dcol = np.ascontiguousarray(np.tile(d_skip.reshape(NG, GC).T, (8, 1)))
    tt = np.arange(128) // GC
    maskf = (tt[None, :] >= tt[:, None]).astype(np.float32)
    maskb = (tt[None, :] <= tt[:, None]).astype(np.float32)
    return {"s5_lam_re": lr, "s5_lam_im": li, "s5_logdt": ldt, "s5_b1": st(br, bi), "s5_b2": st(bi, br),
            "s5_c1": st(cr, ci), "s5_c2": st(ci, cr), "s5_sg": sg, "s5_dcol": dcol, "s5_maskf": maskf,
            "s5_maskb": maskb}


S5F_SHAPES = {"s5_lam_re": [2, 128, NG], "s5_lam_im": [2, 128, NG], "s5_logdt": [2, 128, NG],
              "s5_b1": [2, 128, NG, GC], "s5_b2": [2, 128, NG, GC], "s5_c1": [2, 128, NG, GC],
              "s5_c2": [2, 128, NG, GC], "s5_sg": [128, 1], "s5_dcol": [128, NG], "s5_maskf": [128, 128],
              "s5_maskb": [128, 128]}


def emit_s5f(nc, S, P, ident, ex, x_ap, ys_ap, NK, NKF):
    din = {k[3:]: ex[k] for k in S5F_SHAPES}
    for k in ("lam_re", "lam_im", "logdt", "b1", "b2", "c1", "c2"):
        din[k] = [din[k][d] for d in range(2)]
    opW = nc.dram_tensor("s5_opW", [NG, 128, 4 * 2 * 128], BF16, kind="Internal").ap()
    opV = nc.dram_tensor("s5_opV", [NG, 128, 2 * T2 * GC], BF16, kind="Internal").ap()
    opM = nc.dram_tensor("s5_opM", [NG, 128, 2 * T2 * GC], BF16, kind="Internal").ap()
    import os
    f_ops = emit_s5f_ops_old if os.environ.get("S5_OPS") == "old" else emit_s5f_ops
    f_main = emit_s5f_main_old if os.environ.get("S5_MAIN") == "old" else emit_s5f_main
    for h in range(2):
        f_ops(nc, S, P, ident, din, opW, opV, opM, h)
    for h in range(2):
        f_main(nc, S, P, ident, din, x_ap, ys_ap, opW, opV, opM, h, NK, NKF)


LSEQ = 4096
NOWN = 2048
NHAL = 2304


def natten_tables_f(rpb, mirrored):
    H = rpb.shape[0]
    ROWS = LSEQ // GW
    tab = np.full((H, 5, 9, GW, GW), NEG, np.float32)
    qc_l = np.arange(GW)
    kc_l = np.arange(GW)
    qc_t = (GW - 1 - qc_l) if mirrored else qc_l
    kc_t = (GW - 1 - kc_l) if mirrored else kc_l
    cs = np.clip(qc_t - 8, 0, GW - 16)
    inwin = (kc_t[:, None] >= cs[None, :]) & (kc_t[:, None] < cs[None, :] + 16)
    coff = np.clip(kc_t[:, None] - qc_t[None, :] + 15, 0, 30)
    for var in range(5):
        i = var
        rows = list(range(9)) if i < 4 else list(range(i - 4, i + 5))
        r_t = (ROWS - 1 - i) if mirrored else i
        rs = int(np.clip(r_t - 4, 0, ROWS - 8))
        for kr in range(9):
            k_t = (ROWS - 1 - rows[kr]) if mirrored else rows[kr]
            if not (rs <= k_t < rs + 8):
                continue
            ro = k_t - r_t + 7
            vals = rpb[:, ro][:, coff]
            tab[:, var, kr] = np.where(inwin[None], vals, np.float32(NEG))
    return np.ascontiguousarray(tab.transpose(0, 3, 1, 2, 4))


def build_fused():
    nc = bass.Bass("TRN2", target_bir_lowering=False)
    def inp(name, shape):
        return nc.dram_tensor(name, shape, F32, kind="ExternalInput").ap()
    x = inp("x", [LSEQ, D])
    W = [(inp(f"wg{k}", [D, FF]), inp(f"wu{k}", [D, FF]), inp(f"wd{k}", [FF, D])) for k in range(4)]
    lng = inp("lng", [6, 128, D])
    lnb = inp("lnb", [6, 128, D])
    idn = inp("idn", [128, 128])
    ex = {k: inp(k, shp) for k, shp in S5F_SHAPES.items()}
    glu_wv, glu_wg = inp("glu_wv", [D, D]), inp("glu_wg", [D, D])
    na_wqkv, na_wo = inp("na_wqkv", [D, 3 * D]), inp("na_wo", [D, D])
    na_tab = inp("na_tab", [NH, GW, 5, 9, GW])
    y = nc.dram_tensor("y", [NOWN, D], F32, kind="ExternalOutput").ap()
    def scr(name, n):
        return nc.dram_tensor(name, [n, D], F32, kind="Internal").ap()
    a1, ys, a2, a3, a4, a5 = (scr("a1", LSEQ), scr("ys", NHAL), scr("a2", NHAL), scr("a3", NHAL),
                              scr("a4", NHAL), scr("a5", NOWN))
    with ExitStack() as es:
        es.enter_context(nc.allow_low_precision("bf16 matmul operands, fp32 accumulation"))
        es.enter_context(nc.allow_non_contiguous_dma("weight block loads"))
        S = Sched(nc, es)
        P = [es.enter_context(nc.psum_tensor(f"ps{i}", [128, 512], F32)) for i in range(8)]
        ident = _sb(nc, es, "ident", [128, 128], F32)
        S.dma("sp", ident[:], idn, writes=["ident"])

        def stage(idx, fn):
            with ExitStack() as es2:
                g_rep = _sb(nc, es2, "g_rep", [128, D], F32)
                b_rep = _sb(nc, es2, "b_rep", [128, D], F32)
                S.dma("sp", g_rep[:], lng[idx], writes=["lng"])
                S.dma("sp", b_rep[:], lnb[idx], writes=["lnb"])
                fn(es2, g_rep, b_rep)
                S.barrier()

        stage(0, lambda e, g, b: emit_ffn(nc, S, e, P, x, a1, *W[0], g, b, ident, LSEQ, "fa"))
        emit_s5f(nc, S, P, ident, ex, a1, ys, LSEQ // T2, NHAL // T2)
        S.barrier()
        stage(1, lambda e, g, b: emit_glu(nc, S, e, P, a1[0:NHAL, :], ys, a2, glu_wv, glu_wg, g, b, ident, NHAL))
        stage(2, lambda e, g, b: emit_ffn(nc, S, e, P, a2, a3, *W[1], g, b, ident, NHAL, "fb"))
        stage(3, lambda e, g, b: emit_ffn(nc, S, e, P, a3, a4, *W[2], g, b, ident, NHAL, "fc"))
        stage(4, lambda e, g, b: emit_natten_f(nc, S, e, P, a4, a5, na_wqkv, na_wo, na_tab, g, b, ident))
        stage(5, lambda e, g, b: emit_ffn(nc, S, e, P, a5, y, *W[3], g, b, ident, NOWN, "fd"))
        S.finish()
    return nc


def kernel(x, ffn_w_gate, ffn_w_up, ffn_w_down, ln_g, ln_b, s5_lam_re, s5_lam_im, s5_log_dt, s5_b_re, s5_b_im,
           s5_c_re, s5_c_im, s5_d, s5_w_glu_val, s5_w_glu_gate, na_w_qkv, na_rpb, na_w_out):
    f = lambda a: np.ascontiguousarray(np.asarray(a, np.float32))
    B, L, _ = x.shape
    xf = f(x)
    common = {"idn": np.eye(128, dtype=np.float32)}
    for k, (i, j) in enumerate(((0, 0), (0, 1), (1, 0), (1, 1))):
        common[f"wg{k}"] = f(ffn_w_gate[i, j])
        common[f"wu{k}"] = f(ffn_w_up[i, j])
        common[f"wd{k}"] = f(ffn_w_down[i, j])
    lg = f(ln_g).reshape(6, D)
    lb = f(ln_b).reshape(6, D)
    common["lng"] = np.ascontiguousarray(np.broadcast_to(lg[:, None, :], (6, 128, D)))
    common["lnb"] = np.ascontiguousarray(np.broadcast_to(lb[:, None, :], (6, 128, D)))
    common["glu_wv"] = f(s5_w_glu_val[0])
    common["glu_wg"] = f(s5_w_glu_gate[0])
    common["na_wqkv"] = f(na_w_qkv[0])
    common["na_wo"] = f(na_w_out[0])
    s5p = [s5f_host_inputs(f(s5_lam_re[0]), f(s5_lam_im[0]), f(s5_log_dt[0]), f(s5_b_re[0]), f(s5_b_im[0]),
                           f(s5_c_re[0]), f(s5_c_im[0]), f(s5_d[0]), swap) for swap in (False, True)]
    tabs = [natten_tables_f(f(na_rpb[0]), m) for m in (False, True)]
    in_maps = []
    for c in range(NCORES):
        b, hf = c // 2, c % 2
        xl = xf[b] if hf == 0 else np.ascontiguousarray(xf[b][::-1])
        in_maps.append(dict(common, x=xl, na_tab=tabs[hf], **s5p[hf]))
    nc = build_fused()
    res = run_bass_kernel_spmd(nc, in_maps, core_ids=list(range(NCORES)))
    out = np.empty((B, L, D), np.float32)
    for c in range(NCORES):
        b, hf = c // 2, c % 2
        yc = res.results[c]["y"]
        if hf == 0:
            out[b, :NOWN] = yc
        else:
            out[b, NOWN:] = yc[::-1]
    return out
```

```python
import numpy as np
from contextlib import ExitStack

import concourse.bass as bass
import concourse.mybir as mybir
from concourse.bass_utils import run_bass_kernel_spmd

F32 = mybir.dt.float32
BF16 = mybir.dt.bfloat16
ALU = mybir.AluOpType
AF = mybir.ActivationFunctionType

D = 2048
FF = 5504
NFC = FF // 128
NDC = D // 128
DEPTH = 2
ALPHA = (2 * DEPTH) ** 0.25
LN_EPS = 1e-5
NCORES = 8


class Sched:
    def __init__(self, nc, es, n_dma_sems=12):
        self.nc = nc
        self.eng = {"pe": nc.tensor, "act": nc.scalar, "dve": nc.vector, "pool": nc.gpsimd, "sp": nc.sync}
        self.sem = {k: es.enter_context(nc.semaphore("s_" + k)) for k in ["pe", "act", "dve", "pool"]}
        self.cnt = {k: 0 for k in self.sem}
        self.nd = n_dma_sems
        self.dsem = {q: [es.enter_context(nc.semaphore(f"d_{q}{i}")) for i in range(n_dma_sems)]
                     for q in ["sp", "pool"]}
        self.dcnt = {q: [0] * n_dma_sems for q in self.dsem}
        self.dnext = {q: 0 for q in self.dsem}
        self.waited = {e: {} for e in self.eng}
        self.lastw = {}
        self.readers = {}

    def _wait(self, e, tok):
        key, sem, val = tok
        if self.waited[e].get(key, 0) >= val:
            return
        self.eng[e].wait_ge(sem, val)
        self.waited[e][key] = val

    def _deps(self, e, reads, writes):
        toks = []
        for r in reads:
            t = self.lastw.get(r)
            if t is not None:
                toks.append(t)
        for w in writes:
            t = self.lastw.get(w)
            if t is not None:
                toks.append(t)
            toks.extend(self.readers.get(w, {}).values())
        for t in toks:
            if t[0] == "pe" and e == "pe":
                continue
            self._wait(e, t)

    def _record(self, tok, reads, writes):
        for r in reads:
            self.readers.setdefault(r, {})[tok[0]] = tok
        for w in writes:
            self.lastw[w] = tok
            self.readers[w] = {}

    def op(self, e, fn, reads=(), writes=(), signal=True):
        self._deps(e, reads, writes)
        inst = fn()
        if signal:
            self.cnt[e] += 1
            inst.then_inc(self.sem[e], 1)
            tok = (e, self.sem[e], self.cnt[e])
        else:
            tok = (e, self.sem[e], self.cnt[e] + 1)
        self._record(tok, reads, writes)
        return tok

    def dma(self, q, out, in_, reads=(), writes=()):
        i = self.dnext[q]
        self.dnext[q] = (i + 1) % self.nd
        sem = self.dsem[q][i]
        key = f"d_{q}{i}"
        if self.dcnt[q][i]:
            self._wait(q, (key, sem, self.dcnt[q][i] * 16))
        self._deps(q, reads, writes)
        self.eng[q].dma_start(out=out, in_=in_).then_inc(sem, 16)
        self.dcnt[q][i] += 1
        tok = (key, sem, self.dcnt[q][i] * 16)
        self._record(tok, reads, writes)
        return tok

    def barrier(self):
        toks = [(e, self.sem[e], self.cnt[e]) for e in self.sem if self.cnt[e]]
        for q in self.dsem:
            for i in range(self.nd):
                if self.dcnt[q][i]:
                    toks.append((f"d_{q}{i}", self.dsem[q][i], self.dcnt[q][i] * 16))
        for e in self.eng:
            for t in toks:
                if t[0] != e:
                    self._wait(e, t)

    def finish(self):
        for q in self.dsem:
            for i in range(self.nd):
                if self.dcnt[q][i]:
                    self._wait(q, (f"d_{q}{i}", self.dsem[q][i], self.dcnt[q][i] * 16))


_UNIQ = [0]


def _sb(nc, es, name, shape, dt):
    _UNIQ[0] += 1
    return es.enter_context(nc.sbuf_tensor(f"{name}_u{_UNIQ[0]}", shape, dt))


def emit_ffn(nc, S, es, P, x_ap, y_ap, wg, wu, wd, g_rep, b_rep, ident, NT, tag):
    TT = 512
    NS = TT // 128
    FW = 256
    DW = 256
    xs = [_sb(nc, es, f"{tag}_xs{i}", [128, D], F32) for i in range(2)]
    xT = _sb(nc, es, f"{tag}_xT", [128, NDC, TT], BF16)
    wgb = [_sb(nc, es, f"{tag}_wg{i}", [128, NDC, FW], BF16) for i in range(2)]
    wub = [_sb(nc, es, f"{tag}_wu{i}", [128, NDC, FW], BF16) for i in range(2)]
    hT = _sb(nc, es, f"{tag}_hT", [128, NFC, TT], BF16)
    wdb = [_sb(nc, es, f"{tag}_wd{i}", [128, NFC, DW], BF16) for i in range(2)]
    sg = [_sb(nc, es, f"{tag}_sg{i}", [128, TT], F32) for i in range(2)]
    z = [_sb(nc, es, f"{tag}_z{i}", [128, D], F32) for i in range(2)]
    st = [_sb(nc, es, f"{tag}_st{i}", [128, 4, nc.vector.BN_STATS_DIM], F32) for i in range(2)]
    mv = [_sb(nc, es, f"{tag}_mv{i}", [128, nc.vector.BN_AGGR_DIM], F32) for i in range(2)]
    rs = [_sb(nc, es, f"{tag}_rs{i}", [128, 1], F32) for i in range(2)]

    wg_v = wg.rearrange("(c p) f -> p c f", p=128)
    wu_v = wu.rearrange("(c p) f -> p c f", p=128)
    wd_v = wd.rearrange("(c p) d -> p c d", p=128)
    nfb = (FF + FW - 1) // FW
    ndb = D // DW
    xi = 0
    wi = 0
    di = 0
    pi = 0
    tiles = [(i * TT, TT) for i in range(NT // TT)]
    if NT % TT:
        tiles.append((NT - NT % TT, NT % TT))
    for (t0, tt) in tiles:
        NS = tt // 128
        for s in range(NS):
            xb = xs[xi % 2]
            rx = f"{tag}_xs{xi % 2}"
            xi += 1
            S.dma("sp", xb[:], x_ap[t0 + s * 128:t0 + (s + 1) * 128, :], writes=[rx])
            for cq in range(NDC // 4):
                bank = 4 + (cq % 4)
                for k in range(4):
                    c = cq * 4 + k
                    S.op("pe", lambda c=c, k=k, bank=bank: nc.tensor.transpose(
                        out=P[bank][:, k * 128:(k + 1) * 128], in_=xb[:, c * 128:(c + 1) * 128],
                        identity=ident[:]), reads=[rx, "ident"], writes=[f"ps{bank}"], signal=(k == 3))
                S.op("act" if cq % 2 else "dve",
                     (lambda cq=cq, bank=bank, s=s: nc.scalar.copy(
                         out=xT[:, cq * 4:(cq + 1) * 4, s * 128:(s + 1) * 128],
                         in_=P[bank][:].rearrange("p (k n) -> p k n", k=4))) if cq % 2 else
                     (lambda cq=cq, bank=bank, s=s: nc.vector.tensor_copy(
                         out=xT[:, cq * 4:(cq + 1) * 4, s * 128:(s + 1) * 128],
                         in_=P[bank][:].rearrange("p (k n) -> p k n", k=4))),
                     reads=[f"ps{bank}"], writes=[f"{tag}_xT"])
        for fb in range(nfb):
            f0 = fb * FW
            fw = min(FW, FF - f0)
            wgt, wut = wgb[wi % 2], wub[wi % 2]
            rwg, rwu = f"{tag}_wg{wi % 2}", f"{tag}_wu{wi % 2}"
            wi += 1
            S.dma("pool", wgt[:, :, 0:fw], wg_v[:, :, f0:f0 + fw], writes=[rwg])
            S.dma("pool", wut[:, :, 0:fw], wu_v[:, :, f0:f0 + fw], writes=[rwu])
            for jj in range(fw // 128):
                j = (f0 // 128) + jj
                pg, pu = (pi % 2) * 2, (pi % 2) * 2 + 1
                sgt = sg[pi % 2]
                rsg = f"{tag}_sg{pi % 2}"
                pi += 1
                for c in range(NDC):
                    S.op("pe", lambda c=c, jj=jj, pg=pg, wgt=wgt, tt=tt: nc.tensor.matmul(
                        out=P[pg][:, 0:tt], lhsT=wgt[:, c, jj * 128:(jj + 1) * 128], rhs=xT[:, c, 0:tt],
                        start=(c == 0), stop=(c == NDC - 1)),
                        reads=[rwg, f"{tag}_xT"], writes=[f"ps{pg}"], signal=(c == NDC - 1))
                for c in range(NDC):
                    S.op("pe", lambda c=c, jj=jj, pu=pu, wut=wut, tt=tt: nc.tensor.matmul(
                        out=P[pu][:, 0:tt], lhsT=wut[:, c, jj * 128:(jj + 1) * 128], rhs=xT[:, c, 0:tt],
                        start=(c == 0), stop=(c == NDC - 1)),
                        reads=[rwu, f"{tag}_xT"], writes=[f"ps{pu}"], signal=(c == NDC - 1))
                S.op("act", lambda pg=pg, sgt=sgt, tt=tt: nc.scalar.activation(
                    out=sgt[:, 0:tt], in_=P[pg][:, 0:tt], func=AF.Silu), reads=[f"ps{pg}"], writes=[rsg])
                S.op("dve", lambda j=j, pu=pu, sgt=sgt, tt=tt: nc.vector.tensor_tensor(
                    out=hT[:, j, 0:tt], in0=sgt[:, 0:tt], in1=P[pu][:, 0:tt], op=ALU.mult),
                    reads=[rsg, f"ps{pu}"], writes=[f"{tag}_hT"])
        for sp in range(NS // 2):
            zs = []
            for q in range(2):
                s = sp * 2 + q
                xb = xs[xi % 2]
                rx = f"{tag}_xs{xi % 2}"
                zt, rz = z[q], f"{tag}_z{q}"
                xi += 1
                S.dma("sp", xb[:], x_ap[t0 + s * 128:t0 + (s + 1) * 128, :], writes=[rx])
                zs.append((s, xb, rx, zt, rz))
            for db in range(ndb):
                wdt = wdb[di % 2]
                rwd = f"{tag}_wd{di % 2}"
                di += 1
                S.dma("pool", wdt[:], wd_v[:, :, db * DW:(db + 1) * DW], writes=[rwd])
                for q, (s, xb, rx, zt, rz) in enumerate(zs):
                    bank = 4 + ((db * 2 + q) % 4)
                    for j in range(NFC):
                        S.op("pe", lambda j=j, s=s, bank=bank, wdt=wdt: nc.tensor.matmul(
                            out=P[bank][:, 0:DW], lhsT=hT[:, j, s * 128:(s + 1) * 128], rhs=wdt[:, j, :],
                            start=(j == 0), stop=(j == NFC - 1)),
                            reads=[f"{tag}_hT", rwd], writes=[f"ps{bank}"], signal=(j == NFC - 1))
                    S.op("act", lambda xb=xb, db=db: nc.scalar.mul(
                        out=xb[:, db * DW:(db + 1) * DW], in_=xb[:, db * DW:(db + 1) * DW], mul=ALPHA),
                        reads=[rx], writes=[rx])
                    S.op("dve", lambda zt=zt, xb=xb, db=db, bank=bank: nc.vector.scalar_tensor_tensor(
                        out=zt[:, db * DW:(db + 1) * DW], in0=P[bank][:, 0:DW], scalar=0.5,
                        in1=xb[:, db * DW:(db + 1) * DW], op0=ALU.mult, op1=ALU.add),
                        reads=[f"ps{bank}", rx], writes=[rz])
            for q, (s, xb, rx, zt, rz) in enumerate(zs):
                emit_ln(nc, S, zt, rz, st[q], mv[q], rs[q], f"{tag}_ln{q}", g_rep, b_rep)
                S.dma("sp", y_ap[t0 + s * 128:t0 + (s + 1) * 128, :], zt[:], reads=[rz])


def emit_ln(nc, S, zt, rz, stt, mvt, rst, rtag, g_rep, b_rep):
    for k in range(4):
        S.op("dve", lambda k=k: nc.vector.bn_stats(out=stt[:, k, :], in_=zt[:, k * 512:(k + 1) * 512]),
             reads=[rz], writes=[rtag + "st"])
    S.op("dve", lambda: nc.vector.bn_aggr(out=mvt[:], in_=stt[:]), reads=[rtag + "st"], writes=[rtag + "mv"])
    S.op("dve", lambda: nc.vector.tensor_scalar(out=rst[:], in0=mvt[:, 1:2], scalar1=LN_EPS, scalar2=None,
                                                op0=ALU.add),
         reads=[rtag + "mv"], writes=[rtag + "rs"])
    S.op("act", lambda: nc.scalar.sqrt(out=rst[:], in_=rst[:]), reads=[rtag + "rs"], writes=[rtag + "rs"])
    S.op("dve", lambda: nc.vector.reciprocal(out=rst[:], in_=rst[:]), reads=[rtag + "rs"], writes=[rtag + "rs"])
    S.op("dve", lambda: nc.vector.tensor_scalar(out=zt[:], in0=zt[:], scalar1=mvt[:, 0:1], scalar2=rst[:, 0:1],
                                                op0=ALU.subtract, op1=ALU.mult),
         reads=[rz, rtag + "mv", rtag + "rs"], writes=[rz])
    S.op("dve", lambda: nc.vector.tensor_tensor(out=zt[:], in0=zt[:], in1=g_rep[:], op=ALU.mult),
         reads=[rz, "lng"], writes=[rz])
    S.op("dve", lambda: nc.vector.tensor_tensor(out=zt[:], in0=zt[:], in1=b_rep[:], op=ALU.add),
         reads=[rz, "lnb"], writes=[rz])


def build_ffn(NT):
    nc = bass.Bass("TRN2", target_bir_lowering=False)
    x = nc.dram_tensor("x", [NT, D], F32, kind="ExternalInput").ap()
    wg = nc.dram_tensor("wg", [D, FF], F32, kind="ExternalInput").ap()
    wu = nc.dram_tensor("wu", [D, FF], F32, kind="ExternalInput").ap()
    wd = nc.dram_tensor("wd", [FF, D], F32, kind="ExternalInput").ap()
    lng = nc.dram_tensor("lng", [128, D], F32, kind="ExternalInput").ap()
    lnb = nc.dram_tensor("lnb", [128, D], F32, kind="ExternalInput").ap()
    idn = nc.dram_tensor("idn", [128, 128], F32, kind="ExternalInput").ap()
    y = nc.dram_tensor("y", [NT, D], F32, kind="ExternalOutput").ap()
    with ExitStack() as es:
        es.enter_context(nc.allow_low_precision("bf16 matmul operands, fp32 accumulation"))
        es.enter_context(nc.allow_non_contiguous_dma("weight block loads"))
        S = Sched(nc, es)
        P = [es.enter_context(nc.psum_tensor(f"ps{i}", [128, 512], F32)) for i in range(8)]
        ident = _sb(nc, es, "ident", [128, 128], F32)
        g_rep = _sb(nc, es, "g_rep", [128, D], F32)
        b_rep = _sb(nc, es, "b_rep", [128, D], F32)
        S.dma("sp", ident[:], idn, writes=["ident"])
        S.dma("sp", g_rep[:], lng, writes=["lng"])
        S.dma("sp", b_rep[:], lnb, writes=["lnb"])
        emit_ffn(nc, S, es, P, x, y, wg, wu, wd, g_rep, b_rep, ident, NT, "f")
        S.finish()
    return nc


def _rep(v):
    return np.ascontiguousarray(np.broadcast_to(np.asarray(v, np.float32)[None, :], (128, v.shape[-1])))


def run_ffn(x_tok, wg, wu, wd, g, b):
    NT = x_tok.shape[0] // NCORES
    nc = build_ffn(NT)
    idn = np.eye(128, dtype=np.float32)
    common = {"wg": np.ascontiguousarray(wg), "wu": np.ascontiguousarray(wu), "wd": np.ascontiguousarray(wd),
              "lng": _rep(g), "lnb": _rep(b), "idn": idn}
    in_maps = [dict(common, x=np.ascontiguousarray(x_tok[c * NT:(c + 1) * NT])) for c in range(NCORES)]
    res = run_bass_kernel_spmd(nc, in_maps, core_ids=list(range(NCORES)))
    return np.concatenate([r["y"] for r in res.results], axis=0)


TC = 8
GC = 16
NP = 64
MIN_NEG_RE = -1e-4
TWO_PI = 6.283185307179586


class _V:
    def __init__(self, nc, S, e="dve"):
        self.nc, self.S, self.e = nc, S, e
        self.E = nc.vector if e == "dve" else nc.gpsimd

    def tt(self, out, a, b, op, R, W):
        self.S.op(self.e, lambda: self.E.tensor_tensor(out=out, in0=a, in1=b, op=op), reads=R, writes=W)

    def ts(self, out, a, s1, op0, R, W, s2=None, op1=None):
        if op1 is None:
            self.S.op(self.e, lambda: self.E.tensor_scalar(out=out, in0=a, scalar1=s1, scalar2=None, op0=op0),
                      reads=R, writes=W)
        else:
            self.S.op(self.e, lambda: self.E.tensor_scalar(out=out, in0=a, scalar1=s1, scalar2=s2, op0=op0,
                                                           op1=op1), reads=R, writes=W)

    def stt(self, out, a, s, b, op0, op1, R, W):
        self.S.op(self.e, lambda: self.nc.vector.scalar_tensor_tensor(out=out, in0=a, scalar=s, in1=b, op0=op0,
                                                                      op1=op1), reads=R, writes=W)

    def cp(self, out, a, R, W):
        self.S.op(self.e, lambda: self.E.tensor_copy(out=out, in_=a), reads=R, writes=W)


def _poly(V, out, x2, coefs, tmpn, R):
    first = True
    for c in reversed(coefs):
        if first:
            V.ts(out, x2, float(c), ALU.mult, R, [tmpn])
            first = False
        else:
            V.stt(out, out, float(c), x2, ALU.add, ALU.mult, R + [tmpn], [tmpn])


def emit_s5_params(nc, S, es, lam_re, lam_im, logdt, sg, GL, d, npw=8):
    V = _V(nc, S, "dve")
    n = [0]

    def T(nm):
        n[0] += 1
        return _sb(nc, es, f"s5p{d}_{nm}{n[0]}", [128, GL], F32), f"s5p{d}_{nm}{n[0]}"

    lr, lrn = T("lr")
    li, lin = T("li")
    dt, dtn = T("dt")
    S.dma("sp", lr[:], lam_re, writes=[lrn])
    S.dma("sp", li[:], lam_im, writes=[lin])
    S.dma("sp", dt[:], logdt, writes=[dtn])
    V.ts(lr[:], lr[:], MIN_NEG_RE, ALU.min, [lrn], [lrn])
    S.op("act", lambda: nc.scalar.activation(out=dt[:], in_=dt[:], func=AF.Exp), reads=[dtn], writes=[dtn])
    x, xn = T("x")
    V.tt(x[:], lr[:], dt[:], ALU.mult, [lrn, dtn], [xn])
    mag, magn = T("mag")
    cf = [1.0, 1 / 2.0, 1 / 6.0, 1 / 24.0, 1 / 120.0, 1 / 720.0, 1 / 5040.0]
    _poly(V, mag[:], x[:], cf, magn, [xn])
    V.ts(mag[:], mag[:], 1.0, ALU.add, [magn], [magn])
    th, thn = T("th")
    V.tt(th[:], li[:], dt[:], ALU.mult, [lin, dtn], [thn])
    ki = _sb(nc, es, f"s5p{d}_ki", [128, GL], mybir.dt.int32)
    kin = f"s5p{d}_ki"
    kf, kfn = T("kf")
    V.ts(kf[:], th[:], 1.0 / TWO_PI, ALU.mult, [thn], [kfn])
    V.cp(ki[:], kf[:], [kfn], [kin])
    V.cp(kf[:], ki[:], [kin], [kfn])
    C1 = 6.28125
    C2 = TWO_PI - C1
    r, rn = T("r")
    V.stt(r[:], kf[:], -C1, th[:], ALU.mult, ALU.add, [kfn, thn], [rn])
    V.stt(r[:], kf[:], -C2, r[:], ALU.mult, ALU.add, [kfn, rn], [rn])
    m, mn = T("m")
    V.ts(m[:], r[:], float(np.pi), ALU.is_gt, [rn], [mn], s2=-TWO_PI, op1=ALU.mult)
    V.tt(r[:], r[:], m[:], ALU.add, [rn, mn], [rn])
    V.ts(m[:], r[:], -float(np.pi), ALU.is_lt, [rn], [mn], s2=TWO_PI, op1=ALU.mult)
    V.tt(r[:], r[:], m[:], ALU.add, [rn, mn], [rn])
    r2, r2n = T("r2")
    V.tt(r2[:], r[:], r[:], ALU.mult, [rn], [r2n])
    import math
    sc = [(-1.0) ** k / math.factorial(2 * k + 1) for k in range(1, 12)]
    cc = [(-1.0) ** k / math.factorial(2 * k) for k in range(1, 12)]
    sn, snn = T("sn")
    cs, csn = T("cs")
    _poly(V, sn[:], r2[:], sc, snn, [r2n])
    V.stt(sn[:], sn[:], 1.0, r[:], ALU.add, ALU.mult, [snn, rn], [snn])
    _poly(V, cs[:], r2[:], cc, csn, [r2n])
    V.ts(cs[:], cs[:], 1.0, ALU.add, [csn], [csn])
    out = {}
    PR, PI, NR, NI = [], [], [], []
    p0r, p0rn = T("p0r")
    p0i, p0in = T("p0i")
    S.op("dve", lambda: nc.vector.memset(p0r[:], 1.0), writes=[p0rn])
    S.op("dve", lambda: nc.vector.memset(p0i[:], 0.0), writes=[p0in])
    l1r, l1rn = T("l1r")
    l1i, l1in = T("l1i")
    V.tt(l1r[:], mag[:], cs[:], ALU.mult, [magn, csn], [l1rn])
    V.tt(l1i[:], mag[:], sn[:], ALU.mult, [magn, snn], [l1in])
    PR.append((p0r, p0rn)); PI.append((p0i, p0in))
    PR.append((l1r, l1rn)); PI.append((l1i, l1in))
    t1, t1n = T("t1")
    t2, t2n = T("t2")

    def cmul(ar, ai, br, bi):
        (a_r, a_rn), (a_i, a_in), (b_r, b_rn), (b_i, b_in) = ar, ai, br, bi
        o_r, o_rn = T("pw")
        o_i, o_in = T("pw")
        V.tt(t1[:], a_r[:], b_r[:], ALU.mult, [a_rn, b_rn], [t1n])
        V.tt(t2[:], a_i[:], b_i[:], ALU.mult, [a_in, b_in], [t2n])
        V.tt(o_r[:], t1[:], t2[:], ALU.subtract, [t1n, t2n], [o_rn])
        V.tt(t1[:], a_r[:], b_i[:], ALU.mult, [a_rn, b_in], [t1n])
        V.tt(t2[:], a_i[:], b_r[:], ALU.mult, [a_in, b_rn], [t2n])
        V.tt(o_i[:], t1[:], t2[:], ALU.add, [t1n, t2n], [o_in])
        return (o_r, o_rn), (o_i, o_in)

    for j in range(2, npw + 1):
        pr, pi_ = cmul(PR[-1], PI[-1], PR[1], PI[1])
        PR.append(pr); PI.append(pi_)
    m2, m2n = T("m2")
    V.tt(m2[:], mag[:], mag[:], ALU.mult, [magn], [m2n])
    S.op("dve", lambda: nc.vector.reciprocal(out=m2[:], in_=m2[:]), reads=[m2n], writes=[m2n])
    n1r, n1rn = T("n1r")
    n1i, n1in = T("n1i")
    V.tt(n1r[:], l1r[:], m2[:], ALU.mult, [l1rn, m2n], [n1rn])
    V.stt(n1i[:], l1i[:], -1.0, m2[:], ALU.mult, ALU.mult, [l1in, m2n], [n1in])
    NR.append((p0r, p0rn)); NI.append((p0i, p0in))
    NR.append((n1r, n1rn)); NI.append((n1i, n1in))
    for j in range(2, npw):
        pr, pi_ = cmul(NR[-1], NI[-1], NR[1], NI[1])
        NR.append(pr); NI.append(pi_)
    den, denn = T("den")
    V.tt(den[:], lr[:], lr[:], ALU.mult, [lrn], [denn])
    V.tt(t1[:], li[:], li[:], ALU.mult, [lin], [t1n])
    V.tt(den[:], den[:], t1[:], ALU.add, [denn, t1n], [denn])
    S.op("dve", lambda: nc.vector.reciprocal(out=den[:], in_=den[:]), reads=[denn], writes=[denn])
    nr, nrn = T("nr")
    V.ts(nr[:], l1r[:], -1.0, ALU.add, [l1rn], [nrn])
    fr, frn = T("fr")
    fi, fin = T("fi")
    V.tt(t1[:], nr[:], lr[:], ALU.mult, [nrn, lrn], [t1n])
    V.tt(t2[:], l1i[:], li[:], ALU.mult, [l1in, lin], [t2n])
    V.tt(fr[:], t1[:], t2[:], ALU.add, [t1n, t2n], [frn])
    V.tt(fr[:], fr[:], den[:], ALU.mult, [frn, denn], [frn])
    V.tt(t1[:], l1i[:], lr[:], ALU.mult, [l1in, lrn], [t1n])
    V.tt(t2[:], nr[:], li[:], ALU.mult, [nrn, lin], [t2n])
    V.tt(fi[:], t1[:], t2[:], ALU.subtract, [t1n, t2n], [fin])
    V.tt(fi[:], fi[:], den[:], ALU.mult, [fin, denn], [fin])
    return {"PR": PR, "PI": PI, "NR": NR, "NI": NI, "fr": (fr, frn), "fi": (fi, fin)}


def emit_s5(nc, S, es, P, ident, din, y_out, GL, NK, GB=16):
    V = _V(nc, S, "dve")
    NKB = NK // 128
    sg = _sb(nc, es, "s5_sg", [128, 1], F32)
    sr = _sb(nc, es, "s5_sr", [128, 1], F32)
    S.dma("sp", sg[:], din["sg"], writes=["s5_sg"])
    V.ts(sr[:], sg[:], -1.0, ALU.mult, ["s5_sg"], ["s5_sr"])
    maskf = _sb(nc, es, "s5_mf", [128, 128], F32)
    maskb = _sb(nc, es, "s5_mb", [128, 128], F32)
    dcol = _sb(nc, es, "s5_dcol", [128, GL], F32)
    S.dma("sp", maskf[:], din["maskf"], writes=["s5_mf"])
    S.dma("sp", maskb[:], din["maskb"], writes=["s5_mb"])
    S.dma("sp", dcol[:], din["dcol"], writes=["s5_dcol"])
    prm = [emit_s5_params(nc, S, es, din["lam_re"][d], din["lam_im"][d], din["logdt"][d], sg, GL, d)
           for d in range(2)]
    def signed(tbl, sgn, sgn_n, d, nm):
        res = []
        for j, (t, tn) in enumerate(tbl):
            o = _sb(nc, es, f"s5s{d}_{nm}{j}", [128, GL], F32)
            on = f"s5s{d}_{nm}{j}"
            V.ts(o[:], t[:], sgn[:, 0:1], ALU.mult, [tn, sgn_n], [on])
            res.append((o, on))
        return res
    for d in range(2):
        p = prm[d]
        p["sgPI"] = signed(p["PI"], sg, "s5_sg", d, "gpi")
        p["srPR"] = signed(p["PR"], sr, "s5_sr", d, "rpr")
        p["sgNI"] = signed(p["NI"], sg, "s5_sg", d, "gni")
        p["srNR"] = signed(p["NR"], sr, "s5_sr", d, "rnr")
        p["sgfi"] = signed([p["fi"]], sg, "s5_sg", d, "gfi")[0]
        p["srfi"] = signed([p["fi"]], sr, "s5_sr", d, "rfi")[0]
        p["srPI8"] = signed([p["PI"][8]], sr, "s5_sr", d, "rpi8")[0]
    bb1, bb2, c1, c2 = [], [], [], []
    tmpA = _sb(nc, es, "s5_tmpA", [128, GL, GC], F32)
    tmpB = _sb(nc, es, "s5_tmpB", [128, GL, GC], F32)

    def bc(t):
        return t[:].unsqueeze(2).to_broadcast([128, GL, GC])

    for d in range(2):
        b1 = _sb(nc, es, f"s5_b1{d}", [128, GL, GC], F32)
        b2 = _sb(nc, es, f"s5_b2{d}", [128, GL, GC], F32)
        S.dma("sp", b1[:], din["b1"][d], writes=[f"s5_b1{d}"])
        S.dma("sp", b2[:], din["b2"][d], writes=[f"s5_b2{d}"])
        o1 = _sb(nc, es, f"s5_bb1{d}", [128, GL, GC], F32)
        o2 = _sb(nc, es, f"s5_bb2{d}", [128, GL, GC], F32)
        p = prm[d]
        (fr, frn), (sgfi, sgfin), (srfi, srfin) = p["fr"], p["sgfi"], p["srfi"]
        V.tt(tmpA[:], b1[:], bc(fr), ALU.mult, [f"s5_b1{d}", frn], ["s5_tmpA"])
        V.tt(tmpB[:], b2[:], bc(sgfi), ALU.mult, [f"s5_b2{d}", sgfin], ["s5_tmpB"])
        V.tt(o1[:], tmpA[:], tmpB[:], ALU.add, ["s5_tmpA", "s5_tmpB"], [f"s5_bb1{d}"])
        V.tt(tmpA[:], b2[:], bc(fr), ALU.mult, [f"s5_b2{d}", frn], ["s5_tmpA"])
        V.tt(tmpB[:], b1[:], bc(srfi), ALU.mult, [f"s5_b1{d}", srfin], ["s5_tmpB"])
        V.tt(o2[:], tmpA[:], tmpB[:], ALU.add, ["s5_tmpA", "s5_tmpB"], [f"s5_bb2{d}"])
        bb1.append((o1, f"s5_bb1{d}")); bb2.append((o2, f"s5_bb2{d}"))
        cc1 = _sb(nc, es, f"s5_c1{d}", [128, GL, GC], F32)
        cc2 = _sb(nc, es, f"s5_c2{d}", [128, GL, GC], F32)
        S.dma("sp", cc1[:], din["c1"][d], writes=[f"s5_c1{d}"])
        S.dma("sp", cc2[:], din["c2"][d], writes=[f"s5_c2{d}"])
        c1.append((cc1, f"s5_c1{d}")); c2.append((cc2, f"s5_c2{d}"))

    names = ["WTf", "Af", "Bf", "Vf", "WTb", "Bb", "Vb"]
    WfA = _sb(nc, es, "s5_WfA", [128, GB, 128], BF16)
    WfB = _sb(nc, es, "s5_WfB", [128, GB, 128], BF16)
    WbA = _sb(nc, es, "s5_WbA", [128, GB, 128], BF16)
    WbB = _sb(nc, es, "s5_WbB", [128, GB, 128], BF16)
    Vfb = _sb(nc, es, "s5_Vfb", [128, GB, 128], BF16)
    Vbb = _sb(nc, es, "s5_Vbb", [128, GB, 128], BF16)
    Mb = _sb(nc, es, "s5_Mb", [128, GB, 128], BF16)
    X = [_sb(nc, es, f"s5_X{d}", [128, 2, GB], F32) for d in range(2)]
    q1 = [_sb(nc, es, f"s5_q1{d}", [128, 2, GB], F32) for d in range(2)]
    q2 = [_sb(nc, es, f"s5_q2{d}", [128, 2, GB], F32) for d in range(2)]
    AR2 = [_sb(nc, es, f"s5_AR2{d}", [128, 2, GB], F32) for d in range(2)]
    AIx = [_sb(nc, es, f"s5_AIx{d}", [128, 2, GB], F32) for d in range(2)]
    tmp = {}
    bufs = {}

    def gen(dst, dstn, tsl, src1, src2, pa, pb, g0):
        tA, tB = bufs["tA"], bufs["tB"]
        (s1, s1n), (s2, s2n), (a, an), (b, bn) = src1, src2, pa, pb
        ab = a[:, g0:g0 + GB].unsqueeze(2).to_broadcast([128, GB, GC])
        bbb = b[:, g0:g0 + GB].unsqueeze(2).to_broadcast([128, GB, GC])
        V.tt(tA[:], s1[:, g0:g0 + GB, :], ab, ALU.mult, [s1n, an], ["s5_tA"])
        V.tt(tB[:], s2[:, g0:g0 + GB, :], bbb, ALU.mult, [s2n, bn], ["s5_tB"])
        V.tt(dst[:, :, tsl, :], tA[:], tB[:], ALU.add, ["s5_tA", "s5_tB"], [dstn])

    for g0 in range(0, GL, GB):
      pf, pb_ = prm[0], prm[1]
      S.barrier()
      with ExitStack() as es2:
        for nm in names:
            tmp[nm] = _sb(nc, es2, f"s5t_{nm}", [128, GB, TC, GC], F32)
        tA = bufs["tA"] = _sb(nc, es2, "s5_tA", [128, GB, GC], F32)
        tB = bufs["tB"] = _sb(nc, es2, "s5_tB", [128, GB, GC], F32)
        m1 = _sb(nc, es2, "s5_m1", [128, 128], F32)
        m2 = _sb(nc, es2, "s5_m2", [128, 128], F32)
        for j in range(TC):
            gen(tmp["WTf"], "s5t_WTf", j, bb1[0], bb2[0], pf["PR"][7 - j], pf["sgPI"][7 - j], g0)
            gen(tmp["Af"], "s5t_Af", j, bb1[0], bb2[0], pf["NR"][j], pf["sgNI"][j], g0)
            gen(tmp["WTb"], "s5t_WTb", j, bb1[1], bb2[1], pb_["PR"][j], pb_["sgPI"][j], g0)
        def genB(dst, dstn, tsl, d, srP, Pi):
            (s1, s1n), (s2, s2n), (a, an), (b, bn) = c1[d], c2[d], srP, Pi
            ab = a[:, g0:g0 + GB].unsqueeze(2).to_broadcast([128, GB, GC])
            bbb = b[:, g0:g0 + GB].unsqueeze(2).to_broadcast([128, GB, GC])
            V.tt(tA[:], s1[:, g0:g0 + GB, :], ab, ALU.mult, [s1n, an], ["s5_tA"])
            V.tt(tB[:], s2[:, g0:g0 + GB, :], bbb, ALU.mult, [s2n, bn], ["s5_tB"])
            V.tt(dst[:, :, tsl, :], tA[:], tB[:], ALU.subtract, ["s5_tA", "s5_tB"], [dstn])
        for t in range(TC):
            genB(tmp["Bf"], "s5t_Bf", t, 0, pf["srPR"][t], pf["PI"][t])
            genB(tmp["Vf"], "s5t_Vf", t, 0, pf["srPR"][t + 1], pf["PI"][t + 1])
            genB(tmp["Bb"], "s5t_Bb", t, 1, pb_["srNR"][t], pb_["NI"][t])
            genB(tmp["Vb"], "s5t_Vb", t, 1, pb_["srPR"][8 - t], pb_["PI"][8 - t])
        V.cp(Vfb[:], tmp["Vf"][:].rearrange("p g t c -> p g (t c)"), ["s5t_Vf"], ["s5_Vfb"])
        V.cp(Vbb[:], tmp["Vb"][:].rearrange("p g t c -> p g (t c)"), ["s5t_Vb"], ["s5_Vbb"])
        for d in range(2):
            p = prm[d]
            (ar, arn), (sgai, sgain), (srai, srain) = p["PR"][8], p["sgPI"][8], p["srPI8"]
            V.cp(AR2[d][:, 0, :], ar[:, g0:g0 + GB], [arn], [f"s5_AR2{d}"])
            V.cp(AR2[d][:, 1, :], ar[:, g0:g0 + GB], [arn], [f"s5_AR2{d}"])
            V.cp(AIx[d][:, 0, :], srai[:, g0:g0 + GB], [srain], [f"s5_AIx{d}"])
            V.cp(AIx[d][:, 1, :], sgai[:, g0:g0 + GB], [sgain], [f"s5_AIx{d}"])
        for gi in range(GB):
            g = g0 + gi
            for (src, srcn, WA, WAn, WB, WBn, bank) in [("WTf", "s5t_WTf", WfA, "s5_WfA", WfB, "s5_WfB", 0),
                                                       ("WTb", "s5t_WTb", WbA, "s5_WbA", WbB, "s5_WbB", 1)]:
                S.op("pe", lambda src=src, gi=gi, bank=bank: nc.tensor.transpose(
                    out=P[bank][:, 0:128], in_=tmp[src][:, gi].rearrange("p t c -> p (t c)"), identity=ident[:]),
                    reads=[srcn, "ident"], writes=[f"ps{bank}"])
                S.op("act", lambda WA=WA, gi=gi, bank=bank: nc.scalar.copy(out=WA[:, gi, :], in_=P[bank][:, 0:128]),
                     reads=[f"ps{bank}"], writes=[WAn])
                V.cp(WB[:, gi, 0:64], P[bank][:, 64:128], [f"ps{bank}"], [WBn])
                V.cp(WB[:, gi, 64:128], P[bank][:, 0:64], [f"ps{bank}"], [WBn])
            S.op("pe", lambda gi=gi: nc.tensor.matmul(
                out=P[2][:, 0:128], lhsT=tmp["Af"][:, gi].rearrange("p t c -> p (t c)"),
                rhs=tmp["Bf"][:, gi].rearrange("p t c -> p (t c)"), start=True, stop=True),
                reads=["s5t_Af", "s5t_Bf"], writes=["ps2"])
            S.op("pe", lambda gi=gi: nc.tensor.matmul(
                out=P[3][:, 0:128], lhsT=tmp["WTb"][:, gi].rearrange("p t c -> p (t c)"),
                rhs=tmp["Bb"][:, gi].rearrange("p t c -> p (t c)"), start=True, stop=True),
                reads=["s5t_WTb", "s5t_Bb"], writes=["ps3"])
            V.tt(m1[:], P[2][:, 0:128], maskf[:], ALU.mult, ["ps2", "s5_mf"], ["s5_m1"])
            V.tt(m2[:], P[3][:, 0:128], maskb[:], ALU.mult, ["ps3", "s5_mb"], ["s5_m2"])
            V.tt(m1[:], m1[:], m2[:], ALU.add, ["s5_m1", "s5_m2"], ["s5_m1"])
            V.stt(Mb[:, gi, :], ident[:], dcol[:, g:g + 1], m1[:], ALU.mult, ALU.add,
                  ["ident", "s5_dcol", "s5_m1"], ["s5_Mb"])
        S.barrier()
      with ExitStack() as es3:
        Ub = _sb(nc, es3, "s5_Ub", [128, GB, NK], BF16)
        St = [_sb(nc, es3, f"s5_St{d}", [128, NK + 1, 2, GB], BF16) for d in range(2)]
        Ysb = [_sb(nc, es3, f"s5_Y{i}", [128, TC, GB * GC], F32) for i in range(2)]
        S.dma("pool", Ub[:], din["u"][g0:g0 + GB].rearrange("g p k -> p g k"), writes=["s5_Ub"])
        for d in range(2):
            S.op("pool", lambda d=d: nc.gpsimd.memset(St[d][:, (0 if d == 0 else NK), :, :], 0.0),
                 writes=[f"s5_St{d}"])
            S.op("pool", lambda d=d: nc.gpsimd.memset(X[d][:], 0.0), writes=[f"s5_X{d}"])
        for gi in range(GB):
            for vi, (W, Wn, d, ab) in enumerate([(WfA, "s5_WfA", 0, 0), (WfB, "s5_WfB", 0, 1),
                                                 (WbA, "s5_WbA", 1, 0), (WbB, "s5_WbB", 1, 1)]):
                bank = 4 + vi
                S.op("pe", lambda W=W, gi=gi, bank=bank: nc.tensor.matmul(
                    out=P[bank][:, 0:NK], lhsT=W[:, gi, :], rhs=Ub[:, gi, :], start=True, stop=True),
                    reads=[Wn, "s5_Ub"], writes=[f"ps{bank}"])
                off = 1 if d == 0 else 0
                eng = "act" if vi % 2 else "dve"
                if eng == "act":
                    S.op("act", lambda d=d, ab=ab, gi=gi, bank=bank, off=off: nc.scalar.copy(
                        out=St[d][:, off:off + NK, ab, gi], in_=P[bank][:, 0:NK]),
                        reads=[f"ps{bank}"], writes=[f"s5_St{d}"])
                else:
                    V.cp(St[d][:, off:off + NK, ab, gi], P[bank][:, 0:NK], [f"ps{bank}"], [f"s5_St{d}"])
        for k in range(NK):
            for d, e in ((0, "dve"), (1, "pool")):
                E = nc.vector if e == "dve" else nc.gpsimd
                slot = k + 1 if d == 0 else NK - 1 - k
                Xn, q1n, q2n, Sn = f"s5_X{d}", f"s5_q1{d}", f"s5_q2{d}", f"s5_St{d}"
                S.op(e, lambda E=E, d=d: E.tensor_tensor(out=q1[d][:], in0=X[d][:], in1=AR2[d][:], op=ALU.mult),
                     reads=[Xn, f"s5_AR2{d}"], writes=[q1n])
                S.op(e, lambda E=E, d=d: E.tensor_tensor(out=q2[d][:], in0=X[d][:], in1=AIx[d][:], op=ALU.mult),
                     reads=[Xn, f"s5_AIx{d}"], writes=[q2n])
                S.op(e, lambda E=E, d=d, slot=slot: E.tensor_tensor(out=q1[d][:], in0=q1[d][:],
                                                                    in1=St[d][:, slot, :, :], op=ALU.add),
                     reads=[q1n, Sn], writes=[q1n])
                S.op(e, lambda E=E, d=d: E.tensor_tensor(out=X[d][:, 0, :], in0=q1[d][:, 0, :], in1=q2[d][:, 1, :],
                                                         op=ALU.add), reads=[q1n, q2n], writes=[Xn])
                S.op(e, lambda E=E, d=d: E.tensor_tensor(out=X[d][:, 1, :], in0=q1[d][:, 1, :], in1=q2[d][:, 0, :],
                                                         op=ALU.add), reads=[q1n, q2n], writes=[Xn])
                S.op(e, lambda E=E, d=d, slot=slot: E.tensor_copy(out=St[d][:, slot, :, :], in_=X[d][:]),
                     reads=[Xn], writes=[Sn])
        for kb in range(NKB):
            Yt, Yn = Ysb[kb % 2], f"s5_Y{kb % 2}"
            for gi in range(GB):
                bank = gi % 4
                ysl = P[bank][:, 0:128]
                S.op("pe", lambda gi=gi, kb=kb, ysl=ysl: nc.tensor.matmul(
                    out=ysl, lhsT=Ub[:, gi, kb * 128:(kb + 1) * 128], rhs=Mb[:, gi, :], start=True, stop=False),
                    reads=["s5_Ub", "s5_Mb"], writes=[f"ps{bank}"], signal=False)
                S.op("pe", lambda gi=gi, kb=kb, ysl=ysl: nc.tensor.matmul(
                    out=ysl, lhsT=St[0][:, kb * 128:(kb + 1) * 128, 0, gi], rhs=Vfb[:, gi, :], start=False,
                    stop=False), reads=["s5_St0", "s5_Vfb"], writes=[f"ps{bank}"], signal=False)
                S.op("pe", lambda gi=gi, kb=kb, ysl=ysl: nc.tensor.matmul(
                    out=ysl, lhsT=St[1][:, kb * 128 + 1:(kb + 1) * 128 + 1, 0, gi], rhs=Vbb[:, gi, :], start=False,
                    stop=True), reads=["s5_St1", "s5_Vbb"], writes=[f"ps{bank}"], signal=True)
                src = ysl.rearrange("p (t c) -> p t c", t=TC)
                if gi % 2:
                    S.op("act", lambda Yt=Yt, gi=gi, src=src: nc.scalar.copy(
                        out=Yt[:, :, gi * GC:(gi + 1) * GC], in_=src), reads=[f"ps{bank}"], writes=[Yn])
                else:
                    V.cp(Yt[:, :, gi * GC:(gi + 1) * GC], src, [f"ps{bank}"], [Yn])
            S.dma("sp", y_out[kb * 128:(kb + 1) * 128, :, g0 * GC:(g0 + GB) * GC], Yt[:], reads=[Yn])
      S.barrier()


def build_s5(GL, NK):
    nc = bass.Bass("TRN2", target_bir_lowering=False)
    def din_t(name, shape):
        return nc.dram_tensor(name, shape, F32, kind="ExternalInput").ap()
    din = {
        "u": din_t("u", [GL, 128, NK]),
        "lam_re": din_t("lam_re", [2, 128, GL]), "lam_im": din_t("lam_im", [2, 128, GL]),
        "logdt": din_t("logdt", [2, 128, GL]),
        "b1": din_t("b1", [2, 128, GL, GC]), "b2": din_t("b2", [2, 128, GL, GC]),
        "c1": din_t("c1", [2, 128, GL, GC]), "c2": din_t("c2", [2, 128, GL, GC]),
        "sg": din_t("sg", [128, 1]), "dcol": din_t("dcol", [128, GL]),
        "maskf": din_t("maskf", [128, 128]), "maskb": din_t("maskb", [128, 128]),
    }
    idn = din_t("idn", [128, 128])
    y = nc.dram_tensor("y", [NK, TC, GL * GC], F32, kind="ExternalOutput").ap()
    with ExitStack() as es:
        es.enter_context(nc.allow_low_precision("bf16 matmul operands, fp32 accumulation"))
        es.enter_context(nc.allow_non_contiguous_dma("small strided loads"))
        S = Sched(nc, es)
        P = [es.enter_context(nc.psum_tensor(f"ps{i}", [128, 512], F32)) for i in range(8)]
        ident = _sb(nc, es, "ident", [128, 128], F32)
        S.dma("sp", ident[:], idn, writes=["ident"])
        GH = GL // 2
        for h in range(2):
            gs = slice(h * GH, (h + 1) * GH)
            dsub = {"u": din["u"][gs], "sg": din["sg"], "dcol": din["dcol"][:, gs],
                    "maskf": din["maskf"], "maskb": din["maskb"]}
            for k in ("lam_re", "lam_im", "logdt"):
                dsub[k] = [din[k][d][:, gs] for d in range(2)]
            for k in ("b1", "b2", "c1", "c2"):
                dsub[k] = [din[k][d][:, gs, :] for d in range(2)]
            with ExitStack() as esh:
                emit_s5(nc, S, esh, P, ident, dsub, y[:, :, h * GH * GC:(h + 1) * GH * GC], GH, NK)
            S.barrier()
        S.finish()
    return nc


def s5_host_inputs(xb, G0, GL, lam_re, lam_im, log_dt, b_re, b_im, c_re, c_im, d_skip):
    L = xb.shape[0]
    NK = L // TC
    u = xb.reshape(NK, TC, D // GC, GC)[:, :, G0:G0 + GL, :]
    u = np.ascontiguousarray(u.transpose(2, 1, 3, 0).reshape(GL, TC * GC, NK))
    def pg(a):
        a = a[:, G0:G0 + GL, :].transpose(0, 2, 1)
        return np.ascontiguousarray(np.concatenate([a, a], axis=1))
    lr, li = pg(lam_re), pg(lam_im)
    ldt = np.ascontiguousarray(np.broadcast_to(log_dt[:, None, G0:G0 + GL], (2, 128, GL)))
    br = b_re[:, G0:G0 + GL].transpose(0, 2, 1, 3)
    bi = b_im[:, G0:G0 + GL].transpose(0, 2, 1, 3)
    cr = c_re[:, G0:G0 + GL].transpose(0, 3, 1, 2)
    ci = c_im[:, G0:G0 + GL].transpose(0, 3, 1, 2)
    st = lambda a, b: np.ascontiguousarray(np.concatenate([a, b], axis=1))
    sg = np.concatenate([-np.ones((64, 1), np.float32), np.ones((64, 1), np.float32)], 0)
    dcol = np.ascontiguousarray(np.tile(d_skip.reshape(D // GC, GC)[G0:G0 + GL].T, (TC, 1)))
    tt = np.arange(128) // GC
    maskf = (tt[None, :] >= tt[:, None]).astype(np.float32)
    maskb = (tt[None, :] <= tt[:, None]).astype(np.float32)
    return {"u": u, "lam_re": lr, "lam_im": li, "logdt": ldt, "b1": st(br, bi), "b2": st(bi, br),
            "c1": st(cr, ci), "c2": st(ci, cr), "sg": sg, "dcol": dcol, "maskf": maskf, "maskb": maskb,
            "idn": np.eye(128, dtype=np.float32)}


def run_s5(x_tok, lam_re, lam_im, log_dt, b_re, b_im, c_re, c_im, d_skip, B, L):
    GL = (D // GC) // 2
    NK = L // TC
    nc = build_s5(GL, NK)
    in_maps = []
    for c in range(NCORES):
        b, h = c // 2, c % 2
        in_maps.append(s5_host_inputs(x_tok[b * L:(b + 1) * L], h * GL, GL, lam_re, lam_im, log_dt,
                                      b_re, b_im, c_re, c_im, d_skip))
    res = run_bass_kernel_spmd(nc, in_maps, core_ids=list(range(NCORES)))
    y = np.empty((B * L, D), np.float32)
    for c in range(NCORES):
        b, h = c // 2, c % 2
        y[b * L:(b + 1) * L, h * GL * GC:(h + 1) * GL * GC] = res.results[c]["y"].reshape(L, GL * GC)
    return y


GELU_C = 0.044715
GELU_S = 2.0 * 0.7978845608028654


def emit_glu(nc, S, es, P, x_ap, ys_ap, y_ap, wv, wgt, g_rep, b_rep, ident, NT):
    Wv = _sb(nc, es, "gl_Wv", [128, NDC, D], BF16)
    Wg = _sb(nc, es, "gl_Wg", [128, NDC, D], BF16)
    wv_v = wv.rearrange("(c p) f -> p c f", p=128)
    wg_v = wgt.rearrange("(c p) f -> p c f", p=128)
    for q in range(4):
        S.dma("pool", Wv[:, :, q * 512:(q + 1) * 512], wv_v[:, :, q * 512:(q + 1) * 512], writes=["gl_Wv"])
        S.dma("pool", Wg[:, :, q * 512:(q + 1) * 512], wg_v[:, :, q * 512:(q + 1) * 512], writes=["gl_Wg"])
    yt = [_sb(nc, es, f"gl_y{i}", [128, D], F32) for i in range(2)]
    t1 = _sb(nc, es, "gl_t1", [128, D], F32)
    xt = [_sb(nc, es, f"gl_x{i}", [128, D], F32) for i in range(2)]
    gT = _sb(nc, es, "gl_gT", [128, NDC, 128], BF16)
    sgm = [_sb(nc, es, f"gl_sg{i}", [128, 512], F32) for i in range(2)]
    st = _sb(nc, es, "gl_st", [128, 4, nc.vector.BN_STATS_DIM], F32)
    mv = _sb(nc, es, "gl_mv", [128, nc.vector.BN_AGGR_DIM], F32)
    rs = _sb(nc, es, "gl_rs", [128, 1], F32)
    V = _V(nc, S, "dve")
    for t in range(NT // 128):
        y, yn = yt[t % 2], f"gl_y{t % 2}"
        x, xn = xt[t % 2], f"gl_x{t % 2}"
        S.dma("sp", y[:], ys_ap[t * 128:(t + 1) * 128, :], writes=[yn])
        S.dma("sp", x[:], x_ap[t * 128:(t + 1) * 128, :], writes=[xn])
        V.tt(t1[:], y[:], y[:], ALU.mult, [yn], ["gl_t1"])
        V.ts(t1[:], t1[:], GELU_C, ALU.mult, ["gl_t1"], ["gl_t1"], s2=1.0, op1=ALU.add)
        V.tt(t1[:], t1[:], y[:], ALU.mult, ["gl_t1", yn], ["gl_t1"])
        S.op("act", lambda: nc.scalar.activation(out=t1[:], in_=t1[:], func=AF.Sigmoid, scale=GELU_S),
             reads=["gl_t1"], writes=["gl_t1"])
        V.tt(y[:], y[:], t1[:], ALU.mult, [yn, "gl_t1"], [yn])
        for cq in range(NDC // 4):
            bank = 4 + (cq % 4)
            for k in range(4):
                c = cq * 4 + k
                S.op("pe", lambda c=c, k=k, bank=bank, y=y: nc.tensor.transpose(
                    out=P[bank][:, k * 128:(k + 1) * 128], in_=y[:, c * 128:(c + 1) * 128], identity=ident[:]),
                    reads=[yn, "ident"], writes=[f"ps{bank}"], signal=(k == 3))
            S.op("act", lambda cq=cq, bank=bank: nc.scalar.copy(
                out=gT[:, cq * 4:(cq + 1) * 4, :], in_=P[bank][:].rearrange("p (k n) -> p k n", k=4)),
                reads=[f"ps{bank}"], writes=["gl_gT"])
        for n in range(4):
            pv, pg = (n % 2) * 2, (n % 2) * 2 + 1
            for (W, Wn, bank) in ((Wv, "gl_Wv", pv), (Wg, "gl_Wg", pg)):
                for c in range(NDC):
                    S.op("pe", lambda W=W, c=c, n=n, bank=bank: nc.tensor.matmul(
                        out=P[bank][:], lhsT=gT[:, c, :], rhs=W[:, c, n * 512:(n + 1) * 512],
                        start=(c == 0), stop=(c == NDC - 1)),
                        reads=["gl_gT", Wn], writes=[f"ps{bank}"], signal=(c == NDC - 1))
            sg_, sgn = sgm[n % 2], f"gl_sg{n % 2}"
            S.op("act", lambda sg_=sg_, pg=pg: nc.scalar.activation(out=sg_[:], in_=P[pg][:], func=AF.Sigmoid),
                 reads=[f"ps{pg}"], writes=[sgn])
            V.tt(t1[:, n * 512:(n + 1) * 512], P[pv][:], sg_[:], ALU.mult, [f"ps{pv}", sgn], ["gl_t1"])
        V.stt(t1[:], x[:], ALPHA, t1[:], ALU.mult, ALU.add, [xn, "gl_t1"], ["gl_t1"])
        emit_ln(nc, S, t1, "gl_t1", st, mv, rs, "gl_ln", g_rep, b_rep)
        S.dma("sp", y_ap[t * 128:(t + 1) * 128, :], t1[:], reads=["gl_t1"])


def _std_prog(NT_in, NT_out, extra, body):
    nc = bass.Bass("TRN2", target_bir_lowering=False)
    x = nc.dram_tensor("x", [NT_in, D], F32, kind="ExternalInput").ap()
    lng = nc.dram_tensor("lng", [128, D], F32, kind="ExternalInput").ap()
    lnb = nc.dram_tensor("lnb", [128, D], F32, kind="ExternalInput").ap()
    idn = nc.dram_tensor("idn", [128, 128], F32, kind="ExternalInput").ap()
    ex = {k: nc.dram_tensor(k, shp, F32, kind="ExternalInput").ap() for k, shp in extra.items()}
    y = nc.dram_tensor("y", [NT_out, D], F32, kind="ExternalOutput").ap()
    with ExitStack() as es:
        es.enter_context(nc.allow_low_precision("bf16 matmul operands, fp32 accumulation"))
        es.enter_context(nc.allow_non_contiguous_dma("weight block loads"))
        S = Sched(nc, es)
        P = [es.enter_context(nc.psum_tensor(f"ps{i}", [128, 512], F32)) for i in range(8)]
        ident = _sb(nc, es, "ident", [128, 128], F32)
        g_rep = _sb(nc, es, "g_rep", [128, D], F32)
        b_rep = _sb(nc, es, "b_rep", [128, D], F32)
        S.dma("sp", ident[:], idn, writes=["ident"])
        S.dma("sp", g_rep[:], lng, writes=["lng"])
        S.dma("sp", b_rep[:], lnb, writes=["lnb"])
        body(nc, S, es, P, x, y, ex, g_rep, b_rep, ident)
        S.finish()
    return nc


def run_glu(x_tok, ys_tok, wv, wg, g, b):
    NT = x_tok.shape[0] // NCORES
    nc = _std_prog(NT, NT, {"ys": [NT, D], "wv": [D, D], "wg": [D, D]},
                   lambda nc, S, es, P, x, y, ex, gr, br, idt: emit_glu(nc, S, es, P, x, ex["ys"], y, ex["wv"],
                                                                       ex["wg"], gr, br, idt, NT))
    common = {"wv": np.ascontiguousarray(wv), "wg": np.ascontiguousarray(wg), "lng": _rep(g), "lnb": _rep(b),
              "idn": np.eye(128, dtype=np.float32)}
    in_maps = [dict(common, x=np.ascontiguousarray(x_tok[c * NT:(c + 1) * NT]),
                    ys=np.ascontiguousarray(ys_tok[c * NT:(c + 1) * NT])) for c in range(NCORES)]
    res = run_bass_kernel_spmd(nc, in_maps, core_ids=list(range(NCORES)))
    return np.concatenate([r["y"] for r in res.results], axis=0)


GW = 64
NH = 16
HD = 128
NR = 36
NEG = -30000.0


def emit_natten(nc, S, es, P, x_ap, y_ap, wqkv, wo, tab, g_rep, b_rep, ident):
    NTK = NR * GW
    V = _V(nc, S, "dve")
    at_dram = nc.dram_tensor("na_attn_scratch", [NH, 128, NTK], BF16, kind="Internal").ap()
    ones = _sb(nc, es, "na_ones", [128, 128], BF16)
    S.op("dve", lambda: nc.vector.memset(ones[:], 1.0), writes=["na_ones"])
    wq_v = wqkv.rearrange("(c p) f -> p c f", p=128)
    with ExitStack() as es2:
        xT = _sb(nc, es2, "na_xT", [128, NDC, NTK], BF16)
        xs = [_sb(nc, es2, f"na_xs{i}", [128, D], F32) for i in range(2)]
        for t in range(NTK // 128):
            xb, rx = xs[t % 2], f"na_xs{t % 2}"
            S.dma("sp", xb[:], x_ap[t * 128:(t + 1) * 128, :], writes=[rx])
            for cq in range(NDC // 4):
                bank = 4 + (cq % 4)
                for k in range(4):
                    c = cq * 4 + k
                    S.op("pe", lambda c=c, k=k, bank=bank, xb=xb: nc.tensor.transpose(
                        out=P[bank][:, k * 128:(k + 1) * 128], in_=xb[:, c * 128:(c + 1) * 128],
                        identity=ident[:]), reads=[rx, "ident"], writes=[f"ps{bank}"], signal=(k == 3))
                if cq % 2:
                    S.op("act", lambda cq=cq, bank=bank, t=t: nc.scalar.copy(
                        out=xT[:, cq * 4:(cq + 1) * 4, t * 128:(t + 1) * 128],
                        in_=P[bank][:].rearrange("p (k n) -> p k n", k=4)), reads=[f"ps{bank}"], writes=["na_xT"])
                else:
                    V.cp(xT[:, cq * 4:(cq + 1) * 4, t * 128:(t + 1) * 128],
                         P[bank][:].rearrange("p (k n) -> p k n", k=4), [f"ps{bank}"], ["na_xT"])
        wb = [[_sb(nc, es2, f"na_w{j}{i}", [128, NDC, HD], BF16) for j in range(3)] for i in range(2)]
        tb = [_sb(nc, es2, f"na_tab{i}", [GW, 8, 8, GW], F32) for i in range(2)]
        qT = _sb(nc, es2, "na_qT", [128, NTK], BF16)
        kT = _sb(nc, es2, "na_kT", [128, NTK], BF16)
        v2 = _sb(nc, es2, "na_v2", [GW, NR, HD], BF16)
        E = [_sb(nc, es2, f"na_E{i}", [GW, 8, GW], F32) for i in range(2)]
        Eb = [_sb(nc, es2, f"na_Eb{i}", [GW, 8, GW], BF16) for i in range(2)]
        rc = [_sb(nc, es2, f"na_rc{i}", [128, GW], F32) for i in range(2)]
        ath = [_sb(nc, es2, f"na_ath{i}", [128, NTK], BF16) for i in range(2)]
        tblocks = [(i * 512, min(512, NTK - i * 512)) for i in range((NTK + 511) // 512)]
        for h in range(NH):
            w3, wn = wb[h % 2], [f"na_w{j}{h % 2}" for j in range(3)]
            for j in range(3):
                S.dma("pool", w3[j][:], wq_v[:, :, j * D + h * HD:j * D + (h + 1) * HD], writes=[wn[j]])
            tbt, tbn = tb[h % 2], f"na_tab{h % 2}"
            S.dma("sp", tbt[:], tab[h], writes=[tbn])
            for j, (dst, dstn) in enumerate(((qT, "na_qT"), (kT, "na_kT"))):
                for bi, (t0, tw) in enumerate(tblocks):
                    bank = (j * len(tblocks) + bi) % 2
                    for c in range(NDC):
                        S.op("pe", lambda j=j, c=c, t0=t0, tw=tw, bank=bank, w3=w3: nc.tensor.matmul(
                            out=P[bank][:, 0:tw], lhsT=w3[j][:, c, :], rhs=xT[:, c, t0:t0 + tw],
                            start=(c == 0), stop=(c == NDC - 1)),
                            reads=[wn[j], "na_xT"], writes=[f"ps{bank}"], signal=(c == NDC - 1))
                    S.op("act", lambda dst=dst, t0=t0, tw=tw, bank=bank: nc.scalar.copy(
                        out=dst[:, t0:t0 + tw], in_=P[bank][:, 0:tw]), reads=[f"ps{bank}"], writes=[dstn])
            for r in range(NR):
                bank = 2 + (r % 2)
                for c in range(NDC):
                    S.op("pe", lambda r=r, c=c, bank=bank, w3=w3: nc.tensor.matmul(
                        out=P[bank][0:GW, 0:HD], lhsT=xT[:, c, r * GW:(r + 1) * GW],
                        rhs=w3[2][:, c, :], start=(c == 0), stop=(c == NDC - 1)),
                        reads=[wn[2], "na_xT"], writes=[f"ps{bank}"], signal=(c == NDC - 1))
                V.cp(v2[:, r, :], P[bank][0:GW, 0:HD], [f"ps{bank}"], ["na_v2"])
            for i in range(NR):
                rs_ = min(max(i - 4, 0), NR - 8)
                var = i - rs_
                e, en = E[i % 2], f"na_E{i % 2}"
                eb, ebn = Eb[i % 2], f"na_Eb{i % 2}"
                bs = 4 + (i % 2) * 2
                for kr in range(8):
                    S.op("pe", lambda kr=kr, i=i, rs_=rs_, bs=bs: nc.tensor.matmul(
                        out=P[bs][0:GW, kr * GW:(kr + 1) * GW], lhsT=kT[:, (rs_ + kr) * GW:(rs_ + kr + 1) * GW],
                        rhs=qT[:, i * GW:(i + 1) * GW], start=True, stop=True),
                        reads=["na_kT", "na_qT"], writes=[f"ps{bs}"], signal=(kr == 7))
                V.stt(e[:], P[bs][0:GW, :].rearrange("p (k q) -> p k q", k=8), float(HD) ** -0.5,
                      tbt[:, var, :, :], ALU.mult, ALU.add, [f"ps{bs}", tbn], [en])
                S.op("act", lambda e=e, eb=eb: nc.scalar.activation(out=eb[:], in_=e[:], func=AF.Exp),
                     reads=[en], writes=[ebn])
                for kr in range(8):
                    S.op("pe", lambda kr=kr, rs_=rs_, bs=bs, eb=eb: nc.tensor.matmul(
                        out=P[bs + 1][:, 0:GW], lhsT=v2[:, rs_ + kr, :], rhs=eb[:, kr, :],
                        start=(kr == 0), stop=(kr == 7)),
                        reads=["na_v2", ebn], writes=[f"ps{bs + 1}"], signal=False)
                for kr in range(8):
                    S.op("pe", lambda kr=kr, bs=bs, eb=eb: nc.tensor.matmul(
                        out=P[bs + 1][:, GW:2 * GW], lhsT=ones[0:GW, :], rhs=eb[:, kr, :], start=(kr == 0),
                        stop=(kr == 7)), reads=["na_ones", ebn], writes=[f"ps{bs + 1}"], signal=(kr == 7))
                rct, rcn = rc[i % 2], f"na_rc{i % 2}"
                S.op("dve", lambda rct=rct, bs=bs: nc.vector.reciprocal(out=rct[:], in_=P[bs + 1][:, GW:2 * GW]),
                     reads=[f"ps{bs + 1}"], writes=[rcn])
                V.tt(ath[h % 2][:, i * GW:(i + 1) * GW], P[bs + 1][:, 0:GW], rct[:], ALU.mult,
                     [f"ps{bs + 1}", rcn], [f"na_ath{h % 2}"])
            S.dma("sp", at_dram[h], ath[h % 2][:], reads=[f"na_ath{h % 2}"], writes=["na_at_dram"])
        S.barrier()
    attnT = _sb(nc, es, "na_attnT", [128, NH, NTK], BF16)
    S.dma("sp", attnT[:], at_dram.rearrange("h p n -> p h n"), reads=["na_at_dram"], writes=["na_attnT"])
    Wo = _sb(nc, es, "na_Wo", [128, NH, D], BF16)
    wo_v = wo.rearrange("(c p) f -> p c f", p=128)
    for q in range(4):
        S.dma("pool", Wo[:, :, q * 512:(q + 1) * 512], wo_v[:, :, q * 512:(q + 1) * 512], writes=["na_Wo"])
    xt = [_sb(nc, es, f"na_x{i}", [128, D], F32) for i in range(2)]
    zt = [_sb(nc, es, f"na_z{i}", [128, D], F32) for i in range(2)]
    st = _sb(nc, es, "na_st", [128, 4, nc.vector.BN_STATS_DIM], F32)
    mv = _sb(nc, es, "na_mv", [128, nc.vector.BN_AGGR_DIM], F32)
    rs = _sb(nc, es, "na_rs", [128, 1], F32)
    for t in range(NTK // 128):
        x, xn = xt[t % 2], f"na_x{t % 2}"
        z, zn = zt[t % 2], f"na_z{t % 2}"
        S.dma("sp", x[:], x_ap[t * 128:(t + 1) * 128, :], writes=[xn])
        for n in range(4):
            bank = n
            for hh in range(NH):
                S.op("pe", lambda hh=hh, n=n, t=t, bank=bank: nc.tensor.matmul(
                    out=P[bank][:], lhsT=attnT[:, hh, t * 128:(t + 1) * 128], rhs=Wo[:, hh, n * 512:(n + 1) * 512],
                    start=(hh == 0), stop=(hh == NH - 1)),
                    reads=["na_attnT", "na_Wo"], writes=[f"ps{bank}"], signal=(hh == NH - 1))
            V.stt(z[:, n * 512:(n + 1) * 512], x[:, n * 512:(n + 1) * 512], ALPHA, P[bank][:], ALU.mult, ALU.add,
                  [xn, f"ps{bank}"], [zn])
        emit_ln(nc, S, z, zn, st, mv, rs, "na_ln", g_rep, b_rep)
        S.dma("sp", y_ap[t * 128:(t + 1) * 128, :], z[:], reads=[zn])


def emit_natten_f(nc, S, es, P, x_ap, y_ap, wqkv, wo, tab, g_rep, b_rep, ident):
    NTK = NR * GW
    NQ = 32 * GW
    V = _V(nc, S, "dve")
    at_dram = nc.dram_tensor("naf_attn_scratch", [NH, 128, NQ], BF16, kind="Internal").ap()
    ones = _sb(nc, es, "na_ones", [128, 128], BF16)
    S.op("dve", lambda: nc.vector.memset(ones[:], 1.0), writes=["na_ones"])
    wq_v = wqkv.rearrange("(c p) f -> p c f", p=128)
    with ExitStack() as es2:
        xT = _sb(nc, es2, "na_xT", [128, NDC, NTK], BF16)
        xs = [_sb(nc, es2, f"na_xs{i}", [128, D], F32) for i in range(2)]
        for t in range(NTK // 128):
            xb, rx = xs[t % 2], f"na_xs{t % 2}"
            S.dma("sp", xb[:], x_ap[t * 128:(t + 1) * 128, :], writes=[rx])
            for cq in range(NDC // 4):
                bank = 4 + (cq % 4)
                for k in range(4):
                    c = cq * 4 + k
                    S.op("pe", lambda c=c, k=k, bank=bank, xb=xb: nc.tensor.transpose(
                        out=P[bank][:, k * 128:(k + 1) * 128], in_=xb[:, c * 128:(c + 1) * 128],
                        identity=ident[:]), reads=[rx, "ident"], writes=[f"ps{bank}"], signal=(k == 3))
                if cq % 2:
                    S.op("act", lambda cq=cq, bank=bank, t=t: nc.scalar.copy(
                        out=xT[:, cq * 4:(cq + 1) * 4, t * 128:(t + 1) * 128],
                        in_=P[bank][:].rearrange("p (k n) -> p k n", k=4)), reads=[f"ps{bank}"], writes=["na_xT"])
                else:
                    V.cp(xT[:, cq * 4:(cq + 1) * 4, t * 128:(t + 1) * 128],
                         P[bank][:].rearrange("p (k n) -> p k n", k=4), [f"ps{bank}"], ["na_xT"])
        wb = [[_sb(nc, es2, f"na_w{j}{i}", [128, NDC, HD], BF16) for j in range(3)] for i in range(2)]
        tb = [_sb(nc, es2, f"na_tab{i}", [GW, 5, 9, GW], F32) for i in range(2)]
        qT = _sb(nc, es2, "na_qT", [128, NTK], BF16)
        kT = _sb(nc, es2, "na_kT", [128, NTK], BF16)
        v2 = _sb(nc, es2, "na_v2", [GW, NR, HD], BF16)
        E = [_sb(nc, es2, f"na_E{i}", [GW, 9, GW], F32) for i in range(2)]
        Eb = [_sb(nc, es2, f"na_Eb{i}", [GW, 9, GW], BF16) for i in range(2)]
        rc = [_sb(nc, es2, f"na_rc{i}", [128, GW], F32) for i in range(2)]
        ath = [_sb(nc, es2, f"na_ath{i}", [128, NQ], BF16) for i in range(2)]
        tblocks = [(i * 512, min(512, NTK - i * 512)) for i in range((NTK + 511) // 512)]
        for h in range(NH):
            w3, wn = wb[h % 2], [f"na_w{j}{h % 2}" for j in range(3)]
            for j in range(3):
                S.dma("pool", w3[j][:], wq_v[:, :, j * D + h * HD:j * D + (h + 1) * HD], writes=[wn[j]])
            tbt, tbn = tb[h % 2], f"na_tab{h % 2}"
            S.dma("sp", tbt[:], tab[h], writes=[tbn])
            for j, (dst, dstn) in enumerate(((qT, "na_qT"), (kT, "na_kT"))):
                for bi, (t0, tw) in enumerate(tblocks):
                    bank = (j * len(tblocks) + bi) % 2
                    for c in range(NDC):
                        S.op("pe", lambda j=j, c=c, t0=t0, tw=tw, bank=bank, w3=w3: nc.tensor.matmul(
                            out=P[bank][:, 0:tw], lhsT=w3[j][:, c, :], rhs=xT[:, c, t0:t0 + tw],
                            start=(c == 0), stop=(c == NDC - 1)),
                            reads=[wn[j], "na_xT"], writes=[f"ps{bank}"], signal=(c == NDC - 1))
                    S.op("act", lambda dst=dst, t0=t0, tw=tw, bank=bank: nc.scalar.copy(
                        out=dst[:, t0:t0 + tw], in_=P[bank][:, 0:tw]), reads=[f"ps{bank}"], writes=[dstn])
            for r in range(NR):
                bank = 2 + (r % 2)
                for c in range(NDC):
                    S.op("pe", lambda r=r, c=c, bank=bank, w3=w3: nc.tensor.matmul(
                        out=P[bank][0:GW, 0:HD], lhsT=xT[:, c, r * GW:(r + 1) * GW],
                        rhs=w3[2][:, c, :], start=(c == 0), stop=(c == NDC - 1)),
                        reads=[wn[2], "na_xT"], writes=[f"ps{bank}"], signal=(c == NDC - 1))
                V.cp(v2[:, r, :], P[bank][0:GW, 0:HD], [f"ps{bank}"], ["na_v2"])
            for i in range(32):
                rows = list(range(9)) if i < 4 else list(range(i - 4, i + 5))
                var = i if i < 4 else 4
                e, en = E[i % 2], f"na_E{i % 2}"
                eb, ebn = Eb[i % 2], f"na_Eb{i % 2}"
                bs = 4 + (i % 2) * 2
                for kr in range(9):
                    dst = P[bs][0:GW, kr * GW:(kr + 1) * GW] if kr < 8 else P[bs + 1][0:GW, 2 * GW:3 * GW]
                    wr = f"ps{bs}" if kr < 8 else f"ps{bs + 1}"
                    S.op("pe", lambda kr=kr, i=i, dst=dst, rows=rows: nc.tensor.matmul(
                        out=dst, lhsT=kT[:, rows[kr] * GW:(rows[kr] + 1) * GW],
                        rhs=qT[:, i * GW:(i + 1) * GW], start=True, stop=True),
                        reads=["na_kT", "na_qT"], writes=[wr], signal=(kr >= 7))
                V.stt(e[:, 0:8, :], P[bs][0:GW, :].rearrange("p (k q) -> p k q", k=8), float(HD) ** -0.5,
                      tbt[:, var, 0:8, :], ALU.mult, ALU.add, [f"ps{bs}", tbn], [en])
                V.stt(e[:, 8, :], P[bs + 1][0:GW, 2 * GW:3 * GW], float(HD) ** -0.5,
                      tbt[:, var, 8, :], ALU.mult, ALU.add, [f"ps{bs + 1}", tbn], [en])
                S.op("act", lambda e=e, eb=eb: nc.scalar.activation(out=eb[:], in_=e[:], func=AF.Exp),
                     reads=[en], writes=[ebn])
                for kr in range(9):
                    S.op("pe", lambda kr=kr, rows=rows, bs=bs, eb=eb: nc.tensor.matmul(
                        out=P[bs + 1][:, 0:GW], lhsT=v2[:, rows[kr], :], rhs=eb[:, kr, :],
                        start=(kr == 0), stop=(kr == 8)),
                        reads=["na_v2", ebn], writes=[f"ps{bs + 1}"], signal=False)
                for kr in range(9):
                    S.op("pe", lambda kr=kr, bs=bs, eb=eb: nc.tensor.matmul(
                        out=P[bs + 1][:, GW:2 * GW], lhsT=ones[0:GW, :], rhs=eb[:, kr, :], start=(kr == 0),
                        stop=(kr == 8)), reads=["na_ones", ebn], writes=[f"ps{bs + 1}"], signal=(kr == 8))
                rct, rcn = rc[i % 2], f"na_rc{i % 2}"
                S.op("dve", lambda rct=rct, bs=bs: nc.vector.reciprocal(out=rct[:], in_=P[bs + 1][:, GW:2 * GW]),
                     reads=[f"ps{bs + 1}"], writes=[rcn])
                V.tt(ath[h % 2][:, i * GW:(i + 1) * GW], P[bs + 1][:, 0:GW], rct[:], ALU.mult,
                     [f"ps{bs + 1}", rcn], [f"na_ath{h % 2}"])
            S.dma("sp", at_dram[h], ath[h % 2][:], reads=[f"na_ath{h % 2}"], writes=["na_at_dram"])
        S.barrier()
    attnT = _sb(nc, es, "na_attnT", [128, NH, NQ], BF16)
    S.dma("sp", attnT[:], at_dram.rearrange("h p n -> p h n"), reads=["na_at_dram"], writes=["na_attnT"])
    Wo = _sb(nc, es, "na_Wo", [128, NH, D], BF16)
    wo_v = wo.rearrange("(c p) f -> p c f", p=128)
    for q in range(4):
        S.dma("pool", Wo[:, :, q * 512:(q + 1) * 512], wo_v[:, :, q * 512:(q + 1) * 512], writes=["na_Wo"])
    xt = [_sb(nc, es, f"na_x{i}", [128, D], F32) for i in range(2)]
    zt = [_sb(nc, es, f"na_z{i}", [128, D], F32) for i in range(2)]
    st = _sb(nc, es, "na_st", [128, 4, nc.vector.BN_STATS_DIM], F32)
    mv = _sb(nc, es, "na_mv", [128, nc.vector.BN_AGGR_DIM], F32)
    rs = _sb(nc, es, "na_rs", [128, 1], F32)
    for t in range(NQ // 128):
        x, xn = xt[t % 2], f"na_x{t % 2}"
        z, zn = zt[t % 2], f"na_z{t % 2}"
        S.dma("sp", x[:], x_ap[t * 128:(t + 1) * 128, :], writes=[xn])
        for n in range(4):
            bank = n
            for hh in range(NH):
                S.op("pe", lambda hh=hh, n=n, t=t, bank=bank: nc.tensor.matmul(
                    out=P[bank][:], lhsT=attnT[:, hh, t * 128:(t + 1) * 128], rhs=Wo[:, hh, n * 512:(n + 1) * 512],
                    start=(hh == 0), stop=(hh == NH - 1)),
                    reads=["na_attnT", "na_Wo"], writes=[f"ps{bank}"], signal=(hh == NH - 1))
            V.stt(z[:, n * 512:(n + 1) * 512], x[:, n * 512:(n + 1) * 512], ALPHA, P[bank][:], ALU.mult, ALU.add,
                  [xn, f"ps{bank}"], [zn])
        emit_ln(nc, S, z, zn, st, mv, rs, "na_ln", g_rep, b_rep)
        S.dma("sp", y_ap[t * 128:(t + 1) * 128, :], z[:], reads=[zn])


def natten_tables(rpb):
    H = rpb.shape[0]
    qc = np.arange(GW)
    cs = np.clip(qc - 8, 0, GW - 16)
    kc = np.arange(GW)
    inwin = (kc[:, None] >= cs[None, :]) & (kc[:, None] < cs[None, :] + 16)
    coff = np.clip(kc[:, None] - qc[None, :] + 15, 0, 30)
    tab = np.full((H, 8, 8, GW, GW), NEG, np.float32)
    for var in range(8):
        for kr in range(8):
            ro = kr - var + 7
            vals = rpb[:, ro][:, coff]
            tab[:, var, kr] = np.where(inwin[None], vals, np.float32(NEG))
    return np.ascontiguousarray(tab.transpose(0, 3, 1, 2, 4))


def run_natten(x_tok, wqkv, rpb, wo, g, b, B, L):
    NTK = NR * GW
    nc = _std_prog(NTK, NTK, {"wqkv": [D, 3 * D], "wo": [D, D], "tab": [NH, GW, 8, 8, GW]},
                   lambda nc, S, es, P, x, y, ex, gr, br, idt: emit_natten(nc, S, es, P, x, y, ex["wqkv"],
                                                                          ex["wo"], ex["tab"], gr, br, idt))
    common = {"wqkv": np.ascontiguousarray(wqkv), "wo": np.ascontiguousarray(wo), "tab": natten_tables(rpb),
              "lng": _rep(g), "lnb": _rep(b), "idn": np.eye(128, dtype=np.float32)}
    half = L // 2
    in_maps = []
    for c in range(NCORES):
        bq, hf = c // 2, c % 2
        t0 = bq * L + (0 if hf == 0 else half - 4 * GW)
        in_maps.append(dict(common, x=np.ascontiguousarray(x_tok[t0:t0 + NTK])))
    res = run_bass_kernel_spmd(nc, in_maps, core_ids=list(range(NCORES)))
    out = np.empty_like(x_tok)
    for c in range(NCORES):
        bq, hf = c // 2, c % 2
        yc = res.results[c]["y"]
        if hf == 0:
            out[bq * L:bq * L + half] = yc[:half]
        else:
            out[bq * L + half:(bq + 1) * L] = yc[4 * GW:]
    return out


def kernel_unfused(x, ffn_w_gate, ffn_w_up, ffn_w_down, ln_g, ln_b, s5_lam_re, s5_lam_im, s5_log_dt, s5_b_re, s5_b_im,
           s5_c_re, s5_c_im, s5_d, s5_w_glu_val, s5_w_glu_gate, na_w_qkv, na_rpb, na_w_out):
    f = lambda a: np.asarray(a, np.float32)
    B, L, _ = x.shape
    h = f(x).reshape(B * L, D)
    for i in range(DEPTH):
        h = run_ffn(h, f(ffn_w_gate[i, 0]), f(ffn_w_up[i, 0]), f(ffn_w_down[i, 0]), f(ln_g[i, 0]), f(ln_b[i, 0]))
        j = i // 2
        if i % 2 == 0:
            ys = run_s5(h, f(s5_lam_re[j]), f(s5_lam_im[j]), f(s5_log_dt[j]), f(s5_b_re[j]), f(s5_b_im[j]),
                        f(s5_c_re[j]), f(s5_c_im[j]), f(s5_d[j]), B, L)
            h = run_glu(h, ys, f(s5_w_glu_val[j]), f(s5_w_glu_gate[j]), f(ln_g[i, 1]), f(ln_b[i, 1]))
        else:
            h = run_natten(h, f(na_w_qkv[j]), f(na_rpb[j]), f(na_w_out[j]), f(ln_g[i, 1]), f(ln_b[i, 1]), B, L)
        h = run_ffn(h, f(ffn_w_gate[i, 1]), f(ffn_w_up[i, 1]), f(ffn_w_down[i, 1]), f(ln_g[i, 2]), f(ln_b[i, 2]))
    return h.reshape(B, L, D)


T2 = 16
NG = D // GC


def emit_s5f_ops(nc, S, P, ident, din, opW, opV, opM, h, GH=64, GB=8):
    V = _V(nc, S, "dve")
    VP = _V(nc, S, "pool")
    gsl = slice(h * GH, (h + 1) * GH)
    with ExitStack() as es:
        sg = _sb(nc, es, "o_sg", [128, 1], F32)
        sr = _sb(nc, es, "o_sr", [128, 1], F32)
        S.dma("sp", sg[:], din["sg"], writes=["o_sg"])
        V.ts(sr[:], sg[:], -1.0, ALU.mult, ["o_sg"], ["o_sr"])
        mf = _sb(nc, es, "o_mf", [128, 128], F32)
        mb = _sb(nc, es, "o_mb", [128, 128], F32)
        dcol = _sb(nc, es, "o_dcol", [128, GH], F32)
        S.dma("sp", mf[:], din["maskf"], writes=["o_mf"])
        S.dma("sp", mb[:], din["maskb"], writes=["o_mb"])
        S.dma("sp", dcol[:], din["dcol"][:, gsl], writes=["o_dcol"])
        stk = {}
        for d in range(2):
            for nm, n in (("PA", T2 + 1), ("PD", T2), ("NA", T2)):
                for ri in "ri":
                    stk[(d, nm, ri)] = (_sb(nc, es, f"o_{nm}{ri}{d}", [128, GH, n], F32), f"o_{nm}{ri}{d}")
        bb1, bb2s, c1s, c2 = [], [], [], []
        keep = []
        for d in range(2):
            b1 = _sb(nc, es, f"o_b1{d}", [128, GH, GC], F32)
            b2 = _sb(nc, es, f"o_b2{d}", [128, GH, GC], F32)
            o1 = _sb(nc, es, f"o_bb1{d}", [128, GH, GC], F32)
            o2 = _sb(nc, es, f"o_bb2{d}", [128, GH, GC], F32)
            keep.append((b1, b2, o1, o2))
        with ExitStack() as esp:
            tmpA = _sb(nc, esp, "o_tmpA", [128, GH, GC], F32)
            tmpB = _sb(nc, esp, "o_tmpB", [128, GH, GC], F32)

            def bc(t, n=GH):
                return t[:].unsqueeze(2).to_broadcast([128, n, GC])

            for d in range(2):
                p = emit_s5_params(nc, S, esp, din["lam_re"][d][:, gsl], din["lam_im"][d][:, gsl],
                                   din["logdt"][d][:, gsl], sg, GH, d, npw=T2)
                for j in range(T2 + 1):
                    for ri, key in (("r", "PR"), ("i", "PI")):
                        t, tn = p[key][j]
                        a, an = stk[(d, "PA", ri)]
                        V.cp(a[:, :, j], t[:], [tn], [an])
                        jd = (T2 - 1 - j) if d == 0 else (T2 - j)
                        if 0 <= jd < T2:
                            a, an = stk[(d, "PD", ri)]
                            V.cp(a[:, :, jd], t[:], [tn], [an])
                for j in range(T2):
                    for ri, key in (("r", "NR"), ("i", "NI")):
                        t, tn = p[key][j]
                        a, an = stk[(d, "NA", ri)]
                        V.cp(a[:, :, j], t[:], [tn], [an])
                b1, b2, o1, o2 = keep[d]
                S.dma("sp", b1[:], din["b1"][d][:, gsl, :], writes=[f"o_b1{d}"])
                S.dma("sp", b2[:], din["b2"][d][:, gsl, :], writes=[f"o_b2{d}"])
                (fr, frn), (fi, fin) = p["fr"], p["fi"]
                V.tt(tmpA[:], b1[:], bc(fr), ALU.mult, [f"o_b1{d}", frn], ["o_tmpA"])
                V.tt(tmpB[:], b2[:], bc(fi), ALU.mult, [f"o_b2{d}", fin], ["o_tmpB"])
                V.ts(tmpB[:], tmpB[:], sg[:, 0:1], ALU.mult, ["o_tmpB", "o_sg"], ["o_tmpB"])
                V.tt(o1[:], tmpA[:], tmpB[:], ALU.add, ["o_tmpA", "o_tmpB"], [f"o_bb1{d}"])
                V.tt(tmpA[:], b2[:], bc(fr), ALU.mult, [f"o_b2{d}", frn], ["o_tmpA"])
                V.tt(tmpB[:], b1[:], bc(fi), ALU.mult, [f"o_b1{d}", fin], ["o_tmpB"])
                V.ts(tmpB[:], tmpB[:], sr[:, 0:1], ALU.mult, ["o_tmpB", "o_sr"], ["o_tmpB"])
                V.tt(o2[:], tmpA[:], tmpB[:], ALU.add, ["o_tmpA", "o_tmpB"], [f"o_bb2{d}"])
                V.ts(o2[:], o2[:], sg[:, 0:1], ALU.mult, [f"o_bb2{d}", "o_sg"], [f"o_bb2{d}"])
                bb1.append((o1, f"o_bb1{d}")); bb2s.append((o2, f"o_bb2{d}"))
                S.dma("sp", b1[:], din["c1"][d][:, gsl, :], reads=[f"o_b1{d}"], writes=[f"o_b1{d}"])
                S.dma("sp", b2[:], din["c2"][d][:, gsl, :], reads=[f"o_b2{d}"], writes=[f"o_b2{d}"])
                V.ts(b1[:], b1[:], sr[:, 0:1], ALU.mult, [f"o_b1{d}", "o_sr"], [f"o_b1{d}"])
                c1s.append((b1, f"o_b1{d}")); c2.append((b2, f"o_b2{d}"))
            S.barrier()

        names = ["WTf", "Af", "WTb", "Bf", "Bb"]
        tmp = {nm: _sb(nc, es, f"ot_{nm}", [128, GB, T2, GC], F32) for nm in names}
        tAB = {e: (_sb(nc, es, f"o_tA{e}", [128, GB, T2, GC], F32), _sb(nc, es, f"o_tB{e}", [128, GB, T2, GC], F32))
               for e in ("dve", "pool")}
        m1 = _sb(nc, es, "o_m1", [128, 128], F32)
        m2 = _sb(nc, es, "o_m2", [128, 128], F32)
        Wt = _sb(nc, es, "o_Wst", [128, GB, 4, 2, 128], BF16)
        Vt = _sb(nc, es, "o_Vst", [128, GB, 2, T2 * GC], BF16)
        Mt = _sb(nc, es, "o_Mst", [128, GB, 2, T2 * GC], BF16)
        Wn, Vn, Mn = "o_Wst", "o_Vst", "o_Mst"

        def gen(VV, dst, dstn, src1, src2, d, tbl, j0, g0, op):
            (s1, s1n), (s2, s2n) = src1, src2
            (tr, trn), (ti, tin) = stk[(d, tbl, "r")], stk[(d, tbl, "i")]
            tA, tB = tAB[VV.e]
            shp = [128, GB, T2, GC]
            VV.tt(tA[:], s1[:, g0:g0 + GB, :].unsqueeze(2).to_broadcast(shp),
                  tr[:, g0:g0 + GB, j0:j0 + T2].unsqueeze(3).to_broadcast(shp), ALU.mult, [s1n, trn], [f"o_tA{VV.e}"])
            VV.tt(tB[:], s2[:, g0:g0 + GB, :].unsqueeze(2).to_broadcast(shp),
                  ti[:, g0:g0 + GB, j0:j0 + T2].unsqueeze(3).to_broadcast(shp), ALU.mult, [s2n, tin], [f"o_tB{VV.e}"])
            VV.tt(dst, tA[:], tB[:], op, [f"o_tA{VV.e}", f"o_tB{VV.e}"], [dstn])

        for bi, g0 in enumerate(range(0, GH, GB)):
            gen(V, tmp["WTf"][:], "ot_WTf", bb1[0], bb2s[0], 0, "PD", 0, g0, ALU.add)
            gen(V, tmp["Af"][:], "ot_Af", bb1[0], bb2s[0], 0, "NA", 0, g0, ALU.add)
            gen(V, tmp["WTb"][:], "ot_WTb", bb1[1], bb2s[1], 1, "PA", 0, g0, ALU.add)
            gen(V, tmp["Bf"][:], "ot_Bf", c1s[0], c2[0], 0, "PA", 0, g0, ALU.subtract)
            gen(V, tmp["Bb"][:], "ot_Bb", c1s[1], c2[1], 1, "NA", 0, g0, ALU.subtract)
            gen(VP, Vt[:, :, 0, :].rearrange("p g (t c) -> p g t c", t=T2), Vn, c1s[0], c2[0], 0, "PA", 1, g0,
                ALU.subtract)
            gen(VP, Vt[:, :, 1, :].rearrange("p g (t c) -> p g t c", t=T2), Vn, c1s[1], c2[1], 1, "PD", 0, g0,
                ALU.subtract)
            for gi in range(GB):
                g = g0 + gi
                for vi, (src, srcn) in enumerate((("WTf", "ot_WTf"), ("WTb", "ot_WTb"))):
                    for a in range(2):
                        bank = (vi * 2 + a) % 2
                        S.op("pe", lambda src=src, gi=gi, a=a, bank=bank: nc.tensor.transpose(
                            out=P[bank][:, 0:128],
                            in_=tmp[src][:, gi, a * 8:(a + 1) * 8, :].rearrange("p t c -> p (t c)"),
                            identity=ident[:]), reads=[srcn, "ident"], writes=[f"ps{bank}"])
                        S.op("act", lambda gi=gi, vi=vi, a=a, bank=bank: nc.scalar.copy(
                            out=Wt[:, gi, 2 * vi, a, :], in_=P[bank][:, 0:128]), reads=[f"ps{bank}"], writes=[Wn])
                        S.op("act", lambda gi=gi, vi=vi, a=a, bank=bank: nc.scalar.copy(
                            out=Wt[:, gi, 2 * vi + 1, a, 0:64], in_=P[bank][:, 64:128]),
                            reads=[f"ps{bank}"], writes=[Wn])
                        S.op("act", lambda gi=gi, vi=vi, a=a, bank=bank: nc.scalar.copy(
                            out=Wt[:, gi, 2 * vi + 1, a, 64:128], in_=P[bank][:, 0:64]),
                            reads=[f"ps{bank}"], writes=[Wn])
                for a in range(2):
                    for b in range(2):
                        asl = slice(a * 8, (a + 1) * 8)
                        bsl = slice(b * 8, (b + 1) * 8)
                        dst = Mt[:, gi, a, b * 128:(b + 1) * 128]
                        b1k, b2k = 2 + 2 * ((a * 2 + b) % 2), 3 + 2 * ((a * 2 + b) % 2)
                        if (a, b) != (1, 0):
                            S.op("pe", lambda gi=gi, asl=asl, bsl=bsl, b1k=b1k: nc.tensor.matmul(
                                out=P[b1k][:, 0:128], lhsT=tmp["Af"][:, gi, asl, :].rearrange("p t c -> p (t c)"),
                                rhs=tmp["Bf"][:, gi, bsl, :].rearrange("p t c -> p (t c)"), start=True, stop=True),
                                reads=["ot_Af", "ot_Bf"], writes=[f"ps{b1k}"])
                        if (a, b) != (0, 1):
                            S.op("pe", lambda gi=gi, asl=asl, bsl=bsl, b2k=b2k: nc.tensor.matmul(
                                out=P[b2k][:, 0:128], lhsT=tmp["WTb"][:, gi, asl, :].rearrange("p t c -> p (t c)"),
                                rhs=tmp["Bb"][:, gi, bsl, :].rearrange("p t c -> p (t c)"), start=True, stop=True),
                                reads=["ot_WTb", "ot_Bb"], writes=[f"ps{b2k}"])
                        if a == b:
                            V.tt(m1[:], P[b1k][:, 0:128], mf[:], ALU.mult, [f"ps{b1k}", "o_mf"], ["o_m1"])
                            V.tt(m2[:], P[b2k][:, 0:128], mb[:], ALU.mult, [f"ps{b2k}", "o_mb"], ["o_m2"])
                            V.tt(m1[:], m1[:], m2[:], ALU.add, ["o_m1", "o_m2"], ["o_m1"])
                            V.stt(dst, ident[:], dcol[:, g:g + 1], m1[:], ALU.mult, ALU.add,
                                  ["ident", "o_dcol", "o_m1"], [Mn])
                        elif (a, b) == (0, 1):
                            V.cp(dst, P[b1k][:, 0:128], [f"ps{b1k}"], [Mn])
                        else:
                            V.cp(dst, P[b2k][:, 0:128], [f"ps{b2k}"], [Mn])
            G0 = h * GH + g0
            S.dma("sp", opW[G0:G0 + GB].rearrange("g p n -> p g n"), Wt[:].rearrange("p g v a n -> p g (v a n)"),
                  reads=[Wn])
            S.dma("sp", opV[G0:G0 + GB].rearrange("g p n -> p g n"), Vt[:].rearrange("p g d n -> p g (d n)"),
                  reads=[Vn])
            S.dma("sp", opM[G0:G0 + GB].rearrange("g p n -> p g n"), Mt[:].rearrange("p g a n -> p g (a n)"),
                  reads=[Mn])
        S.barrier()


def emit_s5f_main(nc, S, P, ident, din, x_ap, ys_ap, opW, opV, opM, h, NK, NKF, GH=64, GB=8):
    V = _V(nc, S, "dve")
    gsl = slice(h * GH, (h + 1) * GH)
    x_k = x_ap.rearrange("(k t) d -> k t d", t=T2)
    y_k = ys_ap.rearrange("(k t) d -> k t d", t=T2)
    with ExitStack() as es:
        sg = _sb(nc, es, "m_sg", [128, 1], F32)
        sr = _sb(nc, es, "m_sr", [128, 1], F32)
        S.dma("sp", sg[:], din["sg"], writes=["m_sg"])
        V.ts(sr[:], sg[:], -1.0, ALU.mult, ["m_sg"], ["m_sr"])
        COEF = [_sb(nc, es, f"m_COEF{d}", [128, 2, 2, GH], F32) for d in range(2)]
        QR = [_sb(nc, es, f"m_QR{d}", [128, 2, 2, GH], F32) for d in range(2)]
        X2 = [[_sb(nc, es, f"m_X{d}{i}", [128, 2, GH], F32) for i in range(2)] for d in range(2)]
        with ExitStack() as esp:
            for d in range(2):
                p = emit_s5_params(nc, S, esp, din["lam_re"][d][:, gsl], din["lam_im"][d][:, gsl],
                                   din["logdt"][d][:, gsl], sg, GH, 10 + d, npw=T2)
                (ar, arn), (ai, ain) = p["PR"][T2], p["PI"][T2]
                V.cp(COEF[d][:, 0, 0, :], ar[:], [arn], [f"m_COEF{d}"])
                V.cp(COEF[d][:, 0, 1, :], ar[:], [arn], [f"m_COEF{d}"])
                V.ts(COEF[d][:, 1, 0, :], ai[:], sr[:, 0:1], ALU.mult, [ain, "m_sr"], [f"m_COEF{d}"])
                V.ts(COEF[d][:, 1, 1, :], ai[:], sg[:, 0:1], ALU.mult, [ain, "m_sg"], [f"m_COEF{d}"])
            S.barrier()
        NKS = [NKF, NK]
        St = [_sb(nc, es, f"m_St{d}", [128, NKS[d] + 1, 2, GH], BF16) for d in range(2)]
        Ubk = _sb(nc, es, "m_Ubk", [128, GH, 2, NKF], BF16)
        S.op("pool", lambda: nc.gpsimd.memset(St[0][:, 0, :, :], 0.0), writes=["m_St0"])
        S.op("pool", lambda: nc.gpsimd.memset(St[1][:, NK, :, :], 0.0), writes=["m_St1"])
        for d in range(2):
            for i in range(2):
                S.op("pool", lambda d=d, i=i: nc.gpsimd.memset(X2[d][i][:], 0.0), writes=[f"m_X{d}{i}"])
        with ExitStack() as es1:
            xt = [_sb(nc, es1, f"m_xt{i}", [128, T2, GB * GC], F32) for i in range(1)] * 2
            xr = [_sb(nc, es1, f"m_xr{i}", [128, GB, T2, GC], F32) for i in range(2)]
            Ut = [_sb(nc, es1, f"m_Ut{i}", [128, GB, 2, NK], BF16) for i in range(1)] * 2
            Wt = [_sb(nc, es1, f"m_Wt{i}", [128, GB, 4, 2, 128], BF16) for i in range(1)] * 2
            ci = 0
            for sb in range(GH // GB):
                G0 = h * GH + sb * GB
                c0 = G0 * GC
                W, Wn = Wt[0], "m_Wt0"
                U, Un = Ut[0], "m_Ut0"
                S.dma("sp", W[:].rearrange("p g v a n -> p g (v a n)"), opW[G0:G0 + GB].rearrange("g p n -> p g n"),
                      writes=[Wn])
                for kb in range(NK // 128):
                    xtt, xtn = xt[0], "m_xt0"
                    xrt, xrn = xr[ci % 2], f"m_xr{ci % 2}"
                    ci += 1
                    S.dma("sp", xtt[:], x_k[kb * 128:(kb + 1) * 128, :, c0:c0 + GB * GC], writes=[xtn])
                    S.op("act", lambda xtt=xtt, xrt=xrt: nc.scalar.copy(
                        out=xrt[:], in_=xtt[:].rearrange("p t (g c) -> p g t c", g=GB)), reads=[xtn], writes=[xrn])
                    for gi in range(GB):
                        bank = gi % 2
                        for a in range(2):
                            S.op("pe", lambda xrt=xrt, gi=gi, a=a, bank=bank: nc.tensor.transpose(
                                out=P[bank][:, a * 128:(a + 1) * 128],
                                in_=xrt[:, gi, a * 8:(a + 1) * 8, :].rearrange("p t c -> p (t c)"),
                                identity=ident[:]), reads=[xrn, "ident"], writes=[f"ps{bank}"], signal=(a == 1))
                        V.cp(U[:, gi, :, kb * 128:(kb + 1) * 128],
                             P[bank][:, 0:256].rearrange("p (a k) -> p a k", a=2), [f"ps{bank}"], [Un])
                S.op("act", lambda U=U, sb=sb: nc.scalar.copy(out=Ubk[:, sb * GB:(sb + 1) * GB, :, :],
                                                              in_=U[:, :, :, 0:NKF]), reads=[Un], writes=["m_Ubk"])
                for gi in range(GB):
                    gl = sb * GB + gi
                    for var in range(4):
                        d, ab = var // 2, var % 2
                        nk = NKS[d]
                        bank = 2 + (gi * 4 + var) % 6
                        for a in range(2):
                            S.op("pe", lambda W=W, U=U, gi=gi, var=var, a=a, nk=nk, bank=bank: nc.tensor.matmul(
                                out=P[bank][:, 0:nk], lhsT=W[:, gi, var, a, :], rhs=U[:, gi, a, 0:nk],
                                start=(a == 0), stop=(a == 1)), reads=[Wn, Un], writes=[f"ps{bank}"],
                                signal=(a == 1))
                        off = 1 if d == 0 else 0
                        if var % 2:
                            S.op("act", lambda d=d, ab=ab, gl=gl, bank=bank, off=off, nk=nk: nc.scalar.copy(
                                out=St[d][:, off:off + nk, ab, gl], in_=P[bank][:, 0:nk]),
                                reads=[f"ps{bank}"], writes=[f"m_St{d}"])
                        else:
                            V.cp(St[d][:, off:off + nk, ab, gl], P[bank][:, 0:nk], [f"ps{bank}"], [f"m_St{d}"])
            S.barrier()
        for k in range(NK):
            for d, e in ((1, "dve"), (0, "pool")):
                if d == 0 and k >= NKF:
                    continue
                E = nc.vector if e == "dve" else nc.gpsimd
                slot = k + 1 if d == 0 else NK - 1 - k
                Xp, Xpn = X2[d][(k + 1) % 2], f"m_X{d}{(k + 1) % 2}"
                Xc, Xcn = X2[d][k % 2], f"m_X{d}{k % 2}"
                Sn = f"m_St{d}_{slot}"
                S.op(e, lambda E=E, d=d, Xp=Xp: E.tensor_tensor(
                    out=QR[d][:], in0=COEF[d][:], in1=Xp[:].unsqueeze(1).to_broadcast([128, 2, 2, GH]),
                    op=ALU.mult), reads=[Xpn, f"m_COEF{d}"], writes=[f"m_QR{d}"])
                S.op(e, lambda E=E, d=d, slot=slot: E.tensor_tensor(
                    out=QR[d][:, 0, :, :], in0=QR[d][:, 0, :, :], in1=St[d][:, slot, :, :], op=ALU.add),
                    reads=[f"m_QR{d}", Sn], writes=[f"m_QR{d}"])
                S.op(e, lambda E=E, d=d, Xc=Xc: E.tensor_tensor(
                    out=Xc[:], in0=QR[d][:, 0, :, :], in1=QR[d][:, 1, ::-1, :], op=ALU.add),
                    reads=[f"m_QR{d}"], writes=[Xcn])
                S.op(e, lambda E=E, d=d, slot=slot, Xc=Xc: E.tensor_copy(out=St[d][:, slot, :, :], in_=Xc[:]),
                     reads=[Xcn], writes=[Sn])
        S.barrier()
        with ExitStack() as es3:
            Vt = [_sb(nc, es3, f"m_Vt{i}", [128, GB, 2, T2 * GC], BF16) for i in range(2)]
            Mt = [_sb(nc, es3, f"m_Mt{i}", [128, GB, 2, T2 * GC], BF16) for i in range(2)]
            Ysb = [_sb(nc, es3, f"m_Y{i}", [128, T2, GB * GC], F32) for i in range(2)]
            yi = 0
            for sb in range(GH // GB):
                G0 = h * GH + sb * GB
                c0 = G0 * GC
                Vv, Vn = Vt[sb % 2], f"m_Vt{sb % 2}"
                Mm, Mn = Mt[sb % 2], f"m_Mt{sb % 2}"
                S.dma("sp", Vv[:].rearrange("p g d n -> p g (d n)"), opV[G0:G0 + GB].rearrange("g p n -> p g n"),
                      writes=[Vn])
                S.dma("sp", Mm[:].rearrange("p g a n -> p g (a n)"), opM[G0:G0 + GB].rearrange("g p n -> p g n"),
                      writes=[Mn])
                for (k0, kn) in ((0, 128), (128, NKF - 128)):
                    Yt, Yn = Ysb[yi % 2], f"m_Y{yi % 2}"
                    yi += 1
                    for gi in range(GB):
                        gl = sb * GB + gi
                        bank = gi % 4
                        ysl = P[bank][0:kn, 0:T2 * GC]
                        ops = [(Ubk[:, gl, 0, k0:k0 + kn], Mm[:, gi, 0, :], ["m_Ubk", Mn]),
                               (Ubk[:, gl, 1, k0:k0 + kn], Mm[:, gi, 1, :], ["m_Ubk", Mn]),
                               (St[0][:, k0:k0 + kn, 0, gl], Vv[:, gi, 0, :], ["m_St0", Vn]),
                               (St[1][:, k0 + 1:k0 + kn + 1, 0, gl], Vv[:, gi, 1, :], ["m_St1", Vn])]
                        for oi, (l, r, rd) in enumerate(ops):
                            S.op("pe", lambda l=l, r=r, ysl=ysl, oi=oi: nc.tensor.matmul(
                                out=ysl, lhsT=l, rhs=r, start=(oi == 0), stop=(oi == 3)),
                                reads=rd, writes=[f"ps{bank}"], signal=(oi == 3))
                        src = ysl.rearrange("p (t c) -> p t c", t=T2)
                        if gi % 2:
                            S.op("act", lambda Yt=Yt, gi=gi, src=src, kn=kn: nc.scalar.copy(
                                out=Yt[0:kn, :, gi * GC:(gi + 1) * GC], in_=src), reads=[f"ps{bank}"], writes=[Yn])
                        else:
                            V.cp(Yt[0:kn, :, gi * GC:(gi + 1) * GC], src, [f"ps{bank}"], [Yn])
                    S.dma("sp", y_k[k0:k0 + kn, :, c0:c0 + GB * GC], Yt[0:kn, :, :], reads=[Yn])
            S.barrier()
        S.barrier()


def emit_s5f_main_old(nc, S, P, ident, din, x_ap, ys_ap, opW, opV, opM, h, NK, NKF, GH=64, GB=8):
    V = _V(nc, S, "dve")
    gsl = slice(h * GH, (h + 1) * GH)
    x_k = x_ap.rearrange("(k t) d -> k t d", t=T2)
    y_k = ys_ap.rearrange("(k t) d -> k t d", t=T2)
    with ExitStack() as es:
        sg = _sb(nc, es, "m_sg", [128, 1], F32)
        sr = _sb(nc, es, "m_sr", [128, 1], F32)
        S.dma("sp", sg[:], din["sg"], writes=["m_sg"])
        V.ts(sr[:], sg[:], -1.0, ALU.mult, ["m_sg"], ["m_sr"])
        AR2 = [_sb(nc, es, f"m_AR2{d}", [128, 2, GH], F32) for d in range(2)]
        AIx = [_sb(nc, es, f"m_AIx{d}", [128, 2, GH], F32) for d in range(2)]
        with ExitStack() as esp:
            for d in range(2):
                p = emit_s5_params(nc, S, esp, din["lam_re"][d][:, gsl], din["lam_im"][d][:, gsl],
                                   din["logdt"][d][:, gsl], sg, GH, 10 + d, npw=T2)
                (ar, arn), (ai, ain) = p["PR"][T2], p["PI"][T2]
                V.cp(AR2[d][:, 0, :], ar[:], [arn], [f"m_AR2{d}"])
                V.cp(AR2[d][:, 1, :], ar[:], [arn], [f"m_AR2{d}"])
                V.ts(AIx[d][:, 0, :], ai[:], sr[:, 0:1], ALU.mult, [ain, "m_sr"], [f"m_AIx{d}"])
                V.ts(AIx[d][:, 1, :], ai[:], sg[:, 0:1], ALU.mult, [ain, "m_sg"], [f"m_AIx{d}"])
            S.barrier()
        NKS = [NKF, NK]
        St = [_sb(nc, es, f"m_St{d}", [128, NKS[d] + 1, 2, GH], BF16) for d in range(2)]
        Ubk = _sb(nc, es, "m_Ubk", [128, GH, 2, NKF], BF16)
        X = [_sb(nc, es, f"m_X{d}", [128, 2, GH], F32) for d in range(2)]
        q1 = [_sb(nc, es, f"m_q1{d}", [128, 2, GH], F32) for d in range(2)]
        q2 = [_sb(nc, es, f"m_q2{d}", [128, 2, GH], F32) for d in range(2)]
        S.op("pool", lambda: nc.gpsimd.memset(St[0][:, 0, :, :], 0.0), writes=["m_St0"])
        S.op("pool", lambda: nc.gpsimd.memset(St[1][:, NK, :, :], 0.0), writes=["m_St1"])
        for d in range(2):
            S.op("pool", lambda d=d: nc.gpsimd.memset(X[d][:], 0.0), writes=[f"m_X{d}"])
        with ExitStack() as es1:
            xt = [_sb(nc, es1, f"m_xt{i}", [128, T2, GB * GC], F32) for i in range(1)] * 2
            xr = [_sb(nc, es1, f"m_xr{i}", [128, GB, T2, GC], F32) for i in range(2)]
            Ut = [_sb(nc, es1, f"m_Ut{i}", [128, GB, 2, NK], BF16) for i in range(1)] * 2
            Wt = [_sb(nc, es1, f"m_Wt{i}", [128, GB, 4, 2, 128], BF16) for i in range(1)] * 2
            ci = 0
            for sb in range(GH // GB):
                G0 = h * GH + sb * GB
                c0 = G0 * GC
                W, Wn = Wt[0], "m_Wt0"
                U, Un = Ut[0], "m_Ut0"
                S.dma("sp", W[:].rearrange("p g v a n -> p g (v a n)"), opW[G0:G0 + GB].rearrange("g p n -> p g n"),
                      writes=[Wn])
                for kb in range(NK // 128):
                    xtt, xtn = xt[0], "m_xt0"
                    xrt, xrn = xr[ci % 2], f"m_xr{ci % 2}"
                    ci += 1
                    S.dma("sp", xtt[:], x_k[kb * 128:(kb + 1) * 128, :, c0:c0 + GB * GC], writes=[xtn])
                    S.op("act", lambda xtt=xtt, xrt=xrt: nc.scalar.copy(
                        out=xrt[:], in_=xtt[:].rearrange("p t (g c) -> p g t c", g=GB)), reads=[xtn], writes=[xrn])
                    for gi in range(GB):
                        bank = gi % 2
                        for a in range(2):
                            S.op("pe", lambda xrt=xrt, gi=gi, a=a, bank=bank: nc.tensor.transpose(
                                out=P[bank][:, a * 128:(a + 1) * 128],
                                in_=xrt[:, gi, a * 8:(a + 1) * 8, :].rearrange("p t c -> p (t c)"),
                                identity=ident[:]), reads=[xrn, "ident"], writes=[f"ps{bank}"], signal=(a == 1))
                        V.cp(U[:, gi, :, kb * 128:(kb + 1) * 128],
                             P[bank][:, 0:256].rearrange("p (a k) -> p a k", a=2), [f"ps{bank}"], [Un])
                S.op("act", lambda U=U, sb=sb: nc.scalar.copy(out=Ubk[:, sb * GB:(sb + 1) * GB, :, :],
                                                              in_=U[:, :, :, 0:NKF]), reads=[Un], writes=["m_Ubk"])
                for gi in range(GB):
                    gl = sb * GB + gi
                    for var in range(4):
                        d, ab = var // 2, var % 2
                        nk = NKS[d]
                        bank = 2 + (gi * 4 + var) % 6
                        for a in range(2):
                            S.op("pe", lambda W=W, U=U, gi=gi, var=var, a=a, nk=nk, bank=bank: nc.tensor.matmul(
                                out=P[bank][:, 0:nk], lhsT=W[:, gi, var, a, :], rhs=U[:, gi, a, 0:nk],
                                start=(a == 0), stop=(a == 1)), reads=[Wn, Un], writes=[f"ps{bank}"],
                                signal=(a == 1))
                        off = 1 if d == 0 else 0
                        if var % 2:
                            S.op("act", lambda d=d, ab=ab, gl=gl, bank=bank, off=off, nk=nk: nc.scalar.copy(
                                out=St[d][:, off:off + nk, ab, gl], in_=P[bank][:, 0:nk]),
                                reads=[f"ps{bank}"], writes=[f"m_St{d}"])
                        else:
                            V.cp(St[d][:, off:off + nk, ab, gl], P[bank][:, 0:nk], [f"ps{bank}"], [f"m_St{d}"])
            S.barrier()
        for k in range(NK):
            for d, e in ((0, "dve"), (1, "pool")):
                if d == 0 and k >= NKF:
                    continue
                E = nc.vector if e == "dve" else nc.gpsimd
                slot = k + 1 if d == 0 else NK - 1 - k
                Xn, q1n, q2n, Sn = f"m_X{d}", f"m_q1{d}", f"m_q2{d}", f"m_St{d}"
                S.op(e, lambda E=E, d=d: E.tensor_tensor(out=q1[d][:], in0=X[d][:], in1=AR2[d][:], op=ALU.mult),
                     reads=[Xn, f"m_AR2{d}"], writes=[q1n])
                S.op(e, lambda E=E, d=d: E.tensor_tensor(out=q2[d][:], in0=X[d][:], in1=AIx[d][:], op=ALU.mult),
                     reads=[Xn, f"m_AIx{d}"], writes=[q2n])
                S.op(e, lambda E=E, d=d, slot=slot: E.tensor_tensor(out=q1[d][:], in0=q1[d][:],
                                                                    in1=St[d][:, slot, :, :], op=ALU.add),
                     reads=[q1n, Sn], writes=[q1n])
                S.op(e, lambda E=E, d=d: E.tensor_tensor(out=X[d][:, 0, :], in0=q1[d][:, 0, :], in1=q2[d][:, 1, :],
                                                         op=ALU.add), reads=[q1n, q2n], writes=[Xn])
                S.op(e, lambda E=E, d=d: E.tensor_tensor(out=X[d][:, 1, :], in0=q1[d][:, 1, :], in1=q2[d][:, 0, :],
                                                         op=ALU.add), reads=[q1n, q2n], writes=[Xn])
                S.op(e, lambda E=E, d=d, slot=slot: E.tensor_copy(out=St[d][:, slot, :, :], in_=X[d][:]),
                     reads=[Xn], writes=[Sn])
        with ExitStack() as es3:
            Vt = [_sb(nc, es3, f"m_Vt{i}", [128, GB, 2, T2 * GC], BF16) for i in range(2)]
            Mt = [_sb(nc, es3, f"m_Mt{i}", [128, GB, 2, T2 * GC], BF16) for i in range(2)]
            Ysb = [_sb(nc, es3, f"m_Y{i}", [128, T2, GB * GC], F32) for i in range(2)]
            yi = 0
            for sb in range(GH // GB):
                G0 = h * GH + sb * GB
                c0 = G0 * GC
                Vv, Vn = Vt[sb % 2], f"m_Vt{sb % 2}"
                Mm, Mn = Mt[sb % 2], f"m_Mt{sb % 2}"
                S.dma("sp", Vv[:].rearrange("p g d n -> p g (d n)"), opV[G0:G0 + GB].rearrange("g p n -> p g n"),
                      writes=[Vn])
                S.dma("sp", Mm[:].rearrange("p g a n -> p g (a n)"), opM[G0:G0 + GB].rearrange("g p n -> p g n"),
                      writes=[Mn])
                for (k0, kn) in ((0, 128), (128, NKF - 128)):
                    Yt, Yn = Ysb[yi % 2], f"m_Y{yi % 2}"
                    yi += 1
                    for gi in range(GB):
                        gl = sb * GB + gi
                        bank = gi % 4
                        ysl = P[bank][0:kn, 0:T2 * GC]
                        ops = [(Ubk[:, gl, 0, k0:k0 + kn], Mm[:, gi, 0, :], ["m_Ubk", Mn]),
                               (Ubk[:, gl, 1, k0:k0 + kn], Mm[:, gi, 1, :], ["m_Ubk", Mn]),
                               (St[0][:, k0:k0 + kn, 0, gl], Vv[:, gi, 0, :], ["m_St0", Vn]),
                               (St[1][:, k0 + 1:k0 + kn + 1, 0, gl], Vv[:, gi, 1, :], ["m_St1", Vn])]
                        for oi, (l, r, rd) in enumerate(ops):
                            S.op("pe", lambda l=l, r=r, ysl=ysl, oi=oi: nc.tensor.matmul(
                                out=ysl, lhsT=l, rhs=r, start=(oi == 0), stop=(oi == 3)),
                                reads=rd, writes=[f"ps{bank}"], signal=(oi == 3))
                        src = ysl.rearrange("p (t c) -> p t c", t=T2)
                        if gi % 2:
                            S.op("act", lambda Yt=Yt, gi=gi, src=src, kn=kn: nc.scalar.copy(
                                out=Yt[0:kn, :, gi * GC:(gi + 1) * GC], in_=src), reads=[f"ps{bank}"], writes=[Yn])
                        else:
                            V.cp(Yt[0:kn, :, gi * GC:(gi + 1) * GC], src, [f"ps{bank}"], [Yn])
                    S.dma("sp", y_k[k0:k0 + kn, :, c0:c0 + GB * GC], Yt[0:kn, :, :], reads=[Yn])
            S.barrier()
        S.barrier()


def emit_s5f_ops_old(nc, S, P, ident, din, opW, opV, opM, h, GH=64, GB=8):
    V = _V(nc, S, "dve")
    gsl = slice(h * GH, (h + 1) * GH)
    with ExitStack() as es:
        sg = _sb(nc, es, "o_sg", [128, 1], F32)
        sr = _sb(nc, es, "o_sr", [128, 1], F32)
        S.dma("sp", sg[:], din["sg"], writes=["o_sg"])
        V.ts(sr[:], sg[:], -1.0, ALU.mult, ["o_sg"], ["o_sr"])
        mf = _sb(nc, es, "o_mf", [128, 128], F32)
        mb = _sb(nc, es, "o_mb", [128, 128], F32)
        dcol = _sb(nc, es, "o_dcol", [128, GH], F32)
        S.dma("sp", mf[:], din["maskf"], writes=["o_mf"])
        S.dma("sp", mb[:], din["maskb"], writes=["o_mb"])
        S.dma("sp", dcol[:], din["dcol"][:, gsl], writes=["o_dcol"])
        prm = [emit_s5_params(nc, S, es, din["lam_re"][d][:, gsl], din["lam_im"][d][:, gsl],
                              din["logdt"][d][:, gsl], sg, GH, d, npw=T2) for d in range(2)]
        tmpA = _sb(nc, es, "o_tmpA", [128, GH, GC], F32)
        tmpB = _sb(nc, es, "o_tmpB", [128, GH, GC], F32)

        def bc(t, n=GH):
            return t[:].unsqueeze(2).to_broadcast([128, n, GC])

        bb1, bb2s, c1s, c2 = [], [], [], []
        for d in range(2):
            b1 = _sb(nc, es, f"o_b1{d}", [128, GH, GC], F32)
            b2 = _sb(nc, es, f"o_b2{d}", [128, GH, GC], F32)
            S.dma("sp", b1[:], din["b1"][d][:, gsl, :], writes=[f"o_b1{d}"])
            S.dma("sp", b2[:], din["b2"][d][:, gsl, :], writes=[f"o_b2{d}"])
            o1 = _sb(nc, es, f"o_bb1{d}", [128, GH, GC], F32)
            o2 = _sb(nc, es, f"o_bb2{d}", [128, GH, GC], F32)
            (fr, frn), (fi, fin) = prm[d]["fr"], prm[d]["fi"]
            V.tt(tmpA[:], b1[:], bc(fr), ALU.mult, [f"o_b1{d}", frn], ["o_tmpA"])
            V.tt(tmpB[:], b2[:], bc(fi), ALU.mult, [f"o_b2{d}", fin], ["o_tmpB"])
            V.ts(tmpB[:], tmpB[:], sg[:, 0:1], ALU.mult, ["o_tmpB", "o_sg"], ["o_tmpB"])
            V.tt(o1[:], tmpA[:], tmpB[:], ALU.add, ["o_tmpA", "o_tmpB"], [f"o_bb1{d}"])
            V.tt(tmpA[:], b2[:], bc(fr), ALU.mult, [f"o_b2{d}", frn], ["o_tmpA"])
            V.tt(tmpB[:], b1[:], bc(fi), ALU.mult, [f"o_b1{d}", fin], ["o_tmpB"])
            V.ts(tmpB[:], tmpB[:], sr[:, 0:1], ALU.mult, ["o_tmpB", "o_sr"], ["o_tmpB"])
            V.tt(o2[:], tmpA[:], tmpB[:], ALU.add, ["o_tmpA", "o_tmpB"], [f"o_bb2{d}"])
            V.ts(o2[:], o2[:], sg[:, 0:1], ALU.mult, [f"o_bb2{d}", "o_sg"], [f"o_bb2{d}"])
            bb1.append((o1, f"o_bb1{d}")); bb2s.append((o2, f"o_bb2{d}"))
            S.dma("sp", b1[:], din["c1"][d][:, gsl, :], reads=[f"o_b1{d}"], writes=[f"o_b1{d}"])
            S.dma("sp", b2[:], din["c2"][d][:, gsl, :], reads=[f"o_b2{d}"], writes=[f"o_b2{d}"])
            V.ts(b1[:], b1[:], sr[:, 0:1], ALU.mult, [f"o_b1{d}", "o_sr"], [f"o_b1{d}"])
            c1s.append((b1, f"o_b1{d}")); c2.append((b2, f"o_b2{d}"))

        names = ["WTf", "Af", "Bf", "Vf", "WTb", "Bb", "Vb"]
        tmp = {nm: _sb(nc, es, f"ot_{nm}", [128, GB, T2, GC], F32) for nm in names}
        tA = _sb(nc, es, "o_tA", [128, GB, GC], F32)
        tB = _sb(nc, es, "o_tB", [128, GB, GC], F32)
        m1 = _sb(nc, es, "o_m1", [128, 128], F32)
        m2 = _sb(nc, es, "o_m2", [128, 128], F32)
        Wst = [_sb(nc, es, f"o_Wst{i}", [128, GB, 4, 2, 128], BF16) for i in range(2)]
        Vst = [_sb(nc, es, f"o_Vst{i}", [128, GB, 2, T2 * GC], BF16) for i in range(2)]
        Mst = [_sb(nc, es, f"o_Mst{i}", [128, GB, 2, T2 * GC], BF16) for i in range(2)]

        def gen(dst, dstn, tsl, src1, src2, pa, pb, g0, op):
            (s1, s1n), (s2, s2n), (a, an), (b, bn) = src1, src2, pa, pb
            ab = a[:, g0:g0 + GB].unsqueeze(2).to_broadcast([128, GB, GC])
            bbb = b[:, g0:g0 + GB].unsqueeze(2).to_broadcast([128, GB, GC])
            V.tt(tA[:], s1[:, g0:g0 + GB, :], ab, ALU.mult, [s1n, an], ["o_tA"])
            V.tt(tB[:], s2[:, g0:g0 + GB, :], bbb, ALU.mult, [s2n, bn], ["o_tB"])
            V.tt(dst[:, :, tsl, :], tA[:], tB[:], op, ["o_tA", "o_tB"], [dstn])

        pf, pb_ = prm
        for bi, g0 in enumerate(range(0, GH, GB)):
            Wt, Wn = Wst[bi % 2], f"o_Wst{bi % 2}"
            Vt, Vn = Vst[bi % 2], f"o_Vst{bi % 2}"
            Mt, Mn = Mst[bi % 2], f"o_Mst{bi % 2}"
            for j in range(T2):
                gen(tmp["WTf"], "ot_WTf", j, bb1[0], bb2s[0], pf["PR"][T2 - 1 - j], pf["PI"][T2 - 1 - j], g0, ALU.add)
                gen(tmp["Af"], "ot_Af", j, bb1[0], bb2s[0], pf["NR"][j], pf["NI"][j], g0, ALU.add)
                gen(tmp["WTb"], "ot_WTb", j, bb1[1], bb2s[1], pb_["PR"][j], pb_["PI"][j], g0, ALU.add)
                gen(tmp["Bf"], "ot_Bf", j, c1s[0], c2[0], pf["PR"][j], pf["PI"][j], g0, ALU.subtract)
                gen(tmp["Vf"], "ot_Vf", j, c1s[0], c2[0], pf["PR"][j + 1], pf["PI"][j + 1], g0, ALU.subtract)
                gen(tmp["Bb"], "ot_Bb", j, c1s[1], c2[1], pb_["NR"][j], pb_["NI"][j], g0, ALU.subtract)
                gen(tmp["Vb"], "ot_Vb", j, c1s[1], c2[1], pb_["PR"][T2 - j], pb_["PI"][T2 - j], g0, ALU.subtract)
            V.cp(Vt[:, :, 0, :], tmp["Vf"][:].rearrange("p g t c -> p g (t c)"), ["ot_Vf"], [Vn])
            V.cp(Vt[:, :, 1, :], tmp["Vb"][:].rearrange("p g t c -> p g (t c)"), ["ot_Vb"], [Vn])
            for gi in range(GB):
                g = g0 + gi
                for vi, (src, srcn) in enumerate((("WTf", "ot_WTf"), ("WTb", "ot_WTb"))):
                    for a in range(2):
                        bank = (vi * 2 + a) % 2
                        S.op("pe", lambda src=src, gi=gi, a=a, bank=bank: nc.tensor.transpose(
                            out=P[bank][:, 0:128],
                            in_=tmp[src][:, gi, a * 8:(a + 1) * 8, :].rearrange("p t c -> p (t c)"),
                            identity=ident[:]), reads=[srcn, "ident"], writes=[f"ps{bank}"])
                        S.op("act", lambda Wt=Wt, gi=gi, vi=vi, a=a, bank=bank: nc.scalar.copy(
                            out=Wt[:, gi, 2 * vi, a, :], in_=P[bank][:, 0:128]), reads=[f"ps{bank}"], writes=[Wn])
                        V.cp(Wt[:, gi, 2 * vi + 1, a, 0:64], P[bank][:, 64:128], [f"ps{bank}"], [Wn])
                        V.cp(Wt[:, gi, 2 * vi + 1, a, 64:128], P[bank][:, 0:64], [f"ps{bank}"], [Wn])
                for a in range(2):
                    for b in range(2):
                        asl = slice(a * 8, (a + 1) * 8)
                        bsl = slice(b * 8, (b + 1) * 8)
                        dst = Mt[:, gi, a, b * 128:(b + 1) * 128]
                        if (a, b) != (1, 0):
                            S.op("pe", lambda gi=gi, asl=asl, bsl=bsl: nc.tensor.matmul(
                                out=P[2][:, 0:128], lhsT=tmp["Af"][:, gi, asl, :].rearrange("p t c -> p (t c)"),
                                rhs=tmp["Bf"][:, gi, bsl, :].rearrange("p t c -> p (t c)"), start=True, stop=True),
                                reads=["ot_Af", "ot_Bf"], writes=["ps2"])
                        if (a, b) != (0, 1):
                            S.op("pe", lambda gi=gi, asl=asl, bsl=bsl: nc.tensor.matmul(
                                out=P[3][:, 0:128], lhsT=tmp["WTb"][:, gi, asl, :].rearrange("p t c -> p (t c)"),
                                rhs=tmp["Bb"][:, gi, bsl, :].rearrange("p t c -> p (t c)"), start=True, stop=True),
                                reads=["ot_WTb", "ot_Bb"], writes=["ps3"])
                        if a == b:
                            V.tt(m1[:], P[2][:, 0:128], mf[:], ALU.mult, ["ps2", "o_mf"], ["o_m1"])
                            V.tt(m2[:], P[3][:, 0:128], mb[:], ALU.mult, ["ps3", "o_mb"], ["o_m2"])
                            V.tt(m1[:], m1[:], m2[:], ALU.add, ["o_m1", "o_m2"], ["o_m1"])
                            V.stt(dst, ident[:], dcol[:, g:g + 1], m1[:], ALU.mult, ALU.add,
                                  ["ident", "o_dcol", "o_m1"], [Mn])
                        elif (a, b) == (0, 1):
                            V.cp(dst, P[2][:, 0:128], ["ps2"], [Mn])
                        else:
                            V.cp(dst, P[3][:, 0:128], ["ps3"], [Mn])
            G0 = h * GH + g0
            S.dma("sp", opW[G0:G0 + GB].rearrange("g p n -> p g n"), Wt[:].rearrange("p g v a n -> p g (v a n)"),
                  reads=[Wn])
            S.dma("sp", opV[G0:G0 + GB].rearrange("g p n -> p g n"), Vt[:].rearrange("p g d n -> p g (d n)"),
                  reads=[Vn])
            S.dma("sp", opM[G0:G0 + GB].rearrange("g p n -> p g n"), Mt[:].rearrange("p g a n -> p g (a n)"),
                  reads=[Mn])
        S.barrier()
    return prm


def s5f_host_inputs(lam_re, lam_im, log_dt, b_re, b_im, c_re, c_im, d_skip, swap):
    o = [1, 0] if swap else [0, 1]
    def pg(a):
        a = a[o].transpose(0, 2, 1)
        return np.ascontiguousarray(np.concatenate([a, a], axis=1))
    lr, li = pg(lam_re), pg(lam_im)
    ldt = np.ascontiguousarray(np.broadcast_to(log_dt[o][:, None, :], (2, 128, NG)))
    br = b_re[o].transpose(0, 2, 1, 3)
    bi = b_im[o].transpose(0, 2, 1, 3)
    cr = c_re[o].transpose(0, 3, 1, 2)
    ci = c_im[o].transpose(0, 3, 1, 2)
    st = lambda a, b: np.ascontiguousarray(np.concatenate([a, b], axis=1))
    sg = np.concatenate([-np.ones((64, 1), np.float32), np.ones((64, 1), np.float32)], 0)
    dcol = np.ascontiguousarray(np.tile(d_skip.reshape(NG, GC).T, (8, 1)))
    tt = np.arange(128) // GC
    maskf = (tt[None, :] >= tt[:, None]).astype(np.float32)
    maskb = (tt[None, :] <= tt[:, None]).astype(np.float32)
    return {"s5_lam_re": lr, "s5_lam_im": li, "s5_logdt": ldt, "s5_b1": st(br, bi), "s5_b2": st(bi, br),
            "s5_c1": st(cr, ci), "s5_c2": st(ci, cr), "s5_sg": sg, "s5_dcol": dcol, "s5_maskf": maskf,
            "s5_maskb": maskb}


S5F_SHAPES = {"s5_lam_re": [2, 128, NG], "s5_lam_im": [2, 128, NG], "s5_logdt": [2, 128, NG],
              "s5_b1": [2, 128, NG, GC], "s5_b2": [2, 128, NG, GC], "s5_c1": [2, 128, NG, GC],
              "s5_c2": [2, 128, NG, GC], "s5_sg": [128, 1], "s5_dcol": [128, NG], "s5_maskf": [128, 128],
              "s5_maskb": [128, 128]}


def emit_s5f(nc, S, P, ident, ex, x_ap, ys_ap, NK, NKF):
    din = {k[3:]: ex[k] for k in S5F_SHAPES}
    for k in ("lam_re", "lam_im", "logdt", "b1", "b2", "c1", "c2"):
        din[k] = [din[k][d] for d in range(2)]
    opW = nc.dram_tensor("s5_opW", [NG, 128, 4 * 2 * 128], BF16, kind="Internal").ap()
    opV = nc.dram_tensor("s5_opV", [NG, 128, 2 * T2 * GC], BF16, kind="Internal").ap()
    opM = nc.dram_tensor("s5_opM", [NG, 128, 2 * T2 * GC], BF16, kind="Internal").ap()
    import os
    f_ops = emit_s5f_ops_old if os.environ.get("S5_OPS") == "old" else emit_s5f_ops
    f_main = emit_s5f_main_old if os.environ.get("S5_MAIN") == "old" else emit_s5f_main
    for h in range(2):
        f_ops(nc, S, P, ident, din, opW, opV, opM, h)
    for h in range(2):
        f_main(nc, S, P, ident, din, x_ap, ys_ap, opW, opV, opM, h, NK, NKF)


LSEQ = 4096
NOWN = 2048
NHAL = 2304


def natten_tables_f(rpb, mirrored):
    H = rpb.shape[0]
    ROWS = LSEQ // GW
    tab = np.full((H, 5, 9, GW, GW), NEG, np.float32)
    qc_l = np.arange(GW)
    kc_l = np.arange(GW)
    qc_t = (GW - 1 - qc_l) if mirrored else qc_l
    kc_t = (GW - 1 - kc_l) if mirrored else kc_l
    cs = np.clip(qc_t - 8, 0, GW - 16)
    inwin = (kc_t[:, None] >= cs[None, :]) & (kc_t[:, None] < cs[None, :] + 16)
    coff = np.clip(kc_t[:, None] - qc_t[None, :] + 15, 0, 30)
    for var in range(5):
        i = var
        rows = list(range(9)) if i < 4 else list(range(i - 4, i + 5))
        r_t = (ROWS - 1 - i) if mirrored else i
        rs = int(np.clip(r_t - 4, 0, ROWS - 8))
        for kr in range(9):
            k_t = (ROWS - 1 - rows[kr]) if mirrored else rows[kr]
            if not (rs <= k_t < rs + 8):
                continue
            ro = k_t - r_t + 7
            vals = rpb[:, ro][:, coff]
            tab[:, var, kr] = np.where(inwin[None], vals, np.float32(NEG))
    return np.ascontiguousarray(tab.transpose(0, 3, 1, 2, 4))


def build_fused():
    nc = bass.Bass("TRN2", target_bir_lowering=False)
    def inp(name, shape):
        return nc.dram_tensor(name, shape, F32, kind="ExternalInput").ap()
    x = inp("x", [LSEQ, D])
    W = [(inp(f"wg{k}", [D, FF]), inp(f"wu{k}", [D, FF]), inp(f"wd{k}", [FF, D])) for k in range(4)]
    lng = inp("lng", [6, 128, D])
    lnb = inp("lnb", [6, 128, D])
    idn = inp("idn", [128, 128])
    ex = {k: inp(k, shp) for k, shp in S5F_SHAPES.items()}
    glu_wv, glu_wg = inp("glu_wv", [D, D]), inp("glu_wg", [D, D])
    na_wqkv, na_wo = inp("na_wqkv", [D, 3 * D]), inp("na_wo", [D, D])
    na_tab = inp("na_tab", [NH, GW, 5, 9, GW])
    y = nc.dram_tensor("y", [NOWN, D], F32, kind="ExternalOutput").ap()
    def scr(name, n):
        return nc.dram_tensor(name, [n, D], F32, kind="Internal").ap()
    a1, ys, a2, a3, a4, a5 = (scr("a1", LSEQ), scr("ys", NHAL), scr("a2", NHAL), scr("a3", NHAL),
                              scr("a4", NHAL), scr("a5", NOWN))
    with ExitStack() as es:
        es.enter_context(nc.allow_low_precision("bf16 matmul operands, fp32 accumulation"))
        es.enter_context(nc.allow_non_contiguous_dma("weight block loads"))
        S = Sched(nc, es)
        P = [es.enter_context(nc.psum_tensor(f"ps{i}", [128, 512], F32)) for i in range(8)]
        ident = _sb(nc, es, "ident", [128, 128], F32)
        S.dma("sp", ident[:], idn, writes=["ident"])

        def stage(idx, fn):
            with ExitStack() as es2:
                g_rep = _sb(nc, es2, "g_rep", [128, D], F32)
                b_rep = _sb(nc, es2, "b_rep", [128, D], F32)
                S.dma("sp", g_rep[:], lng[idx], writes=["lng"])
                S.dma("sp", b_rep[:], lnb[idx], writes=["lnb"])
                fn(es2, g_rep, b_rep)
                S.barrier()

        zscr = nc.dram_tensor("ffn_zscr", [LSEQ, D], F32, kind="Internal").ap()
        T4096 = [(i * 768, 768) for i in range(5)] + [(3840, 256)]
        T2304 = [(i * 768, 768) for i in range(3)]
        T2048 = [(0, 768), (768, 768), (1536, 512)]
        stage(0, lambda e, g, b: emit_ffn2(nc, S, e, P, x, a1, zscr, *W[0], g, b, ident, LSEQ, "fa", T4096))
        emit_s5f(nc, S, P, ident, ex, a1, ys, LSEQ // T2, NHAL // T2)
        S.barrier()
        stage(1, lambda e, g, b: emit_glu(nc, S, e, P, a1[0:NHAL, :], ys, a2, glu_wv, glu_wg, g, b, ident, NHAL))
        stage(2, lambda e, g, b: emit_ffn2(nc, S, e, P, a2, a3, zscr, *W[1], g, b, ident, NHAL, "fb", T2304))
        stage(3, lambda e, g, b: emit_ffn2(nc, S, e, P, a3, a4, zscr, *W[2], g, b, ident, NHAL, "fc", T2304))
        stage(4, lambda e, g, b: emit_natten_f(nc, S, e, P, a4, a5, na_wqkv, na_wo, na_tab, g, b, ident))
        stage(5, lambda e, g, b: emit_ffn2(nc, S, e, P, a5, y, zscr, *W[3], g, b, ident, NOWN, "fd", T2048))
        S.finish()
    return nc


def kernel(x, ffn_w_gate, ffn_w_up, ffn_w_down, ln_g, ln_b, s5_lam_re, s5_lam_im, s5_log_dt, s5_b_re, s5_b_im,
           s5_c_re, s5_c_im, s5_d, s5_w_glu_val, s5_w_glu_gate, na_w_qkv, na_rpb, na_w_out):
    f = lambda a: np.ascontiguousarray(np.asarray(a, np.float32))
    B, L, _ = x.shape
    xf = f(x)
    common = {"idn": np.eye(128, dtype=np.float32)}
    for k, (i, j) in enumerate(((0, 0), (0, 1), (1, 0), (1, 1))):
        common[f"wg{k}"] = f(ffn_w_gate[i, j])
        common[f"wu{k}"] = f(ffn_w_up[i, j])
        common[f"wd{k}"] = f(ffn_w_down[i, j])
    lg = f(ln_g).reshape(6, D)
    lb = f(ln_b).reshape(6, D)
    common["lng"] = np.ascontiguousarray(np.broadcast_to(lg[:, None, :], (6, 128, D)))
    common["lnb"] = np.ascontiguousarray(np.broadcast_to(lb[:, None, :], (6, 128, D)))
    common["glu_wv"] = f(s5_w_glu_val[0])
    common["glu_wg"] = f(s5_w_glu_gate[0])
    common["na_wqkv"] = f(na_w_qkv[0])
    common["na_wo"] = f(na_w_out[0])
    s5p = [s5f_host_inputs(f(s5_lam_re[0]), f(s5_lam_im[0]), f(s5_log_dt[0]), f(s5_b_re[0]), f(s5_b_im[0]),
                           f(s5_c_re[0]), f(s5_c_im[0]), f(s5_d[0]), swap) for swap in (False, True)]
    tabs = [natten_tables_f(f(na_rpb[0]), m) for m in (False, True)]
    in_maps = []
    for c in range(NCORES):
        b, hf = c // 2, c % 2
        xl = xf[b] if hf == 0 else np.ascontiguousarray(xf[b][::-1])
        in_maps.append(dict(common, x=xl, na_tab=tabs[hf], **s5p[hf]))
    nc = build_fused()
    res = run_bass_kernel_spmd(nc, in_maps, core_ids=list(range(NCORES)))
    out = np.empty((B, L, D), np.float32)
    for c in range(NCORES):
        b, hf = c // 2, c % 2
        yc = res.results[c]["y"]
        if hf == 0:
            out[b, :NOWN] = yc
        else:
            out[b, NOWN:] = yc[::-1]
    return out


def emit_ffn2(nc, S, es, P, x_ap, y_ap, zscr, wg, wu, wd, g_rep, b_rep, ident, NT, tag, tiles):
    TM = max(tt for _, tt in tiles)
    FW = 128
    DW = 256
    xs = [_sb(nc, es, f"{tag}_xs{i}", [128, D], F32) for i in range(2)]
    xTw = _sb(nc, es, f"{tag}_xTw", [128, NDC * TM], BF16)
    assert NDC * TM >= NFC * DW
    xT = xTw[:, :].rearrange("p (c t) -> p c t", c=NDC)
    wd0 = xTw[:, 0:NFC * DW].rearrange("p (j d) -> p j d", j=NFC)
    wd1t = _sb(nc, es, f"{tag}_wd1", [128, NFC, DW], BF16)
    wdb = [(wd0, f"{tag}_xTw"), (wd1t[:], f"{tag}_wd1")]
    wgb = [_sb(nc, es, f"{tag}_wg{i}", [128, NDC, FW], BF16) for i in range(2)]
    wub = [_sb(nc, es, f"{tag}_wu{i}", [128, NDC, FW], BF16) for i in range(2)]
    hT = _sb(nc, es, f"{tag}_hT", [128, NFC, TM], BF16)
    sg = [_sb(nc, es, f"{tag}_sg{i}", [128, 512], F32) for i in range(2)]
    stg = [_sb(nc, es, f"{tag}_stg{i}", [128, DW], F32) for i in range(2)]
    z = _sb(nc, es, f"{tag}_z", [128, D], F32)
    st = _sb(nc, es, f"{tag}_st", [128, 4, nc.vector.BN_STATS_DIM], F32)
    mv = _sb(nc, es, f"{tag}_mv", [128, nc.vector.BN_AGGR_DIM], F32)
    rs = _sb(nc, es, f"{tag}_rs", [128, 1], F32)
    rxT = f"{tag}_xTw"
    wg_v = wg.rearrange("(c p) f -> p c f", p=128)
    wu_v = wu.rearrange("(c p) f -> p c f", p=128)
    wd_v = wd.rearrange("(c p) d -> p c d", p=128)
    xi = wi = di = si = 0
    for (t0, tt) in tiles:
        NS = tt // 128
        chunks = [(o, min(512, tt - o)) for o in range(0, tt, 512)]
        for s in range(NS):
            xb, rx = xs[xi % 2], f"{tag}_xs{xi % 2}"
            xi += 1
            S.dma("sp", xb[:], x_ap[t0 + s * 128:t0 + (s + 1) * 128, :], writes=[rx])
            for cq in range(NDC // 4):
                bank = 4 + (cq % 4)
                for k in range(4):
                    c = cq * 4 + k
                    S.op("pe", lambda c=c, k=k, bank=bank, xb=xb: nc.tensor.transpose(
                        out=P[bank][:, k * 128:(k + 1) * 128], in_=xb[:, c * 128:(c + 1) * 128],
                        identity=ident[:]), reads=[rx, "ident"], writes=[f"ps{bank}"], signal=(k == 3))
                dst = xT[:, cq * 4:(cq + 1) * 4, s * 128:(s + 1) * 128]
                src = P[bank][:].rearrange("p (k n) -> p k n", k=4)
                if cq % 2:
                    S.op("act", lambda dst=dst, src=src: nc.scalar.copy(out=dst, in_=src),
                         reads=[f"ps{bank}"], writes=[rxT])
                else:
                    S.op("dve", lambda dst=dst, src=src: nc.vector.tensor_copy(out=dst, in_=src),
                         reads=[f"ps{bank}"], writes=[rxT])
        for j in range(NFC):
            wgt, wut = wgb[wi % 2], wub[wi % 2]
            rwg, rwu = f"{tag}_wg{wi % 2}", f"{tag}_wu{wi % 2}"
            pj = wi % 2
            wi += 1
            S.dma("pool", wgt[:], wg_v[:, :, j * FW:(j + 1) * FW], writes=[rwg])
            S.dma("pool", wut[:], wu_v[:, :, j * FW:(j + 1) * FW], writes=[rwu])
            for hi, (o, w) in enumerate(chunks):
                pg, pu = pj * 4 + hi * 2, pj * 4 + hi * 2 + 1
                sgt, rsg = sg[si % 2], f"{tag}_sg{si % 2}"
                si += 1
                for (wt, rw, bank) in ((wgt, rwg, pg), (wut, rwu, pu)):
                    for c in range(NDC):
                        S.op("pe", lambda c=c, wt=wt, bank=bank, o=o, w=w: nc.tensor.matmul(
                            out=P[bank][:, 0:w], lhsT=wt[:, c, :], rhs=xT[:, c, o:o + w],
                            start=(c == 0), stop=(c == NDC - 1)),
                            reads=[rw, rxT], writes=[f"ps{bank}"], signal=(c == NDC - 1))
                S.op("act", lambda pg=pg, sgt=sgt, w=w: nc.scalar.activation(
                    out=sgt[:, 0:w], in_=P[pg][:, 0:w], func=AF.Silu), reads=[f"ps{pg}"], writes=[rsg])
                S.op("dve", lambda j=j, pu=pu, sgt=sgt, o=o, w=w: nc.vector.tensor_tensor(
                    out=hT[:, j, o:o + w], in0=sgt[:, 0:w], in1=P[pu][:, 0:w], op=ALU.mult),
                    reads=[rsg, f"ps{pu}"], writes=[f"{tag}_hT"])
        oi = 0
        for db in range(D // DW):
            wdt, rwd = wdb[di % 2]
            di += 1
            S.dma("pool", wdt, wd_v[:, :, db * DW:(db + 1) * DW], writes=[rwd])
            for s in range(NS):
                bank = oi % 8
                sgt_, rst_ = stg[oi % 2], f"{tag}_stg{oi % 2}"
                oi += 1
                for j in range(NFC):
                    S.op("pe", lambda j=j, s=s, bank=bank, wdt=wdt: nc.tensor.matmul(
                        out=P[bank][:, 0:DW], lhsT=hT[:, j, s * 128:(s + 1) * 128], rhs=wdt[:, j, :],
                        start=(j == 0), stop=(j == NFC - 1)),
                        reads=[f"{tag}_hT", rwd], writes=[f"ps{bank}"], signal=(j == NFC - 1))
                if oi % 2:
                    S.op("act", lambda sgt_=sgt_, bank=bank: nc.scalar.copy(out=sgt_[:], in_=P[bank][:, 0:DW]),
                         reads=[f"ps{bank}"], writes=[rst_])
                else:
                    S.op("dve", lambda sgt_=sgt_, bank=bank: nc.vector.tensor_copy(out=sgt_[:], in_=P[bank][:, 0:DW]),
                         reads=[f"ps{bank}"], writes=[rst_])
                S.dma("sp", zscr[t0 + s * 128:t0 + (s + 1) * 128, db * DW:(db + 1) * DW], sgt_[:],
                      reads=[rst_], writes=[f"{tag}_zscr{s}_{db}"])
        for s in range(NS):
            xb, rx = xs[xi % 2], f"{tag}_xs{xi % 2}"
            xi += 1
            S.dma("sp", xb[:], x_ap[t0 + s * 128:t0 + (s + 1) * 128, :], writes=[rx])
            S.dma("sp", z[:], zscr[t0 + s * 128:t0 + (s + 1) * 128, :], reads=[f"{tag}_zscr{s}_{q}" for q in range(D // DW)], writes=[f"{tag}_z"])
            S.op("act", lambda xb=xb: nc.scalar.mul(out=xb[:], in_=xb[:], mul=ALPHA), reads=[rx], writes=[rx])
            S.op("dve", lambda xb=xb: nc.vector.scalar_tensor_tensor(
                out=z[:], in0=z[:], scalar=0.5, in1=xb[:], op0=ALU.mult, op1=ALU.add),
                reads=[f"{tag}_z", rx], writes=[f"{tag}_z"])
            emit_ln(nc, S, z, f"{tag}_z", st, mv, rs, f"{tag}_ln", g_rep, b_rep)
            S.dma("sp", y_ap[t0 + s * 128:t0 + (s + 1) * 128, :], z[:], reads=[f"{tag}_z"])
```

```python
import numpy as np
from contextlib import ExitStack

import concourse.bass as bass
import concourse.mybir as mybir
from concourse.bass_utils import run_bass_kernel_spmd

F32 = mybir.dt.float32
BF16 = mybir.dt.bfloat16
ALU = mybir.AluOpType
AF = mybir.ActivationFunctionType

D = 2048
FF = 5504
NFC = FF // 128
NDC = D // 128
DEPTH = 2
ALPHA = (2 * DEPTH) ** 0.25
LN_EPS = 1e-5
NCORES = 8


class Sched:
    def __init__(self, nc, es, n_dma_sems=12):
        self.nc = nc
        self.eng = {"pe": nc.tensor, "act": nc.scalar, "dve": nc.vector, "pool": nc.gpsimd, "sp": nc.sync}
        self.sem = {k: es.enter_context(nc.semaphore("s_" + k)) for k in ["pe", "act", "dve", "pool"]}
        self.cnt = {k: 0 for k in self.sem}
        self.nd = n_dma_sems
        self.dsem = {q: [es.enter_context(nc.semaphore(f"d_{q}{i}")) for i in range(n_dma_sems)]
                     for q in ["sp", "pool"]}
        self.dcnt = {q: [0] * n_dma_sems for q in self.dsem}
        self.dnext = {q: 0 for q in self.dsem}
        self.waited = {e: {} for e in self.eng}
        self.lastw = {}
        self.readers = {}

    def _wait(self, e, tok):
        key, sem, val = tok
        if self.waited[e].get(key, 0) >= val:
            return
        self.eng[e].wait_ge(sem, val)
        self.waited[e][key] = val

    def _deps(self, e, reads, writes):
        toks = []
        for r in reads:
            t = self.lastw.get(r)
            if t is not None:
                toks.append(t)
        for w in writes:
            t = self.lastw.get(w)
            if t is not None:
                toks.append(t)
            toks.extend(self.readers.get(w, {}).values())
        for t in toks:
            if t[0] == "pe" and e == "pe":
                continue
            self._wait(e, t)

    def _record(self, tok, reads, writes):
        for r in reads:
            self.readers.setdefault(r, {})[tok[0]] = tok
        for w in writes:
            self.lastw[w] = tok
            self.readers[w] = {}

    def op(self, e, fn, reads=(), writes=(), signal=True):
        self._deps(e, reads, writes)
        inst = fn()
        if signal:
            self.cnt[e] += 1
            inst.then_inc(self.sem[e], 1)
            tok = (e, self.sem[e], self.cnt[e])
        else:
            tok = (e, self.sem[e], self.cnt[e] + 1)
        self._record(tok, reads, writes)
        return tok

    def dma(self, q, out, in_, reads=(), writes=()):
        i = self.dnext[q]
        self.dnext[q] = (i + 1) % self.nd
        sem = self.dsem[q][i]
        key = f"d_{q}{i}"
        if self.dcnt[q][i]:
            self._wait(q, (key, sem, self.dcnt[q][i] * 16))
        self._deps(q, reads, writes)
        self.eng[q].dma_start(out=out, in_=in_).then_inc(sem, 16)
        self.dcnt[q][i] += 1
        tok = (key, sem, self.dcnt[q][i] * 16)
        self._record(tok, reads, writes)
        return tok

    def barrier(self):
        toks = [(e, self.sem[e], self.cnt[e]) for e in self.sem if self.cnt[e]]
        for q in self.dsem:
            for i in range(self.nd):
                if self.dcnt[q][i]:
                    toks.append((f"d_{q}{i}", self.dsem[q][i], self.dcnt[q][i] * 16))
        for e in self.eng:
            for t in toks:
                if t[0] != e:
                    self._wait(e, t)

    def finish(self):
        for q in self.dsem:
            for i in range(self.nd):
                if self.dcnt[q][i]:
                    self._wait(q, (f"d_{q}{i}", self.dsem[q][i], self.dcnt[q][i] * 16))


_UNIQ = [0]


def _sb(nc, es, name, shape, dt):
    _UNIQ[0] += 1
    return es.enter_context(nc.sbuf_tensor(f"{name}_u{_UNIQ[0]}", shape, dt))


def emit_ffn(nc, S, es, P, x_ap, y_ap, wg, wu, wd, g_rep, b_rep, ident, NT, tag):
    TT = 512
    NS = TT // 128
    FW = 256
    DW = 256
    xs = [_sb(nc, es, f"{tag}_xs{i}", [128, D], F32) for i in range(2)]
    xT = _sb(nc, es, f"{tag}_xT", [128, NDC, TT], BF16)
    wgb = [_sb(nc, es, f"{tag}_wg{i}", [128, NDC, FW], BF16) for i in range(2)]
    wub = [_sb(nc, es, f"{tag}_wu{i}", [128, NDC, FW], BF16) for i in range(2)]
    hT = _sb(nc, es, f"{tag}_hT", [128, NFC, TT], BF16)
    wdb = [_sb(nc, es, f"{tag}_wd{i}", [128, NFC, DW], BF16) for i in range(2)]
    sg = [_sb(nc, es, f"{tag}_sg{i}", [128, TT], F32) for i in range(2)]
    z = [_sb(nc, es, f"{tag}_z{i}", [128, D], F32) for i in range(2)]
    st = [_sb(nc, es, f"{tag}_st{i}", [128, 4, nc.vector.BN_STATS_DIM], F32) for i in range(2)]
    mv = [_sb(nc, es, f"{tag}_mv{i}", [128, nc.vector.BN_AGGR_DIM], F32) for i in range(2)]
    rs = [_sb(nc, es, f"{tag}_rs{i}", [128, 1], F32) for i in range(2)]

    wg_v = wg.rearrange("(c p) f -> p c f", p=128)
    wu_v = wu.rearrange("(c p) f -> p c f", p=128)
    wd_v = wd.rearrange("(c p) d -> p c d", p=128)
    nfb = (FF + FW - 1) // FW
    ndb = D // DW
    xi = 0
    wi = 0
    di = 0
    pi = 0
    tiles = [(i * TT, TT) for i in range(NT // TT)]
    if NT % TT:
        tiles.append((NT - NT % TT, NT % TT))
    for (t0, tt) in tiles:
        NS = tt // 128
        for s in range(NS):
            xb = xs[xi % 2]
            rx = f"{tag}_xs{xi % 2}"
            xi += 1
            S.dma("sp", xb[:], x_ap[t0 + s * 128:t0 + (s + 1) * 128, :], writes=[rx])
            for cq in range(NDC // 4):
                bank = 4 + (cq % 4)
                for k in range(4):
                    c = cq * 4 + k
                    S.op("pe", lambda c=c, k=k, bank=bank: nc.tensor.transpose(
                        out=P[bank][:, k * 128:(k + 1) * 128], in_=xb[:, c * 128:(c + 1) * 128],
                        identity=ident[:]), reads=[rx, "ident"], writes=[f"ps{bank}"], signal=(k == 3))
                S.op("act" if cq % 2 else "dve",
                     (lambda cq=cq, bank=bank, s=s: nc.scalar.copy(
                         out=xT[:, cq * 4:(cq + 1) * 4, s * 128:(s + 1) * 128],
                         in_=P[bank][:].rearrange("p (k n) -> p k n", k=4))) if cq % 2 else
                     (lambda cq=cq, bank=bank, s=s: nc.vector.tensor_copy(
                         out=xT[:, cq * 4:(cq + 1) * 4, s * 128:(s + 1) * 128],
                         in_=P[bank][:].rearrange("p (k n) -> p k n", k=4))),
                     reads=[f"ps{bank}"], writes=[f"{tag}_xT"])
        for fb in range(nfb):
            f0 = fb * FW
            fw = min(FW, FF - f0)
            wgt, wut = wgb[wi % 2], wub[wi % 2]
            rwg, rwu = f"{tag}_wg{wi % 2}", f"{tag}_wu{wi % 2}"
            wi += 1
            S.dma("pool", wgt[:, :, 0:fw], wg_v[:, :, f0:f0 + fw], writes=[rwg])
            S.dma("pool", wut[:, :, 0:fw], wu_v[:, :, f0:f0 + fw], writes=[rwu])
            for jj in range(fw // 128):
                j = (f0 // 128) + jj
                pg, pu = (pi % 2) * 2, (pi % 2) * 2 + 1
                sgt = sg[pi % 2]
                rsg = f"{tag}_sg{pi % 2}"
                pi += 1
                for c in range(NDC):
                    S.op("pe", lambda c=c, jj=jj, pg=pg, wgt=wgt, tt=tt: nc.tensor.matmul(
                        out=P[pg][:, 0:tt], lhsT=wgt[:, c, jj * 128:(jj + 1) * 128], rhs=xT[:, c, 0:tt],
                        start=(c == 0), stop=(c == NDC - 1)),
                        reads=[rwg, f"{tag}_xT"], writes=[f"ps{pg}"], signal=(c == NDC - 1))
                for c in range(NDC):
                    S.op("pe", lambda c=c, jj=jj, pu=pu, wut=wut, tt=tt: nc.tensor.matmul(
                        out=P[pu][:, 0:tt], lhsT=wut[:, c, jj * 128:(jj + 1) * 128], rhs=xT[:, c, 0:tt],
                        start=(c == 0), stop=(c == NDC - 1)),
                        reads=[rwu, f"{tag}_xT"], writes=[f"ps{pu}"], signal=(c == NDC - 1))
                S.op("act", lambda pg=pg, sgt=sgt, tt=tt: nc.scalar.activation(
                    out=sgt[:, 0:tt], in_=P[pg][:, 0:tt], func=AF.Silu), reads=[f"ps{pg}"], writes=[rsg])
                S.op("dve", lambda j=j, pu=pu, sgt=sgt, tt=tt: nc.vector.tensor_tensor(
                    out=hT[:, j, 0:tt], in0=sgt[:, 0:tt], in1=P[pu][:, 0:tt], op=ALU.mult),
                    reads=[rsg, f"ps{pu}"], writes=[f"{tag}_hT"])
        for sp in range(NS // 2):
            zs = []
            for q in range(2):
                s = sp * 2 + q
                xb = xs[xi % 2]
                rx = f"{tag}_xs{xi % 2}"
                zt, rz = z[q], f"{tag}_z{q}"
                xi += 1
                S.dma("sp", xb[:], x_ap[t0 + s * 128:t0 + (s + 1) * 128, :], writes=[rx])
                zs.append((s, xb, rx, zt, rz))
            for db in range(ndb):
                wdt = wdb[di % 2]
                rwd = f"{tag}_wd{di % 2}"
                di += 1
                S.dma("pool", wdt[:], wd_v[:, :, db * DW:(db + 1) * DW], writes=[rwd])
                for q, (s, xb, rx, zt, rz) in enumerate(zs):
                    bank = 4 + ((db * 2 + q) % 4)
                    for j in range(NFC):
                        S.op("pe", lambda j=j, s=s, bank=bank, wdt=wdt: nc.tensor.matmul(
                            out=P[bank][:, 0:DW], lhsT=hT[:, j, s * 128:(s + 1) * 128], rhs=wdt[:, j, :],
                            start=(j == 0), stop=(j == NFC - 1)),
                            reads=[f"{tag}_hT", rwd], writes=[f"ps{bank}"], signal=(j == NFC - 1))
                    S.op("act", lambda xb=xb, db=db: nc.scalar.mul(
                        out=xb[:, db * DW:(db + 1) * DW], in_=xb[:, db * DW:(db + 1) * DW], mul=ALPHA),
                        reads=[rx], writes=[rx])
                    S.op("dve", lambda zt=zt, xb=xb, db=db, bank=bank: nc.vector.scalar_tensor_tensor(
                        out=zt[:, db * DW:(db + 1) * DW], in0=P[bank][:, 0:DW], scalar=0.5,
                        in1=xb[:, db * DW:(db + 1) * DW], op0=ALU.mult, op1=ALU.add),
                        reads=[f"ps{bank}", rx], writes=[rz])
            for q, (s, xb, rx, zt, rz) in enumerate(zs):
                emit_ln(nc, S, zt, rz, st[q], mv[q], rs[q], f"{tag}_ln{q}", g_rep, b_rep)
                S.dma("sp", y_ap[t0 + s * 128:t0 + (s + 1) * 128, :], zt[:], reads=[rz])


def emit_ln(nc, S, zt, rz, stt, mvt, rst, rtag, g_rep, b_rep):
    for k in range(4):
        S.op("dve", lambda k=k: nc.vector.bn_stats(out=stt[:, k, :], in_=zt[:, k * 512:(k + 1) * 512]),
             reads=[rz], writes=[rtag + "st"])
    S.op("dve", lambda: nc.vector.bn_aggr(out=mvt[:], in_=stt[:]), reads=[rtag + "st"], writes=[rtag + "mv"])
    S.op("dve", lambda: nc.vector.tensor_scalar(out=rst[:], in0=mvt[:, 1:2], scalar1=LN_EPS, scalar2=None,
                                                op0=ALU.add),
         reads=[rtag + "mv"], writes=[rtag + "rs"])
    S.op("act", lambda: nc.scalar.sqrt(out=rst[:], in_=rst[:]), reads=[rtag + "rs"], writes=[rtag + "rs"])
    S.op("dve", lambda: nc.vector.reciprocal(out=rst[:], in_=rst[:]), reads=[rtag + "rs"], writes=[rtag + "rs"])
    S.op("dve", lambda: nc.vector.tensor_scalar(out=zt[:], in0=zt[:], scalar1=mvt[:, 0:1], scalar2=rst[:, 0:1],
                                                op0=ALU.subtract, op1=ALU.mult),
         reads=[rz, rtag + "mv", rtag + "rs"], writes=[rz])
    S.op("dve", lambda: nc.vector.tensor_tensor(out=zt[:], in0=zt[:], in1=g_rep[:], op=ALU.mult),
         reads=[rz, "lng"], writes=[rz])
    S.op("dve", lambda: nc.vector.tensor_tensor(out=zt[:], in0=zt[:], in1=b_rep[:], op=ALU.add),
         reads=[rz, "lnb"], writes=[rz])


def build_ffn(NT):
    nc = bass.Bass("TRN2", target_bir_lowering=False)
    x = nc.dram_tensor("x", [NT, D], F32, kind="ExternalInput").ap()
    wg = nc.dram_tensor("wg", [D, FF], F32, kind="ExternalInput").ap()
    wu = nc.dram_tensor("wu", [D, FF], F32, kind="ExternalInput").ap()
    wd = nc.dram_tensor("wd", [FF, D], F32, kind="ExternalInput").ap()
    lng = nc.dram_tensor("lng", [128, D], F32, kind="ExternalInput").ap()
    lnb = nc.dram_tensor("lnb", [128, D], F32, kind="ExternalInput").ap()
    idn = nc.dram_tensor("idn", [128, 128], F32, kind="ExternalInput").ap()
    y = nc.dram_tensor("y", [NT, D], F32, kind="ExternalOutput").ap()
    with ExitStack() as es:
        es.enter_context(nc.allow_low_precision("bf16 matmul operands, fp32 accumulation"))
        es.enter_context(nc.allow_non_contiguous_dma("weight block loads"))
        S = Sched(nc, es)
        P = [es.enter_context(nc.psum_tensor(f"ps{i}", [128, 512], F32)) for i in range(8)]
        ident = _sb(nc, es, "ident", [128, 128], F32)
        g_rep = _sb(nc, es, "g_rep", [128, D], F32)
        b_rep = _sb(nc, es, "b_rep", [128, D], F32)
        S.dma("sp", ident[:], idn, writes=["ident"])
        S.dma("sp", g_rep[:], lng, writes=["lng"])
        S.dma("sp", b_rep[:], lnb, writes=["lnb"])
        emit_ffn(nc, S, es, P, x, y, wg, wu, wd, g_rep, b_rep, ident, NT, "f")
        S.finish()
    return nc


def _rep(v):
    return np.ascontiguousarray(np.broadcast_to(np.asarray(v, np.float32)[None, :], (128, v.shape[-1])))


def run_ffn(x_tok, wg, wu, wd, g, b):
    NT = x_tok.shape[0] // NCORES
    nc = build_ffn(NT)
    idn = np.eye(128, dtype=np.float32)
    common = {"wg": np.ascontiguousarray(wg), "wu": np.ascontiguousarray(wu), "wd": np.ascontiguousarray(wd),
              "lng": _rep(g), "lnb": _rep(b), "idn": idn}
    in_maps = [dict(common, x=np.ascontiguousarray(x_tok[c * NT:(c + 1) * NT])) for c in range(NCORES)]
    res = run_bass_kernel_spmd(nc, in_maps, core_ids=list(range(NCORES)))
    return np.concatenate([r["y"] for r in res.results], axis=0)


TC = 8
GC = 16
NP = 64
MIN_NEG_RE = -1e-4
TWO_PI = 6.283185307179586


class _V:
    def __init__(self, nc, S, e="dve"):
        self.nc, self.S, self.e = nc, S, e
        self.E = nc.vector if e == "dve" else nc.gpsimd

    def tt(self, out, a, b, op, R, W):
        self.S.op(self.e, lambda: self.E.tensor_tensor(out=out, in0=a, in1=b, op=op), reads=R, writes=W)

    def ts(self, out, a, s1, op0, R, W, s2=None, op1=None):
        if op1 is None:
            self.S.op(self.e, lambda: self.E.tensor_scalar(out=out, in0=a, scalar1=s1, scalar2=None, op0=op0),
                      reads=R, writes=W)
        else:
            self.S.op(self.e, lambda: self.E.tensor_scalar(out=out, in0=a, scalar1=s1, scalar2=s2, op0=op0,
                                                           op1=op1), reads=R, writes=W)

    def stt(self, out, a, s, b, op0, op1, R, W):
        self.S.op(self.e, lambda: self.nc.vector.scalar_tensor_tensor(out=out, in0=a, scalar=s, in1=b, op0=op0,
                                                                      op1=op1), reads=R, writes=W)

    def cp(self, out, a, R, W):
        self.S.op(self.e, lambda: self.E.tensor_copy(out=out, in_=a), reads=R, writes=W)


def _poly(V, out, x2, coefs, tmpn, R):
    first = True
    for c in reversed(coefs):
        if first:
            V.ts(out, x2, float(c), ALU.mult, R, [tmpn])
            first = False
        else:
            V.stt(out, out, float(c), x2, ALU.add, ALU.mult, R + [tmpn], [tmpn])


def emit_s5_params(nc, S, es, lam_re, lam_im, logdt, sg, GL, d, npw=8):
    V = _V(nc, S, "dve")
    n = [0]

    def T(nm):
        n[0] += 1
        return _sb(nc, es, f"s5p{d}_{nm}{n[0]}", [128, GL], F32), f"s5p{d}_{nm}{n[0]}"

    lr, lrn = T("lr")
    li, lin = T("li")
    dt, dtn = T("dt")
    S.dma("sp", lr[:], lam_re, writes=[lrn])
    S.dma("sp", li[:], lam_im, writes=[lin])
    S.dma("sp", dt[:], logdt, writes=[dtn])
    V.ts(lr[:], lr[:], MIN_NEG_RE, ALU.min, [lrn], [lrn])
    S.op("act", lambda: nc.scalar.activation(out=dt[:], in_=dt[:], func=AF.Exp), reads=[dtn], writes=[dtn])
    x, xn = T("x")
    V.tt(x[:], lr[:], dt[:], ALU.mult, [lrn, dtn], [xn])
    mag, magn = T("mag")
    cf = [1.0, 1 / 2.0, 1 / 6.0, 1 / 24.0, 1 / 120.0, 1 / 720.0, 1 / 5040.0]
    _poly(V, mag[:], x[:], cf, magn, [xn])
    V.ts(mag[:], mag[:], 1.0, ALU.add, [magn], [magn])
    th, thn = T("th")
    V.tt(th[:], li[:], dt[:], ALU.mult, [lin, dtn], [thn])
    ki = _sb(nc, es, f"s5p{d}_ki", [128, GL], mybir.dt.int32)
    kin = f"s5p{d}_ki"
    kf, kfn = T("kf")
    V.ts(kf[:], th[:], 1.0 / TWO_PI, ALU.mult, [thn], [kfn])
    V.cp(ki[:], kf[:], [kfn], [kin])
    V.cp(kf[:], ki[:], [kin], [kfn])
    C1 = 6.28125
    C2 = TWO_PI - C1
    r, rn = T("r")
    V.stt(r[:], kf[:], -C1, th[:], ALU.mult, ALU.add, [kfn, thn], [rn])
    V.stt(r[:], kf[:], -C2, r[:], ALU.mult, ALU.add, [kfn, rn], [rn])
    m, mn = T("m")
    V.ts(m[:], r[:], float(np.pi), ALU.is_gt, [rn], [mn], s2=-TWO_PI, op1=ALU.mult)
    V.tt(r[:], r[:], m[:], ALU.add, [rn, mn], [rn])
    V.ts(m[:], r[:], -float(np.pi), ALU.is_lt, [rn], [mn], s2=TWO_PI, op1=ALU.mult)
    V.tt(r[:], r[:], m[:], ALU.add, [rn, mn], [rn])
    r2, r2n = T("r2")
    V.tt(r2[:], r[:], r[:], ALU.mult, [rn], [r2n])
    import math
    sc = [(-1.0) ** k / math.factorial(2 * k + 1) for k in range(1, 12)]
    cc = [(-1.0) ** k / math.factorial(2 * k) for k in range(1, 12)]
    sn, snn = T("sn")
    cs, csn = T("cs")
    _poly(V, sn[:], r2[:], sc, snn, [r2n])
    V.stt(sn[:], sn[:], 1.0, r[:], ALU.add, ALU.mult, [snn, rn], [snn])
    _poly(V, cs[:], r2[:], cc, csn, [r2n])
    V.ts(cs[:], cs[:], 1.0, ALU.add, [csn], [csn])
    out = {}
    PR, PI, NR, NI = [], [], [], []
    p0r, p0rn = T("p0r")
    p0i, p0in = T("p0i")
    S.op("dve", lambda: nc.vector.memset(p0r[:], 1.0), writes=[p0rn])
    S.op("dve", lambda: nc.vector.memset(p0i[:], 0.0), writes=[p0in])
    l1r, l1rn = T("l1r")
    l1i, l1in = T("l1i")
    V.tt(l1r[:], mag[:], cs[:], ALU.mult, [magn, csn], [l1rn])
    V.tt(l1i[:], mag[:], sn[:], ALU.mult, [magn, snn], [l1in])
    PR.append((p0r, p0rn)); PI.append((p0i, p0in))
    PR.append((l1r, l1rn)); PI.append((l1i, l1in))
    t1, t1n = T("t1")
    t2, t2n = T("t2")

    def cmul(ar, ai, br, bi):
        (a_r, a_rn), (a_i, a_in), (b_r, b_rn), (b_i, b_in) = ar, ai, br, bi
        o_r, o_rn = T("pw")
        o_i, o_in = T("pw")
        V.tt(t1[:], a_r[:], b_r[:], ALU.mult, [a_rn, b_rn], [t1n])
        V.tt(t2[:], a_i[:], b_i[:], ALU.mult, [a_in, b_in], [t2n])
        V.tt(o_r[:], t1[:], t2[:], ALU.subtract, [t1n, t2n], [o_rn])
        V.tt(t1[:], a_r[:], b_i[:], ALU.mult, [a_rn, b_in], [t1n])
        V.tt(t2[:], a_i[:], b_r[:], ALU.mult, [a_in, b_rn], [t2n])
        V.tt(o_i[:], t1[:], t2[:], ALU.add, [t1n, t2n], [o_in])
        return (o_r, o_rn), (o_i, o_in)

    for j in range(2, npw + 1):
        pr, pi_ = cmul(PR[-1], PI[-1], PR[1], PI[1])
        PR.append(pr); PI.append(pi_)
    m2, m2n = T("m2")
    V.tt(m2[:], mag[:], mag[:], ALU.mult, [magn], [m2n])
    S.op("dve", lambda: nc.vector.reciprocal(out=m2[:], in_=m2[:]), reads=[m2n], writes=[m2n])
    n1r, n1rn = T("n1r")
    n1i, n1in = T("n1i")
    V.tt(n1r[:], l1r[:], m2[:], ALU.mult, [l1rn, m2n], [n1rn])
    V.stt(n1i[:], l1i[:], -1.0, m2[:], ALU.mult, ALU.mult, [l1in, m2n], [n1in])
    NR.append((p0r, p0rn)); NI.append((p0i, p0in))
    NR.append((n1r, n1rn)); NI.append((n1i, n1in))
    for j in range(2, npw):
        pr, pi_ = cmul(NR[-1], NI[-1], NR[1], NI[1])
        NR.append(pr); NI.append(pi_)
    den, denn = T("den")
    V.tt(den[:], lr[:], lr[:], ALU.mult, [lrn], [denn])
    V.tt(t1[:], li[:], li[:], ALU.mult, [lin], [t1n])
    V.tt(den[:], den[:], t1[:], ALU.add, [denn, t1n], [denn])
    S.op("dve", lambda: nc.vector.reciprocal(out=den[:], in_=den[:]), reads=[denn], writes=[denn])
    nr, nrn = T("nr")
    V.ts(nr[:], l1r[:], -1.0, ALU.add, [l1rn], [nrn])
    fr, frn = T("fr")
    fi, fin = T("fi")
    V.tt(t1[:], nr[:], lr[:], ALU.mult, [nrn, lrn], [t1n])
    V.tt(t2[:], l1i[:], li[:], ALU.mult, [l1in, lin], [t2n])
    V.tt(fr[:], t1[:], t2[:], ALU.add, [t1n, t2n], [frn])
    V.tt(fr[:], fr[:], den[:], ALU.mult, [frn, denn], [frn])
    V.tt(t1[:], l1i[:], lr[:], ALU.mult, [l1in, lrn], [t1n])
    V.tt(t2[:], nr[:], li[:], ALU.mult, [nrn, lin], [t2n])
    V.tt(fi[:], t1[:], t2[:], ALU.subtract, [t1n, t2n], [fin])
    V.tt(fi[:], fi[:], den[:], ALU.mult, [fin, denn], [fin])
    return {"PR": PR, "PI": PI, "NR": NR, "NI": NI, "fr": (fr, frn), "fi": (fi, fin)}


def emit_s5(nc, S, es, P, ident, din, y_out, GL, NK, GB=16):
    V = _V(nc, S, "dve")
    NKB = NK // 128
    sg = _sb(nc, es, "s5_sg", [128, 1], F32)
    sr = _sb(nc, es, "s5_sr", [128, 1], F32)
    S.dma("sp", sg[:], din["sg"], writes=["s5_sg"])
    V.ts(sr[:], sg[:], -1.0, ALU.mult, ["s5_sg"], ["s5_sr"])
    maskf = _sb(nc, es, "s5_mf", [128, 128], F32)
    maskb = _sb(nc, es, "s5_mb", [128, 128], F32)
    dcol = _sb(nc, es, "s5_dcol", [128, GL], F32)
    S.dma("sp", maskf[:], din["maskf"], writes=["s5_mf"])
    S.dma("sp", maskb[:], din["maskb"], writes=["s5_mb"])
    S.dma("sp", dcol[:], din["dcol"], writes=["s5_dcol"])
    prm = [emit_s5_params(nc, S, es, din["lam_re"][d], din["lam_im"][d], din["logdt"][d], sg, GL, d)
           for d in range(2)]
    def signed(tbl, sgn, sgn_n, d, nm):
        res = []
        for j, (t, tn) in enumerate(tbl):
            o = _sb(nc, es, f"s5s{d}_{nm}{j}", [128, GL], F32)
            on = f"s5s{d}_{nm}{j}"
            V.ts(o[:], t[:], sgn[:, 0:1], ALU.mult, [tn, sgn_n], [on])
            res.append((o, on))
        return res
    for d in range(2):
        p = prm[d]
        p["sgPI"] = signed(p["PI"], sg, "s5_sg", d, "gpi")
        p["srPR"] = signed(p["PR"], sr, "s5_sr", d, "rpr")
        p["sgNI"] = signed(p["NI"], sg, "s5_sg", d, "gni")
        p["srNR"] = signed(p["NR"], sr, "s5_sr", d, "rnr")
        p["sgfi"] = signed([p["fi"]], sg, "s5_sg", d, "gfi")[0]
        p["srfi"] = signed([p["fi"]], sr, "s5_sr", d, "rfi")[0]
        p["srPI8"] = signed([p["PI"][8]], sr, "s5_sr", d, "rpi8")[0]
    bb1, bb2, c1, c2 = [], [], [], []
    tmpA = _sb(nc, es, "s5_tmpA", [128, GL, GC], F32)
    tmpB = _sb(nc, es, "s5_tmpB", [128, GL, GC], F32)

    def bc(t):
        return t[:].unsqueeze(2).to_broadcast([128, GL, GC])

    for d in range(2):
        b1 = _sb(nc, es, f"s5_b1{d}", [128, GL, GC], F32)
        b2 = _sb(nc, es, f"s5_b2{d}", [128, GL, GC], F32)
        S.dma("sp", b1[:], din["b1"][d], writes=[f"s5_b1{d}"])
        S.dma("sp", b2[:], din["b2"][d], writes=[f"s5_b2{d}"])
        o1 = _sb(nc, es, f"s5_bb1{d}", [128, GL, GC], F32)
        o2 = _sb(nc, es, f"s5_bb2{d}", [128, GL, GC], F32)
        p = prm[d]
        (fr, frn), (sgfi, sgfin), (srfi, srfin) = p["fr"], p["sgfi"], p["srfi"]
        V.tt(tmpA[:], b1[:], bc(fr), ALU.mult, [f"s5_b1{d}", frn], ["s5_tmpA"])
        V.tt(tmpB[:], b2[:], bc(sgfi), ALU.mult, [f"s5_b2{d}", sgfin], ["s5_tmpB"])
        V.tt(o1[:], tmpA[:], tmpB[:], ALU.add, ["s5_tmpA", "s5_tmpB"], [f"s5_bb1{d}"])
        V.tt(tmpA[:], b2[:], bc(fr), ALU.mult, [f"s5_b2{d}", frn], ["s5_tmpA"])
        V.tt(tmpB[:], b1[:], bc(srfi), ALU.mult, [f"s5_b1{d}", srfin], ["s5_tmpB"])
        V.tt(o2[:], tmpA[:], tmpB[:], ALU.add, ["s5_tmpA", "s5_tmpB"], [f"s5_bb2{d}"])
        bb1.append((o1, f"s5_bb1{d}")); bb2.append((o2, f"s5_bb2{d}"))
        cc1 = _sb(nc, es, f"s5_c1{d}", [128, GL, GC], F32)
        cc2 = _sb(nc, es, f"s5_c2{d}", [128, GL, GC], F32)
        S.dma("sp", cc1[:], din["c1"][d], writes=[f"s5_c1{d}"])
        S.dma("sp", cc2[:], din["c2"][d], writes=[f"s5_c2{d}"])
        c1.append((cc1, f"s5_c1{d}")); c2.append((cc2, f"s5_c2{d}"))

    names = ["WTf", "Af", "Bf", "Vf", "WTb", "Bb", "Vb"]
    WfA = _sb(nc, es, "s5_WfA", [128, GB, 128], BF16)
    WfB = _sb(nc, es, "s5_WfB", [128, GB, 128], BF16)
    WbA = _sb(nc, es, "s5_WbA", [128, GB, 128], BF16)
    WbB = _sb(nc, es, "s5_WbB", [128, GB, 128], BF16)
    Vfb = _sb(nc, es, "s5_Vfb", [128, GB, 128], BF16)
    Vbb = _sb(nc, es, "s5_Vbb", [128, GB, 128], BF16)
    Mb = _sb(nc, es, "s5_Mb", [128, GB, 128], BF16)
    X = [_sb(nc, es, f"s5_X{d}", [128, 2, GB], F32) for d in range(2)]
    q1 = [_sb(nc, es, f"s5_q1{d}", [128, 2, GB], F32) for d in range(2)]
    q2 = [_sb(nc, es, f"s5_q2{d}", [128, 2, GB], F32) for d in range(2)]
    AR2 = [_sb(nc, es, f"s5_AR2{d}", [128, 2, GB], F32) for d in range(2)]
    AIx = [_sb(nc, es, f"s5_AIx{d}", [128, 2, GB], F32) for d in range(2)]
    tmp = {}
    bufs = {}

    def gen(dst, dstn, tsl, src1, src2, pa, pb, g0):
        tA, tB = bufs["tA"], bufs["tB"]
        (s1, s1n), (s2, s2n), (a, an), (b, bn) = src1, src2, pa, pb
        ab = a[:, g0:g0 + GB].unsqueeze(2).to_broadcast([128, GB, GC])
        bbb = b[:, g0:g0 + GB].unsqueeze(2).to_broadcast([128, GB, GC])
        V.tt(tA[:], s1[:, g0:g0 + GB, :], ab, ALU.mult, [s1n, an], ["s5_tA"])
        V.tt(tB[:], s2[:, g0:g0 + GB, :], bbb, ALU.mult, [s2n, bn], ["s5_tB"])
        V.tt(dst[:, :, tsl, :], tA[:], tB[:], ALU.add, ["s5_tA", "s5_tB"], [dstn])

    for g0 in range(0, GL, GB):
      pf, pb_ = prm[0], prm[1]
      S.barrier()
      with ExitStack() as es2:
        for nm in names:
            tmp[nm] = _sb(nc, es2, f"s5t_{nm}", [128, GB, TC, GC], F32)
        tA = bufs["tA"] = _sb(nc, es2, "s5_tA", [128, GB, GC], F32)
        tB = bufs["tB"] = _sb(nc, es2, "s5_tB", [128, GB, GC], F32)
        m1 = _sb(nc, es2, "s5_m1", [128, 128], F32)
        m2 = _sb(nc, es2, "s5_m2", [128, 128], F32)
        for j in range(TC):
            gen(tmp["WTf"], "s5t_WTf", j, bb1[0], bb2[0], pf["PR"][7 - j], pf["sgPI"][7 - j], g0)
            gen(tmp["Af"], "s5t_Af", j, bb1[0], bb2[0], pf["NR"][j], pf["sgNI"][j], g0)
            gen(tmp["WTb"], "s5t_WTb", j, bb1[1], bb2[1], pb_["PR"][j], pb_["sgPI"][j], g0)
        def genB(dst, dstn, tsl, d, srP, Pi):
            (s1, s1n), (s2, s2n), (a, an), (b, bn) = c1[d], c2[d], srP, Pi
            ab = a[:, g0:g0 + GB].unsqueeze(2).to_broadcast([128, GB, GC])
            bbb = b[:, g0:g0 + GB].unsqueeze(2).to_broadcast([128, GB, GC])
            V.tt(tA[:], s1[:, g0:g0 + GB, :], ab, ALU.mult, [s1n, an], ["s5_tA"])
            V.tt(tB[:], s2[:, g0:g0 + GB, :], bbb, ALU.mult, [s2n, bn], ["s5_tB"])
            V.tt(dst[:, :, tsl, :], tA[:], tB[:], ALU.subtract, ["s5_tA", "s5_tB"], [dstn])
        for t in range(TC):
            genB(tmp["Bf"], "s5t_Bf", t, 0, pf["srPR"][t], pf["PI"][t])
            genB(tmp["Vf"], "s5t_Vf", t, 0, pf["srPR"][t + 1], pf["PI"][t + 1])
            genB(tmp["Bb"], "s5t_Bb", t, 1, pb_["srNR"][t], pb_["NI"][t])
            genB(tmp["Vb"], "s5t_Vb", t, 1, pb_["srPR"][8 - t], pb_["PI"][8 - t])
        V.cp(Vfb[:], tmp["Vf"][:].rearrange("p g t c -> p g (t c)"), ["s5t_Vf"], ["s5_Vfb"])
        V.cp(Vbb[:], tmp["Vb"][:].rearrange("p g t c -> p g (t c)"), ["s5t_Vb"], ["s5_Vbb"])
        for d in range(2):
            p = prm[d]
            (ar, arn), (sgai, sgain), (srai, srain) = p["PR"][8], p["sgPI"][8], p["srPI8"]
            V.cp(AR2[d][:, 0, :], ar[:, g0:g0 + GB], [arn], [f"s5_AR2{d}"])
            V.cp(AR2[d][:, 1, :], ar[:, g0:g0 + GB], [arn], [f"s5_AR2{d}"])
            V.cp(AIx[d][:, 0, :], srai[:, g0:g0 + GB], [srain], [f"s5_AIx{d}"])
            V.cp(AIx[d][:, 1, :], sgai[:, g0:g0 + GB], [sgain], [f"s5_AIx{d}"])
        for gi in range(GB):
            g = g0 + gi
            for (src, srcn, WA, WAn, WB, WBn, bank) in [("WTf", "s5t_WTf", WfA, "s5_WfA", WfB, "s5_WfB", 0),
                                                       ("WTb", "s5t_WTb", WbA, "s5_WbA", WbB, "s5_WbB", 1)]:
                S.op("pe", lambda src=src, gi=gi, bank=bank: nc.tensor.transpose(
                    out=P[bank][:, 0:128], in_=tmp[src][:, gi].rearrange("p t c -> p (t c)"), identity=ident[:]),
                    reads=[srcn, "ident"], writes=[f"ps{bank}"])
                S.op("act", lambda WA=WA, gi=gi, bank=bank: nc.scalar.copy(out=WA[:, gi, :], in_=P[bank][:, 0:128]),
                     reads=[f"ps{bank}"], writes=[WAn])
                V.cp(WB[:, gi, 0:64], P[bank][:, 64:128], [f"ps{bank}"], [WBn])
                V.cp(WB[:, gi, 64:128], P[bank][:, 0:64], [f"ps{bank}"], [WBn])
            S.op("pe", lambda gi=gi: nc.tensor.matmul(
                out=P[2][:, 0:128], lhsT=tmp["Af"][:, gi].rearrange("p t c -> p (t c)"),
                rhs=tmp["Bf"][:, gi].rearrange("p t c -> p (t c)"), start=True, stop=True),
                reads=["s5t_Af", "s5t_Bf"], writes=["ps2"])
            S.op("pe", lambda gi=gi: nc.tensor.matmul(
                out=P[3][:, 0:128], lhsT=tmp["WTb"][:, gi].rearrange("p t c -> p (t c)"),
                rhs=tmp["Bb"][:, gi].rearrange("p t c -> p (t c)"), start=True, stop=True),
                reads=["s5t_WTb", "s5t_Bb"], writes=["ps3"])
            V.tt(m1[:], P[2][:, 0:128], maskf[:], ALU.mult, ["ps2", "s5_mf"], ["s5_m1"])
            V.tt(m2[:], P[3][:, 0:128], maskb[:], ALU.mult, ["ps3", "s5_mb"], ["s5_m2"])
            V.tt(m1[:], m1[:], m2[:], ALU.add, ["s5_m1", "s5_m2"], ["s5_m1"])
            V.stt(Mb[:, gi, :], ident[:], dcol[:, g:g + 1], m1[:], ALU.mult, ALU.add,
                  ["ident", "s5_dcol", "s5_m1"], ["s5_Mb"])
        S.barrier()
      with ExitStack() as es3:
        Ub = _sb(nc, es3, "s5_Ub", [128, GB, NK], BF16)
        St = [_sb(nc, es3, f"s5_St{d}", [128, NK + 1, 2, GB], BF16) for d in range(2)]
        Ysb = [_sb(nc, es3, f"s5_Y{i}", [128, TC, GB * GC], F32) for i in range(2)]
        S.dma("pool", Ub[:], din["u"][g0:g0 + GB].rearrange("g p k -> p g k"), writes=["s5_Ub"])
        for d in range(2):
            S.op("pool", lambda d=d: nc.gpsimd.memset(St[d][:, (0 if d == 0 else NK), :, :], 0.0),
                 writes=[f"s5_St{d}"])
            S.op("pool", lambda d=d: nc.gpsimd.memset(X[d][:], 0.0), writes=[f"s5_X{d}"])
        for gi in range(GB):
            for vi, (W, Wn, d, ab) in enumerate([(WfA, "s5_WfA", 0, 0), (WfB, "s5_WfB", 0, 1),
                                                 (WbA, "s5_WbA", 1, 0), (WbB, "s5_WbB", 1, 1)]):
                bank = 4 + vi
                S.op("pe", lambda W=W, gi=gi, bank=bank: nc.tensor.matmul(
                    out=P[bank][:, 0:NK], lhsT=W[:, gi, :], rhs=Ub[:, gi, :], start=True, stop=True),
                    reads=[Wn, "s5_Ub"], writes=[f"ps{bank}"])
                off = 1 if d == 0 else 0
                eng = "act" if vi % 2 else "dve"
                if eng == "act":
                    S.op("act", lambda d=d, ab=ab, gi=gi, bank=bank, off=off: nc.scalar.copy(
                        out=St[d][:, off:off + NK, ab, gi], in_=P[bank][:, 0:NK]),
                        reads=[f"ps{bank}"], writes=[f"s5_St{d}"])
                else:
                    V.cp(St[d][:, off:off + NK, ab, gi], P[bank][:, 0:NK], [f"ps{bank}"], [f"s5_St{d}"])
        for k in range(NK):
            for d, e in ((0, "dve"), (1, "pool")):
                E = nc.vector if e == "dve" else nc.gpsimd
                slot = k + 1 if d == 0 else NK - 1 - k
                Xn, q1n, q2n, Sn = f"s5_X{d}", f"s5_q1{d}", f"s5_q2{d}", f"s5_St{d}"
                S.op(e, lambda E=E, d=d: E.tensor_tensor(out=q1[d][:], in0=X[d][:], in1=AR2[d][:], op=ALU.mult),
                     reads=[Xn, f"s5_AR2{d}"], writes=[q1n])
                S.op(e, lambda E=E, d=d: E.tensor_tensor(out=q2[d][:], in0=X[d][:], in1=AIx[d][:], op=ALU.mult),
                     reads=[Xn, f"s5_AIx{d}"], writes=[q2n])
                S.op(e, lambda E=E, d=d, slot=slot: E.tensor_tensor(out=q1[d][:], in0=q1[d][:],
                                                                    in1=St[d][:, slot, :, :], op=ALU.add),
                     reads=[q1n, Sn], writes=[q1n])
                S.op(e, lambda E=E, d=d: E.tensor_tensor(out=X[d][:, 0, :], in0=q1[d][:, 0, :], in1=q2[d][:, 1, :],
                                                         op=ALU.add), reads=[q1n, q2n], writes=[Xn])
                S.op(e, lambda E=E, d=d: E.tensor_tensor(out=X[d][:, 1, :], in0=q1[d][:, 1, :], in1=q2[d][:, 0, :],
                                                         op=ALU.add), reads=[q1n, q2n], writes=[Xn])
                S.op(e, lambda E=E, d=d, slot=slot: E.tensor_copy(out=St[d][:, slot, :, :], in_=X[d][:]),
                     reads=[Xn], writes=[Sn])
        for kb in range(NKB):
            Yt, Yn = Ysb[kb % 2], f"s5_Y{kb % 2}"
            for gi in range(GB):
                bank = gi % 4
                ysl = P[bank][:, 0:128]
                S.op("pe", lambda gi=gi, kb=kb, ysl=ysl: nc.tensor.matmul(
                    out=ysl, lhsT=Ub[:, gi, kb * 128:(kb + 1) * 128], rhs=Mb[:, gi, :], start=True, stop=False),
                    reads=["s5_Ub", "s5_Mb"], writes=[f"ps{bank}"], signal=False)
                S.op("pe", lambda gi=gi, kb=kb, ysl=ysl: nc.tensor.matmul(
                    out=ysl, lhsT=St[0][:, kb * 128:(kb + 1) * 128, 0, gi], rhs=Vfb[:, gi, :], start=False,
                    stop=False), reads=["s5_St0", "s5_Vfb"], writes=[f"ps{bank}"], signal=False)
                S.op("pe", lambda gi=gi, kb=kb, ysl=ysl: nc.tensor.matmul(
                    out=ysl, lhsT=St[1][:, kb * 128 + 1:(kb + 1) * 128 + 1, 0, gi], rhs=Vbb[:, gi, :], start=False,
                    stop=True), reads=["s5_St1", "s5_Vbb"], writes=[f"ps{bank}"], signal=True)
                src = ysl.rearrange("p (t c) -> p t c", t=TC)
                if gi % 2:
                    S.op("act", lambda Yt=Yt, gi=gi, src=src: nc.scalar.copy(
                        out=Yt[:, :, gi * GC:(gi + 1) * GC], in_=src), reads=[f"ps{bank}"], writes=[Yn])
                else:
                    V.cp(Yt[:, :, gi * GC:(gi + 1) * GC], src, [f"ps{bank}"], [Yn])
            S.dma("sp", y_out[kb * 128:(kb + 1) * 128, :, g0 * GC:(g0 + GB) * GC], Yt[:], reads=[Yn])
      S.barrier()


def build_s5(GL, NK):
    nc = bass.Bass("TRN2", target_bir_lowering=False)
    def din_t(name, shape):
        return nc.dram_tensor(name, shape, F32, kind="ExternalInput").ap()
    din = {
        "u": din_t("u", [GL, 128, NK]),
        "lam_re": din_t("lam_re", [2, 128, GL]), "lam_im": din_t("lam_im", [2, 128, GL]),
        "logdt": din_t("logdt", [2, 128, GL]),
        "b1": din_t("b1", [2, 128, GL, GC]), "b2": din_t("b2", [2, 128, GL, GC]),
        "c1": din_t("c1", [2, 128, GL, GC]), "c2": din_t("c2", [2, 128, GL, GC]),
        "sg": din_t("sg", [128, 1]), "dcol": din_t("dcol", [128, GL]),
        "maskf": din_t("maskf", [128, 128]), "maskb": din_t("maskb", [128, 128]),
    }
    idn = din_t("idn", [128, 128])
    y = nc.dram_tensor("y", [NK, TC, GL * GC], F32, kind="ExternalOutput").ap()
    with ExitStack() as es:
        es.enter_context(nc.allow_low_precision("bf16 matmul operands, fp32 accumulation"))
        es.enter_context(nc.allow_non_contiguous_dma("small strided loads"))
        S = Sched(nc, es)
        P = [es.enter_context(nc.psum_tensor(f"ps{i}", [128, 512], F32)) for i in range(8)]
        ident = _sb(nc, es, "ident", [128, 128], F32)
        S.dma("sp", ident[:], idn, writes=["ident"])
        GH = GL // 2
        for h in range(2):
            gs = slice(h * GH, (h + 1) * GH)
            dsub = {"u": din["u"][gs], "sg": din["sg"], "dcol": din["dcol"][:, gs],
                    "maskf": din["maskf"], "maskb": din["maskb"]}
            for k in ("lam_re", "lam_im", "logdt"):
                dsub[k] = [din[k][d][:, gs] for d in range(2)]
            for k in ("b1", "b2", "c1", "c2"):
                dsub[k] = [din[k][d][:, gs, :] for d in range(2)]
            with ExitStack() as esh:
                emit_s5(nc, S, esh, P, ident, dsub, y[:, :, h * GH * GC:(h + 1) * GH * GC], GH, NK)
            S.barrier()
        S.finish()
    return nc


def s5_host_inputs(xb, G0, GL, lam_re, lam_im, log_dt, b_re, b_im, c_re, c_im, d_skip):
    L = xb.shape[0]
    NK = L // TC
    u = xb.reshape(NK, TC, D // GC, GC)[:, :, G0:G0 + GL, :]
    u = np.ascontiguousarray(u.transpose(2, 1, 3, 0).reshape(GL, TC * GC, NK))
    def pg(a):
        a = a[:, G0:G0 + GL, :].transpose(0, 2, 1)
        return np.ascontiguousarray(np.concatenate([a, a], axis=1))
    lr, li = pg(lam_re), pg(lam_im)
    ldt = np.ascontiguousarray(np.broadcast_to(log_dt[:, None, G0:G0 + GL], (2, 128, GL)))
    br = b_re[:, G0:G0 + GL].transpose(0, 2, 1, 3)
    bi = b_im[:, G0:G0 + GL].transpose(0, 2, 1, 3)
    cr = c_re[:, G0:G0 + GL].transpose(0, 3, 1, 2)
    ci = c_im[:, G0:G0 + GL].transpose(0, 3, 1, 2)
    st = lambda a, b: np.ascontiguousarray(np.concatenate([a, b], axis=1))
    sg = np.concatenate([-np.ones((64, 1), np.float32), np.ones((64, 1), np.float32)], 0)
    dcol = np.ascontiguousarray(np.tile(d_skip.reshape(D // GC, GC)[G0:G0 + GL].T, (TC, 1)))
    tt = np.arange(128) // GC
    maskf = (tt[None, :] >= tt[:, None]).astype(np.float32)
    maskb = (tt[None, :] <= tt[:, None]).astype(np.float32)
    return {"u": u, "lam_re": lr, "lam_im": li, "logdt": ldt, "b1": st(br, bi), "b2": st(bi, br),
            "c1": st(cr, ci), "c2": st(ci, cr), "sg": sg, "dcol": dcol, "maskf": maskf, "maskb": maskb,
            "idn": np.eye(128, dtype=np.float32)}


def run_s5(x_tok, lam_re, lam_im, log_dt, b_re, b_im, c_re, c_im, d_skip, B, L):
    GL = (D // GC) // 2
    NK = L // TC
    nc = build_s5(GL, NK)
    in_maps = []
    for c in range(NCORES):
        b, h = c // 2, c % 2
        in_maps.append(s5_host_inputs(x_tok[b * L:(b + 1) * L], h * GL, GL, lam_re, lam_im, log_dt,
                                      b_re, b_im, c_re, c_im, d_skip))
    res = run_bass_kernel_spmd(nc, in_maps, core_ids=list(range(NCORES)))
    y = np.empty((B * L, D), np.float32)
    for c in range(NCORES):
        b, h = c // 2, c % 2
        y[b * L:(b + 1) * L, h * GL * GC:(h + 1) * GL * GC] = res.results[c]["y"].reshape(L, GL * GC)
    return y


GELU_C = 0.044715
GELU_S = 2.0 * 0.7978845608028654


def emit_glu(nc, S, es, P, x_ap, ys_ap, y_ap, wv, wgt, g_rep, b_rep, ident, NT):
    Wv = _sb(nc, es, "gl_Wv", [128, NDC, D], BF16)
    Wg = _sb(nc, es, "gl_Wg", [128, NDC, D], BF16)
    wv_v = wv.rearrange("(c p) f -> p c f", p=128)
    wg_v = wgt.rearrange("(c p) f -> p c f", p=128)
    for q in range(4):
        S.dma("pool", Wv[:, :, q * 512:(q + 1) * 512], wv_v[:, :, q * 512:(q + 1) * 512], writes=["gl_Wv"])
        S.dma("pool", Wg[:, :, q * 512:(q + 1) * 512], wg_v[:, :, q * 512:(q + 1) * 512], writes=["gl_Wg"])
    yt = [_sb(nc, es, f"gl_y{i}", [128, D], F32) for i in range(2)]
    t1 = _sb(nc, es, "gl_t1", [128, D], F32)
    xt = [_sb(nc, es, f"gl_x{i}", [128, D], F32) for i in range(2)]
    gT = _sb(nc, es, "gl_gT", [128, NDC, 128], BF16)
    sgm = [_sb(nc, es, f"gl_sg{i}", [128, 512], F32) for i in range(2)]
    st = _sb(nc, es, "gl_st", [128, 4, nc.vector.BN_STATS_DIM], F32)
    mv = _sb(nc, es, "gl_mv", [128, nc.vector.BN_AGGR_DIM], F32)
    rs = _sb(nc, es, "gl_rs", [128, 1], F32)
    V = _V(nc, S, "dve")
    for t in range(NT // 128):
        y, yn = yt[t % 2], f"gl_y{t % 2}"
        x, xn = xt[t % 2], f"gl_x{t % 2}"
        S.dma("sp", y[:], ys_ap[t * 128:(t + 1) * 128, :], writes=[yn])
        S.dma("sp", x[:], x_ap[t * 128:(t + 1) * 128, :], writes=[xn])
        V.tt(t1[:], y[:], y[:], ALU.mult, [yn], ["gl_t1"])
        V.ts(t1[:], t1[:], GELU_C, ALU.mult, ["gl_t1"], ["gl_t1"], s2=1.0, op1=ALU.add)
        V.tt(t1[:], t1[:], y[:], ALU.mult, ["gl_t1", yn], ["gl_t1"])
        S.op("act", lambda: nc.scalar.activation(out=t1[:], in_=t1[:], func=AF.Sigmoid, scale=GELU_S),
             reads=["gl_t1"], writes=["gl_t1"])
        V.tt(y[:], y[:], t1[:], ALU.mult, [yn, "gl_t1"], [yn])
        for cq in range(NDC // 4):
            bank = 4 + (cq % 4)
            for k in range(4):
                c = cq * 4 + k
                S.op("pe", lambda c=c, k=k, bank=bank, y=y: nc.tensor.transpose(
                    out=P[bank][:, k * 128:(k + 1) * 128], in_=y[:, c * 128:(c + 1) * 128], identity=ident[:]),
                    reads=[yn, "ident"], writes=[f"ps{bank}"], signal=(k == 3))
            S.op("act", lambda cq=cq, bank=bank: nc.scalar.copy(
                out=gT[:, cq * 4:(cq + 1) * 4, :], in_=P[bank][:].rearrange("p (k n) -> p k n", k=4)),
                reads=[f"ps{bank}"], writes=["gl_gT"])
        for n in range(4):
            pv, pg = (n % 2) * 2, (n % 2) * 2 + 1
            for (W, Wn, bank) in ((Wv, "gl_Wv", pv), (Wg, "gl_Wg", pg)):
                for c in range(NDC):
                    S.op("pe", lambda W=W, c=c, n=n, bank=bank: nc.tensor.matmul(
                        out=P[bank][:], lhsT=gT[:, c, :], rhs=W[:, c, n * 512:(n + 1) * 512],
                        start=(c == 0), stop=(c == NDC - 1)),
                        reads=["gl_gT", Wn], writes=[f"ps{bank}"], signal=(c == NDC - 1))
            sg_, sgn = sgm[n % 2], f"gl_sg{n % 2}"
            S.op("act", lambda sg_=sg_, pg=pg: nc.scalar.activation(out=sg_[:], in_=P[pg][:], func=AF.Sigmoid),
                 reads=[f"ps{pg}"], writes=[sgn])
            V.tt(t1[:, n * 512:(n + 1) * 512], P[pv][:], sg_[:], ALU.mult, [f"ps{pv}", sgn], ["gl_t1"])
        V.stt(t1[:], x[:], ALPHA, t1[:], ALU.mult, ALU.add, [xn, "gl_t1"], ["gl_t1"])
        emit_ln(nc, S, t1, "gl_t1", st, mv, rs, "gl_ln", g_rep, b_rep)
        S.dma("sp", y_ap[t * 128:(t + 1) * 128, :], t1[:], reads=["gl_t1"])


def _std_prog(NT_in, NT_out, extra, body):
    nc = bass.Bass("TRN2", target_bir_lowering=False)
    x = nc.dram_tensor("x", [NT_in, D], F32, kind="ExternalInput").ap()
    lng = nc.dram_tensor("lng", [128, D], F32, kind="ExternalInput").ap()
    lnb = nc.dram_tensor("lnb", [128, D], F32, kind="ExternalInput").ap()
    idn = nc.dram_tensor("idn", [128, 128], F32, kind="ExternalInput").ap()
    ex = {k: nc.dram_tensor(k, shp, F32, kind="ExternalInput").ap() for k, shp in extra.items()}
    y = nc.dram_tensor("y", [NT_out, D], F32, kind="ExternalOutput").ap()
    with ExitStack() as es:
        es.enter_context(nc.allow_low_precision("bf16 matmul operands, fp32 accumulation"))
        es.enter_context(nc.allow_non_contiguous_dma("weight block loads"))
        S = Sched(nc, es)
        P = [es.enter_context(nc.psum_tensor(f"ps{i}", [128, 512], F32)) for i in range(8)]
        ident = _sb(nc, es, "ident", [128, 128], F32)
        g_rep = _sb(nc, es, "g_rep", [128, D], F32)
        b_rep = _sb(nc, es, "b_rep", [128, D], F32)
        S.dma("sp", ident[:], idn, writes=["ident"])
        S.dma("sp", g_rep[:], lng, writes=["lng"])
        S.dma("sp", b_rep[:], lnb, writes=["lnb"])
        body(nc, S, es, P, x, y, ex, g_rep, b_rep, ident)
        S.finish()
    return nc


def run_glu(x_tok, ys_tok, wv, wg, g, b):
    NT = x_tok.shape[0] // NCORES
    nc = _std_prog(NT, NT, {"ys": [NT, D], "wv": [D, D], "wg": [D, D]},
                   lambda nc, S, es, P, x, y, ex, gr, br, idt: emit_glu(nc, S, es, P, x, ex["ys"], y, ex["wv"],
                                                                       ex["wg"], gr, br, idt, NT))
    common = {"wv": np.ascontiguousarray(wv), "wg": np.ascontiguousarray(wg), "lng": _rep(g), "lnb": _rep(b),
              "idn": np.eye(128, dtype=np.float32)}
    in_maps = [dict(common, x=np.ascontiguousarray(x_tok[c * NT:(c + 1) * NT]),
                    ys=np.ascontiguousarray(ys_tok[c * NT:(c + 1) * NT])) for c in range(NCORES)]
    res = run_bass_kernel_spmd(nc, in_maps, core_ids=list(range(NCORES)))
    return np.concatenate([r["y"] for r in res.results], axis=0)


GW = 64
NH = 16
HD = 128
NR = 36
NEG = -30000.0


def emit_natten(nc, S, es, P, x_ap, y_ap, wqkv, wo, tab, g_rep, b_rep, ident):
    NTK = NR * GW
    V = _V(nc, S, "dve")
    at_dram = nc.dram_tensor("na_attn_scratch", [NH, 128, NTK], BF16, kind="Internal").ap()
    ones = _sb(nc, es, "na_ones", [128, 128], BF16)
    S.op("dve", lambda: nc.vector.memset(ones[:], 1.0), writes=["na_ones"])
    wq_v = wqkv.rearrange("(c p) f -> p c f", p=128)
    with ExitStack() as es2:
        xT = _sb(nc, es2, "na_xT", [128, NDC, NTK], BF16)
        xs = [_sb(nc, es2, f"na_xs{i}", [128, D], F32) for i in range(2)]
        for t in range(NTK // 128):
            xb, rx = xs[t % 2], f"na_xs{t % 2}"
            S.dma("sp", xb[:], x_ap[t * 128:(t + 1) * 128, :], writes=[rx])
            for cq in range(NDC // 4):
                bank = 4 + (cq % 4)
                for k in range(4):
                    c = cq * 4 + k
                    S.op("pe", lambda c=c, k=k, bank=bank, xb=xb: nc.tensor.transpose(
                        out=P[bank][:, k * 128:(k + 1) * 128], in_=xb[:, c * 128:(c + 1) * 128],
                        identity=ident[:]), reads=[rx, "ident"], writes=[f"ps{bank}"], signal=(k == 3))
                if cq % 2:
                    S.op("act", lambda cq=cq, bank=bank, t=t: nc.scalar.copy(
                        out=xT[:, cq * 4:(cq + 1) * 4, t * 128:(t + 1) * 128],
                        in_=P[bank][:].rearrange("p (k n) -> p k n", k=4)), reads=[f"ps{bank}"], writes=["na_xT"])
                else:
                    V.cp(xT[:, cq * 4:(cq + 1) * 4, t * 128:(t + 1) * 128],
                         P[bank][:].rearrange("p (k n) -> p k n", k=4), [f"ps{bank}"], ["na_xT"])
        wb = [[_sb(nc, es2, f"na_w{j}{i}", [128, NDC, HD], BF16) for j in range(3)] for i in range(2)]
        tb = [_sb(nc, es2, f"na_tab{i}", [GW, 8, 8, GW], F32) for i in range(2)]
        qT = _sb(nc, es2, "na_qT", [128, NTK], BF16)
        kT = _sb(nc, es2, "na_kT", [128, NTK], BF16)
        v2 = _sb(nc, es2, "na_v2", [GW, NR, HD], BF16)
        E = [_sb(nc, es2, f"na_E{i}", [GW, 8, GW], F32) for i in range(2)]
        Eb = [_sb(nc, es2, f"na_Eb{i}", [GW, 8, GW], BF16) for i in range(2)]
        rc = [_sb(nc, es2, f"na_rc{i}", [128, GW], F32) for i in range(2)]
        ath = [_sb(nc, es2, f"na_ath{i}", [128, NTK], BF16) for i in range(2)]
        tblocks = [(i * 512, min(512, NTK - i * 512)) for i in range((NTK + 511) // 512)]
        for h in range(NH):
            w3, wn = wb[h % 2], [f"na_w{j}{h % 2}" for j in range(3)]
            for j in range(3):
                S.dma("pool", w3[j][:], wq_v[:, :, j * D + h * HD:j * D + (h + 1) * HD], writes=[wn[j]])
            tbt, tbn = tb[h % 2], f"na_tab{h % 2}"
            S.dma("sp", tbt[:], tab[h], writes=[tbn])
            for j, (dst, dstn) in enumerate(((qT, "na_qT"), (kT, "na_kT"))):
                for bi, (t0, tw) in enumerate(tblocks):
                    bank = (j * len(tblocks) + bi) % 2
                    for c in range(NDC):
                        S.op("pe", lambda j=j, c=c, t0=t0, tw=tw, bank=bank, w3=w3: nc.tensor.matmul(
                            out=P[bank][:, 0:tw], lhsT=w3[j][:, c, :], rhs=xT[:, c, t0:t0 + tw],
                            start=(c == 0), stop=(c == NDC - 1)),
                            reads=[wn[j], "na_xT"], writes=[f"ps{bank}"], signal=(c == NDC - 1))
                    S.op("act", lambda dst=dst, t0=t0, tw=tw, bank=bank: nc.scalar.copy(
                        out=dst[:, t0:t0 + tw], in_=P[bank][:, 0:tw]), reads=[f"ps{bank}"], writes=[dstn])
            for r in range(NR):
                bank = 2 + (r % 2)
                for c in range(NDC):
                    S.op("pe", lambda r=r, c=c, bank=bank, w3=w3: nc.tensor.matmul(
                        out=P[bank][0:GW, 0:HD], lhsT=xT[:, c, r * GW:(r + 1) * GW],
                        rhs=w3[2][:, c, :], start=(c == 0), stop=(c == NDC - 1)),
                        reads=[wn[2], "na_xT"], writes=[f"ps{bank}"], signal=(c == NDC - 1))
                V.cp(v2[:, r, :], P[bank][0:GW, 0:HD], [f"ps{bank}"], ["na_v2"])
            for i in range(NR):
                rs_ = min(max(i - 4, 0), NR - 8)
                var = i - rs_
                e, en = E[i % 2], f"na_E{i % 2}"
                eb, ebn = Eb[i % 2], f"na_Eb{i % 2}"
                bs = 4 + (i % 2) * 2
                for kr in range(8):
                    S.op("pe", lambda kr=kr, i=i, rs_=rs_, bs=bs: nc.tensor.matmul(
                        out=P[bs][0:GW, kr * GW:(kr + 1) * GW], lhsT=kT[:, (rs_ + kr) * GW:(rs_ + kr + 1) * GW],
                        rhs=qT[:, i * GW:(i + 1) * GW], start=True, stop=True),
                        reads=["na_kT", "na_qT"], writes=[f"ps{bs}"], signal=(kr == 7))
                V.stt(e[:], P[bs][0:GW, :].rearrange("p (k q) -> p k q", k=8), float(HD) ** -0.5,
                      tbt[:, var, :, :], ALU.mult, ALU.add, [f"ps{bs}", tbn], [en])
                S.op("act", lambda e=e, eb=eb: nc.scalar.activation(out=eb[:], in_=e[:], func=AF.Exp),
                     reads=[en], writes=[ebn])
                for kr in range(8):
                    S.op("pe", lambda kr=kr, rs_=rs_, bs=bs, eb=eb: nc.tensor.matmul(
                        out=P[bs + 1][:, 0:GW], lhsT=v2[:, rs_ + kr, :], rhs=eb[:, kr, :],
                        start=(kr == 0), stop=(kr == 7)),
                        reads=["na_v2", ebn], writes=[f"ps{bs + 1}"], signal=False)
                for kr in range(8):
                    S.op("pe", lambda kr=kr, bs=bs, eb=eb: nc.tensor.matmul(
                        out=P[bs + 1][:, GW:2 * GW], lhsT=ones[0:GW, :], rhs=eb[:, kr, :], start=(kr == 0),
                        stop=(kr == 7)), reads=["na_ones", ebn], writes=[f"ps{bs + 1}"], signal=(kr == 7))
                rct, rcn = rc[i % 2], f"na_rc{i % 2}"
                S.op("dve", lambda rct=rct, bs=bs: nc.vector.reciprocal(out=rct[:], in_=P[bs + 1][:, GW:2 * GW]),
                     reads=[f"ps{bs + 1}"], writes=[rcn])
                V.tt(ath[h % 2][:, i * GW:(i + 1) * GW], P[bs + 1][:, 0:GW], rct[:], ALU.mult,
                     [f"ps{bs + 1}", rcn], [f"na_ath{h % 2}"])
            S.dma("sp", at_dram[h], ath[h % 2][:], reads=[f"na_ath{h % 2}"], writes=["na_at_dram"])
        S.barrier()
    attnT = _sb(nc, es, "na_attnT", [128, NH, NTK], BF16)
    S.dma("sp", attnT[:], at_dram.rearrange("h p n -> p h n"), reads=["na_at_dram"], writes=["na_attnT"])
    Wo = _sb(nc, es, "na_Wo", [128, NH, D], BF16)
    wo_v = wo.rearrange("(c p) f -> p c f", p=128)
    for q in range(4):
        S.dma("pool", Wo[:, :, q * 512:(q + 1) * 512], wo_v[:, :, q * 512:(q + 1) * 512], writes=["na_Wo"])
    xt = [_sb(nc, es, f"na_x{i}", [128, D], F32) for i in range(2)]
    zt = [_sb(nc, es, f"na_z{i}", [128, D], F32) for i in range(2)]
    st = _sb(nc, es, "na_st", [128, 4, nc.vector.BN_STATS_DIM], F32)
    mv = _sb(nc, es, "na_mv", [128, nc.vector.BN_AGGR_DIM], F32)
    rs = _sb(nc, es, "na_rs", [128, 1], F32)
    for t in range(NTK // 128):
        x, xn = xt[t % 2], f"na_x{t % 2}"
        z, zn = zt[t % 2], f"na_z{t % 2}"
        S.dma("sp", x[:], x_ap[t * 128:(t + 1) * 128, :], writes=[xn])
        for n in range(4):
            bank = n
            for hh in range(NH):
                S.op("pe", lambda hh=hh, n=n, t=t, bank=bank: nc.tensor.matmul(
                    out=P[bank][:], lhsT=attnT[:, hh, t * 128:(t + 1) * 128], rhs=Wo[:, hh, n * 512:(n + 1) * 512],
                    start=(hh == 0), stop=(hh == NH - 1)),
                    reads=["na_attnT", "na_Wo"], writes=[f"ps{bank}"], signal=(hh == NH - 1))
            V.stt(z[:, n * 512:(n + 1) * 512], x[:, n * 512:(n + 1) * 512], ALPHA, P[bank][:], ALU.mult, ALU.add,
                  [xn, f"ps{bank}"], [zn])
        emit_ln(nc, S, z, zn, st, mv, rs, "na_ln", g_rep, b_rep)
        S.dma("sp", y_ap[t * 128:(t + 1) * 128, :], z[:], reads=[zn])


def emit_natten_f(nc, S, es, P, x_ap, y_ap, wqkv, wo, tab, g_rep, b_rep, ident):
    NTK = NR * GW
    NQ = 32 * GW
    V = _V(nc, S, "dve")
    at_dram = nc.dram_tensor("naf_attn_scratch", [NH, 128, NQ], BF16, kind="Internal").ap()
    ones = _sb(nc, es, "na_ones", [128, 128], BF16)
    S.op("dve", lambda: nc.vector.memset(ones[:], 1.0), writes=["na_ones"])
    wq_v = wqkv.rearrange("(c p) f -> p c f", p=128)
    with ExitStack() as es2:
        xT = _sb(nc, es2, "na_xT", [128, NDC, NTK], BF16)
        xs = [_sb(nc, es2, f"na_xs{i}", [128, D], F32) for i in range(2)]
        for t in range(NTK // 128):
            xb, rx = xs[t % 2], f"na_xs{t % 2}"
            S.dma("sp", xb[:], x_ap[t * 128:(t + 1) * 128, :], writes=[rx])
            for cq in range(NDC // 4):
                bank = 4 + (cq % 4)
                for k in range(4):
                    c = cq * 4 + k
                    S.op("pe", lambda c=c, k=k, bank=bank, xb=xb: nc.tensor.transpose(
                        out=P[bank][:, k * 128:(k + 1) * 128], in_=xb[:, c * 128:(c + 1) * 128],
                        identity=ident[:]), reads=[rx, "ident"], writes=[f"ps{bank}"], signal=(k == 3))
                if cq % 2:
                    S.op("act", lambda cq=cq, bank=bank, t=t: nc.scalar.copy(
                        out=xT[:, cq * 4:(cq + 1) * 4, t * 128:(t + 1) * 128],
                        in_=P[bank][:].rearrange("p (k n) -> p k n", k=4)), reads=[f"ps{bank}"], writes=["na_xT"])
                else:
                    V.cp(xT[:, cq * 4:(cq + 1) * 4, t * 128:(t + 1) * 128],
                         P[bank][:].rearrange("p (k n) -> p k n", k=4), [f"ps{bank}"], ["na_xT"])
        wb = [[_sb(nc, es2, f"na_w{j}{i}", [128, NDC, HD], BF16) for j in range(3)] for i in range(2)]
        tb = [_sb(nc, es2, f"na_tab{i}", [GW, 5, 9, GW], F32) for i in range(2)]
        qT = _sb(nc, es2, "na_qT", [128, NTK], BF16)
        kT = _sb(nc, es2, "na_kT", [128, NTK], BF16)
        v2 = _sb(nc, es2, "na_v2", [GW, NR, HD], BF16)
        E = [_sb(nc, es2, f"na_E{i}", [GW, 9, GW], F32) for i in range(2)]
        Eb = [_sb(nc, es2, f"na_Eb{i}", [GW, 9, GW], BF16) for i in range(2)]
        rc = [_sb(nc, es2, f"na_rc{i}", [128, GW], F32) for i in range(2)]
        ath = [_sb(nc, es2, f"na_ath{i}", [128, NQ], BF16) for i in range(2)]
        tblocks = [(i * 512, min(512, NTK - i * 512)) for i in range((NTK + 511) // 512)]
        for h in range(NH):
            w3, wn = wb[h % 2], [f"na_w{j}{h % 2}" for j in range(3)]
            for j in range(3):
                S.dma("pool", w3[j][:], wq_v[:, :, j * D + h * HD:j * D + (h + 1) * HD], writes=[wn[j]])
            tbt, tbn = tb[h % 2], f"na_tab{h % 2}"
            S.dma("sp", tbt[:], tab[h], writes=[tbn])
            for j, (dst, dstn) in enumerate(((qT, "na_qT"), (kT, "na_kT"))):
                for bi, (t0, tw) in enumerate(tblocks):
                    bank = (j * len(tblocks) + bi) % 2
                    for c in range(NDC):
                        S.op("pe", lambda j=j, c=c, t0=t0, tw=tw, bank=bank, w3=w3: nc.tensor.matmul(
                            out=P[bank][:, 0:tw], lhsT=w3[j][:, c, :], rhs=xT[:, c, t0:t0 + tw],
                            start=(c == 0), stop=(c == NDC - 1)),
                            reads=[wn[j], "na_xT"], writes=[f"ps{bank}"], signal=(c == NDC - 1))
                    S.op("act", lambda dst=dst, t0=t0, tw=tw, bank=bank: nc.scalar.copy(
                        out=dst[:, t0:t0 + tw], in_=P[bank][:, 0:tw]), reads=[f"ps{bank}"], writes=[dstn])
            for r in range(NR):
                bank = 2 + (r % 2)
                for c in range(NDC):
                    S.op("pe", lambda r=r, c=c, bank=bank, w3=w3: nc.tensor.matmul(
                        out=P[bank][0:GW, 0:HD], lhsT=xT[:, c, r * GW:(r + 1) * GW],
                        rhs=w3[2][:, c, :], start=(c == 0), stop=(c == NDC - 1)),
                        reads=[wn[2], "na_xT"], writes=[f"ps{bank}"], signal=(c == NDC - 1))
                V.cp(v2[:, r, :], P[bank][0:GW, 0:HD], [f"ps{bank}"], ["na_v2"])
            for i in range(32):
                rows = list(range(9)) if i < 4 else list(range(i - 4, i + 5))
                var = i if i < 4 else 4
                e, en = E[i % 2], f"na_E{i % 2}"
                eb, ebn = Eb[i % 2], f"na_Eb{i % 2}"
                bs = 4 + (i % 2) * 2
                for kr in range(9):
                    dst = P[bs][0:GW, kr * GW:(kr + 1) * GW] if kr < 8 else P[bs + 1][0:GW, 2 * GW:3 * GW]
                    wr = f"ps{bs}" if kr < 8 else f"ps{bs + 1}"
                    S.op("pe", lambda kr=kr, i=i, dst=dst, rows=rows: nc.tensor.matmul(
                        out=dst, lhsT=kT[:, rows[kr] * GW:(rows[kr] + 1) * GW],
                        rhs=qT[:, i * GW:(i + 1) * GW], start=True, stop=True),
                        reads=["na_kT", "na_qT"], writes=[wr], signal=(kr >= 7))
                V.stt(e[:, 0:8, :], P[bs][0:GW, :].rearrange("p (k q) -> p k q", k=8), float(HD) ** -0.5,
                      tbt[:, var, 0:8, :], ALU.mult, ALU.add, [f"ps{bs}", tbn], [en])
                V.stt(e[:, 8, :], P[bs + 1][0:GW, 2 * GW:3 * GW], float(HD) ** -0.5,
                      tbt[:, var, 8, :], ALU.mult, ALU.add, [f"ps{bs + 1}", tbn], [en])
                S.op("act", lambda e=e, eb=eb: nc.scalar.activation(out=eb[:], in_=e[:], func=AF.Exp),
                     reads=[en], writes=[ebn])
                for kr in range(9):
                    S.op("pe", lambda kr=kr, rows=rows, bs=bs, eb=eb: nc.tensor.matmul(
                        out=P[bs + 1][:, 0:GW], lhsT=v2[:, rows[kr], :], rhs=eb[:, kr, :],
                        start=(kr == 0), stop=(kr == 8)),
                        reads=["na_v2", ebn], writes=[f"ps{bs + 1}"], signal=False)
                for kr in range(9):
                    S.op("pe", lambda kr=kr, bs=bs, eb=eb: nc.tensor.matmul(
                        out=P[bs + 1][:, GW:2 * GW], lhsT=ones[0:GW, :], rhs=eb[:, kr, :], start=(kr == 0),
                        stop=(kr == 8)), reads=["na_ones", ebn], writes=[f"ps{bs + 1}"], signal=(kr == 8))
                rct, rcn = rc[i % 2], f"na_rc{i % 2}"
                S.op("dve", lambda rct=rct, bs=bs: nc.vector.reciprocal(out=rct[:], in_=P[bs + 1][:, GW:2 * GW]),
                     reads=[f"ps{bs + 1}"], writes=[rcn])
                V.tt(ath[h % 2][:, i * GW:(i + 1) * GW], P[bs + 1][:, 0:GW], rct[:], ALU.mult,
                     [f"ps{bs + 1}", rcn], [f"na_ath{h % 2}"])
            S.dma("sp", at_dram[h], ath[h % 2][:], reads=[f"na_ath{h % 2}"], writes=["na_at_dram"])
        S.barrier()
    attnT = _sb(nc, es, "na_attnT", [128, NH, NQ], BF16)
    S.dma("sp", attnT[:], at_dram.rearrange("h p n -> p h n"), reads=["na_at_dram"], writes=["na_attnT"])
    Wo = _sb(nc, es, "na_Wo", [128, NH, D], BF16)
    wo_v = wo.rearrange("(c p) f -> p c f", p=128)
    for q in range(4):
        S.dma("pool", Wo[:, :, q * 512:(q + 1) * 512], wo_v[:, :, q * 512:(q + 1) * 512], writes=["na_Wo"])
    xt = [_sb(nc, es, f"na_x{i}", [128, D], F32) for i in range(2)]
    zt = [_sb(nc, es, f"na_z{i}", [128, D], F32) for i in range(2)]
    st = _sb(nc, es, "na_st", [128, 4, nc.vector.BN_STATS_DIM], F32)
    mv = _sb(nc, es, "na_mv", [128, nc.vector.BN_AGGR_DIM], F32)
    rs = _sb(nc, es, "na_rs", [128, 1], F32)
    for t in range(NQ // 128):
        x, xn = xt[t % 2], f"na_x{t % 2}"
        z, zn = zt[t % 2], f"na_z{t % 2}"
        S.dma("sp", x[:], x_ap[t * 128:(t + 1) * 128, :], writes=[xn])
        for n in range(4):
            bank = n
            for hh in range(NH):
                S.op("pe", lambda hh=hh, n=n, t=t, bank=bank: nc.tensor.matmul(
                    out=P[bank][:], lhsT=attnT[:, hh, t * 128:(t + 1) * 128], rhs=Wo[:, hh, n * 512:(n + 1) * 512],
                    start=(hh == 0), stop=(hh == NH - 1)),
                    reads=["na_attnT", "na_Wo"], writes=[f"ps{bank}"], signal=(hh == NH - 1))
            V.stt(z[:, n * 512:(n + 1) * 512], x[:, n * 512:(n + 1) * 512], ALPHA, P[bank][:], ALU.mult, ALU.add,
                  [xn, f"ps{bank}"], [zn])
        emit_ln(nc, S, z, zn, st, mv, rs, "na_ln", g_rep, b_rep)
        S.dma("sp", y_ap[t * 128:(t + 1) * 128, :], z[:], reads=[zn])


def natten_tables(rpb):
    H = rpb.shape[0]
    qc = np.arange(GW)
    cs = np.clip(qc - 8, 0, GW - 16)
    kc = np.arange(GW)
    inwin = (kc[:, None] >= cs[None, :]) & (kc[:, None] < cs[None, :] + 16)
    coff = np.clip(kc[:, None] - qc[None, :] + 15, 0, 30)
    tab = np.full((H, 8, 8, GW, GW), NEG, np.float32)
    for var in range(8):
        for kr in range(8):
            ro = kr - var + 7
            vals = rpb[:, ro][:, coff]
            tab[:, var, kr] = np.where(inwin[None], vals, np.float32(NEG))
    return np.ascontiguousarray(tab.transpose(0, 3, 1, 2, 4))


def run_natten(x_tok, wqkv, rpb, wo, g, b, B, L):
    NTK = NR * GW
    nc = _std_prog(NTK, NTK, {"wqkv": [D, 3 * D], "wo": [D, D], "tab": [NH, GW, 8, 8, GW]},
                   lambda nc, S, es, P, x, y, ex, gr, br, idt: emit_natten(nc, S, es, P, x, y, ex["wqkv"],
                                                                          ex["wo"], ex["tab"], gr, br, idt))
    common = {"wqkv": np.ascontiguousarray(wqkv), "wo": np.ascontiguousarray(wo), "tab": natten_tables(rpb),
              "lng": _rep(g), "lnb": _rep(b), "idn": np.eye(128, dtype=np.float32)}
    half = L // 2
    in_maps = []
    for c in range(NCORES):
        bq, hf = c // 2, c % 2
        t0 = bq * L + (0 if hf == 0 else half - 4 * GW)
        in_maps.append(dict(common, x=np.ascontiguousarray(x_tok[t0:t0 + NTK])))
    res = run_bass_kernel_spmd(nc, in_maps, core_ids=list(range(NCORES)))
    out = np.empty_like(x_tok)
    for c in range(NCORES):
        bq, hf = c // 2, c % 2
        yc = res.results[c]["y"]
        if hf == 0:
            out[bq * L:bq * L + half] = yc[:half]
        else:
            out[bq * L + half:(bq + 1) * L] = yc[4 * GW:]
    return out


def kernel_unfused(x, ffn_w_gate, ffn_w_up, ffn_w_down, ln_g, ln_b, s5_lam_re, s5_lam_im, s5_log_dt, s5_b_re, s5_b_im,
           s5_c_re, s5_c_im, s5_d, s5_w_glu_val, s5_w_glu_gate, na_w_qkv, na_rpb, na_w_out):
    f = lambda a: np.asarray(a, np.float32)
    B, L, _ = x.shape
    h = f(x).reshape(B * L, D)
    for i in range(DEPTH):
        h = run_ffn(h, f(ffn_w_gate[i, 0]), f(ffn_w_up[i, 0]), f(ffn_w_down[i, 0]), f(ln_g[i, 0]), f(ln_b[i, 0]))
        j = i // 2
        if i % 2 == 0:
            ys = run_s5(h, f(s5_lam_re[j]), f(s5_lam_im[j]), f(s5_log_dt[j]), f(s5_b_re[j]), f(s5_b_im[j]),
                        f(s5_c_re[j]), f(s5_c_im[j]), f(s5_d[j]), B, L)
            h = run_glu(h, ys, f(s5_w_glu_val[j]), f(s5_w_glu_gate[j]), f(ln_g[i, 1]), f(ln_b[i, 1]))
        else:
            h = run_natten(h, f(na_w_qkv[j]), f(na_rpb[j]), f(na_w_out[j]), f(ln_g[i, 1]), f(ln_b[i, 1]), B, L)
        h = run_ffn(h, f(ffn_w_gate[i, 1]), f(ffn_w_up[i, 1]), f(ffn_w_down[i, 1]), f(ln_g[i, 2]), f(ln_b[i, 2]))
    return h.reshape(B, L, D)


T2 = 16
NG = D // GC


def emit_s5f_ops(nc, S, P, ident, din, opW, opV, opM, h, GH=64, GB=8):
    V = _V(nc, S, "dve")
    VP = _V(nc, S, "pool")
    gsl = slice(h * GH, (h + 1) * GH)
    with ExitStack() as es:
        sg = _sb(nc, es, "o_sg", [128, 1], F32)
        sr = _sb(nc, es, "o_sr", [128, 1], F32)
        S.dma("sp", sg[:], din["sg"], writes=["o_sg"])
        V.ts(sr[:], sg[:], -1.0, ALU.mult, ["o_sg"], ["o_sr"])
        mf = _sb(nc, es, "o_mf", [128, 128], F32)
        mb = _sb(nc, es, "o_mb", [128, 128], F32)
        dcol = _sb(nc, es, "o_dcol", [128, GH], F32)
        S.dma("sp", mf[:], din["maskf"], writes=["o_mf"])
        S.dma("sp", mb[:], din["maskb"], writes=["o_mb"])
        S.dma("sp", dcol[:], din["dcol"][:, gsl], writes=["o_dcol"])
        stk = {}
        for d in range(2):
            for nm, n in (("PA", T2 + 1), ("PD", T2), ("NA", T2)):
                for ri in "ri":
                    stk[(d, nm, ri)] = (_sb(nc, es, f"o_{nm}{ri}{d}", [128, GH, n], F32), f"o_{nm}{ri}{d}")
        bb1, bb2s, c1s, c2 = [], [], [], []
        keep = []
        for d in range(2):
            b1 = _sb(nc, es, f"o_b1{d}", [128, GH, GC], F32)
            b2 = _sb(nc, es, f"o_b2{d}", [128, GH, GC], F32)
            o1 = _sb(nc, es, f"o_bb1{d}", [128, GH, GC], F32)
            o2 = _sb(nc, es, f"o_bb2{d}", [128, GH, GC], F32)
            keep.append((b1, b2, o1, o2))
        with ExitStack() as esp:
            tmpA = _sb(nc, esp, "o_tmpA", [128, GH, GC], F32)
            tmpB = _sb(nc, esp, "o_tmpB", [128, GH, GC], F32)

            def bc(t, n=GH):
                return t[:].unsqueeze(2).to_broadcast([128, n, GC])

            for d in range(2):
                p = emit_s5_params(nc, S, esp, din["lam_re"][d][:, gsl], din["lam_im"][d][:, gsl],
                                   din["logdt"][d][:, gsl], sg, GH, d, npw=T2)
                for j in range(T2 + 1):
                    for ri, key in (("r", "PR"), ("i", "PI")):
                        t, tn = p[key][j]
                        a, an = stk[(d, "PA", ri)]
                        V.cp(a[:, :, j], t[:], [tn], [an])
                        jd = (T2 - 1 - j) if d == 0 else (T2 - j)
                        if 0 <= jd < T2:
                            a, an = stk[(d, "PD", ri)]
                            V.cp(a[:, :, jd], t[:], [tn], [an])
                for j in range(T2):
                    for ri, key in (("r", "NR"), ("i", "NI")):
                        t, tn = p[key][j]
                        a, an = stk[(d, "NA", ri)]
                        V.cp(a[:, :, j], t[:], [tn], [an])
                b1, b2, o1, o2 = keep[d]
                S.dma("sp", b1[:], din["b1"][d][:, gsl, :], writes=[f"o_b1{d}"])
                S.dma("sp", b2[:], din["b2"][d][:, gsl, :], writes=[f"o_b2{d}"])
                (fr, frn), (fi, fin) = p["fr"], p["fi"]
                V.tt(tmpA[:], b1[:], bc(fr), ALU.mult, [f"o_b1{d}", frn], ["o_tmpA"])
                V.tt(tmpB[:], b2[:], bc(fi), ALU.mult, [f"o_b2{d}", fin], ["o_tmpB"])
                V.ts(tmpB[:], tmpB[:], sg[:, 0:1], ALU.mult, ["o_tmpB", "o_sg"], ["o_tmpB"])
                V.tt(o1[:], tmpA[:], tmpB[:], ALU.add, ["o_tmpA", "o_tmpB"], [f"o_bb1{d}"])
                V.tt(tmpA[:], b2[:], bc(fr), ALU.mult, [f"o_b2{d}", frn], ["o_tmpA"])
                V.tt(tmpB[:], b1[:], bc(fi), ALU.mult, [f"o_b1{d}", fin], ["o_tmpB"])
                V.ts(tmpB[:], tmpB[:], sr[:, 0:1], ALU.mult, ["o_tmpB", "o_sr"], ["o_tmpB"])
                V.tt(o2[:], tmpA[:], tmpB[:], ALU.add, ["o_tmpA", "o_tmpB"], [f"o_bb2{d}"])
                V.ts(o2[:], o2[:], sg[:, 0:1], ALU.mult, [f"o_bb2{d}", "o_sg"], [f"o_bb2{d}"])
                bb1.append((o1, f"o_bb1{d}")); bb2s.append((o2, f"o_bb2{d}"))
                S.dma("sp", b1[:], din["c1"][d][:, gsl, :], reads=[f"o_b1{d}"], writes=[f"o_b1{d}"])
                S.dma("sp", b2[:], din["c2"][d][:, gsl, :], reads=[f"o_b2{d}"], writes=[f"o_b2{d}"])
                V.ts(b1[:], b1[:], sr[:, 0:1], ALU.mult, [f"o_b1{d}", "o_sr"], [f"o_b1{d}"])
                c1s.append((b1, f"o_b1{d}")); c2.append((b2, f"o_b2{d}"))
            S.barrier()

        names = ["WTf", "Af", "WTb", "Bf", "Bb"]
        tmp = {nm: _sb(nc, es, f"ot_{nm}", [128, GB, T2, GC], F32) for nm in names}
        tAB = {e: (_sb(nc, es, f"o_tA{e}", [128, GB, T2, GC], F32), _sb(nc, es, f"o_tB{e}", [128, GB, T2, GC], F32))
               for e in ("dve", "pool")}
        m1 = _sb(nc, es, "o_m1", [128, 128], F32)
        m2 = _sb(nc, es, "o_m2", [128, 128], F32)
        Wt = _sb(nc, es, "o_Wst", [128, GB, 4, 2, 128], BF16)
        Vt = _sb(nc, es, "o_Vst", [128, GB, 2, T2 * GC], BF16)
        Mt = _sb(nc, es, "o_Mst", [128, GB, 2, T2 * GC], BF16)
        Wn, Vn, Mn = "o_Wst", "o_Vst", "o_Mst"

        def gen(VV, dst, dstn, src1, src2, d, tbl, j0, g0, op):
            (s1, s1n), (s2, s2n) = src1, src2
            (tr, trn), (ti, tin) = stk[(d, tbl, "r")], stk[(d, tbl, "i")]
            tA, tB = tAB[VV.e]
            shp = [128, GB, T2, GC]
            VV.tt(tA[:], s1[:, g0:g0 + GB, :].unsqueeze(2).to_broadcast(shp),
                  tr[:, g0:g0 + GB, j0:j0 + T2].unsqueeze(3).to_broadcast(shp), ALU.mult, [s1n, trn], [f"o_tA{VV.e}"])
            VV.tt(tB[:], s2[:, g0:g0 + GB, :].unsqueeze(2).to_broadcast(shp),
                  ti[:, g0:g0 + GB, j0:j0 + T2].unsqueeze(3).to_broadcast(shp), ALU.mult, [s2n, tin], [f"o_tB{VV.e}"])
            VV.tt(dst, tA[:], tB[:], op, [f"o_tA{VV.e}", f"o_tB{VV.e}"], [dstn])

        for bi, g0 in enumerate(range(0, GH, GB)):
            gen(V, tmp["WTf"][:], "ot_WTf", bb1[0], bb2s[0], 0, "PD", 0, g0, ALU.add)
            gen(V, tmp["Af"][:], "ot_Af", bb1[0], bb2s[0], 0, "NA", 0, g0, ALU.add)
            gen(V, tmp["WTb"][:], "ot_WTb", bb1[1], bb2s[1], 1, "PA", 0, g0, ALU.add)
            gen(V, tmp["Bf"][:], "ot_Bf", c1s[0], c2[0], 0, "PA", 0, g0, ALU.subtract)
            gen(V, tmp["Bb"][:], "ot_Bb", c1s[1], c2[1], 1, "NA", 0, g0, ALU.subtract)
            gen(VP, Vt[:, :, 0, :].rearrange("p g (t c) -> p g t c", t=T2), Vn, c1s[0], c2[0], 0, "PA", 1, g0,
                ALU.subtract)
            gen(VP, Vt[:, :, 1, :].rearrange("p g (t c) -> p g t c", t=T2), Vn, c1s[1], c2[1], 1, "PD", 0, g0,
                ALU.subtract)
            for gi in range(GB):
                g = g0 + gi
                for vi, (src, srcn) in enumerate((("WTf", "ot_WTf"), ("WTb", "ot_WTb"))):
                    for a in range(2):
                        bank = (vi * 2 + a) % 2
                        S.op("pe", lambda src=src, gi=gi, a=a, bank=bank: nc.tensor.transpose(
                            out=P[bank][:, 0:128],
                            in_=tmp[src][:, gi, a * 8:(a + 1) * 8, :].rearrange("p t c -> p (t c)"),
                            identity=ident[:]), reads=[srcn, "ident"], writes=[f"ps{bank}"])
                        S.op("act", lambda gi=gi, vi=vi, a=a, bank=bank: nc.scalar.copy(
                            out=Wt[:, gi, 2 * vi, a, :], in_=P[bank][:, 0:128]), reads=[f"ps{bank}"], writes=[Wn])
                        S.op("act", lambda gi=gi, vi=vi, a=a, bank=bank: nc.scalar.copy(
                            out=Wt[:, gi, 2 * vi + 1, a, 0:64], in_=P[bank][:, 64:128]),
                            reads=[f"ps{bank}"], writes=[Wn])
                        S.op("act", lambda gi=gi, vi=vi, a=a, bank=bank: nc.scalar.copy(
                            out=Wt[:, gi, 2 * vi + 1, a, 64:128], in_=P[bank][:, 0:64]),
                            reads=[f"ps{bank}"], writes=[Wn])
                for a in range(2):
                    for b in range(2):
                        asl = slice(a * 8, (a + 1) * 8)
                        bsl = slice(b * 8, (b + 1) * 8)
                        dst = Mt[:, gi, a, b * 128:(b + 1) * 128]
                        b1k, b2k = 2 + 2 * ((a * 2 + b) % 2), 3 + 2 * ((a * 2 + b) % 2)
                        if (a, b) != (1, 0):
                            S.op("pe", lambda gi=gi, asl=asl, bsl=bsl, b1k=b1k: nc.tensor.matmul(
                                out=P[b1k][:, 0:128], lhsT=tmp["Af"][:, gi, asl, :].rearrange("p t c -> p (t c)"),
                                rhs=tmp["Bf"][:, gi, bsl, :].rearrange("p t c -> p (t c)"), start=True, stop=True),
                                reads=["ot_Af", "ot_Bf"], writes=[f"ps{b1k}"])
                        if (a, b) != (0, 1):
                            S.op("pe", lambda gi=gi, asl=asl, bsl=bsl, b2k=b2k: nc.tensor.matmul(
                                out=P[b2k][:, 0:128], lhsT=tmp["WTb"][:, gi, asl, :].rearrange("p t c -> p (t c)"),
                                rhs=tmp["Bb"][:, gi, bsl, :].rearrange("p t c -> p (t c)"), start=True, stop=True),
                                reads=["ot_WTb", "ot_Bb"], writes=[f"ps{b2k}"])
                        if a == b:
                            V.tt(m1[:], P[b1k][:, 0:128], mf[:], ALU.mult, [f"ps{b1k}", "o_mf"], ["o_m1"])
                            V.tt(m2[:], P[b2k][:, 0:128], mb[:], ALU.mult, [f"ps{b2k}", "o_mb"], ["o_m2"])
                            V.tt(m1[:], m1[:], m2[:], ALU.add, ["o_m1", "o_m2"], ["o_m1"])
                            V.stt(dst, ident[:], dcol[:, g:g + 1], m1[:], ALU.mult, ALU.add,
                                  ["ident", "o_dcol", "o_m1"], [Mn])
                        elif (a, b) == (0, 1):
                            V.cp(dst, P[b1k][:, 0:128], [f"ps{b1k}"], [Mn])
                        else:
                            V.cp(dst, P[b2k][:, 0:128], [f"ps{b2k}"], [Mn])
            G0 = h * GH + g0
            S.dma("sp", opW[G0:G0 + GB].rearrange("g p n -> p g n"), Wt[:].rearrange("p g v a n -> p g (v a n)"),
                  reads=[Wn])
            S.dma("sp", opV[G0:G0 + GB].rearrange("g p n -> p g n"), Vt[:].rearrange("p g d n -> p g (d n)"),
                  reads=[Vn])
            S.dma("sp", opM[G0:G0 + GB].rearrange("g p n -> p g n"), Mt[:].rearrange("p g a n -> p g (a n)"),
                  reads=[Mn])
        S.barrier()


def emit_s5f_main(nc, S, P, ident, din, x_ap, ys_ap, opW, opV, opM, h, NK, NKF, GH=64, GB=8):
    V = _V(nc, S, "dve")
    gsl = slice(h * GH, (h + 1) * GH)
    x_k = x_ap.rearrange("(k t) d -> k t d", t=T2)
    y_k = ys_ap.rearrange("(k t) d -> k t d", t=T2)
    with ExitStack() as es:
        sg = _sb(nc, es, "m_sg", [128, 1], F32)
        sr = _sb(nc, es, "m_sr", [128, 1], F32)
        S.dma("sp", sg[:], din["sg"], writes=["m_sg"])
        V.ts(sr[:], sg[:], -1.0, ALU.mult, ["m_sg"], ["m_sr"])
        COEF = [_sb(nc, es, f"m_COEF{d}", [128, 2, 2, GH], F32) for d in range(2)]
        QR = [_sb(nc, es, f"m_QR{d}", [128, 2, 2, GH], F32) for d in range(2)]
        X2 = [[_sb(nc, es, f"m_X{d}{i}", [128, 2, GH], F32) for i in range(2)] for d in range(2)]
        with ExitStack() as esp:
            for d in range(2):
                p = emit_s5_params(nc, S, esp, din["lam_re"][d][:, gsl], din["lam_im"][d][:, gsl],
                                   din["logdt"][d][:, gsl], sg, GH, 10 + d, npw=T2)
                (ar, arn), (ai, ain) = p["PR"][T2], p["PI"][T2]
                V.cp(COEF[d][:, 0, 0, :], ar[:], [arn], [f"m_COEF{d}"])
                V.cp(COEF[d][:, 0, 1, :], ar[:], [arn], [f"m_COEF{d}"])
                V.ts(COEF[d][:, 1, 0, :], ai[:], sr[:, 0:1], ALU.mult, [ain, "m_sr"], [f"m_COEF{d}"])
                V.ts(COEF[d][:, 1, 1, :], ai[:], sg[:, 0:1], ALU.mult, [ain, "m_sg"], [f"m_COEF{d}"])
            S.barrier()
        NKS = [NKF, NK]
        St = [_sb(nc, es, f"m_St{d}", [128, NKS[d] + 1, 2, GH], BF16) for d in range(2)]
        Ubk = _sb(nc, es, "m_Ubk", [128, GH, 2, NKF], BF16)
        S.op("pool", lambda: nc.gpsimd.memset(St[0][:, 0, :, :], 0.0), writes=["m_St0"])
        S.op("pool", lambda: nc.gpsimd.memset(St[1][:, NK, :, :], 0.0), writes=["m_St1"])
        for d in range(2):
            for i in range(2):
                S.op("pool", lambda d=d, i=i: nc.gpsimd.memset(X2[d][i][:], 0.0), writes=[f"m_X{d}{i}"])
        with ExitStack() as es1:
            xt = [_sb(nc, es1, f"m_xt{i}", [128, T2, GB * GC], F32) for i in range(1)] * 2
            xr = [_sb(nc, es1, f"m_xr{i}", [128, GB, T2, GC], F32) for i in range(2)]
            Ut = [_sb(nc, es1, f"m_Ut{i}", [128, GB, 2, NK], BF16) for i in range(1)] * 2
            Wt = [_sb(nc, es1, f"m_Wt{i}", [128, GB, 4, 2, 128], BF16) for i in range(1)] * 2
            ci = 0
            for sb in range(GH // GB):
                G0 = h * GH + sb * GB
                c0 = G0 * GC
                W, Wn = Wt[0], "m_Wt0"
                U, Un = Ut[0], "m_Ut0"
                S.dma("sp", W[:].rearrange("p g v a n -> p g (v a n)"), opW[G0:G0 + GB].rearrange("g p n -> p g n"),
                      writes=[Wn])
                for kb in range(NK // 128):
                    xtt, xtn = xt[0], "m_xt0"
                    xrt, xrn = xr[ci % 2], f"m_xr{ci % 2}"
                    ci += 1
                    S.dma("sp", xtt[:], x_k[kb * 128:(kb + 1) * 128, :, c0:c0 + GB * GC], writes=[xtn])
                    S.op("act", lambda xtt=xtt, xrt=xrt: nc.scalar.copy(
                        out=xrt[:], in_=xtt[:].rearrange("p t (g c) -> p g t c", g=GB)), reads=[xtn], writes=[xrn])
                    for gi in range(GB):
                        bank = gi % 2
                        for a in range(2):
                            S.op("pe", lambda xrt=xrt, gi=gi, a=a, bank=bank: nc.tensor.transpose(
                                out=P[bank][:, a * 128:(a + 1) * 128],
                                in_=xrt[:, gi, a * 8:(a + 1) * 8, :].rearrange("p t c -> p (t c)"),
                                identity=ident[:]), reads=[xrn, "ident"], writes=[f"ps{bank}"], signal=(a == 1))
                        V.cp(U[:, gi, :, kb * 128:(kb + 1) * 128],
                             P[bank][:, 0:256].rearrange("p (a k) -> p a k", a=2), [f"ps{bank}"], [Un])
                S.op("act", lambda U=U, sb=sb: nc.scalar.copy(out=Ubk[:, sb * GB:(sb + 1) * GB, :, :],
                                                              in_=U[:, :, :, 0:NKF]), reads=[Un], writes=["m_Ubk"])
                for gi in range(GB):
                    gl = sb * GB + gi
                    for var in range(4):
                        d, ab = var // 2, var % 2
                        nk = NKS[d]
                        bank = 2 + (gi * 4 + var) % 6
                        for a in range(2):
                            S.op("pe", lambda W=W, U=U, gi=gi, var=var, a=a, nk=nk, bank=bank: nc.tensor.matmul(
                                out=P[bank][:, 0:nk], lhsT=W[:, gi, var, a, :], rhs=U[:, gi, a, 0:nk],
                                start=(a == 0), stop=(a == 1)), reads=[Wn, Un], writes=[f"ps{bank}"],
                                signal=(a == 1))
                        off = 1 if d == 0 else 0
                        if var % 2:
                            S.op("act", lambda d=d, ab=ab, gl=gl, bank=bank, off=off, nk=nk: nc.scalar.copy(
                                out=St[d][:, off:off + nk, ab, gl], in_=P[bank][:, 0:nk]),
                                reads=[f"ps{bank}"], writes=[f"m_St{d}"])
                        else:
                            V.cp(St[d][:, off:off + nk, ab, gl], P[bank][:, 0:nk], [f"ps{bank}"], [f"m_St{d}"])
            S.barrier()
        for k in range(NK):
            for d, e in ((1, "dve"), (0, "pool")):
                if d == 0 and k >= NKF:
                    continue
                E = nc.vector if e == "dve" else nc.gpsimd
                slot = k + 1 if d == 0 else NK - 1 - k
                Xp, Xpn = X2[d][(k + 1) % 2], f"m_X{d}{(k + 1) % 2}"
                Xc, Xcn = X2[d][k % 2], f"m_X{d}{k % 2}"
                Sn = f"m_St{d}_{slot}"
                S.op(e, lambda E=E, d=d, Xp=Xp: E.tensor_tensor(
                    out=QR[d][:], in0=COEF[d][:], in1=Xp[:].unsqueeze(1).to_broadcast([128, 2, 2, GH]),
                    op=ALU.mult), reads=[Xpn, f"m_COEF{d}"], writes=[f"m_QR{d}"])
                S.op(e, lambda E=E, d=d, slot=slot: E.tensor_tensor(
                    out=QR[d][:, 0, :, :], in0=QR[d][:, 0, :, :], in1=St[d][:, slot, :, :], op=ALU.add),
                    reads=[f"m_QR{d}", Sn], writes=[f"m_QR{d}"])
                S.op(e, lambda E=E, d=d, Xc=Xc: E.tensor_tensor(
                    out=Xc[:], in0=QR[d][:, 0, :, :], in1=QR[d][:, 1, ::-1, :], op=ALU.add),
                    reads=[f"m_QR{d}"], writes=[Xcn])
                S.op(e, lambda E=E, d=d, slot=slot, Xc=Xc: E.tensor_copy(out=St[d][:, slot, :, :], in_=Xc[:]),
                     reads=[Xcn], writes=[Sn])
        S.barrier()
        with ExitStack() as es3:
            Vt = [_sb(nc, es3, f"m_Vt{i}", [128, GB, 2, T2 * GC], BF16) for i in range(2)]
            Mt = [_sb(nc, es3, f"m_Mt{i}", [128, GB, 2, T2 * GC], BF16) for i in range(2)]
            Ysb = [_sb(nc, es3, f"m_Y{i}", [128, T2, GB * GC], F32) for i in range(2)]
            yi = 0
            for sb in range(GH // GB):
                G0 = h * GH + sb * GB
                c0 = G0 * GC
                Vv, Vn = Vt[sb % 2], f"m_Vt{sb % 2}"
                Mm, Mn = Mt[sb % 2], f"m_Mt{sb % 2}"
                S.dma("sp", Vv[:].rearrange("p g d n -> p g (d n)"), opV[G0:G0 + GB].rearrange("g p n -> p g n"),
                      writes=[Vn])
                S.dma("sp", Mm[:].rearrange("p g a n -> p g (a n)"), opM[G0:G0 + GB].rearrange("g p n -> p g n"),
                      writes=[Mn])
                for (k0, kn) in ((0, 128), (128, NKF - 128)):
                    Yt, Yn = Ysb[yi % 2], f"m_Y{yi % 2}"
                    yi += 1
                    for gi in range(GB):
                        gl = sb * GB + gi
                        bank = gi % 4
                        ysl = P[bank][0:kn, 0:T2 * GC]
                        ops = [(Ubk[:, gl, 0, k0:k0 + kn], Mm[:, gi, 0, :], ["m_Ubk", Mn]),
                               (Ubk[:, gl, 1, k0:k0 + kn], Mm[:, gi, 1, :], ["m_Ubk", Mn]),
                               (St[0][:, k0:k0 + kn, 0, gl], Vv[:, gi, 0, :], ["m_St0", Vn]),
                               (St[1][:, k0 + 1:k0 + kn + 1, 0, gl], Vv[:, gi, 1, :], ["m_St1", Vn])]
                        for oi, (l, r, rd) in enumerate(ops):
                            S.op("pe", lambda l=l, r=r, ysl=ysl, oi=oi: nc.tensor.matmul(
                                out=ysl, lhsT=l, rhs=r, start=(oi == 0), stop=(oi == 3)),
                                reads=rd, writes=[f"ps{bank}"], signal=(oi == 3))
                        src = ysl.rearrange("p (t c) -> p t c", t=T2)
                        if gi % 2:
                            S.op("act", lambda Yt=Yt, gi=gi, src=src, kn=kn: nc.scalar.copy(
                                out=Yt[0:kn, :, gi * GC:(gi + 1) * GC], in_=src), reads=[f"ps{bank}"], writes=[Yn])
                        else:
                            V.cp(Yt[0:kn, :, gi * GC:(gi + 1) * GC], src, [f"ps{bank}"], [Yn])
                    S.dma("sp", y_k[k0:k0 + kn, :, c0:c0 + GB * GC], Yt[0:kn, :, :], reads=[Yn])
            S.barrier()
        S.barrier()


def emit_s5f_main_old(nc, S, P, ident, din, x_ap, ys_ap, opW, opV, opM, h, NK, NKF, GH=64, GB=8):
    V = _V(nc, S, "dve")
    gsl = slice(h * GH, (h + 1) * GH)
    x_k = x_ap.rearrange("(k t) d -> k t d", t=T2)
    y_k = ys_ap.rearrange("(k t) d -> k t d", t=T2)
    with ExitStack() as es:
        sg = _sb(nc, es, "m_sg", [128, 1], F32)
        sr = _sb(nc, es, "m_sr", [128, 1], F32)
        S.dma("sp", sg[:], din["sg"], writes=["m_sg"])
        V.ts(sr[:], sg[:], -1.0, ALU.mult, ["m_sg"], ["m_sr"])
        AR2 = [_sb(nc, es, f"m_AR2{d}", [128, 2, GH], F32) for d in range(2)]
        AIx = [_sb(nc, es, f"m_AIx{d}", [128, 2, GH], F32) for d in range(2)]
        with ExitStack() as esp:
            for d in range(2):
                p = emit_s5_params(nc, S, esp, din["lam_re"][d][:, gsl], din["lam_im"][d][:, gsl],
                                   din["logdt"][d][:, gsl], sg, GH, 10 + d, npw=T2)
                (ar, arn), (ai, ain) = p["PR"][T2], p["PI"][T2]
                V.cp(AR2[d][:, 0, :], ar[:], [arn], [f"m_AR2{d}"])
                V.cp(AR2[d][:, 1, :], ar[:], [arn], [f"m_AR2{d}"])
                V.ts(AIx[d][:, 0, :], ai[:], sr[:, 0:1], ALU.mult, [ain, "m_sr"], [f"m_AIx{d}"])
                V.ts(AIx[d][:, 1, :], ai[:], sg[:, 0:1], ALU.mult, [ain, "m_sg"], [f"m_AIx{d}"])
            S.barrier()
        NKS = [NKF, NK]
        St = [_sb(nc, es, f"m_St{d}", [128, NKS[d] + 1, 2, GH], BF16) for d in range(2)]
        Ubk = _sb(nc, es, "m_Ubk", [128, GH, 2, NKF], BF16)
        X = [_sb(nc, es, f"m_X{d}", [128, 2, GH], F32) for d in range(2)]
        q1 = [_sb(nc, es, f"m_q1{d}", [128, 2, GH], F32) for d in range(2)]
        q2 = [_sb(nc, es, f"m_q2{d}", [128, 2, GH], F32) for d in range(2)]
        S.op("pool", lambda: nc.gpsimd.memset(St[0][:, 0, :, :], 0.0), writes=["m_St0"])
        S.op("pool", lambda: nc.gpsimd.memset(St[1][:, NK, :, :], 0.0), writes=["m_St1"])
        for d in range(2):
            S.op("pool", lambda d=d: nc.gpsimd.memset(X[d][:], 0.0), writes=[f"m_X{d}"])
        with ExitStack() as es1:
            xt = [_sb(nc, es1, f"m_xt{i}", [128, T2, GB * GC], F32) for i in range(1)] * 2
            xr = [_sb(nc, es1, f"m_xr{i}", [128, GB, T2, GC], F32) for i in range(2)]
            Ut = [_sb(nc, es1, f"m_Ut{i}", [128, GB, 2, NK], BF16) for i in range(1)] * 2
            Wt = [_sb(nc, es1, f"m_Wt{i}", [128, GB, 4, 2, 128], BF16) for i in range(1)] * 2
            ci = 0
            for sb in range(GH // GB):
                G0 = h * GH + sb * GB
                c0 = G0 * GC
                W, Wn = Wt[0], "m_Wt0"
                U, Un = Ut[0], "m_Ut0"
                S.dma("sp", W[:].rearrange("p g v a n -> p g (v a n)"), opW[G0:G0 + GB].rearrange("g p n -> p g n"),
                      writes=[Wn])
                for kb in range(NK // 128):
                    xtt, xtn = xt[0], "m_xt0"
                    xrt, xrn = xr[ci % 2], f"m_xr{ci % 2}"
                    ci += 1
                    S.dma("sp", xtt[:], x_k[kb * 128:(kb + 1) * 128, :, c0:c0 + GB * GC], writes=[xtn])
                    S.op("act", lambda xtt=xtt, xrt=xrt: nc.scalar.copy(
                        out=xrt[:], in_=xtt[:].rearrange("p t (g c) -> p g t c", g=GB)), reads=[xtn], writes=[xrn])
                    for gi in range(GB):
                        bank = gi % 2
                        for a in range(2):
                            S.op("pe", lambda xrt=xrt, gi=gi, a=a, bank=bank: nc.tensor.transpose(
                                out=P[bank][:, a * 128:(a + 1) * 128],
                                in_=xrt[:, gi, a * 8:(a + 1) * 8, :].rearrange("p t c -> p (t c)"),
                                identity=ident[:]), reads=[xrn, "ident"], writes=[f"ps{bank}"], signal=(a == 1))
                        V.cp(U[:, gi, :, kb * 128:(kb + 1) * 128],
                             P[bank][:, 0:256].rearrange("p (a k) -> p a k", a=2), [f"ps{bank}"], [Un])
                S.op("act", lambda U=U, sb=sb: nc.scalar.copy(out=Ubk[:, sb * GB:(sb + 1) * GB, :, :],
                                                              in_=U[:, :, :, 0:NKF]), reads=[Un], writes=["m_Ubk"])
                for gi in range(GB):
                    gl = sb * GB + gi
                    for var in range(4):
                        d, ab = var // 2, var % 2
                        nk = NKS[d]
                        bank = 2 + (gi * 4 + var) % 6
                        for a in range(2):
                            S.op("pe", lambda W=W, U=U, gi=gi, var=var, a=a, nk=nk, bank=bank: nc.tensor.matmul(
                                out=P[bank][:, 0:nk], lhsT=W[:, gi, var, a, :], rhs=U[:, gi, a, 0:nk],
                                start=(a == 0), stop=(a == 1)), reads=[Wn, Un], writes=[f"ps{bank}"],
                                signal=(a == 1))
                        off = 1 if d == 0 else 0
                        if var % 2:
                            S.op("act", lambda d=d, ab=ab, gl=gl, bank=bank, off=off, nk=nk: nc.scalar.copy(
                                out=St[d][:, off:off + nk, ab, gl], in_=P[bank][:, 0:nk]),
                                reads=[f"ps{bank}"], writes=[f"m_St{d}"])
                        else:
                            V.cp(St[d][:, off:off + nk, ab, gl], P[bank][:, 0:nk], [f"ps{bank}"], [f"m_St{d}"])
            S.barrier()
        for k in range(NK):
            for d, e in ((0, "dve"), (1, "pool")):
                if d == 0 and k >= NKF:
                    continue
                E = nc.vector if e == "dve" else nc.gpsimd
                slot = k + 1 if d == 0 else NK - 1 - k
                Xn, q1n, q2n, Sn = f"m_X{d}", f"m_q1{d}", f"m_q2{d}", f"m_St{d}"
                S.op(e, lambda E=E, d=d: E.tensor_tensor(out=q1[d][:], in0=X[d][:], in1=AR2[d][:], op=ALU.mult),
                     reads=[Xn, f"m_AR2{d}"], writes=[q1n])
                S.op(e, lambda E=E, d=d: E.tensor_tensor(out=q2[d][:], in0=X[d][:], in1=AIx[d][:], op=ALU.mult),
                     reads=[Xn, f"m_AIx{d}"], writes=[q2n])
                S.op(e, lambda E=E, d=d, slot=slot: E.tensor_tensor(out=q1[d][:], in0=q1[d][:],
                                                                    in1=St[d][:, slot, :, :], op=ALU.add),
                     reads=[q1n, Sn], writes=[q1n])
                S.op(e, lambda E=E, d=d: E.tensor_tensor(out=X[d][:, 0, :], in0=q1[d][:, 0, :], in1=q2[d][:, 1, :],
                                                         op=ALU.add), reads=[q1n, q2n], writes=[Xn])
                S.op(e, lambda E=E, d=d: E.tensor_tensor(out=X[d][:, 1, :], in0=q1[d][:, 1, :], in1=q2[d][:, 0, :],
                                                         op=ALU.add), reads=[q1n, q2n], writes=[Xn])
                S.op(e, lambda E=E, d=d, slot=slot: E.tensor_copy(out=St[d][:, slot, :, :], in_=X[d][:]),
                     reads=[Xn], writes=[Sn])
        with ExitStack() as es3:
            Vt = [_sb(nc, es3, f"m_Vt{i}", [128, GB, 2, T2 * GC], BF16) for i in range(2)]
            Mt = [_sb(nc, es3, f"m_Mt{i}", [128, GB, 2, T2 * GC], BF16) for i in range(2)]
            Ysb = [_sb(nc, es3, f"m_Y{i}", [128, T2, GB * GC], F32) for i in range(2)]
            yi = 0
            for sb in range(GH // GB):
                G0 = h * GH + sb * GB
                c0 = G0 * GC
                Vv, Vn = Vt[sb % 2], f"m_Vt{sb % 2}"
                Mm, Mn = Mt[sb % 2], f"m_Mt{sb % 2}"
                S.dma("sp", Vv[:].rearrange("p g d n -> p g (d n)"), opV[G0:G0 + GB].rearrange("g p n -> p g n"),
                      writes=[Vn])
                S.dma("sp", Mm[:].rearrange("p g a n -> p g (a n)"), opM[G0:G0 + GB].rearrange("g p n -> p g n"),
                      writes=[Mn])
                for (k0, kn) in ((0, 128), (128, NKF - 128)):
                    Yt, Yn = Ysb[yi % 2], f"m_Y{yi % 2}"
                    yi += 1
                    for gi in range(GB):
                        gl = sb * GB + gi
                        bank = gi % 4
                        ysl = P[bank][0:kn, 0:T2 * GC]
                        ops = [(Ubk[:, gl, 0, k0:k0 + kn], Mm[:, gi, 0, :], ["m_Ubk", Mn]),
                               (Ubk[:, gl, 1, k0:k0 + kn], Mm[:, gi, 1, :], ["m_Ubk", Mn]),
                               (St[0][:, k0:k0 + kn, 0, gl], Vv[:, gi, 0, :], ["m_St0", Vn]),
                               (St[1][:, k0 + 1:k0 + kn + 1, 0, gl], Vv[:, gi, 1, :], ["m_St1", Vn])]
                        for oi, (l, r, rd) in enumerate(ops):
                            S.op("pe", lambda l=l, r=r, ysl=ysl, oi=oi: nc.tensor.matmul(
                                out=ysl, lhsT=l, rhs=r, start=(oi == 0), stop=(oi == 3)),
                                reads=rd, writes=[f"ps{bank}"], signal=(oi == 3))
                        src = ysl.rearrange("p (t c) -> p t c", t=T2)
                        if gi % 2:
                            S.op("act", lambda Yt=Yt, gi=gi, src=src, kn=kn: nc.scalar.copy(
                                out=Yt[0:kn, :, gi * GC:(gi + 1) * GC], in_=src), reads=[f"ps{bank}"], writes=[Yn])
                        else:
                            V.cp(Yt[0:kn, :, gi * GC:(gi + 1) * GC], src, [f"ps{bank}"], [Yn])
                    S.dma("sp", y_k[k0:k0 + kn, :, c0:c0 + GB * GC], Yt[0:kn, :, :], reads=[Yn])
            S.barrier()
        S.barrier()


def emit_s5f_ops_old(nc, S, P, ident, din, opW, opV, opM, h, GH=64, GB=8):
    V = _V(nc, S, "dve")
    gsl = slice(h * GH, (h + 1) * GH)
    with ExitStack() as es:
        sg = _sb(nc, es, "o_sg", [128, 1], F32)
        sr = _sb(nc, es, "o_sr", [128, 1], F32)
        S.dma("sp", sg[:], din["sg"], writes=["o_sg"])
        V.ts(sr[:], sg[:], -1.0, ALU.mult, ["o_sg"], ["o_sr"])
        mf = _sb(nc, es, "o_mf", [128, 128], F32)
        mb = _sb(nc, es, "o_mb", [128, 128], F32)
        dcol = _sb(nc, es, "o_dcol", [128, GH], F32)
        S.dma("sp", mf[:], din["maskf"], writes=["o_mf"])
        S.dma("sp", mb[:], din["maskb"], writes=["o_mb"])
        S.dma("sp", dcol[:], din["dcol"][:, gsl], writes=["o_dcol"])
        prm = [emit_s5_params(nc, S, es, din["lam_re"][d][:, gsl], din["lam_im"][d][:, gsl],
                              din["logdt"][d][:, gsl], sg, GH, d, npw=T2) for d in range(2)]
        tmpA = _sb(nc, es, "o_tmpA", [128, GH, GC], F32)
        tmpB = _sb(nc, es, "o_tmpB", [128, GH, GC], F32)

        def bc(t, n=GH):
            return t[:].unsqueeze(2).to_broadcast([128, n, GC])

        bb1, bb2s, c1s, c2 = [], [], [], []
        for d in range(2):
            b1 = _sb(nc, es, f"o_b1{d}", [128, GH, GC], F32)
            b2 = _sb(nc, es, f"o_b2{d}", [128, GH, GC], F32)
            S.dma("sp", b1[:], din["b1"][d][:, gsl, :], writes=[f"o_b1{d}"])
            S.dma("sp", b2[:], din["b2"][d][:, gsl, :], writes=[f"o_b2{d}"])
            o1 = _sb(nc, es, f"o_bb1{d}", [128, GH, GC], F32)
            o2 = _sb(nc, es, f"o_bb2{d}", [128, GH, GC], F32)
            (fr, frn), (fi, fin) = prm[d]["fr"], prm[d]["fi"]
            V.tt(tmpA[:], b1[:], bc(fr), ALU.mult, [f"o_b1{d}", frn], ["o_tmpA"])
            V.tt(tmpB[:], b2[:], bc(fi), ALU.mult, [f"o_b2{d}", fin], ["o_tmpB"])
            V.ts(tmpB[:], tmpB[:], sg[:, 0:1], ALU.mult, ["o_tmpB", "o_sg"], ["o_tmpB"])
            V.tt(o1[:], tmpA[:], tmpB[:], ALU.add, ["o_tmpA", "o_tmpB"], [f"o_bb1{d}"])
            V.tt(tmpA[:], b2[:], bc(fr), ALU.mult, [f"o_b2{d}", frn], ["o_tmpA"])
            V.tt(tmpB[:], b1[:], bc(fi), ALU.mult, [f"o_b1{d}", fin], ["o_tmpB"])
            V.ts(tmpB[:], tmpB[:], sr[:, 0:1], ALU.mult, ["o_tmpB", "o_sr"], ["o_tmpB"])
            V.tt(o2[:], tmpA[:], tmpB[:], ALU.add, ["o_tmpA", "o_tmpB"], [f"o_bb2{d}"])
            V.ts(o2[:], o2[:], sg[:, 0:1], ALU.mult, [f"o_bb2{d}", "o_sg"], [f"o_bb2{d}"])
            bb1.append((o1, f"o_bb1{d}")); bb2s.append((o2, f"o_bb2{d}"))
            S.dma("sp", b1[:], din["c1"][d][:, gsl, :], reads=[f"o_b1{d}"], writes=[f"o_b1{d}"])
            S.dma("sp", b2[:], din["c2"][d][:, gsl, :], reads=[f"o_b2{d}"], writes=[f"o_b2{d}"])
            V.ts(b1[:], b1[:], sr[:, 0:1], ALU.mult, [f"o_b1{d}", "o_sr"], [f"o_b1{d}"])
            c1s.append((b1, f"o_b1{d}")); c2.append((b2, f"o_b2{d}"))

        names = ["WTf", "Af", "Bf", "Vf", "WTb", "Bb", "Vb"]
        tmp = {nm: _sb(nc, es, f"ot_{nm}", [128, GB, T2, GC], F32) for nm in names}
        tA = _sb(nc, es, "o_tA", [128, GB, GC], F32)
        tB = _sb(nc, es, "o_tB", [128, GB, GC], F32)
        m1 = _sb(nc, es, "o_m1", [128, 128], F32)
        m2 = _sb(nc, es, "o_m2", [128, 128], F32)
        Wst = [_sb(nc, es, f"o_Wst{i}", [128, GB, 4, 2, 128], BF16) for i in range(2)]
        Vst = [_sb(nc, es, f"o_Vst{i}", [128, GB, 2, T2 * GC], BF16) for i in range(2)]
        Mst = [_sb(nc, es, f"o_Mst{i}", [128, GB, 2, T2 * GC], BF16) for i in range(2)]

        def gen(dst, dstn, tsl, src1, src2, pa, pb, g0, op):
            (s1, s1n), (s2, s2n), (a, an), (b, bn) = src1, src2, pa, pb
            ab = a[:, g0:g0 + GB].unsqueeze(2).to_broadcast([128, GB, GC])
            bbb = b[:, g0:g0 + GB].unsqueeze(2).to_broadcast([128, GB, GC])
            V.tt(tA[:], s1[:, g0:g0 + GB, :], ab, ALU.mult, [s1n, an], ["o_tA"])
            V.tt(tB[:], s2[:, g0:g0 + GB, :], bbb, ALU.mult, [s2n, bn], ["o_tB"])
            V.tt(dst[:, :, tsl, :], tA[:], tB[:], op, ["o_tA", "o_tB"], [dstn])

        pf, pb_ = prm
        for bi, g0 in enumerate(range(0, GH, GB)):
            Wt, Wn = Wst[bi % 2], f"o_Wst{bi % 2}"
            Vt, Vn = Vst[bi % 2], f"o_Vst{bi % 2}"
            Mt, Mn = Mst[bi % 2], f"o_Mst{bi % 2}"
            for j in range(T2):
                gen(tmp["WTf"], "ot_WTf", j, bb1[0], bb2s[0], pf["PR"][T2 - 1 - j], pf["PI"][T2 - 1 - j], g0, ALU.add)
                gen(tmp["Af"], "ot_Af", j, bb1[0], bb2s[0], pf["NR"][j], pf["NI"][j], g0, ALU.add)
                gen(tmp["WTb"], "ot_WTb", j, bb1[1], bb2s[1], pb_["PR"][j], pb_["PI"][j], g0, ALU.add)
                gen(tmp["Bf"], "ot_Bf", j, c1s[0], c2[0], pf["PR"][j], pf["PI"][j], g0, ALU.subtract)
                gen(tmp["Vf"], "ot_Vf", j, c1s[0], c2[0], pf["PR"][j + 1], pf["PI"][j + 1], g0, ALU.subtract)
                gen(tmp["Bb"], "ot_Bb", j, c1s[1], c2[1], pb_["NR"][j], pb_["NI"][j], g0, ALU.subtract)
                gen(tmp["Vb"], "ot_Vb", j, c1s[1], c2[1], pb_["PR"][T2 - j], pb_["PI"][T2 - j], g0, ALU.subtract)
            V.cp(Vt[:, :, 0, :], tmp["Vf"][:].rearrange("p g t c -> p g (t c)"), ["ot_Vf"], [Vn])
            V.cp(Vt[:, :, 1, :], tmp["Vb"][:].rearrange("p g t c -> p g (t c)"), ["ot_Vb"], [Vn])
            for gi in range(GB):
                g = g0 + gi
                for vi, (src, srcn) in enumerate((("WTf", "ot_WTf"), ("WTb", "ot_WTb"))):
                    for a in range(2):
                        bank = (vi * 2 + a) % 2
                        S.op("pe", lambda src=src, gi=gi, a=a, bank=bank: nc.tensor.transpose(
                            out=P[bank][:, 0:128],
                            in_=tmp[src][:, gi, a * 8:(a + 1) * 8, :].rearrange("p t c -> p (t c)"),
                            identity=ident[:]), reads=[srcn, "ident"], writes=[f"ps{bank}"])
                        S.op("act", lambda Wt=Wt, gi=gi, vi=vi, a=a, bank=bank: nc.scalar.copy(
                            out=Wt[:, gi, 2 * vi, a, :], in_=P[bank][:, 0:128]), reads=[f"ps{bank}"], writes=[Wn])
                        V.cp(Wt[:, gi, 2 * vi + 1, a, 0:64], P[bank][:, 64:128], [f"ps{bank}"], [Wn])
                        V.cp(Wt[:, gi, 2 * vi + 1, a, 64:128], P[bank][:, 0:64], [f"ps{bank}"], [Wn])
                for a in range(2):
                    for b in range(2):
                        asl = slice(a * 8, (a + 1) * 8)
                        bsl = slice(b * 8, (b + 1) * 8)
                        dst = Mt[:, gi, a, b * 128:(b + 1) * 128]
                        if (a, b) != (1, 0):
                            S.op("pe", lambda gi=gi, asl=asl, bsl=bsl: nc.tensor.matmul(
                                out=P[2][:, 0:128], lhsT=tmp["Af"][:, gi, asl, :].rearrange("p t c -> p (t c)"),
                                rhs=tmp["Bf"][:, gi, bsl, :].rearrange("p t c -> p (t c)"), start=True, stop=True),
                                reads=["ot_Af", "ot_Bf"], writes=["ps2"])
                        if (a, b) != (0, 1):
                            S.op("pe", lambda gi=gi, asl=asl, bsl=bsl: nc.tensor.matmul(
                                out=P[3][:, 0:128], lhsT=tmp["WTb"][:, gi, asl, :].rearrange("p t c -> p (t c)"),
                                rhs=tmp["Bb"][:, gi, bsl, :].rearrange("p t c -> p (t c)"), start=True, stop=True),
                                reads=["ot_WTb", "ot_Bb"], writes=["ps3"])
                        if a == b:
                            V.tt(m1[:], P[2][:, 0:128], mf[:], ALU.mult, ["ps2", "o_mf"], ["o_m1"])
                            V.tt(m2[:], P[3][:, 0:128], mb[:], ALU.mult, ["ps3", "o_mb"], ["o_m2"])
                            V.tt(m1[:], m1[:], m2[:], ALU.add, ["o_m1", "o_m2"], ["o_m1"])
                            V.stt(dst, ident[:], dcol[:, g:g + 1], m1[:], ALU.mult, ALU.add,
                                  ["ident", "o_dcol", "o_m1"], [Mn])
                        elif (a, b) == (0, 1):
                            V.cp(dst, P[2][:, 0:128], ["ps2"], [Mn])
                        else:
                            V.cp(dst, P[3][:, 0:128], ["ps3"], [Mn])
            G0 = h * GH + g0
            S.dma("sp", opW[G0:G0 + GB].rearrange("g p n -> p g n"), Wt[:].rearrange("p g v a n -> p g (v a n)"),
                  reads=[Wn])
            S.dma("sp", opV[G0:G0 + GB].rearrange("g p n -> p g n"), Vt[:].rearrange("p g d n -> p g (d n)"),
                  reads=[Vn])
            S.dma("sp", opM[G0:G0 + GB].rearrange("g p n -> p g n"), Mt[:].rearrange("p g a n -> p g (a n)"),
                  reads=[Mn])
        S.barrier()
    return prm


def s5f_host_inputs(lam_re, lam_im, log_dt, b_re, b_im, c_re, c_im, d_skip, swap):
    o = [1, 0] if swap else [0, 1]
    def pg(a):
        a = a[o].transpose(0, 2, 1)
        return np.ascontiguousarray(np.concatenate([a, a], axis=1))
    lr, li = pg(lam_re), pg(lam_im)
    ldt = np.ascontiguousarray(np.broadcast_to(log_dt[o][:, None, :], (2, 128, NG)))
    br = b_re[o].transpose(0, 2, 1, 3)
    bi = b_im[o].transpose(0, 2, 1, 3)
    cr = c_re[o].transpose(0, 3, 1, 2)
    ci = c_im[o].transpose(0, 3, 1, 2)
    st = lambda a, b: np.ascontiguousarray(np.concatenate([a, b], axis=1))
    sg = np.concatenate([-np.ones((64, 1), np.float32), np.ones((64, 1), np.float32)], 0)
    dcol = np.ascontiguousarray(np.tile(d_skip.reshape(NG, GC).T, (8, 1)))
    tt = np.arange(128) // GC
    maskf = (tt[None, :] >= tt[:, None]).astype(np.float32)
    maskb = (tt[None, :] <= tt[:, None]).astype(np.float32)
    return {"s5_lam_re": lr, "s5_lam_im": li, "s5_logdt": ldt, "s5_b1": st(br, bi), "s5_b2": st(bi, br),
            "s5_c1": st(cr, ci), "s5_c2": st(ci, cr), "s5_sg": sg, "s5_dcol": dcol, "s5_maskf": maskf,
            "s5_maskb": maskb}


S5F_SHAPES = {"s5_lam_re": [2, 128, NG], "s5_lam_im": [2, 128, NG], "s5_logdt": [2, 128, NG],
              "s5_b1": [2, 128, NG, GC], "s5_b2": [2, 128, NG, GC], "s5_c1": [2, 128, NG, GC],
              "s5_c2": [2, 128, NG, GC], "s5_sg": [128, 1], "s5_dcol": [128, NG], "s5_maskf": [128, 128],
              "s5_maskb": [128, 128]}


def emit_s5f(nc, S, P, ident, ex, x_ap, ys_ap, NK, NKF):
    din = {k[3:]: ex[k] for k in S5F_SHAPES}
    for k in ("lam_re", "lam_im", "logdt", "b1", "b2", "c1", "c2"):
        din[k] = [din[k][d] for d in range(2)]
    opW = nc.dram_tensor("s5_opW", [NG, 128, 4 * 2 * 128], BF16, kind="Internal").ap()
    opV = nc.dram_tensor("s5_opV", [NG, 128, 2 * T2 * GC], BF16, kind="Internal").ap()
    opM = nc.dram_tensor("s5_opM", [NG, 128, 2 * T2 * GC], BF16, kind="Internal").ap()
    import os
    f_ops = emit_s5f_ops_old if os.environ.get("S5_OPS") == "old" else emit_s5f_ops
    f_main = emit_s5f_main_old if os.environ.get("S5_MAIN") == "old" else emit_s5f_main
    for h in range(2):
        f_ops(nc, S, P, ident, din, opW, opV, opM, h)
    for h in range(2):
        f_main(nc, S, P, ident, din, x_ap, ys_ap, opW, opV, opM, h, NK, NKF)


LSEQ = 4096
NOWN = 2048
NHAL = 2304


def natten_tables_f(rpb, mirrored):
    H = rpb.shape[0]
    ROWS = LSEQ // GW
    tab = np.full((H, 5, 9, GW, GW), NEG, np.float32)
    qc_l = np.arange(GW)
    kc_l = np.arange(GW)
    qc_t = (GW - 1 - qc_l) if mirrored else qc_l
    kc_t = (GW - 1 - kc_l) if mirrored else kc_l
    cs = np.clip(qc_t - 8, 0, GW - 16)
    inwin = (kc_t[:, None] >= cs[None, :]) & (kc_t[:, None] < cs[None, :] + 16)
    coff = np.clip(kc_t[:, None] - qc_t[None, :] + 15, 0, 30)
    for var in range(5):
        i = var
        rows = list(range(9)) if i < 4 else list(range(i - 4, i + 5))
        r_t = (ROWS - 1 - i) if mirrored else i
        rs = int(np.clip(r_t - 4, 0, ROWS - 8))
        for kr in range(9):
            k_t = (ROWS - 1 - rows[kr]) if mirrored else rows[kr]
            if not (rs <= k_t < rs + 8):
                continue
            ro = k_t - r_t + 7
            vals = rpb[:, ro][:, coff]
            tab[:, var, kr] = np.where(inwin[None], vals, np.float32(NEG))
    return np.ascontiguousarray(tab.transpose(0, 3, 1, 2, 4))


def build_fused():
    nc = bass.Bass("TRN2", target_bir_lowering=False)
    def inp(name, shape):
        return nc.dram_tensor(name, shape, F32, kind="ExternalInput").ap()
    x = inp("x", [LSEQ, D])
    W = [(inp(f"wg{k}", [D, FF]), inp(f"wu{k}", [D, FF]), inp(f"wd{k}", [FF, D])) for k in range(4)]
    lng = inp("lng", [6, 128, D])
    lnb = inp("lnb", [6, 128, D])
    idn = inp("idn", [128, 128])
    ex = {k: inp(k, shp) for k, shp in S5F_SHAPES.items()}
    glu_wv, glu_wg = inp("glu_wv", [D, D]), inp("glu_wg", [D, D])
    na_wqkv, na_wo = inp("na_wqkv", [D, 3 * D]), inp("na_wo", [D, D])
    na_tab = inp("na_tab", [NH, GW, 5, 9, GW])
    y = nc.dram_tensor("y", [NOWN, D], F32, kind="ExternalOutput").ap()
    def scr(name, n):
        return nc.dram_tensor(name, [n, D], F32, kind="Internal").ap()
    a1, ys, a2, a3, a4, a5 = (scr("a1", LSEQ), scr("ys", NHAL), scr("a2", NHAL), scr("a3", NHAL),
                              scr("a4", NHAL), scr("a5", NOWN))
    with ExitStack() as es:
        es.enter_context(nc.allow_low_precision("bf16 matmul operands, fp32 accumulation"))
        es.enter_context(nc.allow_non_contiguous_dma("weight block loads"))
        S = Sched(nc, es)
        P = [es.enter_context(nc.psum_tensor(f"ps{i}", [128, 512], F32)) for i in range(8)]
        ident = _sb(nc, es, "ident", [128, 128], F32)
        S.dma("sp", ident[:], idn, writes=["ident"])

        def stage(idx, fn):
            with ExitStack() as es2:
                g_rep = _sb(nc, es2, "g_rep", [128, D], F32)
                b_rep = _sb(nc, es2, "b_rep", [128, D], F32)
                S.dma("sp", g_rep[:], lng[idx], writes=["lng"])
                S.dma("sp", b_rep[:], lnb[idx], writes=["lnb"])
                fn(es2, g_rep, b_rep)
                S.barrier()

        zscr = nc.dram_tensor("ffn_zscr", [LSEQ, D], F32, kind="Internal").ap()
        T4096 = [(i * 768, 768) for i in range(5)] + [(3840, 256)]
        T2304 = [(i * 768, 768) for i in range(3)]
        T2048 = [(0, 768), (768, 768), (1536, 512)]
        stage(0, lambda e, g, b: emit_ffn2(nc, S, e, P, x, a1, zscr, *W[0], g, b, ident, LSEQ, "fa", T4096))
        emit_s5f(nc, S, P, ident, ex, a1, ys, LSEQ // T2, NHAL // T2)
        S.barrier()
        stage(1, lambda e, g, b: emit_glu(nc, S, e, P, a1[0:NHAL, :], ys, a2, glu_wv, glu_wg, g, b, ident, NHAL))
        stage(2, lambda e, g, b: emit_ffn2(nc, S, e, P, a2, a3, zscr, *W[1], g, b, ident, NHAL, "fb", T2304))
        stage(3, lambda e, g, b: emit_ffn2(nc, S, e, P, a3, a4, zscr, *W[2], g, b, ident, NHAL, "fc", T2304))
        stage(4, lambda e, g, b: emit_natten_f(nc, S, e, P, a4, a5, na_wqkv, na_wo, na_tab, g, b, ident))
        stage(5, lambda e, g, b: emit_ffn2(nc, S, e, P, a5, y, zscr, *W[3], g, b, ident, NOWN, "fd", T2048))
        S.finish()
    return nc


def kernel(x, ffn_w_gate, ffn_w_up, ffn_w_down, ln_g, ln_b, s5_lam_re, s5_lam_im, s5_log_dt, s5_b_re, s5_b_im,
           s5_c_re, s5_c_im, s5_d, s5_w_glu_val, s5_w_glu_gate, na_w_qkv, na_rpb, na_w_out):
    f = lambda a: np.ascontiguousarray(np.asarray(a, np.float32))
    B, L, _ = x.shape
    xf = f(x)
    common = {"idn": np.eye(128, dtype=np.float32)}
    for k, (i, j) in enumerate(((0, 0), (0, 1), (1, 0), (1, 1))):
        common[f"wg{k}"] = f(ffn_w_gate[i, j])
        common[f"wu{k}"] = f(ffn_w_up[i, j])
        common[f"wd{k}"] = f(ffn_w_down[i, j])
    lg = f(ln_g).reshape(6, D)
    lb = f(ln_b).reshape(6, D)
    common["lng"] = np.ascontiguousarray(np.broadcast_to(lg[:, None, :], (6, 128, D)))
    common["lnb"] = np.ascontiguousarray(np.broadcast_to(lb[:, None, :], (6, 128, D)))
    common["glu_wv"] = f(s5_w_glu_val[0])
    common["glu_wg"] = f(s5_w_glu_gate[0])
    common["na_wqkv"] = f(na_w_qkv[0])
    common["na_wo"] = f(na_w_out[0])
    s5p = [s5f_host_inputs(f(s5_lam_re[0]), f(s5_lam_im[0]), f(s5_log_dt[0]), f(s5_b_re[0]), f(s5_b_im[0]),
                           f(s5_c_re[0]), f(s5_c_im[0]), f(s5_d[0]), swap) for swap in (False, True)]
    tabs = [natten_tables_f(f(na_rpb[0]), m) for m in (False, True)]
    in_maps = []
    for c in range(NCORES):
        b, hf = c // 2, c % 2
        xl = xf[b] if hf == 0 else np.ascontiguousarray(xf[b][::-1])
        in_maps.append(dict(common, x=xl, na_tab=tabs[hf], **s5p[hf]))
    nc = build_fused()
    res = run_bass_kernel_spmd(nc, in_maps, core_ids=list(range(NCORES)))
    out = np.empty((B, L, D), np.float32)
    for c in range(NCORES):
        b, hf = c // 2, c % 2
        yc = res.results[c]["y"]
        if hf == 0:
            out[b, :NOWN] = yc
        else:
            out[b, NOWN:] = yc[::-1]
    return out


def emit_ffn2(nc, S, es, P, x_ap, y_ap, zscr, wg, wu, wd, g_rep, b_rep, ident, NT, tag, tiles):
    TM = max(tt for _, tt in tiles)
    FW = 128
    DW = 256
    xs = [_sb(nc, es, f"{tag}_xs{i}", [128, D], F32) for i in range(2)]
    xTw = _sb(nc, es, f"{tag}_xTw", [128, NDC * TM], BF16)
    assert NDC * TM >= NFC * DW
    xT = xTw[:, :].rearrange("p (c t) -> p c t", c=NDC)
    wd0 = xTw[:, 0:NFC * DW].rearrange("p (j d) -> p j d", j=NFC)
    wd1t = _sb(nc, es, f"{tag}_wd1", [128, NFC, DW], BF16)
    wdb = [(wd0, f"{tag}_xTw"), (wd1t[:], f"{tag}_wd1")]
    wgb = [_sb(nc, es, f"{tag}_wg{i}", [128, NDC, FW], BF16) for i in range(2)]
    wub = [_sb(nc, es, f"{tag}_wu{i}", [128, NDC, FW], BF16) for i in range(2)]
    hT = _sb(nc, es, f"{tag}_hT", [128, NFC, TM], BF16)
    sg = [_sb(nc, es, f"{tag}_sg{i}", [128, 512], F32) for i in range(2)]
    stg = [_sb(nc, es, f"{tag}_stg{i}", [128, DW], F32) for i in range(2)]
    z = _sb(nc, es, f"{tag}_z", [128, D], F32)
    st = _sb(nc, es, f"{tag}_st", [128, 4, nc.vector.BN_STATS_DIM], F32)
    mv = _sb(nc, es, f"{tag}_mv", [128, nc.vector.BN_AGGR_DIM], F32)
    rs = _sb(nc, es, f"{tag}_rs", [128, 1], F32)
    rxT = f"{tag}_xTw"
    wg_v = wg.rearrange("(c p) f -> p c f", p=128)
    wu_v = wu.rearrange("(c p) f -> p c f", p=128)
    wd_v = wd.rearrange("(c p) d -> p c d", p=128)
    cnt = {"xi": 0}
    pending = []

    def ln_one(t0, s):
        xb, rx = xs[cnt["xi"] % 2], f"{tag}_xs{cnt['xi'] % 2}"
        cnt["xi"] += 1
        S.dma("sp", xb[:], x_ap[t0 + s * 128:t0 + (s + 1) * 128, :], writes=[rx])
        S.dma("sp", z[:], zscr[t0 + s * 128:t0 + (s + 1) * 128, :],
              reads=[f"{tag}_zscr{s}_{q}" for q in range(D // DW)], writes=[f"{tag}_z"])
        S.op("act", lambda xb=xb: nc.scalar.mul(out=xb[:], in_=xb[:], mul=ALPHA), reads=[rx], writes=[rx])
        S.op("dve", lambda xb=xb: nc.vector.scalar_tensor_tensor(
            out=z[:], in0=z[:], scalar=0.5, in1=xb[:], op0=ALU.mult, op1=ALU.add),
            reads=[f"{tag}_z", rx], writes=[f"{tag}_z"])
        emit_ln(nc, S, z, f"{tag}_z", st, mv, rs, f"{tag}_ln", g_rep, b_rep)
        S.dma("sp", y_ap[t0 + s * 128:t0 + (s + 1) * 128, :], z[:], reads=[f"{tag}_z"])

    wi = di = si = 0
    for (t0, tt) in tiles:
        NS = tt // 128
        chunks = [(o, min(512, tt - o)) for o in range(0, tt, 512)]
        for s in range(NS):
            xb, rx = xs[cnt["xi"] % 2], f"{tag}_xs{cnt['xi'] % 2}"
            cnt["xi"] += 1
            S.dma("sp", xb[:], x_ap[t0 + s * 128:t0 + (s + 1) * 128, :], writes=[rx])
            for cq in range(NDC // 4):
                bank = 4 + (cq % 4)
                for k in range(4):
                    c = cq * 4 + k
                    S.op("pe", lambda c=c, k=k, bank=bank, xb=xb: nc.tensor.transpose(
                        out=P[bank][:, k * 128:(k + 1) * 128], in_=xb[:, c * 128:(c + 1) * 128],
                        identity=ident[:]), reads=[rx, "ident"], writes=[f"ps{bank}"], signal=(k == 3))
                dst = xT[:, cq * 4:(cq + 1) * 4, s * 128:(s + 1) * 128]
                src = P[bank][:].rearrange("p (k n) -> p k n", k=4)
                if cq % 2:
                    S.op("act", lambda dst=dst, src=src: nc.scalar.copy(out=dst, in_=src),
                         reads=[f"ps{bank}"], writes=[rxT])
                else:
                    S.op("dve", lambda dst=dst, src=src: nc.vector.tensor_copy(out=dst, in_=src),
                         reads=[f"ps{bank}"], writes=[rxT])
        for j in range(NFC):
            if pending and j % 5 == 2:
                ln_one(*pending.pop(0))
            wgt, wut = wgb[wi % 2], wub[wi % 2]
            rwg, rwu = f"{tag}_wg{wi % 2}", f"{tag}_wu{wi % 2}"
            pj = wi % 2
            wi += 1
            S.dma("pool", wgt[:], wg_v[:, :, j * FW:(j + 1) * FW], writes=[rwg])
            S.dma("pool", wut[:], wu_v[:, :, j * FW:(j + 1) * FW], writes=[rwu])
            for hi, (o, w) in enumerate(chunks):
                pg, pu = pj * 4 + hi * 2, pj * 4 + hi * 2 + 1
                sgt, rsg = sg[si % 2], f"{tag}_sg{si % 2}"
                si += 1
                for (wt, rw, bank) in ((wgt, rwg, pg), (wut, rwu, pu)):
                    for c in range(NDC):
                        S.op("pe", lambda c=c, wt=wt, bank=bank, o=o, w=w: nc.tensor.matmul(
                            out=P[bank][:, 0:w], lhsT=wt[:, c, :], rhs=xT[:, c, o:o + w],
                            start=(c == 0), stop=(c == NDC - 1)),
                            reads=[rw, rxT], writes=[f"ps{bank}"], signal=(c == NDC - 1))
                S.op("act", lambda pg=pg, sgt=sgt, w=w: nc.scalar.activation(
                    out=sgt[:, 0:w], in_=P[pg][:, 0:w], func=AF.Silu), reads=[f"ps{pg}"], writes=[rsg])
                S.op("dve", lambda j=j, pu=pu, sgt=sgt, o=o, w=w: nc.vector.tensor_tensor(
                    out=hT[:, j, o:o + w], in0=sgt[:, 0:w], in1=P[pu][:, 0:w], op=ALU.mult),
                    reads=[rsg, f"ps{pu}"], writes=[f"{tag}_hT"])
        oi = 0
        for db in range(D // DW):
            wdt, rwd = wdb[di % 2]
            di += 1
            S.dma("pool", wdt, wd_v[:, :, db * DW:(db + 1) * DW], writes=[rwd])
            for s in range(NS):
                bank = oi % 8
                sgt_, rst_ = stg[oi % 2], f"{tag}_stg{oi % 2}"
                oi += 1
                for j in range(NFC):
                    S.op("pe", lambda j=j, s=s, bank=bank, wdt=wdt: nc.tensor.matmul(
                        out=P[bank][:, 0:DW], lhsT=hT[:, j, s * 128:(s + 1) * 128], rhs=wdt[:, j, :],
                        start=(j == 0), stop=(j == NFC - 1)),
                        reads=[f"{tag}_hT", rwd], writes=[f"ps{bank}"], signal=(j == NFC - 1))
                if oi % 2:
                    S.op("act", lambda sgt_=sgt_, bank=bank: nc.scalar.copy(out=sgt_[:], in_=P[bank][:, 0:DW]),
                         reads=[f"ps{bank}"], writes=[rst_])
                else:
                    S.op("dve", lambda sgt_=sgt_, bank=bank: nc.vector.tensor_copy(out=sgt_[:], in_=P[bank][:, 0:DW]),
                         reads=[f"ps{bank}"], writes=[rst_])
                S.dma("sp", zscr[t0 + s * 128:t0 + (s + 1) * 128, db * DW:(db + 1) * DW], sgt_[:],
                      reads=[rst_], writes=[f"{tag}_zscr{s}_{db}"])
        pending = [(t0, s) for s in range(NS)]
    while pending:
        ln_one(*pending.pop(0))
```
